# Optimizing a Trainium2 kernel written in Bass

```python
import math
import jax, jax.numpy as jnp
from jax import lax
import numpy as np

D_MODEL = 1024
BATCH = 4
SEQ = 8192
DEPTH = 2
DEC_BATCH = 128
DEC_SEQ = 4
PAST_LEN = 16384
PAGE_SIZE = 128

N_A = DEPTH // 2
N_B = DEPTH - N_A
EXPAND = 2
D_INNER = EXPAND * D_MODEL
SSM_HEAD_DIM = 64
SSM_HEADS = D_INNER // SSM_HEAD_DIM
SSM_GROUPS = 4
HEADS_PER_GROUP = SSM_HEADS // SSM_GROUPS
D_STATE = 128
CONV_W = 4
CONV_DIM = D_INNER + 2 * SSM_GROUPS * D_STATE
D_IN_PROJ = D_INNER + CONV_DIM + SSM_HEADS
CHUNK = 128
WINDOW = 128
HEAD_DIM = 64
N_Q_HEADS = D_MODEL // HEAD_DIM
N_KV_HEADS = 4
Q_PER_KV = N_Q_HEADS // N_KV_HEADS
D_FF = 2816
EPS = 1e-6

kernel_name = 'yoco_ssd_swa_sink_decoder_step'


def rmsnorm(x, g):
    xf = x.astype(jnp.float32)
    y = xf * lax.rsqrt(jnp.mean(xf * xf, axis=-1, keepdims=True) + EPS)
    return (y * g.astype(jnp.float32)).astype(x.dtype)


def swiglu(h, w_gate, w_up, w_down):
    return (jax.nn.silu(h @ w_gate) * (h @ w_up)) @ w_down


def causal_dwconv(xbc, buf, w, b):
    xpad = jnp.concatenate([buf.astype(xbc.dtype), xbc], axis=1)
    out = lax.conv_general_dilated(xpad, w[:, None, :].astype(xbc.dtype), window_strides=(1,),
                                   padding='VALID', dimension_numbers=('NWC', 'WIO', 'NWC'),
                                   feature_group_count=CONV_DIM)
    return out + b.astype(xbc.dtype), xpad[:, xpad.shape[1] - (CONV_W - 1):]


def ssd_scan(xs, dt, a, bm, cm, h0):
    f32 = jnp.float32
    bsz, L = xs.shape[0], xs.shape[1]
    lc = min(CHUNK, L)
    pad = (-L) % lc
    def padt(t):
        return jnp.pad(t, [(0, 0), (0, pad)] + [(0, 0)] * (t.ndim - 2))
    xs, dt, bm, cm = (padt(t.astype(f32)) for t in (xs, dt, bm, cm))
    nc = (L + pad) // lc
    xg = (xs * dt[..., None]).reshape(bsz, nc, lc, SSM_GROUPS, HEADS_PER_GROUP, SSM_HEAD_DIM)
    ag = (dt * a).reshape(bsz, nc, lc, SSM_GROUPS, HEADS_PER_GROUP)
    bg = bm.reshape(bsz, nc, lc, SSM_GROUPS, D_STATE)
    cg = cm.reshape(bsz, nc, lc, SSM_GROUPS, D_STATE)
    seq = tuple(jnp.moveaxis(t, 1, 0) for t in (xg, ag, bg, cg))
    causal = jnp.tril(jnp.ones((lc, lc), dtype=bool))

    def step(h, inp):
        xk, ak, bk, ck = inp
        acum = jnp.cumsum(ak, axis=1)
        seg = acum[:, :, None] - acum[:, None, :]
        decay = jnp.exp(jnp.where(causal[None, :, :, None, None], seg, -jnp.inf))
        cb = jnp.einsum('btgn,bsgn->btsg', ck, bk)
        y = jnp.einsum('btsg,btsgh,bsghp->btghp', cb, decay, xk)
        y = y + jnp.einsum('btgn,bghpn->btghp', ck, h) * jnp.exp(acum)[..., None]
        last = acum[:, -1]
        w_in = jnp.exp(last[:, None] - acum)
        h = h * jnp.exp(last)[..., None, None] + jnp.einsum('bsgn,bsgh,bsghp->bghpn', bk, w_in, xk)
        return h, y

    h0g = h0.astype(f32).reshape(bsz, SSM_GROUPS, HEADS_PER_GROUP, SSM_HEAD_DIM, D_STATE)
    h, ys = lax.scan(step, h0g, seq)
    y = jnp.moveaxis(ys, 0, 1).reshape(bsz, nc * lc, SSM_HEADS, SSM_HEAD_DIM)[:, :L]
    return y, h.reshape(bsz, SSM_HEADS, SSM_HEAD_DIM, D_STATE)


def mamba_mixer(h, conv_buf, ssm_state, w_in, conv_w, conv_b, dt_bias, a_log, d_skip, norm_g, w_out):
    bsz, L, _ = h.shape
    f32 = jnp.float32
    zxbcdt = h @ w_in
    z, xbc, dt_raw = jnp.split(zxbcdt, [D_INNER, D_INNER + CONV_DIM], axis=-1)
    xbc, new_buf = causal_dwconv(xbc, conv_buf, conv_w, conv_b)
    xbc = jax.nn.silu(xbc)
    xs, bm, cm = jnp.split(xbc, [D_INNER, D_INNER + SSM_GROUPS * D_STATE], axis=-1)
    xs = xs.reshape(bsz, L, SSM_HEADS, SSM_HEAD_DIM)
    bm = bm.reshape(bsz, L, SSM_GROUPS, D_STATE)
    cm = cm.reshape(bsz, L, SSM_GROUPS, D_STATE)
    dt = jax.nn.softplus(dt_raw.astype(f32) + dt_bias.astype(f32))
    a = -jnp.exp(a_log.astype(f32))
    y, new_state = ssd_scan(xs, dt, a, bm, cm, ssm_state)
    y = y + d_skip.astype(f32)[:, None] * xs.astype(f32)
    y = y.reshape(bsz, L, D_INNER) * jax.nn.silu(z.astype(f32))
    yg = y.reshape(bsz, L, SSM_GROUPS, D_INNER // SSM_GROUPS)
    yg = yg * lax.rsqrt(jnp.mean(yg * yg, axis=-1, keepdims=True) + EPS)
    y = (yg.reshape(bsz, L, D_INNER) * norm_g.astype(f32)).astype(h.dtype)
    return y @ w_out, new_buf, new_state.astype(ssm_state.dtype)


def shared_kv(x, kv_norm, w_kv, b_kv):
    bsz, L, _ = x.shape
    kv = rmsnorm(x, kv_norm) @ w_kv + b_kv
    k, v = jnp.split(kv, 2, axis=-1)
    return (k.reshape(bsz, L, N_KV_HEADS, HEAD_DIM), v.reshape(bsz, L, N_KV_HEADS, HEAD_DIM))


def sink_attention(q, k, v, qpos, kpos, sinks):
    s = jnp.einsum('...qgrd,...kgd->...grqk', q, k).astype(jnp.float32) * (HEAD_DIM ** -0.5)
    rel = qpos[..., :, None] - kpos[..., None, :]
    mask = (rel >= 0) & (rel < WINDOW) & (kpos[..., None, :] >= 0)
    s = jnp.where(mask[..., None, None, :, :], s, -jnp.inf)
    sink = sinks.astype(jnp.float32).reshape(N_KV_HEADS, Q_PER_KV)[:, :, None, None]
    m = jnp.maximum(jnp.max(s, axis=-1, keepdims=True), sink)
    p = jnp.exp(s - m)
    denom = jnp.sum(p, axis=-1, keepdims=True) + jnp.exp(sink - m)
    return jnp.einsum('...grqk,...kgd->...qgrd', (p / denom).astype(v.dtype), v)


def window_attention_prompt(q, k, v, sinks):
    bsz, S = q.shape[0], q.shape[1]
    nb = S // WINDOW
    qb = q.reshape(bsz, nb, WINDOW, N_KV_HEADS, Q_PER_KV, HEAD_DIM)
    def band(t):
        tb = t.reshape(bsz, nb, WINDOW, N_KV_HEADS, HEAD_DIM)
        prev = jnp.pad(tb, ((0, 0), (1, 0), (0, 0), (0, 0), (0, 0)))[:, :-1]
        return jnp.concatenate([prev, tb], axis=2)
    pos = jnp.arange(S, dtype=jnp.int32).reshape(nb, WINDOW)
    kpos = jnp.concatenate([pos - WINDOW, pos], axis=1)
    o = sink_attention(qb, band(k), band(v), pos, kpos, sinks)
    return o.reshape(bsz, S, N_Q_HEADS * HEAD_DIM)


def window_attention_sample(q, kk, vv, sinks):
    bsz, T = q.shape[0], q.shape[1]
    qpos = PAST_LEN + jnp.arange(T, dtype=jnp.int32)
    kpos = PAST_LEN - WINDOW + jnp.arange(WINDOW + T, dtype=jnp.int32)
    o = sink_attention(q, kk, vv, qpos, kpos, sinks)
    return o.reshape(bsz, T, N_Q_HEADS * HEAD_DIM)


def trunk(x, conv_bufs, ssm_states, k_win, v_win, p):
    bsz, L, _ = x.shape
    is_prompt = k_win is None
    new_conv, new_ssm = [], []
    k = v = k_state = v_state = None
    for layer in range(DEPTH):
        g = p['norm_gain'][layer]
        x = x + 0.5 * swiglu(rmsnorm(x, g[0]), p['ffn_w_gate'][layer, 0], p['ffn_w_up'][layer, 0], p['ffn_w_down'][layer, 0])
        hn = rmsnorm(x, g[1])
        if layer < N_A:
            i = layer
            mix, cb, ss = mamba_mixer(hn, conv_bufs[i], ssm_states[i], p['ssm_w_in'][i], p['ssm_conv_w'][i],
                                      p['ssm_conv_b'][i], p['ssm_dt_bias'][i], p['ssm_a_log'][i],
                                      p['ssm_d'][i], p['ssm_norm'][i], p['ssm_w_out'][i])
            new_conv.append(cb)
            new_ssm.append(ss)
        else:
            j = layer - N_A
            q = (hn @ p['attn_w_q'][j] + p['attn_b_q'][j]).reshape(bsz, L, N_KV_HEADS, Q_PER_KV, HEAD_DIM)
            if is_prompt:
                o = window_attention_prompt(q, k, v, p['attn_sinks'][j])
            else:
                o = window_attention_sample(q, k, v, p['attn_sinks'][j])
            mix = o @ p['attn_w_o'][j] + p['attn_b_o'][j]
        x = x + mix
        x = x + 0.5 * swiglu(rmsnorm(x, g[2]), p['ffn_w_gate'][layer, 1], p['ffn_w_up'][layer, 1], p['ffn_w_down'][layer, 1])
        if layer == N_A - 1:
            k, v = shared_kv(x, p['kv_norm'], p['attn_w_kv'], p['attn_b_kv'])
            if not is_prompt:
                k = jnp.concatenate([k_win.astype(k.dtype), k], axis=1)
                v = jnp.concatenate([v_win.astype(v.dtype), v], axis=1)
            k_state = k[:, k.shape[1] - WINDOW:]
            v_state = v[:, v.shape[1] - WINDOW:]
    y = rmsnorm(x, p['final_norm'])
    return y, jnp.stack(new_conv), jnp.stack(new_ssm), k_state, v_state


def setup_inputs(seed: int = 0) -> dict:
    key = jax.random.key(seed)
    ks = iter(jax.random.split(key, 40))
    f32 = jnp.float32
    def nrm(shape, scale):
        return jax.random.normal(next(ks), shape, f32) * scale
    dt0 = jnp.exp(jax.random.uniform(next(ks), (N_A, SSM_HEADS), f32, math.log(1e-3), math.log(1e-1)))
    return {
        'x_prompt': nrm((BATCH, SEQ, D_MODEL), 1.0),
        'x_sample': nrm((DEC_BATCH, DEC_SEQ, D_MODEL), 1.0),
        'state_conv': nrm((N_A, DEC_BATCH, CONV_W - 1, CONV_DIM), 0.5),
        'state_ssm': nrm((N_A, DEC_BATCH, SSM_HEADS, SSM_HEAD_DIM, D_STATE), 0.1),
        'cache_k_win': nrm((DEC_BATCH, WINDOW, N_KV_HEADS, HEAD_DIM), 1.0),
        'cache_v_win': nrm((DEC_BATCH, WINDOW, N_KV_HEADS, HEAD_DIM), 1.0),
        'norm_gain': 1.0 + nrm((DEPTH, 3, D_MODEL), 0.02),
        'ffn_w_gate': nrm((DEPTH, 2, D_MODEL, D_FF), D_MODEL ** -0.5),
        'ffn_w_up': nrm((DEPTH, 2, D_MODEL, D_FF), D_MODEL ** -0.5),
        'ffn_w_down': nrm((DEPTH, 2, D_FF, D_MODEL), D_FF ** -0.5),
        'ssm_w_in': nrm((N_A, D_MODEL, D_IN_PROJ), D_MODEL ** -0.5),
        'ssm_conv_w': nrm((N_A, CONV_W, CONV_DIM), CONV_W ** -0.5),
        'ssm_conv_b': nrm((N_A, CONV_DIM), 0.02),
        'ssm_dt_bias': dt0 + jnp.log(-jnp.expm1(-dt0)),
        'ssm_a_log': jnp.log(jax.random.uniform(next(ks), (N_A, SSM_HEADS), f32, 1.0, 16.0)),
        'ssm_d': 1.0 + nrm((N_A, SSM_HEADS), 0.02),
        'ssm_norm': 1.0 + nrm((N_A, D_INNER), 0.02),
        'ssm_w_out': nrm((N_A, D_INNER, D_MODEL), D_INNER ** -0.5),
        'kv_norm': 1.0 + nrm((D_MODEL,), 0.02),
        'attn_w_kv': nrm((D_MODEL, 2 * N_KV_HEADS * HEAD_DIM), D_MODEL ** -0.5),
        'attn_b_kv': nrm((2 * N_KV_HEADS * HEAD_DIM,), 0.02),
        'attn_w_q': nrm((N_B, D_MODEL, N_Q_HEADS * HEAD_DIM), D_MODEL ** -0.5),
        'attn_b_q': nrm((N_B, N_Q_HEADS * HEAD_DIM), 0.02),
        'attn_sinks': nrm((N_B, N_Q_HEADS), 0.5),
        'attn_w_o': nrm((N_B, N_Q_HEADS * HEAD_DIM, D_MODEL), (N_Q_HEADS * HEAD_DIM) ** -0.5),
        'attn_b_o': nrm((N_B, D_MODEL), 0.02),
        'final_norm': 1.0 + nrm((D_MODEL,), 0.02),
    }


def reference(x_prompt, x_sample, state_conv, state_ssm, cache_k_win, cache_v_win, norm_gain,
              ffn_w_gate, ffn_w_up, ffn_w_down, ssm_w_in, ssm_conv_w, ssm_conv_b, ssm_dt_bias,
              ssm_a_log, ssm_d, ssm_norm, ssm_w_out, kv_norm, attn_w_kv, attn_b_kv, attn_w_q,
              attn_b_q, attn_sinks, attn_w_o, attn_b_o, final_norm):
    p = {'norm_gain': norm_gain, 'ffn_w_gate': ffn_w_gate, 'ffn_w_up': ffn_w_up, 'ffn_w_down': ffn_w_down,
         'ssm_w_in': ssm_w_in, 'ssm_conv_w': ssm_conv_w, 'ssm_conv_b': ssm_conv_b,
         'ssm_dt_bias': ssm_dt_bias, 'ssm_a_log': ssm_a_log, 'ssm_d': ssm_d, 'ssm_norm': ssm_norm,
         'ssm_w_out': ssm_w_out, 'kv_norm': kv_norm, 'attn_w_kv': attn_w_kv, 'attn_b_kv': attn_b_kv,
         'attn_w_q': attn_w_q, 'attn_b_q': attn_b_q, 'attn_sinks': attn_sinks, 'attn_w_o': attn_w_o,
         'attn_b_o': attn_b_o, 'final_norm': final_norm}
    bsz = x_prompt.shape[0]
    conv0 = jnp.zeros((N_A, bsz, CONV_W - 1, CONV_DIM), x_prompt.dtype)
    ssm0 = jnp.zeros((N_A, bsz, SSM_HEADS, SSM_HEAD_DIM, D_STATE), jnp.float32)
    y_prompt, conv_p, ssm_p, k_p, v_p = trunk(x_prompt, conv0, ssm0, None, None, p)
    y_sample, conv_s, ssm_s, k_s, v_s = trunk(x_sample, state_conv, state_ssm, cache_k_win, cache_v_win, p)
    return (y_prompt, y_sample, conv_p, ssm_p, k_p, v_p, conv_s, ssm_s, k_s, v_s)
```

```python
import os
import numpy as np
import ml_dtypes
from contextlib import ExitStack
import concourse.bass as bass
import concourse.mybir as mybir
from concourse.bass_utils import run_bass_kernel_spmd

F32, BF16 = mybir.dt.float32, mybir.dt.bfloat16
AF = mybir.ActivationFunctionType
ALU = mybir.AluOpType
AX = mybir.AxisListType

D = 1024
DFF = 2816
NF = 22
DIN = 2048
EPS = 1e-6
NEG = -30000.0
WSLOT = 4096
NWS = 3

QH = [[c, c + 4] if c < 4 else [c + 4, c + 8] for c in range(8)]


class Op:
    __slots__ = ("eng", "fn", "deps", "dma", "signal", "value")

    def __init__(self, eng, fn, deps, dma):
        self.eng, self.fn, self.deps, self.dma = eng, fn, deps, dma
        self.signal = False
        self.value = 0


class Sched:
    def __init__(self):
        self.ops = []
        self.lw = {}
        self.rd = {}

    def op(self, eng, fn, reads=(), writes=(), dma=None):
        idx = len(self.ops)
        deps = {}
        for k in reads:
            w = self.lw.get(k)
            if w is not None:
                deps[w] = 2
        for k in writes:
            w = self.lw.get(k)
            if w is not None:
                deps.setdefault(w, 1)
            r = self.rd.get(k)
            if r:
                for e_, i_ in r[0].items():
                    deps.setdefault(i_, 1)
                for i_ in r[1]:
                    deps.setdefault(i_, 1)
        for k in writes:
            self.lw[k] = idx
            self.rd[k] = [{}, []]
        for k in reads:
            r = self.rd.setdefault(k, [{}, []])
            if dma is None:
                r[0][eng] = idx
            else:
                r[1].append(idx)
        self.ops.append(Op(eng, fn, deps, dma))
        return idx

    def finalize(self):
        ops = self.ops
        for o in ops:
            need = []
            for d, kind in o.deps.items():
                p = ops[d]
                if p.dma is not None:
                    need.append(d)
                elif o.dma is None and p.eng == o.eng:
                    if o.eng != "pe" and kind == 2:
                        need.append(d)
                else:
                    need.append(d)
            o.deps = need
            for d in need:
                ops[d].signal = True
        cnt = {}
        for o in ops:
            if o.dma is not None:
                s, n = o.dma
                cnt[s] = cnt.get(s, 0) + 16 * n
                o.value = cnt[s]
            elif o.signal:
                cnt[o.eng] = cnt.get(o.eng, 0) + 1
                o.value = cnt[o.eng]
        self.semnames = sorted(cnt.keys())

    def emit_engine(self, engname, e, semh):
        ops = self.ops
        waited = {}
        for o in ops:
            if o.eng != engname:
                continue
            for d in o.deps:
                p = ops[d]
                s = p.dma[0] if p.dma is not None else p.eng
                if waited.get(s, 0) < p.value:
                    e.wait_ge(semh[s], p.value)
                    waited[s] = p.value
            ins = o.fn(e)
            if o.dma is not None:
                assert len(ins) == o.dma[1], (len(ins), o.dma)
                for i_ in ins:
                    i_.then_inc(semh[o.dma[0]], 16)
            elif o.signal:
                ins[-1].then_inc(semh[engname], 1)


class Rot:
    def __init__(self, tensors, name):
        self.t = tensors
        self.n = len(tensors)
        self.i = 0
        self.name = name

    def next(self):
        j = self.i % self.n
        self.i += 1
        return self.t[j], f"{self.name}{j}"


def subtiles(T):
    out = []
    c = 0
    while c < T:
        n = min(512, T - c)
        out.append((c, c + n))
        c += n
    return out


class Tile:
    def __init__(self, kind, src, col0, nch, sample=False, first=False, last=False, out0=0):
        self.kind = kind
        self.src = src
        self.col0 = col0
        self.nch = nch
        self.Tp = nch * 128
        self.sample = sample
        self.T = self.Tp + (64 if sample else 0)
        self.first = first
        self.last = last
        self.out0 = out0
        self.subs = subtiles(self.Tp) + ([(self.Tp, self.Tp + 64)] if sample else [])


def make_tiles(npre, nmain, tch=4):
    def split(n):
        k = -(-n // tch)
        base, rem = divmod(n, k)
        return [base + 1] * rem + [base] * (k - rem)
    tiles = []
    sp = split(npre)
    c = 0
    for i, n in enumerate(sp):
        tiles.append(Tile("prefull" if i == len(sp) - 1 else "pre", "xpre", c * 128, n))
        c += n
    sp = split(nmain)
    c = 0
    for i, n in enumerate(sp):
        tiles.append(Tile("main", "xmain", c * 128, n, sample=(i == len(sp) - 1), first=(i == 0),
                          last=(i == len(sp) - 1), out0=c * 128))
        c += n
    return tiles


PV = {}
_o = 0
for _n, _w in [("g", 48), ("kvn", 8), ("fin", 8), ("cw", 96), ("cb", 24), ("dsk", 16), ("ng", 16),
               ("bq", 8), ("bk", 2), ("bo", 8), ("one", 1), ("eps", 1)]:
    PV[_n] = _o
    _o += _w
NPV = _o
PR = {"dtb": 0, "alog": 32, "bv": 64, "bkr": 320, "snk": 576}
NPR = 592
CF = {"um": 0, "ums": 128, "mb": 192, "ms": 448, "mn": 576, "sm": 700, "ss": 716}
NCF = 720
CB = {"id": 0, "on": 128, "um": 256, "us": 384, "ums": 512, "uss": 576, "sel": 640}
NCB = 640 + 4096


def build_program(npre, nmain):
    tiles = make_tiles(npre, nmain)
    TMAX = max(t.T for t in tiles)
    nc = bass.Bass("TRN2", target_bir_lowering=False)
    S = Sched()

    def din(name, shape, dt=F32):
        return nc.dram_tensor(name, list(shape), dt, kind="ExternalInput").ap()

    def dout(name, shape):
        return nc.dram_tensor(name, list(shape), F32, kind="ExternalOutput").ap()

    NOUT = nmain * 128
    dr = {}
    dr["xpre"] = din("xpre", [D, npre * 128]).rearrange("(c p) t -> p c t", p=128)
    dr["xmain"] = din("xmain", [D, nmain * 128]).rearrange("(c p) t -> p c t", p=128)
    xsmp_d = din("xsmp", [D, 64]).rearrange("(c p) t -> p c t", p=128)
    wd = {
        "wgu": din("wgu", [4, 11, 128, WSLOT]),
        "wdn": din("wdn", [4, 8, 128, NF * 128]),
        "winz": din("winz", [4, 128, WSLOT]),
        "winx": din("winx", [6, 128, WSLOT]),
        "wout": din("wout", [4, 128, WSLOT]),
        "wkv": din("wkv", [1, 128, WSLOT]),
        "wq": din("wq", [2, 128, WSLOT]),
        "wo": din("wo", [2, 128, WSLOT]),
    }
    wdt_d = din("wdt", [128, 256])
    pv_d = din("pv", [128, NPV])
    prow_d = din("prow", [128, NPR])
    cf_d = din("cf", [128, NCF])
    cbf_d = din("cbf", [128, NCB], BF16)
    cmask_d = din("cmask", [128, 1])
    sconv_d = din("sconv", [128, 24, 16, 3])
    sssm_d = din("sssm", [16, 128, DIN])
    ckT_d = din("ckT", [16, 128, 256])
    ck_d = din("ck", [16, 128, 256])
    cv_d = din("cv", [16, 128, 256])
    yT_d = dout("yT", [D, NOUT]).rearrange("(c p) t -> p c t", p=128)
    ysT_d = dout("ysT", [D, 64]).rearrange("(c p) t -> p c t", p=128)
    convp_d = dout("convp", [128, 24, 3])
    ssmp_d = dout("ssmp", [128, DIN])
    kwp_d = dout("kwp", [128, 256])
    vwp_d = dout("vwp", [128, 256])
    convs_d = dout("convs", [128, 24, 16, 3])
    ssms_d = dout("ssms", [16, 128, DIN])
    kws_d = dout("kws", [16, 128, 256])
    vws_d = dout("vws", [16, 128, 256])

    es = ExitStack()

    def sb(name, shape, dt):
        return es.enter_context(nc.sbuf_tensor(name, list(shape), dt))

    xT = sb("xT", [128, 8, TMAX], F32)
    hn = sb("hn", [128, 8, TMAX], BF16)
    act = sb("act", [128, 40, TMAX], BF16)
    wsl = [sb(f"wsl{i}", [128, WSLOT], BF16) for i in range(NWS)]
    xsT = act[:, 16:40, :]
    XTM = Rot([sb(f"xtm{i}", [128, 2560], BF16) for i in range(2)], "xtm")
    hT = sb("hT", [128, DIN], F32)
    hTb = sb("hTb", [128, DIN], BF16)
    Ce = sb("Ce", [128, 32, 64], BF16)
    CEP = Rot([sb(f"cep{i}", [128, 2, 128], BF16) for i in range(2)], "cep")
    Gm = sb("Gm", [128, 4, 128], F32)
    SEG = Rot([sb(f"seg{i}", [128, 2, 128], F32) for i in range(2)], "seg")
    MTB = Rot([sb(f"mtb{i}", [128, 2, 128], BF16) for i in range(2)], "mtb")
    EBC = Rot([sb(f"ebc{i}", [128, 2, 128], F32) for i in range(2)], "ebc")
    edec = sb("edec", [128, 32, 16], F32)
    Y1 = Rot([sb(f"y1{i}", [128, 128], F32) for i in range(2)], "y1")
    YG = Rot([sb(f"yg{i}", [128, 4, 128], F32) for i in range(1)], "yg")
    SQ = Rot([sb(f"sq{i}", [128, 2, 512], BF16) for i in range(2)], "sq")
    RS = Rot([sb(f"rs{i}", [128, 512], F32) for i in range(1)], "rs")
    SG = Rot([sb(f"sg{i}", [128, 512], F32) for i in range(2)], "sg")
    XIN = Rot([sb(f"xin{i}", [128, 3 + TMAX], F32) for i in range(2)], "xin")
    XINS = Rot([sb(f"xins{i}", [128, 16, 7], F32) for i in range(2)], "xins")
    CACC = Rot([sb(f"cacc{i}", [128, TMAX], F32) for i in range(1)], "cacc")
    hist = sb("hist", [128, 24, 3], F32)
    sconv = sb("sconv_t", [128, 24, 16, 3], F32)
    convs = sconv
    DTT = Rot([sb(f"dtt{i}", [128, 320], F32) for i in range(2)], "dtt")
    AG3 = Rot([sb(f"ag3{i}", [128, 3, 32], BF16) for i in range(2)], "ag3")
    TMPF = Rot([sb(f"tmpf{i}", [128, 2, 32], F32) for i in range(2)], "tmpf")
    ACF = Rot([sb(f"acf{i}", [32, 3, 128], F32) for i in range(1)], "acf")
    AC3 = Rot([sb(f"ac3{i}", [32, 3, 128], BF16) for i in range(2)], "ac3")
    XW = Rot([sb(f"xw{i}", [128, 512], BF16) for i in range(2)], "xw")
    xwall = sb("xwall", [64, DIN], BF16)
    BM = Rot([sb(f"bm{i}", [64, 128], BF16) for i in range(2)], "bm")
    kT = sb("kT", [128, 2, 128 + TMAX], BF16)
    vtm = sb("vtm", [128, 6, 256], BF16)
    kvf = sb("kvf", [128, 512], F32)
    SM = Rot([sb(f"sm{i}", [128, 4, 256], F32) for i in range(1)], "sm")
    PN = Rot([sb(f"pn{i}", [128, 4, 256], BF16) for i in range(1)], "pn")
    PT = Rot([sb(f"pt{i}", [128, 4, 256], BF16) for i in range(1)], "pt")
    STAT = Rot([sb(f"stat{i}", [128, 4, 8], F32) for i in range(4)], "stat")
    qs = sb("qs", [128, 16, 8, 4], BF16)
    CKT = Rot([sb(f"ckt{i}", [128, 256], BF16) for i in range(2)], "ckt")
    CV = Rot([sb(f"cvt{i}", [128, 256], BF16) for i in range(2)], "cvt")
    wdt = sb("wdt_t", [128, 8, 32], BF16)
    pv = sb("pv_t", [128, NPV], F32)
    prow = sb("prow_t", [128, NPR], F32)
    abc = sb("abc", [128, 32], F32)
    cf = sb("cf_t", [128, NCF], F32)
    cbf = sb("cbf_t", [128, NCB], BF16)
    cmask = sb("cmask_t", [128, 1], F32)
    hbias = sb("hbias", [128, 1], F32)
    ps = [es.enter_context(nc.psum_tensor(f"ps{i}", [128, 512], F32)) for i in range(8)]
    pools = {"mm": [0, 1, 2, 3], "hd": [4, 5], "acc": [6, 7]}
    pctr = {"mm": 0, "hd": 0, "acc": 0}

    def PS(pool):
        l = pools[pool]
        i = l[pctr[pool] % len(l)]
        pctr[pool] += 1
        return ps[i], f"ps{i}"

    ident = cbf[:, 0:128]
    ones_bf = cbf[:, 128:256]
    pcol = lambda name, j=0: pv[:, PV[name] + j:PV[name] + j + 1]

    wseq = []
    wstate = {"dry": True, "i": 0, "issued": 0}

    def wdram(name, idx):
        a = wd[name]
        return a[idx[0], idx[1]] if len(idx) == 2 else a[idx[0]]

    def W_issue(upto):
        while wstate["issued"] <= min(upto, len(wseq) - 1):
            k = wstate["issued"]
            name, idx, nel = wseq[k]
            slot = wsl[k % NWS]
            src = wdram(name, idx)
            S.op("pool", (lambda e, slot=slot, src=src, nel=nel: [e.dma_start(out=slot[:, 0:nel], in_=src[:, 0:nel])]),
                 writes=[f"wsl{k % NWS}"], dma=(f"w{k % NWS}", 1))
            wstate["issued"] += 1

    def W_get(name, idx, nel):
        k = wstate["i"]
        wstate["i"] += 1
        if wstate["dry"]:
            wseq.append((name, idx, nel))
            return wsl[k % NWS], f"wsl{k % NWS}"
        assert wseq[k] == (name, idx, nel)
        W_issue(k + NWS - 1)
        return wsl[k % NWS], f"wsl{k % NWS}"

    def OP(eng, fn, reads=(), writes=(), dma=None):
        if wstate["dry"]:
            return
        S.op(eng, fn, reads, writes, dma)

    def barrier():
        if wstate["dry"] or os.environ.get("KNOBAR"):
            return
        last = {}
        for i_, o_ in enumerate(S.ops):
            if o_.dma is None and o_.eng in ("pe", "act", "dve"):
                last[o_.eng] = i_
        for eng in ("pe", "act", "dve"):
            S.op(eng, lambda e: [e.nop()])
            for en2, i_ in last.items():
                if en2 != eng:
                    S.ops[-1].deps[i_] = 2

    def mm_group(out_ap, pairs):
        def fn(e):
            r = []
            n = len(pairs)
            for i, (l, rr) in enumerate(pairs):
                r.append(e.matmul(out_ap, lhsT=l, rhs=rr, start=(i == 0), stop=(i == n - 1)))
            return r
        return fn

    def rmsnorm(tile, gbase, ndim=D):
        for (c0, c1) in tile.subs:
            n = c1 - c0
            pss, kps = PS("mm")
            for cp in range(4):
                sq, ksq = SQ.next()
                OP("act", lambda e, sq=sq, cp=cp, c0=c0, c1=c1, n=n: [
                    e.activation(out=sq[:, :, 0:n], in_=xT[:, 2 * cp:2 * cp + 2, c0:c1], func=AF.Square)],
                   reads=["xT"], writes=[ksq])
                OP("pe", lambda e, sq=sq, cp=cp, pss=pss, n=n: [
                    e.matmul(pss[:, 0:n], lhsT=ones_bf, rhs=sq[:, j, 0:n], start=(cp == 0 and j == 0),
                             stop=(cp == 3 and j == 1)) for j in range(2)],
                   reads=[ksq, "cbf"], writes=[kps])
            rs, krs = RS.next()
            OP("act", lambda e, rs=rs, pss=pss, n=n: [
                e.activation(out=rs[:, 0:n], in_=pss[:, 0:n], func=AF.Ln, bias=pcol("eps"), scale=1.0 / ndim)],
               reads=[kps, "pv"], writes=[krs])
            OP("act", lambda e, rs=rs, n=n: [
                e.activation(out=rs[:, 0:n], in_=rs[:, 0:n], func=AF.Exp, scale=-0.5)],
               reads=[krs], writes=[krs])
            OP("dve", lambda e, rs=rs, c0=c0, c1=c1, n=n: [
                e.scalar_tensor_tensor(out=hn[:, c, c0:c1], in0=xT[:, c, c0:c1], scalar=pv[:, gbase + c:gbase + c + 1],
                                       in1=rs[:, 0:n], op0=ALU.mult, op1=ALU.mult) for c in range(8)],
               reads=["xT", krs, "pv"], writes=["hn"])

    def ffn(tile, fi):
        rmsnorm(tile, PV["g"] + [0, 16, 24, 40][fi])
        for blk in range(11):
            wb, kw = W_get("wgu", (fi, blk), WSLOT)
            wv = wb[:, :].rearrange("p (a k n) -> p a k n", a=2, k=8)
            for jj in range(2):
                j = blk * 2 + jj
                for (c0, c1) in tile.subs:
                    n = c1 - c0
                    pg, kpg = PS("mm")
                    pu, kpu = PS("mm")
                    OP("pe", lambda e, wv=wv, jj=jj, c0=c0, c1=c1, n=n, pg=pg, pu=pu: (
                        mm_group(pg[:, 0:n], [(wv[:, 0, k, jj * 128:(jj + 1) * 128], hn[:, k, c0:c1]) for k in range(8)])(e)
                        + mm_group(pu[:, 0:n], [(wv[:, 1, k, jj * 128:(jj + 1) * 128], hn[:, k, c0:c1]) for k in range(8)])(e)),
                       reads=["hn", kw], writes=[kpg, kpu])
                    sg, ksg = SG.next()
                    OP("act", lambda e, sg=sg, pg=pg, n=n: [e.activation(out=sg[:, 0:n], in_=pg[:, 0:n], func=AF.Silu)],
                       reads=[kpg], writes=[ksg])
                    OP("dve", lambda e, sg=sg, pu=pu, j=j, c0=c0, c1=c1, n=n: [
                        e.tensor_tensor(out=act[:, j, c0:c1], in0=sg[:, 0:n], in1=pu[:, 0:n], op=ALU.mult)],
                       reads=[ksg, kpu], writes=["act"])
        for blk in range(8):
            wb, kw = W_get("wdn", (fi, blk), NF * 128)
            wv = wb[:, 0:NF * 128].rearrange("p (j n) -> p j n", j=NF)
            for (c0, c1) in tile.subs:
                n = c1 - c0
                pd, kpd = PS("mm")
                OP("pe", mm_group(pd[:, 0:n], [(wv[:, j, :], act[:, j, c0:c1]) for j in range(NF)]),
                   reads=["act", kw], writes=[kpd])
                OP("dve", lambda e, pd=pd, blk=blk, c0=c0, c1=c1, n=n: [
                    e.scalar_tensor_tensor(out=xT[:, blk, c0:c1], in0=pd[:, 0:n], scalar=0.5, in1=xT[:, blk, c0:c1],
                                           op0=ALU.mult, op1=ALU.add)],
                   reads=[kpd, "xT"], writes=["xT"])

    def load_x(tile):
        src = dr[tile.src]
        OP("sp", lambda e: [e.dma_start(out=xT[:, :, 0:tile.Tp], in_=src[:, :, tile.col0:tile.col0 + tile.Tp])],
           writes=["xT"], dma=("xld", 1))
        if tile.sample:
            OP("sp", lambda e: [e.dma_start(out=xT[:, :, tile.Tp:tile.T], in_=xsmp_d[:, :, :])],
               writes=["xT"], dma=("xld", 1))

    def conv_chunk(tile, m, pre_only):
        pass

    def mixer_in(tile, prefix):
        rmsnorm(tile, PV["g"] + 8)
        Tp = tile.Tp
        if not prefix:
            for blk in range(4):
                wb, kw = W_get("winz", (blk,), WSLOT)
                wv = wb[:, :].rearrange("p (k n) -> p k n", k=8)
                for jj in range(4):
                    m = blk * 4 + jj
                    for (c0, c1) in tile.subs:
                        n = c1 - c0
                        pz, kpz = PS("mm")
                        OP("pe", mm_group(pz[:, 0:n], [(wv[:, k, jj * 128:(jj + 1) * 128], hn[:, k, c0:c1]) for k in range(8)]),
                           reads=["hn", kw], writes=[kpz])
                        OP("act", lambda e, pz=pz, m=m, c0=c0, c1=c1, n=n: [
                            e.activation(out=act[:, m, c0:c1], in_=pz[:, 0:n], func=AF.Silu)],
                           reads=[kpz], writes=["sz"])
        nblk = 6
        for blk in range(nblk):
            wb, kw = W_get("winx", (blk,), WSLOT)
            wv = wb[:, :].rearrange("p (k n) -> p k n", k=8)
            for jj in range(4):
                m = blk * 4 + jj
                xin, kxin = XIN.next()
                xins, kxins = XINS.next()
                OP("dve", lambda e, xin=xin, m=m: [e.tensor_copy(out=xin[:, 0:3], in_=hist[:, m, :])],
                   reads=["hist"], writes=[kxin])
                for (c0, c1) in tile.subs:
                    n = c1 - c0
                    px, kpx = PS("mm")
                    OP("pe", mm_group(px[:, 0:n], [(wv[:, k, jj * 128:(jj + 1) * 128], hn[:, k, c0:c1]) for k in range(8)]),
                       reads=["hn", kw], writes=[kpx])
                    if c0 < Tp:
                        OP("act", lambda e, px=px, xin=xin, c0=c0, c1=c1, n=n: [
                            e.activation(out=xin[:, 3 + c0:3 + c1], in_=px[:, 0:n], func=AF.Copy)],
                           reads=[kpx], writes=[kxin])
                    else:
                        OP("act", lambda e, px=px, xins=xins: [
                            e.activation(out=xins[:, :, 3:7], in_=px[:, 0:64].rearrange("p (b t) -> p b t", t=4), func=AF.Copy)],
                           reads=[kpx], writes=[kxins])
                cacc, kc = CACC.next()
                wc = lambda k, m=m: pv[:, PV["cw"] + k * 24 + m:PV["cw"] + k * 24 + m + 1]
                OP("dve", lambda e, xin=xin, cacc=cacc, m=m, wc=wc: [
                    e.tensor_scalar(out=cacc[:, 0:Tp], in0=xin[:, 3:3 + Tp], scalar1=wc(3), scalar2=pcol("cb", m),
                                    op0=ALU.mult, op1=ALU.add)],
                   reads=[kxin, "pv"], writes=[kc])
                for k in range(3):
                    OP("dve", lambda e, xin=xin, cacc=cacc, k=k, wc=wc: [
                        e.scalar_tensor_tensor(out=cacc[:, 0:Tp], in0=xin[:, k:k + Tp], scalar=wc(k), in1=cacc[:, 0:Tp],
                                               op0=ALU.mult, op1=ALU.add)],
                       reads=[kxin, kc, "pv"], writes=[kc])
                OP("act", lambda e, cacc=cacc, m=m: [e.activation(out=xsT[:, m, 0:Tp], in_=cacc[:, 0:Tp], func=AF.Silu)],
                   reads=[kc], writes=["xsT"])
                OP("dve", lambda e, xin=xin, m=m: [e.tensor_copy(out=hist[:, m, :], in_=xin[:, Tp:Tp + 3])],
                   reads=[kxin], writes=["hist"])
                if tile.sample:
                    OP("dve", lambda e, xins=xins, m=m: [e.tensor_copy(out=xins[:, :, 0:3], in_=sconv[:, m, :, :])],
                       reads=["sconv"], writes=[kxins])
                    cacc, kc = CACC.next()
                    cv3 = lambda cacc=cacc: cacc[:, 0:64].rearrange("p (b t) -> p b t", t=4)
                    OP("dve", lambda e, xins=xins, cv3=cv3, m=m, wc=wc: [
                        e.tensor_scalar(out=cv3(), in0=xins[:, :, 3:7], scalar1=wc(3), scalar2=pcol("cb", m),
                                        op0=ALU.mult, op1=ALU.add)],
                       reads=[kxins, "pv"], writes=[kc])
                    for k in range(3):
                        OP("dve", lambda e, xins=xins, cv3=cv3, k=k, wc=wc: [
                            e.scalar_tensor_tensor(out=cv3(), in0=xins[:, :, k:k + 4], scalar=wc(k), in1=cv3(),
                                                   op0=ALU.mult, op1=ALU.add)],
                           reads=[kxins, kc, "pv"], writes=[kc])
                    OP("act", lambda e, cacc=cacc, m=m: [e.activation(out=xsT[:, m, Tp:Tp + 64], in_=cacc[:, 0:64], func=AF.Silu)],
                       reads=[kc], writes=["xsT"])
                    OP("dve", lambda e, xins=xins, m=m: [e.tensor_copy(out=convs[:, m, :, :], in_=xins[:, :, 4:7])],
                       reads=[kxins], writes=["sconv"])

    def ssd_chunk(tile, ci, sample, prefix):
        L = 64 if sample else 128
        col0 = tile.Tp if sample else ci * 128
        c1 = col0 + L
        um = cf[0:L, CF["ums"]:CF["ums"] + L] if sample else cf[0:L, CF["um"]:CF["um"] + L]
        xtm, kx = XTM.next()
        for grp in range(3):
            ms = list(range(grp * 8, min(grp * 8 + 8, 20)))
            pb, kpb = PS("mm")
            pbv = pb[:, :].bitcast(BF16)
            OP("pe", lambda e, ms=ms, pbv=pbv: [
                e.transpose(pbv[0:L, j * 128:(j + 1) * 128], xsT[:, m, col0:c1], ident) for j, m in enumerate(ms)],
               reads=["xsT", "cbf"], writes=[kpb])
            eng = "act" if grp % 2 == 0 else "dve"
            w = len(ms) * 128
            if eng == "act":
                OP("act", lambda e, pbv=pbv, grp=grp, w=w: [
                    e.activation(out=xtm[0:L, grp * 1024:grp * 1024 + w], in_=pbv[0:L, 0:w], func=AF.Copy)],
                   reads=[kpb], writes=[kx])
            else:
                OP("dve", lambda e, pbv=pbv, grp=grp, w=w: [
                    e.tensor_copy(out=xtm[0:L, grp * 1024:grp * 1024 + w], in_=pbv[0:L, 0:w])],
                   reads=[kpb], writes=[kx])
        dtt, kd = DTT.next()
        pdt, kpd = PS("mm")
        OP("pe", mm_group(pdt[0:L, 0:32], [(hn[:, k, col0:c1], wdt[:, k, :]) for k in range(8)]),
           reads=["hn", "wdt"], writes=[kpd])
        XB, AXc, EX, LN1, DT, AG, WW, NAC = [slice(32 * i, 32 * i + 32) for i in range(8)]
        OP("dve", lambda e: [e.tensor_tensor(out=dtt[0:L, XB], in0=pdt[0:L, 0:32], in1=prow[0:L, PR["dtb"]:PR["dtb"] + 32], op=ALU.add)],
           reads=[kpd, "prow"], writes=[kd])
        OP("act", lambda e: [e.activation(out=dtt[0:L, AXc], in_=dtt[0:L, XB], func=AF.Abs)],
           reads=[kd], writes=[kd])
        OP("act", lambda e: [e.activation(out=dtt[0:L, EX], in_=dtt[0:L, AXc], func=AF.Exp, scale=-1.0)],
           reads=[kd], writes=[kd])
        OP("act", lambda e: [e.activation(out=dtt[0:L, LN1], in_=dtt[0:L, EX], func=AF.Ln, bias=pv[0:L, PV["one"]:PV["one"] + 1])],
           reads=[kd, "pv"], writes=[kd])
        OP("dve", lambda e: [e.scalar_tensor_tensor(out=dtt[0:L, DT], in0=dtt[0:L, XB], scalar=0.0, in1=dtt[0:L, LN1],
                                                    op0=ALU.max, op1=ALU.add)],
           reads=[kd], writes=[kd])
        OP("dve", lambda e: [e.tensor_tensor(out=dtt[0:L, AG], in0=dtt[0:L, DT], in1=abc[0:L, :], op=ALU.mult)],
           reads=[kd, "abc"], writes=[kd])
        ag3, kag = AG3.next()
        tmpf, ktf = TMPF.next()

        def split3(src, dst3, tmp, rk, wk, tk, P, n):
            OP("dve", lambda e: [e.tensor_copy(out=dst3[0:P, 0, 0:n], in_=src)], reads=[rk], writes=[wk])
            OP("dve", lambda e: [e.tensor_tensor(out=tmp[0:P, 0, 0:n], in0=src, in1=dst3[0:P, 0, 0:n], op=ALU.subtract)],
               reads=[rk, wk], writes=[tk])
            OP("dve", lambda e: [e.tensor_copy(out=dst3[0:P, 1, 0:n], in_=tmp[0:P, 0, 0:n])], reads=[tk], writes=[wk])
            OP("dve", lambda e: [e.tensor_tensor(out=tmp[0:P, 1, 0:n], in0=tmp[0:P, 0, 0:n], in1=dst3[0:P, 1, 0:n], op=ALU.subtract)],
               reads=[tk, wk], writes=[tk])
            OP("dve", lambda e: [e.tensor_copy(out=dst3[0:P, 2, 0:n], in_=tmp[0:P, 1, 0:n])], reads=[tk], writes=[wk])
        split3(dtt[0:L, AG], ag3, tmpf, kd, kag, ktf, L, 32)
        umb = cbf[0:L, CB["ums"]:CB["ums"] + L] if sample else cbf[0:L, CB["um"]:CB["um"] + L]
        usb = cbf[0:L, CB["uss"]:CB["uss"] + L] if sample else cbf[0:L, CB["us"]:CB["us"] + L]
        pc, kpc = PS("mm")
        def cums(e):
            r = []
            for i in range(3):
                r.append(e.matmul(pc[0:L, 0:32], lhsT=usb, rhs=ag3[0:L, i, :], start=(i == 0), stop=(i == 2)))
            for i in range(3):
                r.append(e.matmul(pc[0:L, 32:64], lhsT=umb, rhs=ag3[0:L, i, :], start=(i == 0), stop=(i == 2)))
            for i in range(3):
                r.append(e.matmul(pc[0:32, 64:64 + L], lhsT=ag3[0:L, i, :], rhs=umb, start=(i == 0), stop=(i == 2)))
            if not sample:
                for i in range(3):
                    r.append(e.matmul(pc[:, 192:224], lhsT=ones_bf[0:L, :], rhs=ag3[0:L, i, :], start=(i == 0), stop=(i == 2)))
            return r
        OP("pe", cums, reads=[kag, "cbf"], writes=[kpc])
        OP("act", lambda e: [e.activation(out=dtt[0:L, WW], in_=pc[0:L, 0:32], func=AF.Exp)], reads=[kpc], writes=[kd])
        OP("dve", lambda e: [e.tensor_tensor(out=dtt[0:L, WW], in0=dtt[0:L, WW], in1=dtt[0:L, DT], op=ALU.mult)],
           reads=[kd], writes=[kd])
        OP("dve", lambda e: [e.tensor_scalar(out=dtt[0:L, NAC], in0=pc[0:L, 32:64], scalar1=-1.0, scalar2=None, op0=ALU.mult)],
           reads=[kpc], writes=[kd])
        ETOT = slice(256, 288)
        if not sample:
            OP("act", lambda e: [e.activation(out=dtt[:, ETOT], in_=pc[:, 192:224], func=AF.Exp)], reads=[kpc], writes=[kd])
        if not prefix:
            acf, kacf = ACF.next()
            ac3, kac3 = AC3.next()
            OP("act", lambda e: [e.activation(out=acf[0:32, 0, 0:L], in_=pc[0:32, 64:64 + L], func=AF.Copy)], reads=[kpc], writes=[kacf])
            split3(acf[0:32, 0, 0:L], ac3, acf[:, 1:3, :], kacf, kac3, kacf + "t", 32, L)
            for g in range(4):
                pg, kpg = PS("mm")
                OP("pe", lambda e, pg=pg, g=g: [e.matmul(pg[0:L, 0:L], lhsT=xsT[:, 16 + g, col0:c1], rhs=xsT[:, 20 + g, col0:c1],
                                                         start=True, stop=True)],
                   reads=["xsT"], writes=[kpg])
                OP("dve", lambda e, pg=pg, g=g: [e.tensor_tensor(out=Gm[0:L, g, 0:L], in0=pg[0:L, 0:L], in1=um, op=ALU.mult)],
                   reads=[kpg, "cf"], writes=["Gm"])
            ypss = []
            if sample:
                ypss = [PS("acc"), PS("acc")]
            def head_pair(hb, mode, yint=None):
                g = hb // 4
                heads = [2 * hb, 2 * hb + 1]
                pbc, kpbc = PS("mm" if mode == "B" else "hd")
                OP("pe", lambda e, pbc=pbc, heads=heads: [
                    e.matmul(pbc[:, j * L:(j + 1) * L], lhsT=cbf[0:32, CB["sel"] + h * 128:CB["sel"] + (h + 1) * 128],
                             rhs=ac3[0:32, i, 0:L], start=(i == 0), stop=(i == 2)) for j, h in enumerate(heads) for i in range(3)],
                   reads=[kac3, "cbf"], writes=[kpbc])
                if mode != "A":
                    seg, ksg = SEG.next()
                    OP("dve", lambda e, pbc=pbc, heads=heads, seg=seg: [
                        e.tensor_scalar(out=seg[0:L, j, 0:L], in0=pbc[0:L, j * L:(j + 1) * L], scalar1=dtt[0:L, 224 + h:225 + h],
                                        scalar2=0.0, op0=ALU.add, op1=ALU.min) for j, h in enumerate(heads)],
                       reads=[kpbc, kd], writes=[ksg])
                    OP("act", lambda e, seg=seg: [e.activation(out=seg[0:L, :, 0:L], in_=seg[0:L, :, 0:L], func=AF.Exp)],
                       reads=[ksg], writes=[ksg])
                    mtb, kmt = MTB.next()
                    OP("dve", lambda e, seg=seg, mtb=mtb, heads=heads, g=g: [
                        e.scalar_tensor_tensor(out=mtb[0:L, j, 0:L], in0=seg[0:L, j, 0:L], scalar=dtt[0:L, 128 + h:129 + h],
                                               in1=Gm[0:L, g, 0:L], op0=ALU.mult, op1=ALU.mult) for j, h in enumerate(heads)],
                       reads=[ksg, kd, "Gm"], writes=[kmt])
                if mode == "B":
                    yp, kyp = yint[hb // 8]
                    ypv = yp[:, :].rearrange("p (c t) -> p c t", t=64)
                    OP("pe", lambda e, mtb=mtb, heads=heads, ypv=ypv, hb=hb: [
                        e.matmul(ypv[(h % 2) * 64:(h % 2) * 64 + 64, hb % 8, :], lhsT=xtm[0:L, h * 64:(h + 1) * 64],
                                 rhs=mtb[0:L, j, 0:L], start=True, stop=True) for j, h in enumerate(heads)],
                       reads=[kx, kmt], writes=[kyp])
                    return
                ebc, keb = EBC.next()
                OP("act", lambda e, pbc=pbc, ebc=ebc: [
                    e.activation(out=ebc[:, :, 0:L], in_=pbc[:, 0:2 * L].rearrange("p (j t) -> p j t", j=2), func=AF.Exp)],
                   reads=[kpbc], writes=[keb])
                if mode == "A":
                    cet, kce, ceo = Ce, "Ce", 2 * hb
                else:
                    cet, kce = CEP.next()
                    ceo = 0
                OP("dve", lambda e, ebc=ebc, hb=hb, g=g, cet=cet, ceo=ceo: [
                    e.tensor_tensor(out=cet[:, ceo:ceo + 2, 0:L], in0=ebc[:, :, 0:L],
                                    in1=xsT[:, 20 + g, col0:c1].unsqueeze(1).to_broadcast([128, 2, L]), op=ALU.mult)],
                   reads=[keb, "xsT"], writes=[kce])
                if mode == "A":
                    OP("dve", lambda e, ebc=ebc, hb=hb: [
                        e.tensor_copy(out=edec[:, 2 * hb:2 * hb + 2, :],
                                      in_=ebc[:, :, 0:64].rearrange("p j (b t) -> p j b t", t=4)[:, :, :, 3])],
                       reads=[keb], writes=["edec"])
                    return
                yp, kyp = PS("acc")
                def yfn(e, mtb=mtb, heads=heads, yp=yp, cet=cet):
                    r = []
                    for j, h in enumerate(heads):
                        o = yp[j * 64:j * 64 + 64, 0:L]
                        r.append(e.matmul(o, lhsT=xtm[0:L, h * 64:(h + 1) * 64], rhs=mtb[0:L, j, 0:L], start=True, stop=False))
                        r.append(e.matmul(o, lhsT=hTb[:, h * 64:(h + 1) * 64], rhs=cet[:, j, 0:L], start=False, stop=True))
                    return r
                OP("pe", yfn, reads=[kx, kmt, "hTb", kce], writes=[kyp])
                post_y(tile, col0, L, [(hb, yp[:, 0:L], kyp, None, None)])

            for hb in range(16):
                if os.environ.get("KSKIP") == "hb" and tile.kind == "main":
                    continue
                head_pair(hb, "A" if sample else "all")
        if sample:
            OP("dve", lambda e: [
                e.tensor_tensor(out=xwall[0:64, :].rearrange("p (h d) -> p h d", d=64),
                                in0=xtm[0:64, 0:DIN].rearrange("p (h d) -> p h d", d=64),
                                in1=dtt[0:64, WW].unsqueeze(2).to_broadcast([64, 32, 64]), op=ALU.mult)],
               reads=[kx, kd], writes=["xwall"])
            for b in range(16):
                OP("sp", lambda e, b=b: [e.dma_start(out=hT[:, :], in_=sssm_d[b])], writes=["hT"], dma=("hld", 1))
                OP("act", lambda e: [e.activation(out=hTb[:, :], in_=hT[:, :], func=AF.Copy)], reads=["hT"], writes=["hTb"])
                for half in range(2):
                    yp, kyp = ypss[half]
                    ypv = yp[:, :].rearrange("p (c t) -> p c t", t=64)
                    OP("pe", lambda e, b=b, half=half, ypv=ypv: [
                        e.matmul(ypv[(h % 2) * 64:(h % 2) * 64 + 64, (h // 2) % 8, 4 * b:4 * b + 4],
                                 lhsT=hTb[:, h * 64:(h + 1) * 64], rhs=Ce[:, h, 4 * b:4 * b + 4], start=True, stop=True)
                        for h in range(16 * half, 16 * half + 16)],
                       reads=["hTb", "Ce"], writes=[kyp])
                for g in range(4):
                    bm, kbm = BM.next()
                    OP("dve", lambda e, bm=bm, g=g, b=b: [
                        e.tensor_scalar(out=bm[:, :], in0=xtm[0:64, DIN + g * 128:DIN + (g + 1) * 128],
                                        scalar1=cf[0:64, CF["sm"] + b:CF["sm"] + b + 1], scalar2=None, op0=ALU.mult)],
                       reads=[kx, "cf"], writes=[kbm])
                    pst, kps_ = PS("mm")
                    OP("pe", lambda e, bm=bm, g=g, pst=pst: [
                        e.matmul(pst[:, :], lhsT=bm[:, :], rhs=xwall[0:64, g * 512:(g + 1) * 512], start=True, stop=True)],
                       reads=[kbm, "xwall"], writes=[kps_])
                    hv = hT[:, g * 512:(g + 1) * 512].rearrange("p (h d) -> p h d", d=64)
                    OP("dve", lambda e, hv=hv, g=g, b=b: [
                        e.tensor_tensor(out=hv, in0=hv, in1=edec[:, 8 * g:8 * g + 8, b:b + 1].to_broadcast([128, 8, 64]), op=ALU.mult)],
                       reads=["hT", "edec"], writes=["hT"])
                    OP("dve", lambda e, g=g, pst=pst: [
                        e.tensor_tensor(out=hT[:, g * 512:(g + 1) * 512], in0=hT[:, g * 512:(g + 1) * 512], in1=pst[:, :], op=ALU.add)],
                       reads=["hT", kps_], writes=["hT"])
                OP("sp", lambda e, b=b: [e.dma_start(out=ssms_d[b], in_=hT[:, :])], reads=["hT"], writes=["o_ssms"], dma=("ost", 1))
            yint = [PS("hd"), PS("hd")]
            for hb in range(16):
                head_pair(hb, "B", yint)
            post_y(tile, col0, L, [(c, ypss[c // 8][0][:, (c % 8) * 64:(c % 8) * 64 + 64], ypss[c // 8][1],
                                    yint[c // 8][0][:, (c % 8) * 64:(c % 8) * 64 + 64], yint[c // 8][1]) for c in range(16)])
        else:
            for g in range(4):
                xw, kxw = XW.next()
                OP("dve", lambda e, xw=xw, g=g: [
                    e.tensor_tensor(out=xw[0:L, :].rearrange("p (h d) -> p h d", d=64),
                                    in0=xtm[0:L, g * 512:(g + 1) * 512].rearrange("p (h d) -> p h d", d=64),
                                    in1=dtt[0:L, 192 + 8 * g:192 + 8 * g + 8].unsqueeze(2).to_broadcast([L, 8, 64]), op=ALU.mult)],
                   reads=[kx, kd], writes=[kxw])
                pst, kps_ = PS("mm")
                OP("pe", lambda e, xw=xw, g=g, pst=pst: [
                    e.matmul(pst[:, :], lhsT=xtm[0:L, DIN + g * 128:DIN + (g + 1) * 128], rhs=xw[0:L, :], start=True, stop=True)],
                   reads=[kx, kxw], writes=[kps_])
                hv = hT[:, g * 512:(g + 1) * 512].rearrange("p (h d) -> p h d", d=64)
                OP("dve", lambda e, hv=hv, g=g: [
                    e.tensor_tensor(out=hv, in0=hv, in1=dtt[:, 256 + 8 * g:256 + 8 * g + 8].unsqueeze(2).to_broadcast([128, 8, 64]),
                                    op=ALU.mult)],
                   reads=["hT", kd], writes=["hT"])
                OP("dve", lambda e, g=g, pst=pst: [
                    e.tensor_tensor(out=hT[:, g * 512:(g + 1) * 512], in0=hT[:, g * 512:(g + 1) * 512], in1=pst[:, :], op=ALU.add)],
                   reads=["hT", kps_], writes=["hT"])
                OP("act", lambda e, g=g: [e.activation(out=hTb[:, g * 512:(g + 1) * 512], in_=hT[:, g * 512:(g + 1) * 512], func=AF.Copy)],
                   reads=["hT"], writes=["hTb"])

    ygstate = {}

    def post_y(tile, col0, L, items):
        c1 = col0 + L
        for (c, yap, kyp, yap2, kyp2) in items:
            y1, ky1 = Y1.next()
            OP("dve", lambda e, y1=y1, c=c, yap=yap: [
                e.scalar_tensor_tensor(out=y1[:, 0:L], in0=xsT[:, c, col0:c1], scalar=pcol("dsk", c), in1=yap,
                                       op0=ALU.mult, op1=ALU.add)],
               reads=["xsT", kyp, "pv"], writes=[ky1])
            if yap2 is not None:
                OP("dve", lambda e, y1=y1, yap2=yap2: [e.tensor_tensor(out=y1[:, 0:L], in0=y1[:, 0:L], in1=yap2, op=ALU.add)],
                   reads=[ky1, kyp2], writes=[ky1])
            if c % 4 == 0:
                ygstate["yg"] = YG.next()
                ygstate["ss"] = PS("mm")
            yg, kyg = ygstate["yg"]
            pss, kss = ygstate["ss"]
            OP("dve", lambda e, y1=y1, yg=yg, c=c: [
                e.tensor_tensor(out=yg[:, c % 4, 0:L], in0=y1[:, 0:L], in1=act[:, c, col0:c1], op=ALU.mult)],
               reads=[ky1, "sz"], writes=[kyg])
            sq, ksq = SQ.next()
            OP("act", lambda e, sq=sq, yg=yg, c=c: [e.activation(out=sq[:, 0, 0:L], in_=yg[:, c % 4, 0:L], func=AF.Square)],
               reads=[kyg], writes=[ksq])
            OP("pe", lambda e, sq=sq, pss=pss, c=c: [
                e.matmul(pss[:, 0:L], lhsT=ones_bf, rhs=sq[:, 0, 0:L], start=(c % 4 == 0), stop=(c % 4 == 3))],
               reads=[ksq, "cbf"], writes=[kss])
            if c % 4 == 3:
                rs, krs = RS.next()
                OP("act", lambda e, rs=rs, pss=pss: [
                    e.activation(out=rs[:, 0:L], in_=pss[:, 0:L], func=AF.Ln, bias=pcol("eps"), scale=1.0 / 512)],
                   reads=[kss, "pv"], writes=[krs])
                OP("act", lambda e, rs=rs: [e.activation(out=rs[:, 0:L], in_=rs[:, 0:L], func=AF.Exp, scale=-0.5)],
                   reads=[krs], writes=[krs])
                OP("dve", lambda e, rs=rs, yg=yg, c=c: [
                    e.scalar_tensor_tensor(out=act[:, c - 3 + k, col0:c1], in0=yg[:, k, 0:L], scalar=pcol("ng", c - 3 + k),
                                           in1=rs[:, 0:L], op0=ALU.mult, op1=ALU.mult) for k in range(4)],
                   reads=[kyg, krs, "pv"], writes=["sz"])

    def mixer_out(tile):
        for blk in range(4):
            wb, kw = W_get("wout", (blk,), WSLOT)
            wv = wb[:, :].rearrange("p (k n) -> p k n", k=16)
            for jj in range(2):
                dm = blk * 2 + jj
                for (c0, c1) in tile.subs:
                    n = c1 - c0
                    po, kpo = PS("mm")
                    OP("pe", mm_group(po[:, 0:n], [(wv[:, k, jj * 128:(jj + 1) * 128], act[:, k, c0:c1]) for k in range(16)]),
                       reads=["sz", kw], writes=[kpo])
                    OP("dve", lambda e, po=po, dm=dm, c0=c0, c1=c1, n=n: [
                        e.tensor_tensor(out=xT[:, dm, c0:c1], in0=po[:, 0:n], in1=xT[:, dm, c0:c1], op=ALU.add)],
                       reads=[kpo, "xT"], writes=["xT"])

    def mixer0(tile, prefix):
        mixer_in(tile, prefix)
        barrier()
        for ci in range(tile.nch):
            if os.environ.get("KSKIP") == "ssd" and tile.kind == "main":
                continue
            ssd_chunk(tile, ci, False, prefix)
        if tile.last:
            OP("sp", lambda e: [e.dma_start(out=ssmp_d[:, :], in_=hT[:, :])], reads=["hT"], writes=["o_ssmp"], dma=("ost", 1))
            OP("sp", lambda e: [e.dma_start(out=convp_d[:, :, :], in_=hist[:, :, :])], reads=["hist"], writes=["o_convp"], dma=("ost", 1))
        if tile.sample:
            ssd_chunk(tile, None, True, False)
            OP("sp", lambda e: [e.dma_start(out=convs_d[:, :, :, :], in_=convs[:, :, :, :])], reads=["sconv"], writes=["o_convs"],
               dma=("ost", 1))
        barrier()
        if not prefix and not (os.environ.get("KSKIP") == "out" and tile.kind == "main"):
            mixer_out(tile)

    def kv_proj(tile):
        rmsnorm(tile, PV["kvn"])
        wb, kw = W_get("wkv", (0,), WSLOT)
        wv = wb[:, :].rearrange("p (k n) -> p k n", k=8)
        for m in range(2):
            for (c0, c1) in tile.subs:
                n = c1 - c0
                pk, kpk = PS("mm")
                OP("pe", mm_group(pk[:, 0:n], [(wv[:, k, m * 128:(m + 1) * 128], hn[:, k, c0:c1]) for k in range(8)]),
                   reads=["hn", kw], writes=[kpk])
                OP("act", lambda e, pk=pk, m=m, c0=c0, c1=c1, n=n: [
                    e.activation(out=kT[:, m, 128 + c0:128 + c1], in_=pk[:, 0:n], func=AF.Identity, bias=pcol("bk", m))],
                   reads=[kpk, "pv"], writes=["kT"])
        chunks = [(ci * 128, 128, 1 + ci) for ci in range(tile.nch)] + ([(tile.Tp, 64, 5)] if tile.sample else [])
        for (c0, L, slot) in chunks:
            pvv, kpv = PS("mm")
            OP("pe", mm_group(pvv[0:L, 0:256], [(hn[:, k, c0:c0 + L], wv[:, k, 256:512]) for k in range(8)]),
               reads=["hn", kw], writes=[kpv])
            OP("dve", lambda e, pvv=pvv, L=L, slot=slot: [
                e.tensor_tensor(out=vtm[0:L, slot, :], in0=pvv[0:L, 0:256], in1=prow[0:L, PR["bv"]:PR["bv"] + 256], op=ALU.add)],
               reads=[kpv, "prow"], writes=["vtm"])
            is_lastp = tile.last and slot == tile.nch
            if is_lastp or slot == 5:
                pkk, kpkk = PS("mm")
                OP("pe", mm_group(pkk[0:L, 0:256], [(hn[:, k, c0:c0 + L], wv[:, k, 0:256]) for k in range(8)]),
                   reads=["hn", kw], writes=[kpkk])
                OP("dve", lambda e, pkk=pkk, L=L: [
                    e.tensor_tensor(out=kvf[0:L, 0:256], in0=pkk[0:L, 0:256], in1=prow[0:L, PR["bkr"]:PR["bkr"] + 256], op=ALU.add)],
                   reads=[kpkk, "prow"], writes=["kvf"])
                OP("dve", lambda e, pvv=pvv, L=L: [
                    e.tensor_tensor(out=kvf[0:L, 256:512], in0=pvv[0:L, 0:256], in1=prow[0:L, PR["bv"]:PR["bv"] + 256], op=ALU.add)],
                   reads=[kpv, "prow"], writes=["kvf"])
                if is_lastp:
                    OP("sp", lambda e: [e.dma_start(out=kwp_d[:, :], in_=kvf[:, 0:256]),
                                        e.dma_start(out=vwp_d[:, :], in_=kvf[:, 256:512])],
                       reads=["kvf"], writes=["o_kvp"], dma=("ost", 2))
                else:
                    OP("sp", lambda e: (
                        [e.dma_start(out=kws_d[b, 124:128, :], in_=kvf[4 * b:4 * b + 4, 0:256]) for b in range(16)]
                        + [e.dma_start(out=vws_d[b, 124:128, :], in_=kvf[4 * b:4 * b + 4, 256:512]) for b in range(16)]
                        + [e.dma_start(out=kws_d[:, 0:124, :], in_=ck_d[:, 4:128, :]),
                           e.dma_start(out=vws_d[:, 0:124, :], in_=cv_d[:, 4:128, :])]),
                       reads=["kvf"], writes=["o_kvs"], dma=("ost", 34))

    def kv_carry(tile):
        n = tile.nch
        OP("dve", lambda e: [e.tensor_copy(out=kT[:, :, 0:128], in_=kT[:, :, n * 128:n * 128 + 128])], reads=["kT"], writes=["kT"])
        OP("dve", lambda e: [e.tensor_copy(out=vtm[:, 0, :], in_=vtm[:, n, :])], reads=["vtm"], writes=["vtm"])

    qT = act
    OT0 = 8

    def attn_batch(items, nq, segs_n, masks, extra_bias_first=None):
        nk = sum(segs_n)
        nb = len(items)
        psS = [PS("hd"), PS("hd")]
        sm, ksm = SM.next()
        pn, kpn = PN.next()
        pt, kpt = PT.next()
        st, kst = STAT.next()

        def sfn(e):
            r = []
            for j, it in enumerate(items):
                o = 0
                for si, n_ in enumerate(segs_n):
                    r.append(e.matmul(psS[j % 2][0][0:nq, (j // 2) * 256 + o:(j // 2) * 256 + o + n_], lhsT=it["q"], rhs=it["k"][si],
                                      start=True, stop=True))
                    o += n_
            return r
        AST = int(os.environ.get("KASTAGE", "99"))
        if os.environ.get("KSKIP3") != "sfn":
            OP("pe", sfn, reads=["qT", "kT", "qs", "ckt0", "ckt1"], writes=[psS[0][1], psS[1][1]])
        if AST < 1:
            return
        OP("dve", lambda e: [
            e.scalar_tensor_tensor(out=sm[0:nq, j, m0:m0 + ml], in0=psS[j % 2][0][0:nq, (j // 2) * 256 + m0:(j // 2) * 256 + m0 + ml],
                                   scalar=0.125, in1=map_, op0=ALU.mult, op1=ALU.add) for j in range(nb) for (m0, ml, map_) in masks],
           reads=[psS[0][1], psS[1][1], "cf"], writes=[ksm])
        OP("dve", lambda e: [e.memset(st[0:nq, 0:nb, 2:3], 0.0)], writes=[kst + "a"])
        if extra_bias_first is not None:
            OP("dve", lambda e: [
                e.tensor_scalar(out=sm[0:nq, 0:nb, 0:128], in0=sm[0:nq, 0:nb, 0:128], scalar1=extra_bias_first, scalar2=None, op0=ALU.add)],
               reads=[ksm, "hbias"], writes=[ksm])
        if AST < 2:
            return
        OP("dve", lambda e: [e.reduce_max(out=st[0:nq, 0:nb, 0:1], in_=sm[0:nq, 0:nb, 0:nk], axis=AX.X)], reads=[ksm], writes=[kst])
        OP("dve", lambda e: [
            e.tensor_scalar(out=st[0:nq, j, 1:2], in0=st[0:nq, j, 0:1], scalar1=it["sink"], scalar2=-1.0, op0=ALU.max, op1=ALU.mult)
            for j, it in enumerate(items)], reads=[kst, "prow", "cf"], writes=[kst])
        if AST < 3:
            return
        OP("act", lambda e: [
            e.activation(out=sm[0:nq, j, 0:nk], in_=sm[0:nq, j, 0:nk], func=AF.Exp, bias=st[0:nq, j, 1:2], accum_out=st[0:nq, j, 2:3])
            for j in range(nb)], reads=[ksm, kst], writes=[ksm, kst + "a"])
        OP("act", lambda e: [
            e.activation(out=st[0:nq, j, 3:4], in_=st[0:nq, j, 1:2], func=AF.Exp, bias=it["sink"]) for j, it in enumerate(items)],
           reads=[kst, "prow", "cf"], writes=[kst + "b"])
        OP("dve", lambda e: [e.tensor_tensor(out=st[0:nq, 0:nb, 4:5], in0=st[0:nq, 0:nb, 2:3], in1=st[0:nq, 0:nb, 3:4], op=ALU.add)],
           reads=[kst + "a", kst + "b"], writes=[kst + "c"])
        OP("dve", lambda e: [e.reciprocal(out=st[0:nq, 0:nb, 5:6], in_=st[0:nq, 0:nb, 4:5])], reads=[kst + "c"], writes=[kst + "d"])
        OP("dve", lambda e: [
            e.tensor_scalar(out=pn[0:nq, j, 0:nk], in0=sm[0:nq, j, 0:nk], scalar1=st[0:nq, j, 5:6], scalar2=None, op0=ALU.mult)
            for j in range(nb)], reads=[ksm, kst + "d"], writes=[kpn])
        if AST < 4:
            return
        ptp, kptp = PS("mm")
        ptv = ptp[:, :].bitcast(BF16)

        def tfn(e):
            r = []
            for j in range(nb):
                o = 0
                for si, n_ in enumerate(segs_n):
                    r.append(e.transpose(ptv[0:n_, j * 256 + si * 128:j * 256 + si * 128 + nq], pn[0:nq, j, o:o + n_], ident[0:nq, 0:nq]))
                    o += n_
            return r
        OP("pe", tfn, reads=[kpn, "cbf"], writes=[kptp])
        nmax = max(segs_n)

        def cfn(e):
            if len(set(segs_n)) == 1:
                return [e.activation(out=pt[0:nmax, 0:nb, :], in_=ptv[0:nmax, 0:nb * 256].rearrange("p (j t) -> p j t", t=256), func=AF.Copy)]
            r = []
            for si, n_ in enumerate(segs_n):
                r.append(e.activation(out=pt[0:n_, 0:nb, si * 128:si * 128 + nq],
                                      in_=ptv[0:n_, 0:nb * 256].rearrange("p (j t) -> p j t", t=256)[:, :, si * 128:si * 128 + nq],
                                      func=AF.Copy))
            return r
        OP("act", cfn, reads=[kptp], writes=[kpt])

        if AST < 5:
            return

        def pvfn(e):
            r = []
            for j, it in enumerate(items):
                for si, n_ in enumerate(segs_n):
                    r.append(e.matmul(it["out"], lhsT=it["v"][si], rhs=pt[0:n_, j, si * 128:si * 128 + nq],
                                      start=(si == 0), stop=(si == len(segs_n) - 1)))
            return r
        OP("pe", pvfn, reads=[kpt, "vtm", "cvt0", "cvt1"], writes=sorted(set(it["okey"] for it in items)))

    def attention(tile):
        rmsnorm(tile, PV["g"] + 32)
        for blk in range(2):
            if os.environ.get("KSKIP2") == "aq":
                continue
            wb, kw = W_get("wq", (blk,), WSLOT)
            wv = wb[:, :].rearrange("p (k n) -> p k n", k=8)
            for jj in range(4):
                c = blk * 4 + jj
                for (c0, c1) in tile.subs:
                    n = c1 - c0
                    pq, kpq = PS("mm")
                    OP("pe", mm_group(pq[:, 0:n], [(wv[:, k, jj * 128:(jj + 1) * 128], hn[:, k, c0:c1]) for k in range(8)]),
                       reads=["hn", kw], writes=[kpq])
                    OP("act", lambda e, pq=pq, c=c, c0=c0, c1=c1, n=n: [
                        e.activation(out=qT[:, c, c0:c1], in_=pq[:, 0:n], func=AF.Identity, bias=pcol("bq", c))],
                       reads=[kpq, "pv"], writes=["qT"])
        maskb = cf[:, CF["mb"]:CF["mb"] + 256]
        barrier()
        for ci in range(tile.nch):
            if os.environ.get("KSKIP") == "ablk":
                continue
            q0 = ci * 128
            oA = PS("acc")
            oB = PS("acc")
            obanks = [oA, oB]
            for c in range(8):
                for half in range(1):
                    pass
            for bi in range(4):
                items = []
                for cc in (2 * bi, 2 * bi + 1):
                    for e_ in range(2):
                        g = 2 * (cc // 4) + e_
                        pr = slice(e_ * 64, e_ * 64 + 64)
                        prk = slice(0, 128) if os.environ.get("KFULLK") else pr
                        ob, kob = obanks[cc // 4]
                        items.append(dict(
                            q=qT[prk, cc, q0:q0 + 128],
                            k=[kT[prk, cc // 4, q0:q0 + 128], kT[prk, cc // 4, q0 + 128:q0 + 256]],
                            v=[vtm[:, ci, g * 64:(g + 1) * 64], vtm[:, ci + 1, g * 64:(g + 1) * 64]],
                            sink=prow[:, PR["snk"] + 2 * cc + e_:PR["snk"] + 2 * cc + e_ + 1],
                            out=ob[pr, (cc % 4) * 128:(cc % 4) * 128 + 128], okey=kob))
                attn_batch(items, 128, [128, 128], [(0, 256, maskb)],
                           extra_bias_first=(hbias[:, 0:1] if (tile.first and ci == 0) else None))
            for hb_, (ob, kob) in enumerate(obanks):
                if os.environ.get("KSKIP3") == "evac":
                    continue
                OP("act", lambda e, ob=ob, hb_=hb_, q0=q0: [
                    e.activation(out=act[:, OT0 + 4 * hb_:OT0 + 4 * hb_ + 4, q0:q0 + 128],
                                 in_=ob[:, :].rearrange("p (c t) -> p c t", t=128), func=AF.Copy)],
                   reads=[kob], writes=["oT"])
        if tile.sample:
            Tp = tile.Tp
            OP("act", lambda e: [
                e.activation(out=qs[:, :, c, :], in_=qT[:, c, Tp:Tp + 64].rearrange("p (b t) -> p b t", t=4), func=AF.Copy)
                for c in range(8)], reads=["qT"], writes=["qs"])
            pvp, kpvp = PS("acc")
            pvv = pvp[:, :].rearrange("p (b c q) -> p b c q", b=16, c=2)
            maskC = cf[0:16, CF["ms"]:CF["ms"] + 128]
            for b in range(16):
                ckt, kck = CKT.next()
                cvt, kcv = CV.next()
                OP("pool", lambda e, ckt=ckt, b=b: [e.dma_start(out=ckt[:, :], in_=ckT_d[b])], writes=[kck], dma=(kck, 1))
                OP("pool", lambda e, cvt=cvt, b=b: [e.dma_start(out=cvt[:, :], in_=cv_d[b])], writes=[kcv], dma=(kcv, 1))
                items = []
                for g in range(4):
                    e_ = g % 2
                    pr = slice(e_ * 64, e_ * 64 + 64)
                    cc0 = 4 * (g // 2)
                    items.append(dict(
                        q=qs[pr, b, cc0:cc0 + 4, :].rearrange("p c t -> p (c t)"),
                        k=[ckt[pr, (g // 2) * 128:(g // 2) * 128 + 128], kT[pr, g // 2, 128 + Tp:128 + Tp + 64]],
                        v=[cvt[:, g * 64:(g + 1) * 64], vtm[0:64, 5, g * 64:(g + 1) * 64]],
                        sink=cf[0:16, CF["ss"] + g:CF["ss"] + g + 1],
                        out=pvv[pr, b, g // 2, :], okey=kpvp))
                mN = cf[0:16, CF["mn"] + 60 - 4 * b:CF["mn"] + 60 - 4 * b + 64]
                attn_batch(items, 16, [128, 64], [(0, 128, maskC), (128, 64, mN)])
            for cc in range(2):
                OP("act", lambda e, cc=cc: [
                    e.activation(out=act[:, OT0 + 4 * cc:OT0 + 4 * cc + 4, Tp:Tp + 64].rearrange("p i (b t) -> p b i t", t=4),
                                 in_=pvv[:, :, cc, :].rearrange("p b (i t) -> p b i t", t=4), func=AF.Copy)],
                   reads=[kpvp], writes=["oT"])
        barrier()
        for blk in range(2):
            if os.environ.get("KSKIP2") == "ao":
                continue
            wb, kw = W_get("wo", (blk,), WSLOT)
            wv = wb[:, :].rearrange("p (k n) -> p k n", k=8)
            for jj in range(4):
                dm = blk * 4 + jj
                for (c0, c1) in tile.subs:
                    if tile.first and c1 <= 128:
                        pass
                    n = c1 - c0
                    po, kpo = PS("mm")
                    OP("pe", mm_group(po[:, 0:n], [(wv[:, k, jj * 128:(jj + 1) * 128], act[:, OT0 + k, c0:c1]) for k in range(8)]),
                       reads=["oT", kw], writes=[kpo])
                    OP("dve", lambda e, po=po, dm=dm, c0=c0, c1=c1, n=n: [
                        e.scalar_tensor_tensor(out=xT[:, dm, c0:c1], in0=po[:, 0:n], scalar=pcol("bo", dm), in1=xT[:, dm, c0:c1],
                                               op0=ALU.add, op1=ALU.add)],
                       reads=[kpo, "xT", "pv"], writes=["xT"])

    def final_out(tile):
        for (c0, c1) in tile.subs:
            n = c1 - c0
            pss, kps = PS("mm")
            for cp in range(4):
                sq, ksq = SQ.next()
                OP("act", lambda e, sq=sq, cp=cp, c0=c0, c1=c1, n=n: [
                    e.activation(out=sq[:, :, 0:n], in_=xT[:, 2 * cp:2 * cp + 2, c0:c1], func=AF.Square)],
                   reads=["xT"], writes=[ksq])
                OP("pe", lambda e, sq=sq, cp=cp, pss=pss, n=n: [
                    e.matmul(pss[:, 0:n], lhsT=ones_bf, rhs=sq[:, j, 0:n], start=(cp == 0 and j == 0),
                             stop=(cp == 3 and j == 1)) for j in range(2)],
                   reads=[ksq, "cbf"], writes=[kps])
            rs, krs = RS.next()
            OP("act", lambda e, rs=rs, pss=pss, n=n: [
                e.activation(out=rs[:, 0:n], in_=pss[:, 0:n], func=AF.Ln, bias=pcol("eps"), scale=1.0 / D)],
               reads=[kps, "pv"], writes=[krs])
            OP("act", lambda e, rs=rs, n=n: [e.activation(out=rs[:, 0:n], in_=rs[:, 0:n], func=AF.Exp, scale=-0.5)],
               reads=[krs], writes=[krs])
            OP("dve", lambda e, rs=rs, c0=c0, c1=c1, n=n: [
                e.scalar_tensor_tensor(out=xT[:, c, c0:c1], in0=xT[:, c, c0:c1], scalar=pcol("fin", c), in1=rs[:, 0:n],
                                       op0=ALU.mult, op1=ALU.mult) for c in range(8)],
               reads=["xT", krs, "pv"], writes=["xT"])
        s0 = 0
        nout = tile.Tp
        OP("sp", lambda e: [e.dma_start(out=yT_d[:, :, tile.out0:tile.out0 + nout], in_=xT[:, :, s0:tile.Tp])],
           reads=["xT"], writes=["o_y"], dma=("ost", 1))
        if tile.sample:
            OP("sp", lambda e: [e.dma_start(out=ysT_d[:, :, :], in_=xT[:, :, tile.Tp:tile.T])], reads=["xT"], writes=["o_ys"],
               dma=("ost", 1))

    def prologue():
        OP("sp", lambda e: [e.dma_start(out=pv[:, :], in_=pv_d[:, :]), e.dma_start(out=prow[:, :], in_=prow_d[:, :]),
                            e.dma_start(out=cf[:, :], in_=cf_d[:, :]), e.dma_start(out=cbf[:, :], in_=cbf_d[:, :]),
                            e.dma_start(out=cmask[:, :], in_=cmask_d[:, :]), e.dma_start(out=sconv[:, :, :, :], in_=sconv_d[:, :, :, :])],
           writes=["pv", "prow", "cf", "cbf", "cmask", "sconv"], dma=("cld", 6))
        OP("pool", lambda e: [e.dma_start(out=wdt[:, :, :], in_=wdt_d[:, :].rearrange("p (k n) -> p k n", k=8))], writes=["wdt"],
           dma=("wdtl", 1))
        OP("act", lambda e: [e.activation(out=abc[:, :], in_=prow[:, PR["alog"]:PR["alog"] + 32], func=AF.Exp)], reads=["prow"], writes=["abc"])
        OP("dve", lambda e: [e.tensor_scalar(out=abc[:, :], in0=abc[:, :], scalar1=-1.0, scalar2=None, op0=ALU.mult)], reads=["abc"], writes=["abc"])
        OP("dve", lambda e: [e.tensor_scalar(out=hbias[:, :], in0=cmask[:, :], scalar1=-1.0, scalar2=-NEG, op0=ALU.add, op1=ALU.mult)],
           reads=["cmask"], writes=["hbias"])
        OP("dve", lambda e: [e.memset(hT[:, :], 0.0)], writes=["hT"])
        OP("dve", lambda e: [e.memset(hTb[:, :], 0.0)], writes=["hTb"])
        OP("dve", lambda e: [e.memset(hist[:, :, :], 0.0)], writes=["hist"])
        OP("dve", lambda e: [e.memset(kT[:, :, :], 0.0)], writes=["kT"])
        OP("dve", lambda e: [e.memset(vtm[:, :, :], 0.0)], writes=["vtm"])

    import os
    KSTOP = int(os.environ.get("KSTOP", "100000"))

    def program():
        ph = [0]

        def step():
            ph[0] += 1
            return ph[0] > KSTOP

        def finish(dump=None):
            if dump is not None and KSTOP < 100000:
                t = dump
                OP("sp", lambda e: [e.dma_start(out=yT_d[:, :, 0:t.Tp], in_=xT[:, :, 0:t.Tp])], reads=["xT"], writes=["o_y"], dma=("ost", 1))
            OP("sp", lambda e: [], reads=["o_y", "o_ys", "o_ssmp", "o_convp", "o_convs", "o_ssms", "o_kvp", "o_kvs"])
            if not wstate["dry"]:
                last = {}
                for i_, o_ in enumerate(S.ops[:-1]):
                    last[o_.eng if o_.dma is None else o_.dma[0]] = i_
                for i_ in last.values():
                    S.ops[-1].deps.setdefault(i_, 2)

        prologue()
        if step(): return finish()
        for tile in tiles:
            load_x(tile)
            if step(): return finish(tile)
            ffn(tile, 0)
            barrier()
            if step(): return finish(tile)
            if tile.kind == "pre":
                mixer0(tile, True)
                barrier()
                if step(): return finish(tile)
                continue
            mixer0(tile, False)
            barrier()
            if step(): return finish(tile)
            ffn(tile, 1)
            barrier()
            if step(): return finish(tile)
            kv_proj(tile)
            barrier()
            if step(): return finish(tile)
            if tile.kind == "prefull":
                kv_carry(tile)
                OP("dve", lambda e: [e.tensor_scalar(out=hT[:, :], in0=hT[:, :], scalar1=cmask[:, 0:1], scalar2=None, op0=ALU.mult)],
                   reads=["hT", "cmask"], writes=["hT"])
                OP("act", lambda e: [e.activation(out=hTb[:, :], in_=hT[:, :], func=AF.Copy)], reads=["hT"], writes=["hTb"])
                continue
            ffn(tile, 2)
            barrier()
            if step(): return finish(tile)
            attention(tile)
            barrier()
            if step(): return finish(tile)
            kv_carry(tile)
            ffn(tile, 3)
            barrier()
            if step(): return finish(tile)
            final_out(tile)
            barrier()
            if step(): return finish()
        finish()

    program()
    wstate["dry"] = False
    wstate["i"] = 0
    for k in pctr:
        pctr[k] = 0
    for r_ in (XTM, SEG, MTB, EBC, Y1, YG, SQ, RS, SG, XIN, XINS, CACC, DTT, AG3, TMPF, ACF, AC3, CEP, XW, BM, SM, PN, PT, STAT, CKT, CV):
        r_.i = 0
    program()
    S.finalize()

    with ExitStack() as es2:
        semh = {n: es2.enter_context(nc.semaphore(f"s_{n}")) for n in S.semnames}
        for n in ("pe", "act", "dve", "pool"):
            if n not in semh:
                semh[n] = es2.enter_context(nc.semaphore(f"s_{n}"))
        with nc.Block() as block:
            @block.sync
            def _(e):
                S.emit_engine("sp", e, semh)

            @block.gpsimd
            def _(e):
                S.emit_engine("pool", e, semh)

            @block.tensor
            def _(e):
                S.emit_engine("pe", e, semh)

            @block.scalar
            def _(e):
                S.emit_engine("act", e, semh)

            @block.vector
            def _(e):
                S.emit_engine("dve", e, semh)
    es.close()
    return nc, len(S.ops)


def tile_w(Wm, nb):
    K, N = Wm.shape
    a = Wm.reshape(K // 128, 128, N // nb, nb)
    return np.ascontiguousarray(a.transpose(2, 1, 0, 3)).reshape(N // nb, 128, (K // 128) * nb)


def pad_last(a, n):
    if a.shape[-1] == n:
        return a
    out = np.zeros(a.shape[:-1] + (n,), a.dtype)
    out[..., :a.shape[-1]] = a
    return out


def host_consts():
    cfa = np.zeros((128, NCF), np.float32)
    i = np.arange(128)
    um = (i[:, None] <= i[None, :]).astype(np.float32)
    cfa[:, CF["um"]:CF["um"] + 128] = um
    usf = (i[:, None] > i[None, :]).astype(np.float32)
    j = np.arange(64)
    same = (j[:, None] // 4) == (j[None, :] // 4)
    cfa[:64, CF["ums"]:CF["ums"] + 64] = (same & (j[:, None] <= j[None, :])).astype(np.float32)
    ussf = (same & (j[:, None] > j[None, :])).astype(np.float32)
    mb = np.full((128, 256), NEG, np.float32)
    mb[:, :128][i[None, :] > i[:, None]] = 0.0
    mb[:, 128:][i[None, :] <= i[:, None]] = 0.0
    cfa[:, CF["mb"]:CF["mb"] + 256] = mb
    ms = np.full((16, 128), NEG, np.float32)
    mn = np.full((16, 124), NEG, np.float32)
    for r in range(16):
        t = r % 4
        ms[r, t + 1:128] = 0.0
        mn[r, 60:60 + t + 1] = 0.0
    cfa[:16, CF["ms"]:CF["ms"] + 128] = ms
    cfa[:16, CF["mn"]:CF["mn"] + 124] = mn
    cfa[:64, CF["sm"]:CF["sm"] + 16] = (j[:, None] // 4 == np.arange(16)[None, :]).astype(np.float32)
    cb = np.zeros((128, NCB), np.float32)
    cb[:, :128] = np.eye(128)
    cb[:, 128:256] = 1.0
    cb[:, CB["um"]:CB["um"] + 128] = cfa[:, CF["um"]:CF["um"] + 128]
    cb[:, CB["us"]:CB["us"] + 128] = usf
    cb[:64, CB["ums"]:CB["ums"] + 64] = cfa[:64, CF["ums"]:CF["ums"] + 64]
    cb[:64, CB["uss"]:CB["uss"] + 64] = ussf
    sel = np.zeros((32, 32, 128), np.float32)
    for h in range(32):
        sel[h, h, :] = 1.0
    cb[:32, CB["sel"]:CB["sel"] + 4096] = sel.reshape(32, 4096)
    return cfa, cb.astype(ml_dtypes.bfloat16)


def host_weights(p):
    f32 = np.float32
    out = {}
    wgu = np.zeros((4, 11, 128, WSLOT), f32)
    wdn = np.zeros((4, 8, 128, NF * 128), f32)
    for fi in range(4):
        l, i = fi // 2, fi % 2
        g = tile_w(np.asarray(p["ffn_w_gate"][l, i]), 256)
        u = tile_w(np.asarray(p["ffn_w_up"][l, i]), 256)
        wgu[fi] = np.concatenate([g, u], axis=2)
        wdn[fi] = tile_w(np.asarray(p["ffn_w_down"][l, i]), 128)
    out["wgu"], out["wdn"] = wgu, wdn
    win = np.asarray(p["ssm_w_in"][0])
    out["winz"] = tile_w(win[:, 0:2048], 512)
    out["winx"] = tile_w(win[:, 2048:5120], 512)
    out["wdt"] = np.ascontiguousarray(tile_w(win[:, 5120:5152], 32)[0])
    out["wout"] = tile_w(np.asarray(p["ssm_w_out"][0]), 256)
    out["wkv"] = tile_w(np.asarray(p["attn_w_kv"]), 512)
    perm = np.concatenate([np.arange(QH[c][e] * 64, QH[c][e] * 64 + 64) for c in range(8) for e in range(2)])
    out["wq"] = tile_w(np.asarray(p["attn_w_q"][0])[:, perm], 512)
    out["wo"] = tile_w(np.asarray(p["attn_w_o"][0])[perm, :], 512)
    pvh = np.zeros((128, NPV), f32)
    fm = lambda v: np.asarray(v, f32).reshape(-1, 128).T
    ng = np.asarray(p["norm_gain"])
    for l in range(2):
        for i in range(3):
            pvh[:, PV["g"] + (l * 3 + i) * 8:PV["g"] + (l * 3 + i) * 8 + 8] = fm(ng[l, i])
    pvh[:, PV["kvn"]:PV["kvn"] + 8] = fm(p["kv_norm"])
    pvh[:, PV["fin"]:PV["fin"] + 8] = fm(p["final_norm"])
    cw = np.asarray(p["ssm_conv_w"][0])
    for k in range(4):
        pvh[:, PV["cw"] + k * 24:PV["cw"] + k * 24 + 24] = fm(cw[k])
    pvh[:, PV["cb"]:PV["cb"] + 24] = fm(p["ssm_conv_b"][0])
    pvh[:, PV["dsk"]:PV["dsk"] + 16] = fm(np.repeat(np.asarray(p["ssm_d"][0]), 64))
    pvh[:, PV["ng"]:PV["ng"] + 16] = fm(p["ssm_norm"][0])
    pvh[:, PV["bq"]:PV["bq"] + 8] = fm(np.asarray(p["attn_b_q"][0])[perm])
    bkv = np.asarray(p["attn_b_kv"], f32)
    pvh[:, PV["bk"]:PV["bk"] + 2] = fm(bkv[:256])
    pvh[:, PV["bo"]:PV["bo"] + 8] = fm(p["attn_b_o"][0])
    pvh[:, PV["one"]] = 1.0
    pvh[:, PV["eps"]] = EPS
    out["pv"] = pvh
    pr = np.zeros((128, NPR), f32)
    pr[:, PR["dtb"]:PR["dtb"] + 32] = np.asarray(p["ssm_dt_bias"][0])[None, :]
    pr[:, PR["alog"]:PR["alog"] + 32] = np.asarray(p["ssm_a_log"][0])[None, :]
    pr[:, PR["bv"]:PR["bv"] + 256] = bkv[None, 256:]
    pr[:, PR["bkr"]:PR["bkr"] + 256] = bkv[None, :256]
    sinks = np.asarray(p["attn_sinks"][0], f32)
    pr[:, PR["snk"]:PR["snk"] + 16] = np.array([sinks[QH[c][e]] for c in range(8) for e in range(2)], f32)[None, :]
    out["prow"] = pr
    cfa, cb = host_consts()
    for g in range(4):
        for r in range(16):
            cfa[r, CF["ss"] + g] = sinks[QH[4 * (g // 2) + r // 4][g % 2]]
    out["cf"], out["cbf"] = cfa, cb
    return out


_PROG = {}


def run(inputs, seq, npre, nmain):
    f32 = np.float32
    key = (npre, nmain)
    if key not in _PROG:
        _PROG[key] = build_program(npre, nmain)
    nc, nops = _PROG[key]
    w = host_weights(inputs)
    xp = np.asarray(inputs["x_prompt"], f32)
    xs = np.asarray(inputs["x_sample"], f32)
    sconv = np.asarray(inputs["state_conv"], f32)[0]
    sssm = np.asarray(inputs["state_ssm"], f32)[0]
    ck = np.asarray(inputs["cache_k_win"], f32)
    cv = np.asarray(inputs["cache_v_win"], f32)
    half = seq // 2
    in_maps = []
    for c in range(8):
        s, hf = c // 2, c % 2
        m = dict(w)
        if hf == 0:
            m["xpre"] = np.zeros((D, npre * 128), f32)
        else:
            m["xpre"] = np.ascontiguousarray(xp[s, 0:half].T)
        m["xmain"] = np.ascontiguousarray(xp[s, hf * half:(hf + 1) * half].T)
        m["cmask"] = np.full((128, 1), float(hf), f32)
        b0 = 16 * c
        m["xsmp"] = np.ascontiguousarray(xs[b0:b0 + 16].reshape(64, D).T)
        m["sconv"] = np.ascontiguousarray(sconv[b0:b0 + 16].reshape(16, 3, 24, 128).transpose(3, 2, 0, 1))
        m["sssm"] = np.ascontiguousarray(sssm[b0:b0 + 16].reshape(16, DIN, 128).transpose(0, 2, 1))
        kk = ck[b0:b0 + 16].reshape(16, 128, 2, 2, 64)
        m["ckT"] = np.ascontiguousarray(kk.transpose(0, 3, 4, 2, 1)).reshape(16, 128, 256)
        m["ck"] = np.ascontiguousarray(ck[b0:b0 + 16].reshape(16, 128, 256))
        m["cv"] = np.ascontiguousarray(cv[b0:b0 + 16].reshape(16, 128, 256))
        in_maps.append(m)
    if os.environ.get("KTRACE"):
        res = run_bass_kernel_spmd(nc, in_maps, core_ids=list(range(8)), trace=True)
        print("EXEC_TIME_NS", res.exec_time_ns)
    else:
        res = run_bass_kernel_spmd(nc, in_maps, core_ids=list(range(8)))
    R = res.results
    B = xp.shape[0]
    y_p = np.zeros((B, seq, D), f32)
    conv_p = np.zeros((1, B, 3, 3072), f32)
    ssm_p = np.zeros((1, B, 32, 64, 128), f32)
    k_p = np.zeros((B, 128, 4, 64), f32)
    v_p = np.zeros((B, 128, 4, 64), f32)
    y_s = np.zeros((128, 4, D), f32)
    conv_s = np.zeros((1, 128, 3, 3072), f32)
    ssm_s = np.zeros((1, 128, 32, 64, 128), f32)
    k_s = np.zeros((128, 128, 4, 64), f32)
    v_s = np.zeros((128, 128, 4, 64), f32)
    for c in range(8):
        s, hf = c // 2, c % 2
        r = R[c]
        y_p[s, hf * half:(hf + 1) * half] = r["yT"].T
        b0 = 16 * c
        y_s[b0:b0 + 16] = r["ysT"].T.reshape(16, 4, D)
        conv_s[0, b0:b0 + 16] = r["convs"].transpose(2, 3, 1, 0).reshape(16, 3, 3072)
        ssm_s[0, b0:b0 + 16] = r["ssms"].transpose(0, 2, 1).reshape(16, 32, 64, 128)
        k_s[b0:b0 + 16] = r["kws"].reshape(16, 128, 4, 64)
        v_s[b0:b0 + 16] = r["vws"].reshape(16, 128, 4, 64)
        if hf == 1:
            conv_p[0, s] = r["convp"].transpose(2, 1, 0).reshape(3, 3072)
            ssm_p[0, s] = r["ssmp"].T.reshape(32, 64, 128)
            k_p[s] = r["kwp"].reshape(128, 4, 64)
            v_p[s] = r["vwp"].reshape(128, 4, 64)
    return (y_p, y_s, conv_p, ssm_p, k_p, v_p, conv_s, ssm_s, k_s, v_s)


def kernel(**inputs):
    seq = int(np.asarray(inputs["x_prompt"]).shape[1])
    nchunks = seq // 128
    return run(inputs, seq, nchunks // 2, nchunks // 2)
```

```python
import os
import numpy as np
import ml_dtypes
from contextlib import ExitStack
import concourse.bass as bass
import concourse.mybir as mybir
from concourse.bass_utils import run_bass_kernel_spmd

F32, BF16 = mybir.dt.float32, mybir.dt.bfloat16
AF = mybir.ActivationFunctionType
ALU = mybir.AluOpType
AX = mybir.AxisListType

D = 1024
DFF = 2816
NF = 22
DIN = 2048
EPS = 1e-6
NEG = -30000.0
WSLOT = 4096
NWS = 3

QH = [[c, c + 4] if c < 4 else [c + 4, c + 8] for c in range(8)]


class Op:
    __slots__ = ("eng", "fn", "deps", "dma", "signal", "value")

    def __init__(self, eng, fn, deps, dma):
        self.eng, self.fn, self.deps, self.dma = eng, fn, deps, dma
        self.signal = False
        self.value = 0


class Sched:
    def __init__(self):
        self.ops = []
        self.lw = {}
        self.rd = {}

    def op(self, eng, fn, reads=(), writes=(), dma=None):
        idx = len(self.ops)
        deps = {}
        for k in reads:
            w = self.lw.get(k)
            if w is not None:
                deps[w] = 2
        for k in writes:
            w = self.lw.get(k)
            if w is not None:
                deps.setdefault(w, 1)
            r = self.rd.get(k)
            if r:
                for e_, i_ in r[0].items():
                    deps.setdefault(i_, 1)
                for i_ in r[1]:
                    deps.setdefault(i_, 1)
        for k in writes:
            self.lw[k] = idx
            self.rd[k] = [{}, []]
        for k in reads:
            r = self.rd.setdefault(k, [{}, []])
            if dma is None:
                r[0][eng] = idx
            else:
                r[1].append(idx)
        self.ops.append(Op(eng, fn, deps, dma))
        return idx

    def finalize(self):
        ops = self.ops
        for o in ops:
            need = []
            for d, kind in o.deps.items():
                p = ops[d]
                if p.dma is not None:
                    need.append(d)
                elif o.dma is None and p.eng == o.eng:
                    if o.eng != "pe" and kind == 2:
                        need.append(d)
                else:
                    need.append(d)
            o.deps = need
            for d in need:
                ops[d].signal = True
        cnt = {}
        for o in ops:
            if o.dma is not None:
                s, n = o.dma
                cnt[s] = cnt.get(s, 0) + 16 * n
                o.value = cnt[s]
            elif o.signal:
                cnt[o.eng] = cnt.get(o.eng, 0) + 1
                o.value = cnt[o.eng]
        self.semnames = sorted(cnt.keys())

    def emit_engine(self, engname, e, semh):
        ops = self.ops
        waited = {}
        for o in ops:
            if o.eng != engname:
                continue
            for d in o.deps:
                p = ops[d]
                s = p.dma[0] if p.dma is not None else p.eng
                if waited.get(s, 0) < p.value:
                    e.wait_ge(semh[s], p.value)
                    waited[s] = p.value
            ins = o.fn(e)
            if o.dma is not None:
                assert len(ins) == o.dma[1], (len(ins), o.dma)
                for i_ in ins:
                    i_.then_inc(semh[o.dma[0]], 16)
            elif o.signal:
                ins[-1].then_inc(semh[engname], 1)


class Rot:
    def __init__(self, tensors, name):
        self.t = tensors
        self.n = len(tensors)
        self.i = 0
        self.name = name

    def next(self):
        j = self.i % self.n
        self.i += 1
        return self.t[j], f"{self.name}{j}"


def subtiles(T):
    out = []
    c = 0
    while c < T:
        n = min(512, T - c)
        out.append((c, c + n))
        c += n
    return out


class Tile:
    def __init__(self, kind, src, col0, nch, sample=False, first=False, last=False, out0=0):
        self.kind = kind
        self.src = src
        self.col0 = col0
        self.nch = nch
        self.Tp = nch * 128
        self.sample = sample
        self.T = self.Tp + (64 if sample else 0)
        self.first = first
        self.last = last
        self.out0 = out0
        self.subs = subtiles(self.Tp) + ([(self.Tp, self.Tp + 64)] if sample else [])


def make_tiles(npre, nmain, tch=4):
    def split(n):
        k = -(-n // tch)
        base, rem = divmod(n, k)
        return [base + 1] * rem + [base] * (k - rem)
    tiles = []
    sp = split(npre)
    c = 0
    for i, n in enumerate(sp):
        tiles.append(Tile("prefull" if i == len(sp) - 1 else "pre", "xpre", c * 128, n))
        c += n
    sp = split(nmain)
    c = 0
    for i, n in enumerate(sp):
        tiles.append(Tile("main", "xmain", c * 128, n, sample=(i == len(sp) - 1), first=(i == 0),
                          last=(i == len(sp) - 1), out0=c * 128))
        c += n
    return tiles


PV = {}
_o = 0
for _n, _w in [("g", 48), ("kvn", 8), ("fin", 8), ("cw", 96), ("cb", 24), ("dsk", 16), ("ng", 16),
               ("bq", 8), ("bk", 2), ("bo", 8), ("one", 1), ("eps", 1)]:
    PV[_n] = _o
    _o += _w
NPV = _o
PR = {"dtb": 0, "alog": 32, "bv": 64, "bkr": 320, "snk": 576}
NPR = 592
CF = {"um": 0, "ums": 128, "mb": 192, "ms": 448, "mn": 576, "sm": 700, "ss": 716}
NCF = 720
CB = {"id": 0, "on": 128, "um": 256, "us": 384, "ums": 512, "uss": 576, "sel": 640}
NCB = 640 + 4096


def build_program(npre, nmain):
    tiles = make_tiles(npre, nmain)
    TMAX = max(t.T for t in tiles)
    nc = bass.Bass("TRN2", target_bir_lowering=False)
    S = Sched()

    def din(name, shape, dt=F32):
        return nc.dram_tensor(name, list(shape), dt, kind="ExternalInput").ap()

    def dout(name, shape):
        return nc.dram_tensor(name, list(shape), F32, kind="ExternalOutput").ap()

    NOUT = nmain * 128
    dr = {}
    dr["xpre"] = din("xpre", [D, npre * 128]).rearrange("(c p) t -> p c t", p=128)
    dr["xmain"] = din("xmain", [D, nmain * 128]).rearrange("(c p) t -> p c t", p=128)
    xsmp_d = din("xsmp", [D, 64]).rearrange("(c p) t -> p c t", p=128)
    wd = {
        "wgu": din("wgu", [4, 11, 128, WSLOT]),
        "wdn": din("wdn", [4, 8, 128, NF * 128]),
        "winz": din("winz", [4, 128, WSLOT]),
        "winx": din("winx", [6, 128, WSLOT]),
        "wout": din("wout", [4, 128, WSLOT]),
        "wkv": din("wkv", [1, 128, WSLOT]),
        "wq": din("wq", [2, 128, WSLOT]),
        "wo": din("wo", [2, 128, WSLOT]),
    }
    wdt_d = din("wdt", [128, 256])
    pv_d = din("pv", [128, NPV])
    prow_d = din("prow", [128, NPR])
    cf_d = din("cf", [128, NCF])
    cbf_d = din("cbf", [128, NCB], BF16)
    cmask_d = din("cmask", [128, 1])
    sconv_d = din("sconv", [128, 24, 16, 3])
    sssm_d = din("sssm", [16, 128, DIN])
    ckT_d = din("ckT", [16, 128, 256])
    ck_d = din("ck", [16, 128, 256])
    cv_d = din("cv", [16, 128, 256])
    yT_d = dout("yT", [D, NOUT]).rearrange("(c p) t -> p c t", p=128)
    ysT_d = dout("ysT", [D, 64]).rearrange("(c p) t -> p c t", p=128)
    convp_d = dout("convp", [128, 24, 3])
    ssmp_d = dout("ssmp", [128, DIN])
    kwp_d = dout("kwp", [128, 256])
    vwp_d = dout("vwp", [128, 256])
    convs_d = dout("convs", [128, 24, 16, 3])
    ssms_d = dout("ssms", [16, 128, DIN])
    kws_d = dout("kws", [16, 128, 256])
    vws_d = dout("vws", [16, 128, 256])

    es = ExitStack()

    def sb(name, shape, dt):
        return es.enter_context(nc.sbuf_tensor(name, list(shape), dt))

    xT = sb("xT", [128, 8, TMAX], F32)
    hn = sb("hn", [128, 8, TMAX], BF16)
    act = sb("act", [128, 40, TMAX], BF16)
    wsl = [sb(f"wsl{i}", [128, WSLOT], BF16) for i in range(NWS)]
    xsT = act[:, 16:40, :]
    XTM = Rot([sb(f"xtm{i}", [128, 2560], BF16) for i in range(2)], "xtm")
    hT = sb("hT", [128, DIN], F32)
    hTb = sb("hTb", [128, DIN], BF16)
    Ce = sb("Ce", [128, 32, 64], BF16)
    CEP = Rot([sb(f"cep{i}", [128, 2, 128], BF16) for i in range(2)], "cep")
    Gm = sb("Gm", [128, 4, 128], F32)
    SEG = Rot([sb(f"seg{i}", [128, 2, 128], F32) for i in range(2)], "seg")
    MTB = Rot([sb(f"mtb{i}", [128, 2, 128], BF16) for i in range(2)], "mtb")
    EBC = Rot([sb(f"ebc{i}", [128, 2, 128], F32) for i in range(2)], "ebc")
    edec = sb("edec", [128, 32, 16], F32)
    Y1 = Rot([sb(f"y1{i}", [128, 128], F32) for i in range(2)], "y1")
    YG = Rot([sb(f"yg{i}", [128, 4, 128], F32) for i in range(1)], "yg")
    SQ = Rot([sb(f"sq{i}", [128, 2, 512], BF16) for i in range(2)], "sq")
    RS = Rot([sb(f"rs{i}", [128, 512], F32) for i in range(1)], "rs")
    SG = Rot([sb(f"sg{i}", [128, 512], F32) for i in range(2)], "sg")
    XIN = Rot([sb(f"xin{i}", [128, 3 + TMAX], F32) for i in range(2)], "xin")
    XINS = Rot([sb(f"xins{i}", [128, 16, 7], F32) for i in range(2)], "xins")
    CACC = Rot([sb(f"cacc{i}", [128, TMAX], F32) for i in range(1)], "cacc")
    hist = sb("hist", [128, 24, 3], F32)
    sconv = sb("sconv_t", [128, 24, 16, 3], F32)
    convs = sconv
    DTT = Rot([sb(f"dtt{i}", [128, 320], F32) for i in range(2)], "dtt")
    AG3 = Rot([sb(f"ag3{i}", [128, 3, 32], BF16) for i in range(2)], "ag3")
    TMPF = Rot([sb(f"tmpf{i}", [128, 2, 32], F32) for i in range(2)], "tmpf")
    ACF = Rot([sb(f"acf{i}", [32, 3, 128], F32) for i in range(1)], "acf")
    AC3 = Rot([sb(f"ac3{i}", [32, 3, 128], BF16) for i in range(2)], "ac3")
    XW = Rot([sb(f"xw{i}", [128, 512], BF16) for i in range(2)], "xw")
    xwall = sb("xwall", [64, DIN], BF16)
    BM = Rot([sb(f"bm{i}", [64, 128], BF16) for i in range(2)], "bm")
    kT = sb("kT", [128, 2, 128 + TMAX], BF16)
    vtm = sb("vtm", [128, 6, 256], BF16)
    kvf = sb("kvf", [128, 512], F32)
    SM = Rot([sb(f"sm{i}", [128, 4, 256], F32) for i in range(1)], "sm")
    PN = Rot([sb(f"pn{i}", [128, 4, 256], BF16) for i in range(1)], "pn")
    PT = Rot([sb(f"pt{i}", [128, 4, 256], BF16) for i in range(1)], "pt")
    STAT = Rot([sb(f"stat{i}", [128, 4, 8], F32) for i in range(4)], "stat")
    qs = sb("qs", [128, 16, 8, 4], BF16)
    CKT = Rot([sb(f"ckt{i}", [128, 256], BF16) for i in range(2)], "ckt")
    CV = Rot([sb(f"cvt{i}", [128, 256], BF16) for i in range(2)], "cvt")
    wdt = sb("wdt_t", [128, 8, 32], BF16)
    pv = sb("pv_t", [128, NPV], F32)
    prow = sb("prow_t", [128, NPR], F32)
    abc = sb("abc", [128, 32], F32)
    cf = sb("cf_t", [128, NCF], F32)
    cbf = sb("cbf_t", [128, NCB], BF16)
    cmask = sb("cmask_t", [128, 1], F32)
    hbias = sb("hbias", [128, 1], F32)
    ps = [es.enter_context(nc.psum_tensor(f"ps{i}", [128, 512], F32)) for i in range(8)]
    pools = {"mm": [0, 1, 2, 3], "hd": [4, 5], "acc": [6, 7]}
    pctr = {"mm": 0, "hd": 0, "acc": 0}

    def PS(pool):
        l = pools[pool]
        i = l[pctr[pool] % len(l)]
        pctr[pool] += 1
        return ps[i], f"ps{i}"

    ident = cbf[:, 0:128]
    ones_bf = cbf[:, 128:256]
    pcol = lambda name, j=0: pv[:, PV[name] + j:PV[name] + j + 1]

    wseq = []
    wstate = {"dry": True, "i": 0, "issued": 0}

    def wdram(name, idx):
        a = wd[name]
        return a[idx[0], idx[1]] if len(idx) == 2 else a[idx[0]]

    def W_issue(upto):
        while wstate["issued"] <= min(upto, len(wseq) - 1):
            k = wstate["issued"]
            name, idx, nel = wseq[k]
            slot = wsl[k % NWS]
            src = wdram(name, idx)
            S.op("pool", (lambda e, slot=slot, src=src, nel=nel: [e.dma_start(out=slot[:, 0:nel], in_=src[:, 0:nel])]),
                 writes=[f"wsl{k % NWS}"], dma=(f"w{k % NWS}", 1))
            wstate["issued"] += 1

    def W_get(name, idx, nel):
        k = wstate["i"]
        wstate["i"] += 1
        if wstate["dry"]:
            wseq.append((name, idx, nel))
            return wsl[k % NWS], f"wsl{k % NWS}"
        assert wseq[k] == (name, idx, nel)
        W_issue(k + NWS - 1)
        return wsl[k % NWS], f"wsl{k % NWS}"

    def OP(eng, fn, reads=(), writes=(), dma=None):
        if wstate["dry"]:
            return
        S.op(eng, fn, reads, writes, dma)

    def barrier():
        if wstate["dry"] or os.environ.get("KNOBAR"):
            return
        last = {}
        for i_, o_ in enumerate(S.ops):
            if o_.dma is None and o_.eng in ("pe", "act", "dve"):
                last[o_.eng] = i_
        for eng in ("pe", "act", "dve"):
            S.op(eng, lambda e: [e.nop()])
            for en2, i_ in last.items():
                if en2 != eng:
                    S.ops[-1].deps[i_] = 2

    def mm_group(out_ap, pairs):
        def fn(e):
            r = []
            n = len(pairs)
            for i, (l, rr) in enumerate(pairs):
                r.append(e.matmul(out_ap, lhsT=l, rhs=rr, start=(i == 0), stop=(i == n - 1)))
            return r
        return fn

    def rmsnorm(tile, gbase, ndim=D):
        for (c0, c1) in tile.subs:
            n = c1 - c0
            pss, kps = PS("mm")
            for cp in range(4):
                sq, ksq = SQ.next()
                OP("act", lambda e, sq=sq, cp=cp, c0=c0, c1=c1, n=n: [
                    e.activation(out=sq[:, :, 0:n], in_=xT[:, 2 * cp:2 * cp + 2, c0:c1], func=AF.Square)],
                   reads=["xT"], writes=[ksq])
                OP("pe", lambda e, sq=sq, cp=cp, pss=pss, n=n: [
                    e.matmul(pss[:, 0:n], lhsT=ones_bf, rhs=sq[:, j, 0:n], start=(cp == 0 and j == 0),
                             stop=(cp == 3 and j == 1)) for j in range(2)],
                   reads=[ksq, "cbf"], writes=[kps])
            rs, krs = RS.next()
            OP("act", lambda e, rs=rs, pss=pss, n=n: [
                e.activation(out=rs[:, 0:n], in_=pss[:, 0:n], func=AF.Ln, bias=pcol("eps"), scale=1.0 / ndim)],
               reads=[kps, "pv"], writes=[krs])
            OP("act", lambda e, rs=rs, n=n: [
                e.activation(out=rs[:, 0:n], in_=rs[:, 0:n], func=AF.Exp, scale=-0.5)],
               reads=[krs], writes=[krs])
            OP("dve", lambda e, rs=rs, c0=c0, c1=c1, n=n: [
                e.scalar_tensor_tensor(out=hn[:, c, c0:c1], in0=xT[:, c, c0:c1], scalar=pv[:, gbase + c:gbase + c + 1],
                                       in1=rs[:, 0:n], op0=ALU.mult, op1=ALU.mult) for c in range(8)],
               reads=["xT", krs, "pv"], writes=["hn"])

    def ffn(tile, fi):
        rmsnorm(tile, PV["g"] + [0, 16, 24, 40][fi])
        for blk in range(11):
            wb, kw = W_get("wgu", (fi, blk), WSLOT)
            wv = wb[:, :].rearrange("p (a k n) -> p a k n", a=2, k=8)
            for jj in range(2):
                j = blk * 2 + jj
                for (c0, c1) in tile.subs:
                    n = c1 - c0
                    pg, kpg = PS("mm")
                    pu, kpu = PS("mm")
                    OP("pe", lambda e, wv=wv, jj=jj, c0=c0, c1=c1, n=n, pg=pg, pu=pu: (
                        mm_group(pg[:, 0:n], [(wv[:, 0, k, jj * 128:(jj + 1) * 128], hn[:, k, c0:c1]) for k in range(8)])(e)
                        + mm_group(pu[:, 0:n], [(wv[:, 1, k, jj * 128:(jj + 1) * 128], hn[:, k, c0:c1]) for k in range(8)])(e)),
                       reads=["hn", kw], writes=[kpg, kpu])
                    sg, ksg = SG.next()
                    OP("act", lambda e, sg=sg, pg=pg, n=n: [e.activation(out=sg[:, 0:n], in_=pg[:, 0:n], func=AF.Silu)],
                       reads=[kpg], writes=[ksg])
                    OP("dve", lambda e, sg=sg, pu=pu, j=j, c0=c0, c1=c1, n=n: [
                        e.tensor_tensor(out=act[:, j, c0:c1], in0=sg[:, 0:n], in1=pu[:, 0:n], op=ALU.mult)],
                       reads=[ksg, kpu], writes=["act"])
        for blk in range(8):
            wb, kw = W_get("wdn", (fi, blk), NF * 128)
            wv = wb[:, 0:NF * 128].rearrange("p (j n) -> p j n", j=NF)
            for (c0, c1) in tile.subs:
                n = c1 - c0
                pd, kpd = PS("mm")
                OP("pe", mm_group(pd[:, 0:n], [(wv[:, j, :], act[:, j, c0:c1]) for j in range(NF)]),
                   reads=["act", kw], writes=[kpd])
                OP("dve", lambda e, pd=pd, blk=blk, c0=c0, c1=c1, n=n: [
                    e.scalar_tensor_tensor(out=xT[:, blk, c0:c1], in0=pd[:, 0:n], scalar=0.5, in1=xT[:, blk, c0:c1],
                                           op0=ALU.mult, op1=ALU.add)],
                   reads=[kpd, "xT"], writes=["xT"])

    def load_x(tile):
        src = dr[tile.src]
        OP("sp", lambda e: [e.dma_start(out=xT[:, :, 0:tile.Tp], in_=src[:, :, tile.col0:tile.col0 + tile.Tp])],
           writes=["xT"], dma=("xld", 1))
        if tile.sample:
            OP("sp", lambda e: [e.dma_start(out=xT[:, :, tile.Tp:tile.T], in_=xsmp_d[:, :, :])],
               writes=["xT"], dma=("xld", 1))

    def conv_chunk(tile, m, pre_only):
        pass

    def mixer_in(tile, prefix):
        rmsnorm(tile, PV["g"] + 8)
        Tp = tile.Tp
        if not prefix:
            for blk in range(4):
                wb, kw = W_get("winz", (blk,), WSLOT)
                wv = wb[:, :].rearrange("p (k n) -> p k n", k=8)
                for jj in range(4):
                    m = blk * 4 + jj
                    for (c0, c1) in tile.subs:
                        n = c1 - c0
                        pz, kpz = PS("mm")
                        OP("pe", mm_group(pz[:, 0:n], [(wv[:, k, jj * 128:(jj + 1) * 128], hn[:, k, c0:c1]) for k in range(8)]),
                           reads=["hn", kw], writes=[kpz])
                        OP("act", lambda e, pz=pz, m=m, c0=c0, c1=c1, n=n: [
                            e.activation(out=act[:, m, c0:c1], in_=pz[:, 0:n], func=AF.Silu)],
                           reads=[kpz], writes=["sz"])
        nblk = 6
        for blk in range(nblk):
            wb, kw = W_get("winx", (blk,), WSLOT)
            wv = wb[:, :].rearrange("p (k n) -> p k n", k=8)
            for jj in range(4):
                m = blk * 4 + jj
                xin, kxin = XIN.next()
                xins, kxins = XINS.next()
                OP("dve", lambda e, xin=xin, m=m: [e.tensor_copy(out=xin[:, 0:3], in_=hist[:, m, :])],
                   reads=["hist"], writes=[kxin])
                for (c0, c1) in tile.subs:
                    n = c1 - c0
                    px, kpx = PS("mm")
                    OP("pe", mm_group(px[:, 0:n], [(wv[:, k, jj * 128:(jj + 1) * 128], hn[:, k, c0:c1]) for k in range(8)]),
                       reads=["hn", kw], writes=[kpx])
                    if c0 < Tp:
                        OP("act", lambda e, px=px, xin=xin, c0=c0, c1=c1, n=n: [
                            e.activation(out=xin[:, 3 + c0:3 + c1], in_=px[:, 0:n], func=AF.Copy)],
                           reads=[kpx], writes=[kxin])
                    else:
                        OP("act", lambda e, px=px, xins=xins: [
                            e.activation(out=xins[:, :, 3:7], in_=px[:, 0:64].rearrange("p (b t) -> p b t", t=4), func=AF.Copy)],
                           reads=[kpx], writes=[kxins])
                cacc, kc = CACC.next()
                wc = lambda k, m=m: pv[:, PV["cw"] + k * 24 + m:PV["cw"] + k * 24 + m + 1]
                OP("dve", lambda e, xin=xin, cacc=cacc, m=m, wc=wc: [
                    e.tensor_scalar(out=cacc[:, 0:Tp], in0=xin[:, 3:3 + Tp], scalar1=wc(3), scalar2=pcol("cb", m),
                                    op0=ALU.mult, op1=ALU.add)],
                   reads=[kxin, "pv"], writes=[kc])
                for k in range(3):
                    OP("dve", lambda e, xin=xin, cacc=cacc, k=k, wc=wc: [
                        e.scalar_tensor_tensor(out=cacc[:, 0:Tp], in0=xin[:, k:k + Tp], scalar=wc(k), in1=cacc[:, 0:Tp],
                                               op0=ALU.mult, op1=ALU.add)],
                       reads=[kxin, kc, "pv"], writes=[kc])
                OP("act", lambda e, cacc=cacc, m=m: [e.activation(out=xsT[:, m, 0:Tp], in_=cacc[:, 0:Tp], func=AF.Silu)],
                   reads=[kc], writes=["xsT"])
                OP("dve", lambda e, xin=xin, m=m: [e.tensor_copy(out=hist[:, m, :], in_=xin[:, Tp:Tp + 3])],
                   reads=[kxin], writes=["hist"])
                if tile.sample:
                    OP("dve", lambda e, xins=xins, m=m: [e.tensor_copy(out=xins[:, :, 0:3], in_=sconv[:, m, :, :])],
                       reads=["sconv"], writes=[kxins])
                    cacc, kc = CACC.next()
                    cv3 = lambda cacc=cacc: cacc[:, 0:64].rearrange("p (b t) -> p b t", t=4)
                    OP("dve", lambda e, xins=xins, cv3=cv3, m=m, wc=wc: [
                        e.tensor_scalar(out=cv3(), in0=xins[:, :, 3:7], scalar1=wc(3), scalar2=pcol("cb", m),
                                        op0=ALU.mult, op1=ALU.add)],
                       reads=[kxins, "pv"], writes=[kc])
                    for k in range(3):
                        OP("dve", lambda e, xins=xins, cv3=cv3, k=k, wc=wc: [
                            e.scalar_tensor_tensor(out=cv3(), in0=xins[:, :, k:k + 4], scalar=wc(k), in1=cv3(),
                                                   op0=ALU.mult, op1=ALU.add)],
                           reads=[kxins, kc, "pv"], writes=[kc])
                    OP("act", lambda e, cacc=cacc, m=m: [e.activation(out=xsT[:, m, Tp:Tp + 64], in_=cacc[:, 0:64], func=AF.Silu)],
                       reads=[kc], writes=["xsT"])
                    OP("dve", lambda e, xins=xins, m=m: [e.tensor_copy(out=convs[:, m, :, :], in_=xins[:, :, 4:7])],
                       reads=[kxins], writes=["sconv"])

    def ssd_chunk(tile, ci, sample, prefix):
        L = 64 if sample else 128
        col0 = tile.Tp if sample else ci * 128
        c1 = col0 + L
        um = cf[0:L, CF["ums"]:CF["ums"] + L] if sample else cf[0:L, CF["um"]:CF["um"] + L]
        xtm, kx = XTM.next()
        for grp in range(3):
            ms = list(range(grp * 8, min(grp * 8 + 8, 20)))
            pb, kpb = PS("mm")
            pbv = pb[:, :].bitcast(BF16)
            OP("pe", lambda e, ms=ms, pbv=pbv: [
                e.transpose(pbv[0:L, j * 128:(j + 1) * 128], xsT[:, m, col0:c1], ident) for j, m in enumerate(ms)],
               reads=["xsT", "cbf"], writes=[kpb])
            eng = "act" if grp % 2 == 0 else "dve"
            w = len(ms) * 128
            if eng == "act":
                OP("act", lambda e, pbv=pbv, grp=grp, w=w: [
                    e.activation(out=xtm[0:L, grp * 1024:grp * 1024 + w], in_=pbv[0:L, 0:w], func=AF.Copy)],
                   reads=[kpb], writes=[kx])
            else:
                OP("dve", lambda e, pbv=pbv, grp=grp, w=w: [
                    e.tensor_copy(out=xtm[0:L, grp * 1024:grp * 1024 + w], in_=pbv[0:L, 0:w])],
                   reads=[kpb], writes=[kx])
        dtt, kd = DTT.next()
        pdt, kpd = PS("mm")
        OP("pe", mm_group(pdt[0:L, 0:32], [(hn[:, k, col0:c1], wdt[:, k, :]) for k in range(8)]),
           reads=["hn", "wdt"], writes=[kpd])
        XB, AXc, EX, LN1, DT, AG, WW, NAC = [slice(32 * i, 32 * i + 32) for i in range(8)]
        OP("dve", lambda e: [e.tensor_tensor(out=dtt[0:L, XB], in0=pdt[0:L, 0:32], in1=prow[0:L, PR["dtb"]:PR["dtb"] + 32], op=ALU.add)],
           reads=[kpd, "prow"], writes=[kd])
        OP("act", lambda e: [e.activation(out=dtt[0:L, AXc], in_=dtt[0:L, XB], func=AF.Abs)],
           reads=[kd], writes=[kd])
        OP("act", lambda e: [e.activation(out=dtt[0:L, EX], in_=dtt[0:L, AXc], func=AF.Exp, scale=-1.0)],
           reads=[kd], writes=[kd])
        OP("act", lambda e: [e.activation(out=dtt[0:L, LN1], in_=dtt[0:L, EX], func=AF.Ln, bias=pv[0:L, PV["one"]:PV["one"] + 1])],
           reads=[kd, "pv"], writes=[kd])
        OP("dve", lambda e: [e.scalar_tensor_tensor(out=dtt[0:L, DT], in0=dtt[0:L, XB], scalar=0.0, in1=dtt[0:L, LN1],
                                                    op0=ALU.max, op1=ALU.add)],
           reads=[kd], writes=[kd])
        OP("dve", lambda e: [e.tensor_tensor(out=dtt[0:L, AG], in0=dtt[0:L, DT], in1=abc[0:L, :], op=ALU.mult)],
           reads=[kd, "abc"], writes=[kd])
        ag3, kag = AG3.next()
        tmpf, ktf = TMPF.next()

        def split3(src, dst3, tmp, rk, wk, tk, P, n):
            OP("dve", lambda e: [e.tensor_copy(out=dst3[0:P, 0, 0:n], in_=src)], reads=[rk], writes=[wk])
            OP("dve", lambda e: [e.tensor_tensor(out=tmp[0:P, 0, 0:n], in0=src, in1=dst3[0:P, 0, 0:n], op=ALU.subtract)],
               reads=[rk, wk], writes=[tk])
            OP("dve", lambda e: [e.tensor_copy(out=dst3[0:P, 1, 0:n], in_=tmp[0:P, 0, 0:n])], reads=[tk], writes=[wk])
            OP("dve", lambda e: [e.tensor_tensor(out=tmp[0:P, 1, 0:n], in0=tmp[0:P, 0, 0:n], in1=dst3[0:P, 1, 0:n], op=ALU.subtract)],
               reads=[tk, wk], writes=[tk])
            OP("dve", lambda e: [e.tensor_copy(out=dst3[0:P, 2, 0:n], in_=tmp[0:P, 1, 0:n])], reads=[tk], writes=[wk])
        split3(dtt[0:L, AG], ag3, tmpf, kd, kag, ktf, L, 32)
        umb = cbf[0:L, CB["ums"]:CB["ums"] + L] if sample else cbf[0:L, CB["um"]:CB["um"] + L]
        usb = cbf[0:L, CB["uss"]:CB["uss"] + L] if sample else cbf[0:L, CB["us"]:CB["us"] + L]
        yield
        pc, kpc = PS("mm")
        def cums(e):
            r = []
            for i in range(3):
                r.append(e.matmul(pc[0:L, 0:32], lhsT=usb, rhs=ag3[0:L, i, :], start=(i == 0), stop=(i == 2)))
            for i in range(3):
                r.append(e.matmul(pc[0:L, 32:64], lhsT=umb, rhs=ag3[0:L, i, :], start=(i == 0), stop=(i == 2)))
            for i in range(3):
                r.append(e.matmul(pc[0:32, 64:64 + L], lhsT=ag3[0:L, i, :], rhs=umb, start=(i == 0), stop=(i == 2)))
            if not sample:
                for i in range(3):
                    r.append(e.matmul(pc[:, 192:224], lhsT=ones_bf[0:L, :], rhs=ag3[0:L, i, :], start=(i == 0), stop=(i == 2)))
            return r
        OP("pe", cums, reads=[kag, "cbf"], writes=[kpc])
        OP("act", lambda e: [e.activation(out=dtt[0:L, WW], in_=pc[0:L, 0:32], func=AF.Exp)], reads=[kpc], writes=[kd])
        OP("dve", lambda e: [e.tensor_tensor(out=dtt[0:L, WW], in0=dtt[0:L, WW], in1=dtt[0:L, DT], op=ALU.mult)],
           reads=[kd], writes=[kd])
        OP("dve", lambda e: [e.tensor_scalar(out=dtt[0:L, NAC], in0=pc[0:L, 32:64], scalar1=-1.0, scalar2=None, op0=ALU.mult)],
           reads=[kpc], writes=[kd])
        ETOT = slice(256, 288)
        if not sample:
            OP("act", lambda e: [e.activation(out=dtt[:, ETOT], in_=pc[:, 192:224], func=AF.Exp)], reads=[kpc], writes=[kd])
        if not prefix:
            acf, kacf = ACF.next()
            ac3, kac3 = AC3.next()
            OP("act", lambda e: [e.activation(out=acf[0:32, 0, 0:L], in_=pc[0:32, 64:64 + L], func=AF.Copy)], reads=[kpc], writes=[kacf])
            split3(acf[0:32, 0, 0:L], ac3, acf[:, 1:3, :], kacf, kac3, kacf + "t", 32, L)
            for g in range(4):
                pg, kpg = PS("mm")
                OP("pe", lambda e, pg=pg, g=g: [e.matmul(pg[0:L, 0:L], lhsT=xsT[:, 16 + g, col0:c1], rhs=xsT[:, 20 + g, col0:c1],
                                                         start=True, stop=True)],
                   reads=["xsT"], writes=[kpg])
                OP("dve", lambda e, pg=pg, g=g: [e.tensor_tensor(out=Gm[0:L, g, 0:L], in0=pg[0:L, 0:L], in1=um, op=ALU.mult)],
                   reads=[kpg, "cf"], writes=["Gm"])
        yield
        if not prefix:
            ypss = []
            if sample:
                ypss = [PS("acc"), PS("acc")]
            def head_pair(hb, mode, yint=None):
                g = hb // 4
                heads = [2 * hb, 2 * hb + 1]
                pbc, kpbc = PS("mm" if mode == "B" else "hd")
                OP("pe", lambda e, pbc=pbc, heads=heads: [
                    e.matmul(pbc[:, j * L:(j + 1) * L], lhsT=cbf[0:32, CB["sel"] + h * 128:CB["sel"] + (h + 1) * 128],
                             rhs=ac3[0:32, i, 0:L], start=(i == 0), stop=(i == 2)) for j, h in enumerate(heads) for i in range(3)],
                   reads=[kac3, "cbf"], writes=[kpbc])
                if mode != "A":
                    seg, ksg = SEG.next()
                    OP("dve", lambda e, pbc=pbc, heads=heads, seg=seg: [
                        e.tensor_scalar(out=seg[0:L, j, 0:L], in0=pbc[0:L, j * L:(j + 1) * L], scalar1=dtt[0:L, 224 + h:225 + h],
                                        scalar2=0.0, op0=ALU.add, op1=ALU.min) for j, h in enumerate(heads)],
                       reads=[kpbc, kd], writes=[ksg])
                    OP("act", lambda e, seg=seg: [e.activation(out=seg[0:L, :, 0:L], in_=seg[0:L, :, 0:L], func=AF.Exp)],
                       reads=[ksg], writes=[ksg])
                    mtb, kmt = MTB.next()
                    OP("dve", lambda e, seg=seg, mtb=mtb, heads=heads, g=g: [
                        e.scalar_tensor_tensor(out=mtb[0:L, j, 0:L], in0=seg[0:L, j, 0:L], scalar=dtt[0:L, 128 + h:129 + h],
                                               in1=Gm[0:L, g, 0:L], op0=ALU.mult, op1=ALU.mult) for j, h in enumerate(heads)],
                       reads=[ksg, kd, "Gm"], writes=[kmt])
                if mode == "B":
                    yp, kyp = yint[hb // 8]
                    ypv = yp[:, :].rearrange("p (c t) -> p c t", t=64)
                    OP("pe", lambda e, mtb=mtb, heads=heads, ypv=ypv, hb=hb: [
                        e.matmul(ypv[(h % 2) * 64:(h % 2) * 64 + 64, hb % 8, :], lhsT=xtm[0:L, h * 64:(h + 1) * 64],
                                 rhs=mtb[0:L, j, 0:L], start=True, stop=True) for j, h in enumerate(heads)],
                       reads=[kx, kmt], writes=[kyp])
                    return
                ebc, keb = EBC.next()
                OP("act", lambda e, pbc=pbc, ebc=ebc: [
                    e.activation(out=ebc[:, :, 0:L], in_=pbc[:, 0:2 * L].rearrange("p (j t) -> p j t", j=2), func=AF.Exp)],
                   reads=[kpbc], writes=[keb])
                if mode == "A":
                    cet, kce, ceo = Ce, "Ce", 2 * hb
                else:
                    cet, kce = CEP.next()
                    ceo = 0
                OP("dve", lambda e, ebc=ebc, hb=hb, g=g, cet=cet, ceo=ceo: [
                    e.tensor_tensor(out=cet[:, ceo:ceo + 2, 0:L], in0=ebc[:, :, 0:L],
                                    in1=xsT[:, 20 + g, col0:c1].unsqueeze(1).to_broadcast([128, 2, L]), op=ALU.mult)],
                   reads=[keb, "xsT"], writes=[kce])
                if mode == "A":
                    OP("dve", lambda e, ebc=ebc, hb=hb: [
                        e.tensor_copy(out=edec[:, 2 * hb:2 * hb + 2, :],
                                      in_=ebc[:, :, 0:64].rearrange("p j (b t) -> p j b t", t=4)[:, :, :, 3])],
                       reads=[keb], writes=["edec"])
                    return
                yp, kyp = PS("acc")
                def yfn(e, mtb=mtb, heads=heads, yp=yp, cet=cet):
                    r = []
                    for j, h in enumerate(heads):
                        o = yp[j * 64:j * 64 + 64, 0:L]
                        r.append(e.matmul(o, lhsT=xtm[0:L, h * 64:(h + 1) * 64], rhs=mtb[0:L, j, 0:L], start=True, stop=False))
                        r.append(e.matmul(o, lhsT=hTb[:, h * 64:(h + 1) * 64], rhs=cet[:, j, 0:L], start=False, stop=True))
                    return r
                OP("pe", yfn, reads=[kx, kmt, "hTb", kce], writes=[kyp])
                post_y(tile, col0, L, [(hb, yp[:, 0:L], kyp, None, None)])

            for hb in range(16):
                if os.environ.get("KSKIP") == "hb" and tile.kind == "main":
                    continue
                head_pair(hb, "A" if sample else "all")
        yield
        if sample:
            OP("dve", lambda e: [
                e.tensor_tensor(out=xwall[0:64, :].rearrange("p (h d) -> p h d", d=64),
                                in0=xtm[0:64, 0:DIN].rearrange("p (h d) -> p h d", d=64),
                                in1=dtt[0:64, WW].unsqueeze(2).to_broadcast([64, 32, 64]), op=ALU.mult)],
               reads=[kx, kd], writes=["xwall"])
            for b in range(16):
                OP("sp", lambda e, b=b: [e.dma_start(out=hT[:, :], in_=sssm_d[b])], writes=["hT"], dma=("hld", 1))
                OP("act", lambda e: [e.activation(out=hTb[:, :], in_=hT[:, :], func=AF.Copy)], reads=["hT"], writes=["hTb"])
                for half in range(2):
                    yp, kyp = ypss[half]
                    ypv = yp[:, :].rearrange("p (c t) -> p c t", t=64)
                    OP("pe", lambda e, b=b, half=half, ypv=ypv: [
                        e.matmul(ypv[(h % 2) * 64:(h % 2) * 64 + 64, (h // 2) % 8, 4 * b:4 * b + 4],
                                 lhsT=hTb[:, h * 64:(h + 1) * 64], rhs=Ce[:, h, 4 * b:4 * b + 4], start=True, stop=True)
                        for h in range(16 * half, 16 * half + 16)],
                       reads=["hTb", "Ce"], writes=[kyp])
                for g in range(4):
                    bm, kbm = BM.next()
                    OP("dve", lambda e, bm=bm, g=g, b=b: [
                        e.tensor_scalar(out=bm[:, :], in0=xtm[0:64, DIN + g * 128:DIN + (g + 1) * 128],
                                        scalar1=cf[0:64, CF["sm"] + b:CF["sm"] + b + 1], scalar2=None, op0=ALU.mult)],
                       reads=[kx, "cf"], writes=[kbm])
                    pst, kps_ = PS("mm")
                    OP("pe", lambda e, bm=bm, g=g, pst=pst: [
                        e.matmul(pst[:, :], lhsT=bm[:, :], rhs=xwall[0:64, g * 512:(g + 1) * 512], start=True, stop=True)],
                       reads=[kbm, "xwall"], writes=[kps_])
                    hv = hT[:, g * 512:(g + 1) * 512].rearrange("p (h d) -> p h d", d=64)
                    OP("dve", lambda e, hv=hv, g=g, b=b: [
                        e.tensor_tensor(out=hv, in0=hv, in1=edec[:, 8 * g:8 * g + 8, b:b + 1].to_broadcast([128, 8, 64]), op=ALU.mult)],
                       reads=["hT", "edec"], writes=["hT"])
                    OP("dve", lambda e, g=g, pst=pst: [
                        e.tensor_tensor(out=hT[:, g * 512:(g + 1) * 512], in0=hT[:, g * 512:(g + 1) * 512], in1=pst[:, :], op=ALU.add)],
                       reads=["hT", kps_], writes=["hT"])
                OP("sp", lambda e, b=b: [e.dma_start(out=ssms_d[b], in_=hT[:, :])], reads=["hT"], writes=["o_ssms"], dma=("ost", 1))
            yint = [PS("hd"), PS("hd")]
            for hb in range(16):
                head_pair(hb, "B", yint)
            post_y(tile, col0, L, [(c, ypss[c // 8][0][:, (c % 8) * 64:(c % 8) * 64 + 64], ypss[c // 8][1],
                                    yint[c // 8][0][:, (c % 8) * 64:(c % 8) * 64 + 64], yint[c // 8][1]) for c in range(16)])
        else:
            for g in range(4):
                xw, kxw = XW.next()
                OP("dve", lambda e, xw=xw, g=g: [
                    e.tensor_tensor(out=xw[0:L, :].rearrange("p (h d) -> p h d", d=64),
                                    in0=xtm[0:L, g * 512:(g + 1) * 512].rearrange("p (h d) -> p h d", d=64),
                                    in1=dtt[0:L, 192 + 8 * g:192 + 8 * g + 8].unsqueeze(2).to_broadcast([L, 8, 64]), op=ALU.mult)],
                   reads=[kx, kd], writes=[kxw])
                pst, kps_ = PS("mm")
                OP("pe", lambda e, xw=xw, g=g, pst=pst: [
                    e.matmul(pst[:, :], lhsT=xtm[0:L, DIN + g * 128:DIN + (g + 1) * 128], rhs=xw[0:L, :], start=True, stop=True)],
                   reads=[kx, kxw], writes=[kps_])
                hv = hT[:, g * 512:(g + 1) * 512].rearrange("p (h d) -> p h d", d=64)
                OP("dve", lambda e, hv=hv, g=g: [
                    e.tensor_tensor(out=hv, in0=hv, in1=dtt[:, 256 + 8 * g:256 + 8 * g + 8].unsqueeze(2).to_broadcast([128, 8, 64]),
                                    op=ALU.mult)],
                   reads=["hT", kd], writes=["hT"])
                OP("dve", lambda e, g=g, pst=pst: [
                    e.tensor_tensor(out=hT[:, g * 512:(g + 1) * 512], in0=hT[:, g * 512:(g + 1) * 512], in1=pst[:, :], op=ALU.add)],
                   reads=["hT", kps_], writes=["hT"])
                OP("act", lambda e, g=g: [e.activation(out=hTb[:, g * 512:(g + 1) * 512], in_=hT[:, g * 512:(g + 1) * 512], func=AF.Copy)],
                   reads=["hT"], writes=["hTb"])

    ygstate = {}

    def post_y(tile, col0, L, items):
        c1 = col0 + L
        for (c, yap, kyp, yap2, kyp2) in items:
            y1, ky1 = Y1.next()
            OP("dve", lambda e, y1=y1, c=c, yap=yap: [
                e.scalar_tensor_tensor(out=y1[:, 0:L], in0=xsT[:, c, col0:c1], scalar=pcol("dsk", c), in1=yap,
                                       op0=ALU.mult, op1=ALU.add)],
               reads=["xsT", kyp, "pv"], writes=[ky1])
            if yap2 is not None:
                OP("dve", lambda e, y1=y1, yap2=yap2: [e.tensor_tensor(out=y1[:, 0:L], in0=y1[:, 0:L], in1=yap2, op=ALU.add)],
                   reads=[ky1, kyp2], writes=[ky1])
            if c % 4 == 0:
                ygstate["yg"] = YG.next()
                ygstate["ss"] = PS("mm")
            yg, kyg = ygstate["yg"]
            pss, kss = ygstate["ss"]
            OP("dve", lambda e, y1=y1, yg=yg, c=c: [
                e.tensor_tensor(out=yg[:, c % 4, 0:L], in0=y1[:, 0:L], in1=act[:, c, col0:c1], op=ALU.mult)],
               reads=[ky1, "sz"], writes=[kyg])
            sq, ksq = SQ.next()
            OP("act", lambda e, sq=sq, yg=yg, c=c: [e.activation(out=sq[:, 0, 0:L], in_=yg[:, c % 4, 0:L], func=AF.Square)],
               reads=[kyg], writes=[ksq])
            OP("pe", lambda e, sq=sq, pss=pss, c=c: [
                e.matmul(pss[:, 0:L], lhsT=ones_bf, rhs=sq[:, 0, 0:L], start=(c % 4 == 0), stop=(c % 4 == 3))],
               reads=[ksq, "cbf"], writes=[kss])
            if c % 4 == 3:
                rs, krs = RS.next()
                OP("act", lambda e, rs=rs, pss=pss: [
                    e.activation(out=rs[:, 0:L], in_=pss[:, 0:L], func=AF.Ln, bias=pcol("eps"), scale=1.0 / 512)],
                   reads=[kss, "pv"], writes=[krs])
                OP("act", lambda e, rs=rs: [e.activation(out=rs[:, 0:L], in_=rs[:, 0:L], func=AF.Exp, scale=-0.5)],
                   reads=[krs], writes=[krs])
                OP("dve", lambda e, rs=rs, yg=yg, c=c: [
                    e.scalar_tensor_tensor(out=act[:, c - 3 + k, col0:c1], in0=yg[:, k, 0:L], scalar=pcol("ng", c - 3 + k),
                                           in1=rs[:, 0:L], op0=ALU.mult, op1=ALU.mult) for k in range(4)],
                   reads=[kyg, krs, "pv"], writes=["sz"])

    def mixer_out(tile):
        for blk in range(4):
            wb, kw = W_get("wout", (blk,), WSLOT)
            wv = wb[:, :].rearrange("p (k n) -> p k n", k=16)
            for jj in range(2):
                dm = blk * 2 + jj
                for (c0, c1) in tile.subs:
                    n = c1 - c0
                    po, kpo = PS("mm")
                    OP("pe", mm_group(po[:, 0:n], [(wv[:, k, jj * 128:(jj + 1) * 128], act[:, k, c0:c1]) for k in range(16)]),
                       reads=["sz", kw], writes=[kpo])
                    OP("dve", lambda e, po=po, dm=dm, c0=c0, c1=c1, n=n: [
                        e.tensor_tensor(out=xT[:, dm, c0:c1], in0=po[:, 0:n], in1=xT[:, dm, c0:c1], op=ALU.add)],
                       reads=[kpo, "xT"], writes=["xT"])

    def mixer0(tile, prefix):
        mixer_in(tile, prefix)
        barrier()
        gens = [ssd_chunk(tile, ci, False, prefix) for ci in range(tile.nch)]
        next(gens[0])
        next(gens[0])
        for ci in range(tile.nch):
            nxt = gens[ci + 1] if ci + 1 < tile.nch else None
            if nxt is not None:
                next(nxt)
            next(gens[ci])
            if nxt is not None:
                next(nxt)
            for _ in gens[ci]:
                pass
        if tile.last:
            OP("sp", lambda e: [e.dma_start(out=ssmp_d[:, :], in_=hT[:, :])], reads=["hT"], writes=["o_ssmp"], dma=("ost", 1))
            OP("sp", lambda e: [e.dma_start(out=convp_d[:, :, :], in_=hist[:, :, :])], reads=["hist"], writes=["o_convp"], dma=("ost", 1))
        if tile.sample:
            for _ in ssd_chunk(tile, None, True, False):
                pass
            OP("sp", lambda e: [e.dma_start(out=convs_d[:, :, :, :], in_=convs[:, :, :, :])], reads=["sconv"], writes=["o_convs"],
               dma=("ost", 1))
        barrier()
        if not prefix and not (os.environ.get("KSKIP") == "out" and tile.kind == "main"):
            mixer_out(tile)

    def kv_proj(tile):
        rmsnorm(tile, PV["kvn"])
        wb, kw = W_get("wkv", (0,), WSLOT)
        wv = wb[:, :].rearrange("p (k n) -> p k n", k=8)
        for m in range(2):
            for (c0, c1) in tile.subs:
                n = c1 - c0
                pk, kpk = PS("mm")
                OP("pe", mm_group(pk[:, 0:n], [(wv[:, k, m * 128:(m + 1) * 128], hn[:, k, c0:c1]) for k in range(8)]),
                   reads=["hn", kw], writes=[kpk])
                OP("act", lambda e, pk=pk, m=m, c0=c0, c1=c1, n=n: [
                    e.activation(out=kT[:, m, 128 + c0:128 + c1], in_=pk[:, 0:n], func=AF.Identity, bias=pcol("bk", m))],
                   reads=[kpk, "pv"], writes=["kT"])
        chunks = [(ci * 128, 128, 1 + ci) for ci in range(tile.nch)] + ([(tile.Tp, 64, 5)] if tile.sample else [])
        for (c0, L, slot) in chunks:
            pvv, kpv = PS("mm")
            OP("pe", mm_group(pvv[0:L, 0:256], [(hn[:, k, c0:c0 + L], wv[:, k, 256:512]) for k in range(8)]),
               reads=["hn", kw], writes=[kpv])
            OP("dve", lambda e, pvv=pvv, L=L, slot=slot: [
                e.tensor_tensor(out=vtm[0:L, slot, :], in0=pvv[0:L, 0:256], in1=prow[0:L, PR["bv"]:PR["bv"] + 256], op=ALU.add)],
               reads=[kpv, "prow"], writes=["vtm"])
            is_lastp = tile.last and slot == tile.nch
            if is_lastp or slot == 5:
                pkk, kpkk = PS("mm")
                OP("pe", mm_group(pkk[0:L, 0:256], [(hn[:, k, c0:c0 + L], wv[:, k, 0:256]) for k in range(8)]),
                   reads=["hn", kw], writes=[kpkk])
                OP("dve", lambda e, pkk=pkk, L=L: [
                    e.tensor_tensor(out=kvf[0:L, 0:256], in0=pkk[0:L, 0:256], in1=prow[0:L, PR["bkr"]:PR["bkr"] + 256], op=ALU.add)],
                   reads=[kpkk, "prow"], writes=["kvf"])
                OP("dve", lambda e, pvv=pvv, L=L: [
                    e.tensor_tensor(out=kvf[0:L, 256:512], in0=pvv[0:L, 0:256], in1=prow[0:L, PR["bv"]:PR["bv"] + 256], op=ALU.add)],
                   reads=[kpv, "prow"], writes=["kvf"])
                if is_lastp:
                    OP("sp", lambda e: [e.dma_start(out=kwp_d[:, :], in_=kvf[:, 0:256]),
                                        e.dma_start(out=vwp_d[:, :], in_=kvf[:, 256:512])],
                       reads=["kvf"], writes=["o_kvp"], dma=("ost", 2))
                else:
                    OP("sp", lambda e: (
                        [e.dma_start(out=kws_d[b, 124:128, :], in_=kvf[4 * b:4 * b + 4, 0:256]) for b in range(16)]
                        + [e.dma_start(out=vws_d[b, 124:128, :], in_=kvf[4 * b:4 * b + 4, 256:512]) for b in range(16)]
                        + [e.dma_start(out=kws_d[:, 0:124, :], in_=ck_d[:, 4:128, :]),
                           e.dma_start(out=vws_d[:, 0:124, :], in_=cv_d[:, 4:128, :])]),
                       reads=["kvf"], writes=["o_kvs"], dma=("ost", 34))

    def kv_carry(tile):
        n = tile.nch
        OP("dve", lambda e: [e.tensor_copy(out=kT[:, :, 0:128], in_=kT[:, :, n * 128:n * 128 + 128])], reads=["kT"], writes=["kT"])
        OP("dve", lambda e: [e.tensor_copy(out=vtm[:, 0, :], in_=vtm[:, n, :])], reads=["vtm"], writes=["vtm"])

    qT = act
    OT0 = 8

    def attn_batch(items, nq, segs_n, masks, extra_bias_first=None):
        nk = sum(segs_n)
        nb = len(items)
        psS = [PS("hd"), PS("hd")]
        sm, ksm = SM.next()
        pn, kpn = PN.next()
        pt, kpt = PT.next()
        st, kst = STAT.next()

        def sfn(e):
            r = []
            for j, it in enumerate(items):
                o = 0
                for si, n_ in enumerate(segs_n):
                    r.append(e.matmul(psS[j % 2][0][0:nq, (j // 2) * 256 + o:(j // 2) * 256 + o + n_], lhsT=it["q"], rhs=it["k"][si],
                                      start=True, stop=True))
                    o += n_
            return r
        AST = int(os.environ.get("KASTAGE", "99"))
        if os.environ.get("KSKIP3") != "sfn":
            OP("pe", sfn, reads=["qT", "kT", "qs", "ckt0", "ckt1"], writes=[psS[0][1], psS[1][1]])
        if AST < 1:
            return
        OP("dve", lambda e: [
            e.scalar_tensor_tensor(out=sm[0:nq, j, m0:m0 + ml], in0=psS[j % 2][0][0:nq, (j // 2) * 256 + m0:(j // 2) * 256 + m0 + ml],
                                   scalar=0.125, in1=map_, op0=ALU.mult, op1=ALU.add) for j in range(nb) for (m0, ml, map_) in masks],
           reads=[psS[0][1], psS[1][1], "cf"], writes=[ksm])
        OP("dve", lambda e: [e.memset(st[0:nq, 0:nb, 2:3], 0.0)], writes=[kst + "a"])
        if extra_bias_first is not None:
            OP("dve", lambda e: [
                e.tensor_scalar(out=sm[0:nq, 0:nb, 0:128], in0=sm[0:nq, 0:nb, 0:128], scalar1=extra_bias_first, scalar2=None, op0=ALU.add)],
               reads=[ksm, "hbias"], writes=[ksm])
        if AST < 2:
            return
        OP("dve", lambda e: [e.reduce_max(out=st[0:nq, 0:nb, 0:1], in_=sm[0:nq, 0:nb, 0:nk], axis=AX.X)], reads=[ksm], writes=[kst])
        OP("dve", lambda e: [
            e.tensor_scalar(out=st[0:nq, j, 1:2], in0=st[0:nq, j, 0:1], scalar1=it["sink"], scalar2=-1.0, op0=ALU.max, op1=ALU.mult)
            for j, it in enumerate(items)], reads=[kst, "prow", "cf"], writes=[kst])
        if AST < 3:
            return
        OP("act", lambda e: [
            e.activation(out=sm[0:nq, j, 0:nk], in_=sm[0:nq, j, 0:nk], func=AF.Exp, bias=st[0:nq, j, 1:2], accum_out=st[0:nq, j, 2:3])
            for j in range(nb)], reads=[ksm, kst], writes=[ksm, kst + "a"])
        OP("act", lambda e: [
            e.activation(out=st[0:nq, j, 3:4], in_=st[0:nq, j, 1:2], func=AF.Exp, bias=it["sink"]) for j, it in enumerate(items)],
           reads=[kst, "prow", "cf"], writes=[kst + "b"])
        OP("dve", lambda e: [e.tensor_tensor(out=st[0:nq, 0:nb, 4:5], in0=st[0:nq, 0:nb, 2:3], in1=st[0:nq, 0:nb, 3:4], op=ALU.add)],
           reads=[kst + "a", kst + "b"], writes=[kst + "c"])
        OP("dve", lambda e: [e.reciprocal(out=st[0:nq, 0:nb, 5:6], in_=st[0:nq, 0:nb, 4:5])], reads=[kst + "c"], writes=[kst + "d"])
        OP("dve", lambda e: [
            e.tensor_scalar(out=pn[0:nq, j, 0:nk], in0=sm[0:nq, j, 0:nk], scalar1=st[0:nq, j, 5:6], scalar2=None, op0=ALU.mult)
            for j in range(nb)], reads=[ksm, kst + "d"], writes=[kpn])
        if AST < 4:
            return
        ptp, kptp = PS("mm")
        ptv = ptp[:, :].bitcast(BF16)

        def tfn(e):
            r = []
            for j in range(nb):
                o = 0
                for si, n_ in enumerate(segs_n):
                    r.append(e.transpose(ptv[0:n_, j * 256 + si * 128:j * 256 + si * 128 + nq], pn[0:nq, j, o:o + n_], ident[0:nq, 0:nq]))
                    o += n_
            return r
        OP("pe", tfn, reads=[kpn, "cbf"], writes=[kptp])
        nmax = max(segs_n)

        def cfn(e):
            if len(set(segs_n)) == 1:
                return [e.activation(out=pt[0:nmax, 0:nb, :], in_=ptv[0:nmax, 0:nb * 256].rearrange("p (j t) -> p j t", t=256), func=AF.Copy)]
            r = []
            for si, n_ in enumerate(segs_n):
                r.append(e.activation(out=pt[0:n_, 0:nb, si * 128:si * 128 + nq],
                                      in_=ptv[0:n_, 0:nb * 256].rearrange("p (j t) -> p j t", t=256)[:, :, si * 128:si * 128 + nq],
                                      func=AF.Copy))
            return r
        OP("act", cfn, reads=[kptp], writes=[kpt])

        if AST < 5:
            return

        def pvfn(e):
            r = []
            for j, it in enumerate(items):
                for si, n_ in enumerate(segs_n):
                    r.append(e.matmul(it["out"], lhsT=it["v"][si], rhs=pt[0:n_, j, si * 128:si * 128 + nq],
                                      start=(si == 0), stop=(si == len(segs_n) - 1)))
            return r
        OP("pe", pvfn, reads=[kpt, "vtm", "cvt0", "cvt1"], writes=sorted(set(it["okey"] for it in items)))

    def attention(tile):
        rmsnorm(tile, PV["g"] + 32)
        for blk in range(2):
            if os.environ.get("KSKIP2") == "aq":
                continue
            wb, kw = W_get("wq", (blk,), WSLOT)
            wv = wb[:, :].rearrange("p (k n) -> p k n", k=8)
            for jj in range(4):
                c = blk * 4 + jj
                for (c0, c1) in tile.subs:
                    n = c1 - c0
                    pq, kpq = PS("mm")
                    OP("pe", mm_group(pq[:, 0:n], [(wv[:, k, jj * 128:(jj + 1) * 128], hn[:, k, c0:c1]) for k in range(8)]),
                       reads=["hn", kw], writes=[kpq])
                    OP("act", lambda e, pq=pq, c=c, c0=c0, c1=c1, n=n: [
                        e.activation(out=qT[:, c, c0:c1], in_=pq[:, 0:n], func=AF.Identity, bias=pcol("bq", c))],
                       reads=[kpq, "pv"], writes=["qT"])
        maskb = cf[:, CF["mb"]:CF["mb"] + 256]
        barrier()
        for ci in range(tile.nch):
            if os.environ.get("KSKIP") == "ablk":
                continue
            q0 = ci * 128
            oA = PS("acc")
            oB = PS("acc")
            obanks = [oA, oB]
            for c in range(8):
                for half in range(1):
                    pass
            for bi in range(4):
                items = []
                for cc in (2 * bi, 2 * bi + 1):
                    for e_ in range(2):
                        g = 2 * (cc // 4) + e_
                        pr = slice(e_ * 64, e_ * 64 + 64)
                        prk = slice(0, 128) if os.environ.get("KFULLK") else pr
                        ob, kob = obanks[cc // 4]
                        items.append(dict(
                            q=qT[prk, cc, q0:q0 + 128],
                            k=[kT[prk, cc // 4, q0:q0 + 128], kT[prk, cc // 4, q0 + 128:q0 + 256]],
                            v=[vtm[:, ci, g * 64:(g + 1) * 64], vtm[:, ci + 1, g * 64:(g + 1) * 64]],
                            sink=prow[:, PR["snk"] + 2 * cc + e_:PR["snk"] + 2 * cc + e_ + 1],
                            out=ob[pr, (cc % 4) * 128:(cc % 4) * 128 + 128], okey=kob))
                attn_batch(items, 128, [128, 128], [(0, 256, maskb)],
                           extra_bias_first=(hbias[:, 0:1] if (tile.first and ci == 0) else None))
            for hb_, (ob, kob) in enumerate(obanks):
                if os.environ.get("KSKIP3") == "evac":
                    continue
                OP("act", lambda e, ob=ob, hb_=hb_, q0=q0: [
                    e.activation(out=act[:, OT0 + 4 * hb_:OT0 + 4 * hb_ + 4, q0:q0 + 128],
                                 in_=ob[:, :].rearrange("p (c t) -> p c t", t=128), func=AF.Copy)],
                   reads=[kob], writes=["oT"])
        if tile.sample:
            Tp = tile.Tp
            OP("act", lambda e: [
                e.activation(out=qs[:, :, c, :], in_=qT[:, c, Tp:Tp + 64].rearrange("p (b t) -> p b t", t=4), func=AF.Copy)
                for c in range(8)], reads=["qT"], writes=["qs"])
            pvp, kpvp = PS("acc")
            pvv = pvp[:, :].rearrange("p (b c q) -> p b c q", b=16, c=2)
            maskC = cf[0:16, CF["ms"]:CF["ms"] + 128]
            for b in range(16):
                ckt, kck = CKT.next()
                cvt, kcv = CV.next()
                OP("pool", lambda e, ckt=ckt, b=b: [e.dma_start(out=ckt[:, :], in_=ckT_d[b])], writes=[kck], dma=(kck, 1))
                OP("pool", lambda e, cvt=cvt, b=b: [e.dma_start(out=cvt[:, :], in_=cv_d[b])], writes=[kcv], dma=(kcv, 1))
                items = []
                for g in range(4):
                    e_ = g % 2
                    pr = slice(e_ * 64, e_ * 64 + 64)
                    cc0 = 4 * (g // 2)
                    items.append(dict(
                        q=qs[pr, b, cc0:cc0 + 4, :].rearrange("p c t -> p (c t)"),
                        k=[ckt[pr, (g // 2) * 128:(g // 2) * 128 + 128], kT[pr, g // 2, 128 + Tp:128 + Tp + 64]],
                        v=[cvt[:, g * 64:(g + 1) * 64], vtm[0:64, 5, g * 64:(g + 1) * 64]],
                        sink=cf[0:16, CF["ss"] + g:CF["ss"] + g + 1],
                        out=pvv[pr, b, g // 2, :], okey=kpvp))
                mN = cf[0:16, CF["mn"] + 60 - 4 * b:CF["mn"] + 60 - 4 * b + 64]
                attn_batch(items, 16, [128, 64], [(0, 128, maskC), (128, 64, mN)])
            for cc in range(2):
                OP("act", lambda e, cc=cc: [
                    e.activation(out=act[:, OT0 + 4 * cc:OT0 + 4 * cc + 4, Tp:Tp + 64].rearrange("p i (b t) -> p b i t", t=4),
                                 in_=pvv[:, :, cc, :].rearrange("p b (i t) -> p b i t", t=4), func=AF.Copy)],
                   reads=[kpvp], writes=["oT"])
        barrier()
        for blk in range(2):
            if os.environ.get("KSKIP2") == "ao":
                continue
            wb, kw = W_get("wo", (blk,), WSLOT)
            wv = wb[:, :].rearrange("p (k n) -> p k n", k=8)
            for jj in range(4):
                dm = blk * 4 + jj
                for (c0, c1) in tile.subs:
                    if tile.first and c1 <= 128:
                        pass
                    n = c1 - c0
                    po, kpo = PS("mm")
                    OP("pe", mm_group(po[:, 0:n], [(wv[:, k, jj * 128:(jj + 1) * 128], act[:, OT0 + k, c0:c1]) for k in range(8)]),
                       reads=["oT", kw], writes=[kpo])
                    OP("dve", lambda e, po=po, dm=dm, c0=c0, c1=c1, n=n: [
                        e.scalar_tensor_tensor(out=xT[:, dm, c0:c1], in0=po[:, 0:n], scalar=pcol("bo", dm), in1=xT[:, dm, c0:c1],
                                               op0=ALU.add, op1=ALU.add)],
                       reads=[kpo, "xT", "pv"], writes=["xT"])

    def final_out(tile):
        for (c0, c1) in tile.subs:
            n = c1 - c0
            pss, kps = PS("mm")
            for cp in range(4):
                sq, ksq = SQ.next()
                OP("act", lambda e, sq=sq, cp=cp, c0=c0, c1=c1, n=n: [
                    e.activation(out=sq[:, :, 0:n], in_=xT[:, 2 * cp:2 * cp + 2, c0:c1], func=AF.Square)],
                   reads=["xT"], writes=[ksq])
                OP("pe", lambda e, sq=sq, cp=cp, pss=pss, n=n: [
                    e.matmul(pss[:, 0:n], lhsT=ones_bf, rhs=sq[:, j, 0:n], start=(cp == 0 and j == 0),
                             stop=(cp == 3 and j == 1)) for j in range(2)],
                   reads=[ksq, "cbf"], writes=[kps])
            rs, krs = RS.next()
            OP("act", lambda e, rs=rs, pss=pss, n=n: [
                e.activation(out=rs[:, 0:n], in_=pss[:, 0:n], func=AF.Ln, bias=pcol("eps"), scale=1.0 / D)],
               reads=[kps, "pv"], writes=[krs])
            OP("act", lambda e, rs=rs, n=n: [e.activation(out=rs[:, 0:n], in_=rs[:, 0:n], func=AF.Exp, scale=-0.5)],
               reads=[krs], writes=[krs])
            OP("dve", lambda e, rs=rs, c0=c0, c1=c1, n=n: [
                e.scalar_tensor_tensor(out=xT[:, c, c0:c1], in0=xT[:, c, c0:c1], scalar=pcol("fin", c), in1=rs[:, 0:n],
                                       op0=ALU.mult, op1=ALU.mult) for c in range(8)],
               reads=["xT", krs, "pv"], writes=["xT"])
        s0 = 0
        nout = tile.Tp
        OP("sp", lambda e: [e.dma_start(out=yT_d[:, :, tile.out0:tile.out0 + nout], in_=xT[:, :, s0:tile.Tp])],
           reads=["xT"], writes=["o_y"], dma=("ost", 1))
        if tile.sample:
            OP("sp", lambda e: [e.dma_start(out=ysT_d[:, :, :], in_=xT[:, :, tile.Tp:tile.T])], reads=["xT"], writes=["o_ys"],
               dma=("ost", 1))

    def prologue():
        OP("sp", lambda e: [e.dma_start(out=pv[:, :], in_=pv_d[:, :]), e.dma_start(out=prow[:, :], in_=prow_d[:, :]),
                            e.dma_start(out=cf[:, :], in_=cf_d[:, :]), e.dma_start(out=cbf[:, :], in_=cbf_d[:, :]),
                            e.dma_start(out=cmask[:, :], in_=cmask_d[:, :]), e.dma_start(out=sconv[:, :, :, :], in_=sconv_d[:, :, :, :])],
           writes=["pv", "prow", "cf", "cbf", "cmask", "sconv"], dma=("cld", 6))
        OP("pool", lambda e: [e.dma_start(out=wdt[:, :, :], in_=wdt_d[:, :].rearrange("p (k n) -> p k n", k=8))], writes=["wdt"],
           dma=("wdtl", 1))
        OP("act", lambda e: [e.activation(out=abc[:, :], in_=prow[:, PR["alog"]:PR["alog"] + 32], func=AF.Exp)], reads=["prow"], writes=["abc"])
        OP("dve", lambda e: [e.tensor_scalar(out=abc[:, :], in0=abc[:, :], scalar1=-1.0, scalar2=None, op0=ALU.mult)], reads=["abc"], writes=["abc"])
        OP("dve", lambda e: [e.tensor_scalar(out=hbias[:, :], in0=cmask[:, :], scalar1=-1.0, scalar2=-NEG, op0=ALU.add, op1=ALU.mult)],
           reads=["cmask"], writes=["hbias"])
        OP("dve", lambda e: [e.memset(hT[:, :], 0.0)], writes=["hT"])
        OP("dve", lambda e: [e.memset(hTb[:, :], 0.0)], writes=["hTb"])
        OP("dve", lambda e: [e.memset(hist[:, :, :], 0.0)], writes=["hist"])
        OP("dve", lambda e: [e.memset(kT[:, :, :], 0.0)], writes=["kT"])
        OP("dve", lambda e: [e.memset(vtm[:, :, :], 0.0)], writes=["vtm"])

    import os
    KSTOP = int(os.environ.get("KSTOP", "100000"))

    def program():
        ph = [0]

        def step():
            ph[0] += 1
            return ph[0] > KSTOP

        def finish(dump=None):
            if dump is not None and KSTOP < 100000:
                t = dump
                OP("sp", lambda e: [e.dma_start(out=yT_d[:, :, 0:t.Tp], in_=xT[:, :, 0:t.Tp])], reads=["xT"], writes=["o_y"], dma=("ost", 1))
            OP("sp", lambda e: [], reads=["o_y", "o_ys", "o_ssmp", "o_convp", "o_convs", "o_ssms", "o_kvp", "o_kvs"])
            if not wstate["dry"]:
                last = {}
                for i_, o_ in enumerate(S.ops[:-1]):
                    last[o_.eng if o_.dma is None else o_.dma[0]] = i_
                for i_ in last.values():
                    S.ops[-1].deps.setdefault(i_, 2)

        prologue()
        if step(): return finish()
        for tile in tiles:
            load_x(tile)
            if step(): return finish(tile)
            ffn(tile, 0)
            barrier()
            if step(): return finish(tile)
            if tile.kind == "pre":
                mixer0(tile, True)
                barrier()
                if step(): return finish(tile)
                continue
            mixer0(tile, False)
            barrier()
            if step(): return finish(tile)
            ffn(tile, 1)
            barrier()
            if step(): return finish(tile)
            kv_proj(tile)
            barrier()
            if step(): return finish(tile)
            if tile.kind == "prefull":
                kv_carry(tile)
                OP("dve", lambda e: [e.tensor_scalar(out=hT[:, :], in0=hT[:, :], scalar1=cmask[:, 0:1], scalar2=None, op0=ALU.mult)],
                   reads=["hT", "cmask"], writes=["hT"])
                OP("act", lambda e: [e.activation(out=hTb[:, :], in_=hT[:, :], func=AF.Copy)], reads=["hT"], writes=["hTb"])
                continue
            ffn(tile, 2)
            barrier()
            if step(): return finish(tile)
            attention(tile)
            barrier()
            if step(): return finish(tile)
            kv_carry(tile)
            ffn(tile, 3)
            barrier()
            if step(): return finish(tile)
            final_out(tile)
            barrier()
            if step(): return finish()
        finish()

    program()
    wstate["dry"] = False
    wstate["i"] = 0
    for k in pctr:
        pctr[k] = 0
    for r_ in (XTM, SEG, MTB, EBC, Y1, YG, SQ, RS, SG, XIN, XINS, CACC, DTT, AG3, TMPF, ACF, AC3, CEP, XW, BM, SM, PN, PT, STAT, CKT, CV):
        r_.i = 0
    program()
    S.finalize()

    with ExitStack() as es2:
        semh = {n: es2.enter_context(nc.semaphore(f"s_{n}")) for n in S.semnames}
        for n in ("pe", "act", "dve", "pool"):
            if n not in semh:
                semh[n] = es2.enter_context(nc.semaphore(f"s_{n}"))
        with nc.Block() as block:
            @block.sync
            def _(e):
                S.emit_engine("sp", e, semh)

            @block.gpsimd
            def _(e):
                S.emit_engine("pool", e, semh)

            @block.tensor
            def _(e):
                S.emit_engine("pe", e, semh)

            @block.scalar
            def _(e):
                S.emit_engine("act", e, semh)

            @block.vector
            def _(e):
                S.emit_engine("dve", e, semh)
    es.close()
    return nc, len(S.ops)


def tile_w(Wm, nb):
    K, N = Wm.shape
    a = Wm.reshape(K // 128, 128, N // nb, nb)
    return np.ascontiguousarray(a.transpose(2, 1, 0, 3)).reshape(N // nb, 128, (K // 128) * nb)


def pad_last(a, n):
    if a.shape[-1] == n:
        return a
    out = np.zeros(a.shape[:-1] + (n,), a.dtype)
    out[..., :a.shape[-1]] = a
    return out


def host_consts():
    cfa = np.zeros((128, NCF), np.float32)
    i = np.arange(128)
    um = (i[:, None] <= i[None, :]).astype(np.float32)
    cfa[:, CF["um"]:CF["um"] + 128] = um
    usf = (i[:, None] > i[None, :]).astype(np.float32)
    j = np.arange(64)
    same = (j[:, None] // 4) == (j[None, :] // 4)
    cfa[:64, CF["ums"]:CF["ums"] + 64] = (same & (j[:, None] <= j[None, :])).astype(np.float32)
    ussf = (same & (j[:, None] > j[None, :])).astype(np.float32)
    mb = np.full((128, 256), NEG, np.float32)
    mb[:, :128][i[None, :] > i[:, None]] = 0.0
    mb[:, 128:][i[None, :] <= i[:, None]] = 0.0
    cfa[:, CF["mb"]:CF["mb"] + 256] = mb
    ms = np.full((16, 128), NEG, np.float32)
    mn = np.full((16, 124), NEG, np.float32)
    for r in range(16):
        t = r % 4
        ms[r, t + 1:128] = 0.0
        mn[r, 60:60 + t + 1] = 0.0
    cfa[:16, CF["ms"]:CF["ms"] + 128] = ms
    cfa[:16, CF["mn"]:CF["mn"] + 124] = mn
    cfa[:64, CF["sm"]:CF["sm"] + 16] = (j[:, None] // 4 == np.arange(16)[None, :]).astype(np.float32)
    cb = np.zeros((128, NCB), np.float32)
    cb[:, :128] = np.eye(128)
    cb[:, 128:256] = 1.0
    cb[:, CB["um"]:CB["um"] + 128] = cfa[:, CF["um"]:CF["um"] + 128]
    cb[:, CB["us"]:CB["us"] + 128] = usf
    cb[:64, CB["ums"]:CB["ums"] + 64] = cfa[:64, CF["ums"]:CF["ums"] + 64]
    cb[:64, CB["uss"]:CB["uss"] + 64] = ussf
    sel = np.zeros((32, 32, 128), np.float32)
    for h in range(32):
        sel[h, h, :] = 1.0
    cb[:32, CB["sel"]:CB["sel"] + 4096] = sel.reshape(32, 4096)
    return cfa, cb.astype(ml_dtypes.bfloat16)


def host_weights(p):
    f32 = np.float32
    out = {}
    wgu = np.zeros((4, 11, 128, WSLOT), f32)
    wdn = np.zeros((4, 8, 128, NF * 128), f32)
    for fi in range(4):
        l, i = fi // 2, fi % 2
        g = tile_w(np.asarray(p["ffn_w_gate"][l, i]), 256)
        u = tile_w(np.asarray(p["ffn_w_up"][l, i]), 256)
        wgu[fi] = np.concatenate([g, u], axis=2)
        wdn[fi] = tile_w(np.asarray(p["ffn_w_down"][l, i]), 128)
    out["wgu"], out["wdn"] = wgu, wdn
    win = np.asarray(p["ssm_w_in"][0])
    out["winz"] = tile_w(win[:, 0:2048], 512)
    out["winx"] = tile_w(win[:, 2048:5120], 512)
    out["wdt"] = np.ascontiguousarray(tile_w(win[:, 5120:5152], 32)[0])
    out["wout"] = tile_w(np.asarray(p["ssm_w_out"][0]), 256)
    out["wkv"] = tile_w(np.asarray(p["attn_w_kv"]), 512)
    perm = np.concatenate([np.arange(QH[c][e] * 64, QH[c][e] * 64 + 64) for c in range(8) for e in range(2)])
    out["wq"] = tile_w(np.asarray(p["attn_w_q"][0])[:, perm], 512)
    out["wo"] = tile_w(np.asarray(p["attn_w_o"][0])[perm, :], 512)
    pvh = np.zeros((128, NPV), f32)
    fm = lambda v: np.asarray(v, f32).reshape(-1, 128).T
    ng = np.asarray(p["norm_gain"])
    for l in range(2):
        for i in range(3):
            pvh[:, PV["g"] + (l * 3 + i) * 8:PV["g"] + (l * 3 + i) * 8 + 8] = fm(ng[l, i])
    pvh[:, PV["kvn"]:PV["kvn"] + 8] = fm(p["kv_norm"])
    pvh[:, PV["fin"]:PV["fin"] + 8] = fm(p["final_norm"])
    cw = np.asarray(p["ssm_conv_w"][0])
    for k in range(4):
        pvh[:, PV["cw"] + k * 24:PV["cw"] + k * 24 + 24] = fm(cw[k])
    pvh[:, PV["cb"]:PV["cb"] + 24] = fm(p["ssm_conv_b"][0])
    pvh[:, PV["dsk"]:PV["dsk"] + 16] = fm(np.repeat(np.asarray(p["ssm_d"][0]), 64))
    pvh[:, PV["ng"]:PV["ng"] + 16] = fm(p["ssm_norm"][0])
    pvh[:, PV["bq"]:PV["bq"] + 8] = fm(np.asarray(p["attn_b_q"][0])[perm])
    bkv = np.asarray(p["attn_b_kv"], f32)
    pvh[:, PV["bk"]:PV["bk"] + 2] = fm(bkv[:256])
    pvh[:, PV["bo"]:PV["bo"] + 8] = fm(p["attn_b_o"][0])
    pvh[:, PV["one"]] = 1.0
    pvh[:, PV["eps"]] = EPS
    out["pv"] = pvh
    pr = np.zeros((128, NPR), f32)
    pr[:, PR["dtb"]:PR["dtb"] + 32] = np.asarray(p["ssm_dt_bias"][0])[None, :]
    pr[:, PR["alog"]:PR["alog"] + 32] = np.asarray(p["ssm_a_log"][0])[None, :]
    pr[:, PR["bv"]:PR["bv"] + 256] = bkv[None, 256:]
    pr[:, PR["bkr"]:PR["bkr"] + 256] = bkv[None, :256]
    sinks = np.asarray(p["attn_sinks"][0], f32)
    pr[:, PR["snk"]:PR["snk"] + 16] = np.array([sinks[QH[c][e]] for c in range(8) for e in range(2)], f32)[None, :]
    out["prow"] = pr
    cfa, cb = host_consts()
    for g in range(4):
        for r in range(16):
            cfa[r, CF["ss"] + g] = sinks[QH[4 * (g // 2) + r // 4][g % 2]]
    out["cf"], out["cbf"] = cfa, cb
    return out


_PROG = {}


def run(inputs, seq, npre, nmain):
    f32 = np.float32
    key = (npre, nmain)
    if key not in _PROG:
        _PROG[key] = build_program(npre, nmain)
    nc, nops = _PROG[key]
    w = host_weights(inputs)
    xp = np.asarray(inputs["x_prompt"], f32)
    xs = np.asarray(inputs["x_sample"], f32)
    sconv = np.asarray(inputs["state_conv"], f32)[0]
    sssm = np.asarray(inputs["state_ssm"], f32)[0]
    ck = np.asarray(inputs["cache_k_win"], f32)
    cv = np.asarray(inputs["cache_v_win"], f32)
    half = seq // 2
    in_maps = []
    for c in range(8):
        s, hf = c // 2, c % 2
        m = dict(w)
        if hf == 0:
            m["xpre"] = np.zeros((D, npre * 128), f32)
        else:
            m["xpre"] = np.ascontiguousarray(xp[s, 0:half].T)
        m["xmain"] = np.ascontiguousarray(xp[s, hf * half:(hf + 1) * half].T)
        m["cmask"] = np.full((128, 1), float(hf), f32)
        b0 = 16 * c
        m["xsmp"] = np.ascontiguousarray(xs[b0:b0 + 16].reshape(64, D).T)
        m["sconv"] = np.ascontiguousarray(sconv[b0:b0 + 16].reshape(16, 3, 24, 128).transpose(3, 2, 0, 1))
        m["sssm"] = np.ascontiguousarray(sssm[b0:b0 + 16].reshape(16, DIN, 128).transpose(0, 2, 1))
        kk = ck[b0:b0 + 16].reshape(16, 128, 2, 2, 64)
        m["ckT"] = np.ascontiguousarray(kk.transpose(0, 3, 4, 2, 1)).reshape(16, 128, 256)
        m["ck"] = np.ascontiguousarray(ck[b0:b0 + 16].reshape(16, 128, 256))
        m["cv"] = np.ascontiguousarray(cv[b0:b0 + 16].reshape(16, 128, 256))
        in_maps.append(m)
    if os.environ.get("KTRACE"):
        res = run_bass_kernel_spmd(nc, in_maps, core_ids=list(range(8)), trace=True)
        print("EXEC_TIME_NS", res.exec_time_ns)
    else:
        res = run_bass_kernel_spmd(nc, in_maps, core_ids=list(range(8)))
    R = res.results
    B = xp.shape[0]
    y_p = np.zeros((B, seq, D), f32)
    conv_p = np.zeros((1, B, 3, 3072), f32)
    ssm_p = np.zeros((1, B, 32, 64, 128), f32)
    k_p = np.zeros((B, 128, 4, 64), f32)
    v_p = np.zeros((B, 128, 4, 64), f32)
    y_s = np.zeros((128, 4, D), f32)
    conv_s = np.zeros((1, 128, 3, 3072), f32)
    ssm_s = np.zeros((1, 128, 32, 64, 128), f32)
    k_s = np.zeros((128, 128, 4, 64), f32)
    v_s = np.zeros((128, 128, 4, 64), f32)
    for c in range(8):
        s, hf = c // 2, c % 2
        r = R[c]
        y_p[s, hf * half:(hf + 1) * half] = r["yT"].T
        b0 = 16 * c
        y_s[b0:b0 + 16] = r["ysT"].T.reshape(16, 4, D)
        conv_s[0, b0:b0 + 16] = r["convs"].transpose(2, 3, 1, 0).reshape(16, 3, 3072)
        ssm_s[0, b0:b0 + 16] = r["ssms"].transpose(0, 2, 1).reshape(16, 32, 64, 128)
        k_s[b0:b0 + 16] = r["kws"].reshape(16, 128, 4, 64)
        v_s[b0:b0 + 16] = r["vws"].reshape(16, 128, 4, 64)
        if hf == 1:
            conv_p[0, s] = r["convp"].transpose(2, 1, 0).reshape(3, 3072)
            ssm_p[0, s] = r["ssmp"].T.reshape(32, 64, 128)
            k_p[s] = r["kwp"].reshape(128, 4, 64)
            v_p[s] = r["vwp"].reshape(128, 4, 64)
    return (y_p, y_s, conv_p, ssm_p, k_p, v_p, conv_s, ssm_s, k_s, v_s)


def kernel(**inputs):
    seq = int(np.asarray(inputs["x_prompt"]).shape[1])
    nchunks = seq // 128
    return run(inputs, seq, nchunks // 2, nchunks // 2)
```

```python
import os
import numpy as np
import ml_dtypes
from contextlib import ExitStack
import concourse.bass as bass
import concourse.mybir as mybir
from concourse.bass_utils import run_bass_kernel_spmd

F32, BF16 = mybir.dt.float32, mybir.dt.bfloat16
AF = mybir.ActivationFunctionType
ALU = mybir.AluOpType
AX = mybir.AxisListType

D = 1024
DFF = 2816
NF = 22
DIN = 2048
EPS = 1e-6
NEG = -30000.0
WSLOT = 4096
NWS = 3

QH = [[c, c + 4] if c < 4 else [c + 4, c + 8] for c in range(8)]


class Op:
    __slots__ = ("eng", "fn", "deps", "dma", "signal", "value")

    def __init__(self, eng, fn, deps, dma):
        self.eng, self.fn, self.deps, self.dma = eng, fn, deps, dma
        self.signal = False
        self.value = 0


class Sched:
    def __init__(self):
        self.ops = []
        self.lw = {}
        self.rd = {}

    def op(self, eng, fn, reads=(), writes=(), dma=None):
        idx = len(self.ops)
        deps = {}
        for k in reads:
            w = self.lw.get(k)
            if w is not None:
                deps[w] = 2
        for k in writes:
            w = self.lw.get(k)
            if w is not None:
                deps.setdefault(w, 1)
            r = self.rd.get(k)
            if r:
                for e_, i_ in r[0].items():
                    deps.setdefault(i_, 1)
                for i_ in r[1]:
                    deps.setdefault(i_, 1)
        for k in writes:
            self.lw[k] = idx
            self.rd[k] = [{}, []]
        for k in reads:
            r = self.rd.setdefault(k, [{}, []])
            if dma is None:
                r[0][eng] = idx
            else:
                r[1].append(idx)
        self.ops.append(Op(eng, fn, deps, dma))
        return idx

    def finalize(self):
        ops = self.ops
        for o in ops:
            need = []
            for d, kind in o.deps.items():
                p = ops[d]
                if p.dma is not None:
                    need.append(d)
                elif o.dma is None and p.eng == o.eng:
                    if o.eng != "pe" and kind == 2:
                        need.append(d)
                else:
                    need.append(d)
            o.deps = need
            for d in need:
                ops[d].signal = True
        cnt = {}
        for o in ops:
            if o.dma is not None:
                s, n = o.dma
                cnt[s] = cnt.get(s, 0) + 16 * n
                o.value = cnt[s]
            elif o.signal:
                cnt[o.eng] = cnt.get(o.eng, 0) + 1
                o.value = cnt[o.eng]
        self.semnames = sorted(cnt.keys())

    def emit_engine(self, engname, e, semh):
        ops = self.ops
        waited = {}
        for o in ops:
            if o.eng != engname:
                continue
            for d in o.deps:
                p = ops[d]
                s = p.dma[0] if p.dma is not None else p.eng
                if waited.get(s, 0) < p.value:
                    e.wait_ge(semh[s], p.value)
                    waited[s] = p.value
            ins = o.fn(e)
            if o.dma is not None:
                assert len(ins) == o.dma[1], (len(ins), o.dma)
                for i_ in ins:
                    i_.then_inc(semh[o.dma[0]], 16)
            elif o.signal:
                ins[-1].then_inc(semh[engname], 1)


class Rot:
    def __init__(self, tensors, name):
        self.t = tensors
        self.n = len(tensors)
        self.i = 0
        self.name = name

    def next(self):
        j = self.i % self.n
        self.i += 1
        return self.t[j], f"{self.name}{j}"


def subtiles(T):
    out = []
    c = 0
    while c < T:
        n = min(512, T - c)
        out.append((c, c + n))
        c += n
    return out


class Tile:
    def __init__(self, kind, src, col0, nch, sample=False, first=False, last=False, out0=0):
        self.kind = kind
        self.src = src
        self.col0 = col0
        self.nch = nch
        self.Tp = nch * 128
        self.sample = sample
        self.T = self.Tp + (64 if sample else 0)
        self.first = first
        self.last = last
        self.out0 = out0
        self.subs = subtiles(self.Tp) + ([(self.Tp, self.Tp + 64)] if sample else [])


def make_tiles(npre, nmain, tch=4):
    def split(n):
        k = -(-n // tch)
        base, rem = divmod(n, k)
        return [base + 1] * rem + [base] * (k - rem)
    tiles = []
    sp = split(npre)
    c = 0
    for i, n in enumerate(sp):
        tiles.append(Tile("prefull" if i == len(sp) - 1 else "pre", "xpre", c * 128, n))
        c += n
    sp = split(nmain)
    c = 0
    for i, n in enumerate(sp):
        tiles.append(Tile("main", "xmain", c * 128, n, sample=(i == len(sp) - 1), first=(i == 0),
                          last=(i == len(sp) - 1), out0=c * 128))
        c += n
    return tiles


PV = {}
_o = 0
for _n, _w in [("g", 48), ("kvn", 8), ("fin", 8), ("cw", 96), ("cb", 24), ("dsk", 16), ("ng", 16),
               ("bq", 8), ("bk", 2), ("bo", 8), ("one", 1), ("eps", 1)]:
    PV[_n] = _o
    _o += _w
NPV = _o
PR = {"dtb": 0, "alog": 32, "bv": 64, "bkr": 320, "snk": 576}
NPR = 592
CF = {"um": 0, "ums": 128, "mb": 192, "ms": 448, "mn": 576, "sm": 700, "ss": 716}
NCF = 720
CB = {"id": 0, "on": 128, "um": 256, "us": 384, "ums": 512, "uss": 576, "sel": 640}
NCB = 640 + 4096


def build_program(npre, nmain):
    tiles = make_tiles(npre, nmain)
    TMAX = max(t.T for t in tiles)
    nc = bass.Bass("TRN2", target_bir_lowering=False)
    S = Sched()

    def din(name, shape, dt=F32):
        return nc.dram_tensor(name, list(shape), dt, kind="ExternalInput").ap()

    def dout(name, shape):
        return nc.dram_tensor(name, list(shape), F32, kind="ExternalOutput").ap()

    NOUT = nmain * 128
    dr = {}
    dr["xpre"] = din("xpre", [D, npre * 128]).rearrange("(c p) t -> p c t", p=128)
    dr["xmain"] = din("xmain", [D, nmain * 128]).rearrange("(c p) t -> p c t", p=128)
    xsmp_d = din("xsmp", [D, 64]).rearrange("(c p) t -> p c t", p=128)
    wd = {
        "wgu": din("wgu", [4, 11, 128, WSLOT]),
        "wdn": din("wdn", [4, 8, 128, NF * 128]),
        "winz": din("winz", [4, 128, WSLOT]),
        "winx": din("winx", [6, 128, WSLOT]),
        "wout": din("wout", [4, 128, WSLOT]),
        "wkv": din("wkv", [1, 128, WSLOT]),
        "wq": din("wq", [2, 128, WSLOT]),
        "wo": din("wo", [2, 128, WSLOT]),
    }
    wdt_d = din("wdt", [128, 256])
    pv_d = din("pv", [128, NPV])
    prow_d = din("prow", [128, NPR])
    cf_d = din("cf", [128, NCF])
    cbf_d = din("cbf", [128, NCB], BF16)
    cmask_d = din("cmask", [128, 1])
    sconv_d = din("sconv", [128, 24, 16, 3])
    sssm_d = din("sssm", [16, 128, DIN])
    ckT_d = din("ckT", [16, 128, 256])
    ck_d = din("ck", [16, 128, 256])
    cv_d = din("cv", [16, 128, 256])
    yT_d = dout("yT", [D, NOUT]).rearrange("(c p) t -> p c t", p=128)
    ysT_d = dout("ysT", [D, 64]).rearrange("(c p) t -> p c t", p=128)
    convp_d = dout("convp", [128, 24, 3])
    ssmp_d = dout("ssmp", [128, DIN])
    kwp_d = dout("kwp", [128, 256])
    vwp_d = dout("vwp", [128, 256])
    convs_d = dout("convs", [128, 24, 16, 3])
    ssms_d = dout("ssms", [16, 128, DIN])
    kws_d = dout("kws", [16, 128, 256])
    vws_d = dout("vws", [16, 128, 256])

    es = ExitStack()

    def sb(name, shape, dt):
        return es.enter_context(nc.sbuf_tensor(name, list(shape), dt))

    xT = sb("xT", [128, 8, TMAX], F32)
    hn = sb("hn", [128, 8, TMAX], BF16)
    act = sb("act", [128, 40, TMAX], BF16)
    wsl = [sb(f"wsl{i}", [128, WSLOT], BF16) for i in range(NWS)]
    xsT = act[:, 16:40, :]
    XTM = Rot([sb(f"xtm{i}", [128, 2560], BF16) for i in range(2)], "xtm")
    hT = sb("hT", [128, DIN], F32)
    hTb = sb("hTb", [128, DIN], BF16)
    Ce = sb("Ce", [128, 32, 64], BF16)
    CEP = Rot([sb(f"cep{i}", [128, 2, 128], BF16) for i in range(2)], "cep")
    Gm = sb("Gm", [128, 4, 128], F32)
    SEG = Rot([sb(f"seg{i}", [128, 2, 128], F32) for i in range(2)], "seg")
    MTB = Rot([sb(f"mtb{i}", [128, 2, 128], BF16) for i in range(2)], "mtb")
    EBC = Rot([sb(f"ebc{i}", [128, 2, 128], F32) for i in range(2)], "ebc")
    edec = sb("edec", [128, 32, 16], F32)
    Y1 = Rot([sb(f"y1{i}", [128, 128], F32) for i in range(2)], "y1")
    YG = Rot([sb(f"yg{i}", [128, 4, 128], F32) for i in range(1)], "yg")
    SQ = Rot([sb(f"sq{i}", [128, 2, 512], BF16) for i in range(2)], "sq")
    RS = Rot([sb(f"rs{i}", [128, 512], F32) for i in range(1)], "rs")
    SG = Rot([sb(f"sg{i}", [128, 512], F32) for i in range(2)], "sg")
    XIN = Rot([sb(f"xin{i}", [128, 3 + TMAX], F32) for i in range(2)], "xin")
    XINS = Rot([sb(f"xins{i}", [128, 16, 7], F32) for i in range(2)], "xins")
    CACC = Rot([sb(f"cacc{i}", [128, TMAX], F32) for i in range(1)], "cacc")
    hist = sb("hist", [128, 24, 3], F32)
    sconv = sb("sconv_t", [128, 24, 16, 3], F32)
    convs = sconv
    DTT = Rot([sb(f"dtt{i}", [128, 320], F32) for i in range(2)], "dtt")
    AG3 = Rot([sb(f"ag3{i}", [128, 3, 32], BF16) for i in range(2)], "ag3")
    TMPF = Rot([sb(f"tmpf{i}", [128, 2, 32], F32) for i in range(2)], "tmpf")
    ACF = Rot([sb(f"acf{i}", [32, 3, 128], F32) for i in range(1)], "acf")
    AC3 = Rot([sb(f"ac3{i}", [32, 3, 128], BF16) for i in range(2)], "ac3")
    XW = Rot([sb(f"xw{i}", [128, 512], BF16) for i in range(2)], "xw")
    xwall = sb("xwall", [64, DIN], BF16)
    BM = Rot([sb(f"bm{i}", [64, 128], BF16) for i in range(2)], "bm")
    kT = sb("kT", [128, 2, 128 + TMAX], BF16)
    vtm = sb("vtm", [128, 6, 256], BF16)
    kvf = sb("kvf", [128, 512], F32)
    SM = Rot([sb(f"sm{i}", [128, 4, 256], F32) for i in range(1)], "sm")
    PN = Rot([sb(f"pn{i}", [128, 4, 256], BF16) for i in range(1)], "pn")
    PT = Rot([sb(f"pt{i}", [128, 4, 256], BF16) for i in range(1)], "pt")
    STAT = Rot([sb(f"stat{i}", [128, 4, 8], F32) for i in range(4)], "stat")
    qs = sb("qs", [128, 16, 8, 4], BF16)
    CKT = Rot([sb(f"ckt{i}", [128, 256], BF16) for i in range(2)], "ckt")
    CV = Rot([sb(f"cvt{i}", [128, 256], BF16) for i in range(2)], "cvt")
    wdt = sb("wdt_t", [128, 8, 32], BF16)
    pv = sb("pv_t", [128, NPV], F32)
    prow = sb("prow_t", [128, NPR], F32)
    abc = sb("abc", [128, 32], F32)
    cf = sb("cf_t", [128, NCF], F32)
    cbf = sb("cbf_t", [128, NCB], BF16)
    cmask = sb("cmask_t", [128, 1], F32)
    hbias = sb("hbias", [128, 1], F32)
    ps = [es.enter_context(nc.psum_tensor(f"ps{i}", [128, 512], F32)) for i in range(8)]
    pools = {"mm": [0, 1, 2, 3], "hd": [4, 5], "acc": [6, 7]}
    pctr = {"mm": 0, "hd": 0, "acc": 0}

    def PS(pool):
        l = pools[pool]
        i = l[pctr[pool] % len(l)]
        pctr[pool] += 1
        return ps[i], f"ps{i}"

    ident = cbf[:, 0:128]
    ones_bf = cbf[:, 128:256]
    pcol = lambda name, j=0: pv[:, PV[name] + j:PV[name] + j + 1]

    wseq = []
    wstate = {"dry": True, "i": 0, "issued": 0}

    def wdram(name, idx):
        a = wd[name]
        return a[idx[0], idx[1]] if len(idx) == 2 else a[idx[0]]

    def W_issue(upto):
        while wstate["issued"] <= min(upto, len(wseq) - 1):
            k = wstate["issued"]
            name, idx, nel = wseq[k]
            slot = wsl[k % NWS]
            src = wdram(name, idx)
            S.op("pool", (lambda e, slot=slot, src=src, nel=nel: [e.dma_start(out=slot[:, 0:nel], in_=src[:, 0:nel])]),
                 writes=[f"wsl{k % NWS}"], dma=(f"w{k % NWS}", 1))
            wstate["issued"] += 1

    def W_get(name, idx, nel):
        k = wstate["i"]
        wstate["i"] += 1
        if wstate["dry"]:
            wseq.append((name, idx, nel))
            return wsl[k % NWS], f"wsl{k % NWS}"
        assert wseq[k] == (name, idx, nel)
        W_issue(k + NWS - 1)
        return wsl[k % NWS], f"wsl{k % NWS}"

    def OP(eng, fn, reads=(), writes=(), dma=None):
        if wstate["dry"]:
            return
        S.op(eng, fn, reads, writes, dma)

    def barrier():
        if wstate["dry"] or os.environ.get("KNOBAR"):
            return
        last = {}
        for i_, o_ in enumerate(S.ops):
            if o_.dma is None and o_.eng in ("pe", "act", "dve"):
                last[o_.eng] = i_
        for eng in ("pe", "act", "dve"):
            S.op(eng, lambda e: [e.nop()])
            for en2, i_ in last.items():
                if en2 != eng:
                    S.ops[-1].deps[i_] = 2

    def mm_group(out_ap, pairs):
        def fn(e):
            r = []
            n = len(pairs)
            for i, (l, rr) in enumerate(pairs):
                r.append(e.matmul(out_ap, lhsT=l, rhs=rr, start=(i == 0), stop=(i == n - 1)))
            return r
        return fn

    def rmsnorm(tile, gbase, ndim=D):
        for (c0, c1) in tile.subs:
            n = c1 - c0
            pss, kps = PS("mm")
            for cp in range(4):
                sq, ksq = SQ.next()
                OP("act", lambda e, sq=sq, cp=cp, c0=c0, c1=c1, n=n: [
                    e.activation(out=sq[:, :, 0:n], in_=xT[:, 2 * cp:2 * cp + 2, c0:c1], func=AF.Square)],
                   reads=["xT"], writes=[ksq])
                OP("pe", lambda e, sq=sq, cp=cp, pss=pss, n=n: [
                    e.matmul(pss[:, 0:n], lhsT=ones_bf, rhs=sq[:, j, 0:n], start=(cp == 0 and j == 0),
                             stop=(cp == 3 and j == 1)) for j in range(2)],
                   reads=[ksq, "cbf"], writes=[kps])
            rs, krs = RS.next()
            OP("act", lambda e, rs=rs, pss=pss, n=n: [
                e.activation(out=rs[:, 0:n], in_=pss[:, 0:n], func=AF.Ln, bias=pcol("eps"), scale=1.0 / ndim)],
               reads=[kps, "pv"], writes=[krs])
            OP("act", lambda e, rs=rs, n=n: [
                e.activation(out=rs[:, 0:n], in_=rs[:, 0:n], func=AF.Exp, scale=-0.5)],
               reads=[krs], writes=[krs])
            OP("dve", lambda e, rs=rs, c0=c0, c1=c1, n=n: [
                e.scalar_tensor_tensor(out=hn[:, c, c0:c1], in0=xT[:, c, c0:c1], scalar=pv[:, gbase + c:gbase + c + 1],
                                       in1=rs[:, 0:n], op0=ALU.mult, op1=ALU.mult) for c in range(8)],
               reads=["xT", krs, "pv"], writes=["hn"])

    def ffn(tile, fi):
        rmsnorm(tile, PV["g"] + [0, 16, 24, 40][fi])
        for blk in range(11):
            wb, kw = W_get("wgu", (fi, blk), WSLOT)
            wv = wb[:, :].rearrange("p (a k n) -> p a k n", a=2, k=8)
            for jj in range(2):
                j = blk * 2 + jj
                for (c0, c1) in tile.subs:
                    n = c1 - c0
                    pg, kpg = PS("mm")
                    pu, kpu = PS("mm")
                    OP("pe", lambda e, wv=wv, jj=jj, c0=c0, c1=c1, n=n, pg=pg, pu=pu: (
                        mm_group(pg[:, 0:n], [(wv[:, 0, k, jj * 128:(jj + 1) * 128], hn[:, k, c0:c1]) for k in range(8)])(e)
                        + mm_group(pu[:, 0:n], [(wv[:, 1, k, jj * 128:(jj + 1) * 128], hn[:, k, c0:c1]) for k in range(8)])(e)),
                       reads=["hn", kw], writes=[kpg, kpu])
                    sg, ksg = SG.next()
                    OP("act", lambda e, sg=sg, pg=pg, n=n: [e.activation(out=sg[:, 0:n], in_=pg[:, 0:n], func=AF.Silu)],
                       reads=[kpg], writes=[ksg])
                    OP("dve", lambda e, sg=sg, pu=pu, j=j, c0=c0, c1=c1, n=n: [
                        e.tensor_tensor(out=act[:, j, c0:c1], in0=sg[:, 0:n], in1=pu[:, 0:n], op=ALU.mult)],
                       reads=[ksg, kpu], writes=["act"])
        for blk in range(8):
            wb, kw = W_get("wdn", (fi, blk), NF * 128)
            wv = wb[:, 0:NF * 128].rearrange("p (j n) -> p j n", j=NF)
            for (c0, c1) in tile.subs:
                n = c1 - c0
                pd, kpd = PS("mm")
                OP("pe", mm_group(pd[:, 0:n], [(wv[:, j, :], act[:, j, c0:c1]) for j in range(NF)]),
                   reads=["act", kw], writes=[kpd])
                OP("dve", lambda e, pd=pd, blk=blk, c0=c0, c1=c1, n=n: [
                    e.scalar_tensor_tensor(out=xT[:, blk, c0:c1], in0=pd[:, 0:n], scalar=0.5, in1=xT[:, blk, c0:c1],
                                           op0=ALU.mult, op1=ALU.add)],
                   reads=[kpd, "xT"], writes=["xT"])

    def load_x(tile):
        src = dr[tile.src]
        OP("sp", lambda e: [e.dma_start(out=xT[:, :, 0:tile.Tp], in_=src[:, :, tile.col0:tile.col0 + tile.Tp])],
           writes=["xT"], dma=("xld", 1))
        if tile.sample:
            OP("sp", lambda e: [e.dma_start(out=xT[:, :, tile.Tp:tile.T], in_=xsmp_d[:, :, :])],
               writes=["xT"], dma=("xld", 1))

    def conv_chunk(tile, m, pre_only):
        pass

    def mixer_in(tile, prefix):
        rmsnorm(tile, PV["g"] + 8)
        Tp = tile.Tp
        if not prefix:
            for blk in range(4):
                wb, kw = W_get("winz", (blk,), WSLOT)
                wv = wb[:, :].rearrange("p (k n) -> p k n", k=8)
                for jj in range(4):
                    m = blk * 4 + jj
                    for (c0, c1) in tile.subs:
                        n = c1 - c0
                        pz, kpz = PS("mm")
                        OP("pe", mm_group(pz[:, 0:n], [(wv[:, k, jj * 128:(jj + 1) * 128], hn[:, k, c0:c1]) for k in range(8)]),
                           reads=["hn", kw], writes=[kpz])
                        OP("act", lambda e, pz=pz, m=m, c0=c0, c1=c1, n=n: [
                            e.activation(out=act[:, m, c0:c1], in_=pz[:, 0:n], func=AF.Silu)],
                           reads=[kpz], writes=["sz"])
        nblk = 6
        for blk in range(nblk):
            wb, kw = W_get("winx", (blk,), WSLOT)
            wv = wb[:, :].rearrange("p (k n) -> p k n", k=8)
            for jj in range(4):
                m = blk * 4 + jj
                xin, kxin = XIN.next()
                xins, kxins = XINS.next()
                OP("dve", lambda e, xin=xin, m=m: [e.tensor_copy(out=xin[:, 0:3], in_=hist[:, m, :])],
                   reads=["hist"], writes=[kxin])
                for (c0, c1) in tile.subs:
                    n = c1 - c0
                    px, kpx = PS("mm")
                    OP("pe", mm_group(px[:, 0:n], [(wv[:, k, jj * 128:(jj + 1) * 128], hn[:, k, c0:c1]) for k in range(8)]),
                       reads=["hn", kw], writes=[kpx])
                    if c0 < Tp:
                        OP("act", lambda e, px=px, xin=xin, c0=c0, c1=c1, n=n: [
                            e.activation(out=xin[:, 3 + c0:3 + c1], in_=px[:, 0:n], func=AF.Copy)],
                           reads=[kpx], writes=[kxin])
                    else:
                        OP("act", lambda e, px=px, xins=xins: [
                            e.activation(out=xins[:, :, 3:7], in_=px[:, 0:64].rearrange("p (b t) -> p b t", t=4), func=AF.Copy)],
                           reads=[kpx], writes=[kxins])
                cacc, kc = CACC.next()
                wc = lambda k, m=m: pv[:, PV["cw"] + k * 24 + m:PV["cw"] + k * 24 + m + 1]
                OP("dve", lambda e, xin=xin, cacc=cacc, m=m, wc=wc: [
                    e.tensor_scalar(out=cacc[:, 0:Tp], in0=xin[:, 3:3 + Tp], scalar1=wc(3), scalar2=pcol("cb", m),
                                    op0=ALU.mult, op1=ALU.add)],
                   reads=[kxin, "pv"], writes=[kc])
                for k in range(3):
                    OP("dve", lambda e, xin=xin, cacc=cacc, k=k, wc=wc: [
                        e.scalar_tensor_tensor(out=cacc[:, 0:Tp], in0=xin[:, k:k + Tp], scalar=wc(k), in1=cacc[:, 0:Tp],
                                               op0=ALU.mult, op1=ALU.add)],
                       reads=[kxin, kc, "pv"], writes=[kc])
                OP("act", lambda e, cacc=cacc, m=m: [e.activation(out=xsT[:, m, 0:Tp], in_=cacc[:, 0:Tp], func=AF.Silu)],
                   reads=[kc], writes=["xsT"])
                OP("dve", lambda e, xin=xin, m=m: [e.tensor_copy(out=hist[:, m, :], in_=xin[:, Tp:Tp + 3])],
                   reads=[kxin], writes=["hist"])
                if tile.sample:
                    OP("dve", lambda e, xins=xins, m=m: [e.tensor_copy(out=xins[:, :, 0:3], in_=sconv[:, m, :, :])],
                       reads=["sconv"], writes=[kxins])
                    cacc, kc = CACC.next()
                    cv3 = lambda cacc=cacc: cacc[:, 0:64].rearrange("p (b t) -> p b t", t=4)
                    OP("dve", lambda e, xins=xins, cv3=cv3, m=m, wc=wc: [
                        e.tensor_scalar(out=cv3(), in0=xins[:, :, 3:7], scalar1=wc(3), scalar2=pcol("cb", m),
                                        op0=ALU.mult, op1=ALU.add)],
                       reads=[kxins, "pv"], writes=[kc])
                    for k in range(3):
                        OP("dve", lambda e, xins=xins, cv3=cv3, k=k, wc=wc: [
                            e.scalar_tensor_tensor(out=cv3(), in0=xins[:, :, k:k + 4], scalar=wc(k), in1=cv3(),
                                                   op0=ALU.mult, op1=ALU.add)],
                           reads=[kxins, kc, "pv"], writes=[kc])
                    OP("act", lambda e, cacc=cacc, m=m: [e.activation(out=xsT[:, m, Tp:Tp + 64], in_=cacc[:, 0:64], func=AF.Silu)],
                       reads=[kc], writes=["xsT"])
                    OP("dve", lambda e, xins=xins, m=m: [e.tensor_copy(out=convs[:, m, :, :], in_=xins[:, :, 4:7])],
                       reads=[kxins], writes=["sconv"])

    def ssd_chunk(tile, ci, sample, prefix):
        L = 64 if sample else 128
        col0 = tile.Tp if sample else ci * 128
        c1 = col0 + L
        um = cf[0:L, CF["ums"]:CF["ums"] + L] if sample else cf[0:L, CF["um"]:CF["um"] + L]
        xtm, kx = XTM.next()
        for grp in range(3):
            ms = list(range(grp * 8, min(grp * 8 + 8, 20)))
            pb, kpb = PS("mm")
            pbv = pb[:, :].bitcast(BF16)
            OP("pe", lambda e, ms=ms, pbv=pbv: [
                e.transpose(pbv[0:L, j * 128:(j + 1) * 128], xsT[:, m, col0:c1], ident) for j, m in enumerate(ms)],
               reads=["xsT", "cbf"], writes=[kpb])
            eng = "act" if grp % 2 == 0 else "dve"
            w = len(ms) * 128
            if eng == "act":
                OP("act", lambda e, pbv=pbv, grp=grp, w=w: [
                    e.activation(out=xtm[0:L, grp * 1024:grp * 1024 + w], in_=pbv[0:L, 0:w], func=AF.Copy)],
                   reads=[kpb], writes=[kx])
            else:
                OP("dve", lambda e, pbv=pbv, grp=grp, w=w: [
                    e.tensor_copy(out=xtm[0:L, grp * 1024:grp * 1024 + w], in_=pbv[0:L, 0:w])],
                   reads=[kpb], writes=[kx])
        dtt, kd = DTT.next()
        pdt, kpd = PS("mm")
        OP("pe", mm_group(pdt[0:L, 0:32], [(hn[:, k, col0:c1], wdt[:, k, :]) for k in range(8)]),
           reads=["hn", "wdt"], writes=[kpd])
        XB, AXc, EX, LN1, DT, AG, WW, NAC = [slice(32 * i, 32 * i + 32) for i in range(8)]
        OP("dve", lambda e: [e.tensor_tensor(out=dtt[0:L, XB], in0=pdt[0:L, 0:32], in1=prow[0:L, PR["dtb"]:PR["dtb"] + 32], op=ALU.add)],
           reads=[kpd, "prow"], writes=[kd])
        OP("act", lambda e: [e.activation(out=dtt[0:L, AXc], in_=dtt[0:L, XB], func=AF.Abs)],
           reads=[kd], writes=[kd])
        OP("act", lambda e: [e.activation(out=dtt[0:L, EX], in_=dtt[0:L, AXc], func=AF.Exp, scale=-1.0)],
           reads=[kd], writes=[kd])
        OP("act", lambda e: [e.activation(out=dtt[0:L, LN1], in_=dtt[0:L, EX], func=AF.Ln, bias=pv[0:L, PV["one"]:PV["one"] + 1])],
           reads=[kd, "pv"], writes=[kd])
        OP("dve", lambda e: [e.scalar_tensor_tensor(out=dtt[0:L, DT], in0=dtt[0:L, XB], scalar=0.0, in1=dtt[0:L, LN1],
                                                    op0=ALU.max, op1=ALU.add)],
           reads=[kd], writes=[kd])
        OP("dve", lambda e: [e.tensor_tensor(out=dtt[0:L, AG], in0=dtt[0:L, DT], in1=abc[0:L, :], op=ALU.mult)],
           reads=[kd, "abc"], writes=[kd])
        ag3, kag = AG3.next()
        tmpf, ktf = TMPF.next()

        def split3(src, dst3, tmp, rk, wk, tk, P, n):
            OP("dve", lambda e: [e.tensor_copy(out=dst3[0:P, 0, 0:n], in_=src)], reads=[rk], writes=[wk])
            OP("dve", lambda e: [e.tensor_tensor(out=tmp[0:P, 0, 0:n], in0=src, in1=dst3[0:P, 0, 0:n], op=ALU.subtract)],
               reads=[rk, wk], writes=[tk])
            OP("dve", lambda e: [e.tensor_copy(out=dst3[0:P, 1, 0:n], in_=tmp[0:P, 0, 0:n])], reads=[tk], writes=[wk])
            OP("dve", lambda e: [e.tensor_tensor(out=tmp[0:P, 1, 0:n], in0=tmp[0:P, 0, 0:n], in1=dst3[0:P, 1, 0:n], op=ALU.subtract)],
               reads=[tk, wk], writes=[tk])
            OP("dve", lambda e: [e.tensor_copy(out=dst3[0:P, 2, 0:n], in_=tmp[0:P, 1, 0:n])], reads=[tk], writes=[wk])
        split3(dtt[0:L, AG], ag3, tmpf, kd, kag, ktf, L, 32)
        umb = cbf[0:L, CB["ums"]:CB["ums"] + L] if sample else cbf[0:L, CB["um"]:CB["um"] + L]
        usb = cbf[0:L, CB["uss"]:CB["uss"] + L] if sample else cbf[0:L, CB["us"]:CB["us"] + L]
        yield
        pc, kpc = PS("mm")
        def cums(e):
            r = []
            for i in range(3):
                r.append(e.matmul(pc[0:L, 0:32], lhsT=usb, rhs=ag3[0:L, i, :], start=(i == 0), stop=(i == 2)))
            for i in range(3):
                r.append(e.matmul(pc[0:L, 32:64], lhsT=umb, rhs=ag3[0:L, i, :], start=(i == 0), stop=(i == 2)))
            for i in range(3):
                r.append(e.matmul(pc[0:32, 64:64 + L], lhsT=ag3[0:L, i, :], rhs=umb, start=(i == 0), stop=(i == 2)))
            if not sample:
                for i in range(3):
                    r.append(e.matmul(pc[:, 192:224], lhsT=ones_bf[0:L, :], rhs=ag3[0:L, i, :], start=(i == 0), stop=(i == 2)))
            return r
        OP("pe", cums, reads=[kag, "cbf"], writes=[kpc])
        OP("act", lambda e: [e.activation(out=dtt[0:L, WW], in_=pc[0:L, 0:32], func=AF.Exp)], reads=[kpc], writes=[kd])
        OP("dve", lambda e: [e.tensor_tensor(out=dtt[0:L, WW], in0=dtt[0:L, WW], in1=dtt[0:L, DT], op=ALU.mult)],
           reads=[kd], writes=[kd])
        OP("dve", lambda e: [e.tensor_scalar(out=dtt[0:L, NAC], in0=pc[0:L, 32:64], scalar1=-1.0, scalar2=None, op0=ALU.mult)],
           reads=[kpc], writes=[kd])
        ETOT = slice(256, 288)
        if not sample:
            OP("act", lambda e: [e.activation(out=dtt[:, ETOT], in_=pc[:, 192:224], func=AF.Exp)], reads=[kpc], writes=[kd])
        if not prefix:
            acf, kacf = ACF.next()
            ac3, kac3 = AC3.next()
            OP("act", lambda e: [e.activation(out=acf[0:32, 0, 0:L], in_=pc[0:32, 64:64 + L], func=AF.Copy)], reads=[kpc], writes=[kacf])
            split3(acf[0:32, 0, 0:L], ac3, acf[:, 1:3, :], kacf, kac3, kacf + "t", 32, L)
            for g in range(4):
                pg, kpg = PS("mm")
                OP("pe", lambda e, pg=pg, g=g: [e.matmul(pg[0:L, 0:L], lhsT=xsT[:, 16 + g, col0:c1], rhs=xsT[:, 20 + g, col0:c1],
                                                         start=True, stop=True)],
                   reads=["xsT"], writes=[kpg])
                OP("dve", lambda e, pg=pg, g=g: [e.tensor_tensor(out=Gm[0:L, g, 0:L], in0=pg[0:L, 0:L], in1=um, op=ALU.mult)],
                   reads=[kpg, "cf"], writes=["Gm"])
        yield
        if not prefix:
            ypss = []
            if sample:
                ypss = [PS("acc"), PS("acc")]
            def bc_issue(hb, mode):
                heads = [2 * hb, 2 * hb + 1]
                pbc, kpbc = PS("mm" if mode == "B" else "hd")
                OP("pe", lambda e, pbc=pbc, heads=heads: [
                    e.matmul(pbc[:, j * L:(j + 1) * L], lhsT=cbf[0:32, CB["sel"] + h * 128:CB["sel"] + (h + 1) * 128],
                             rhs=ac3[0:32, i, 0:L], start=(i == 0), stop=(i == 2)) for j, h in enumerate(heads) for i in range(3)],
                   reads=[kac3, "cbf"], writes=[kpbc])
                return pbc, kpbc

            def head_pair(hb, mode, yint=None, pre=None):
                g = hb // 4
                heads = [2 * hb, 2 * hb + 1]
                pbc, kpbc = pre if pre is not None else bc_issue(hb, mode)
                if mode != "A":
                    seg, ksg = SEG.next()
                    OP("dve", lambda e, pbc=pbc, heads=heads, seg=seg: [
                        e.tensor_scalar(out=seg[0:L, j, 0:L], in0=pbc[0:L, j * L:(j + 1) * L], scalar1=dtt[0:L, 224 + h:225 + h],
                                        scalar2=0.0, op0=ALU.add, op1=ALU.min) for j, h in enumerate(heads)],
                       reads=[kpbc, kd], writes=[ksg])
                    OP("act", lambda e, seg=seg: [e.activation(out=seg[0:L, :, 0:L], in_=seg[0:L, :, 0:L], func=AF.Exp)],
                       reads=[ksg], writes=[ksg])
                    mtb, kmt = MTB.next()
                    OP("dve", lambda e, seg=seg, mtb=mtb, heads=heads, g=g: [
                        e.scalar_tensor_tensor(out=mtb[0:L, j, 0:L], in0=seg[0:L, j, 0:L], scalar=dtt[0:L, 128 + h:129 + h],
                                               in1=Gm[0:L, g, 0:L], op0=ALU.mult, op1=ALU.mult) for j, h in enumerate(heads)],
                       reads=[ksg, kd, "Gm"], writes=[kmt])
                if mode == "B":
                    yp, kyp = yint[hb // 8]
                    ypv = yp[:, :].rearrange("p (c t) -> p c t", t=64)
                    OP("pe", lambda e, mtb=mtb, heads=heads, ypv=ypv, hb=hb: [
                        e.matmul(ypv[(h % 2) * 64:(h % 2) * 64 + 64, hb % 8, :], lhsT=xtm[0:L, h * 64:(h + 1) * 64],
                                 rhs=mtb[0:L, j, 0:L], start=True, stop=True) for j, h in enumerate(heads)],
                       reads=[kx, kmt], writes=[kyp])
                    return
                ebc, keb = EBC.next()
                OP("act", lambda e, pbc=pbc, ebc=ebc: [
                    e.activation(out=ebc[:, :, 0:L], in_=pbc[:, 0:2 * L].rearrange("p (j t) -> p j t", j=2), func=AF.Exp)],
                   reads=[kpbc], writes=[keb])
                if mode == "A":
                    cet, kce, ceo = Ce, "Ce", 2 * hb
                else:
                    cet, kce = CEP.next()
                    ceo = 0
                OP("dve", lambda e, ebc=ebc, hb=hb, g=g, cet=cet, ceo=ceo: [
                    e.tensor_tensor(out=cet[:, ceo:ceo + 2, 0:L], in0=ebc[:, :, 0:L],
                                    in1=xsT[:, 20 + g, col0:c1].unsqueeze(1).to_broadcast([128, 2, L]), op=ALU.mult)],
                   reads=[keb, "xsT"], writes=[kce])
                if mode == "A":
                    OP("dve", lambda e, ebc=ebc, hb=hb: [
                        e.tensor_copy(out=edec[:, 2 * hb:2 * hb + 2, :],
                                      in_=ebc[:, :, 0:64].rearrange("p j (b t) -> p j b t", t=4)[:, :, :, 3])],
                       reads=[keb], writes=["edec"])
                    return
                yp, kyp = PS("acc")
                def yfn(e, mtb=mtb, heads=heads, yp=yp, cet=cet):
                    r = []
                    for j, h in enumerate(heads):
                        o = yp[j * 64:j * 64 + 64, 0:L]
                        r.append(e.matmul(o, lhsT=xtm[0:L, h * 64:(h + 1) * 64], rhs=mtb[0:L, j, 0:L], start=True, stop=False))
                        r.append(e.matmul(o, lhsT=hTb[:, h * 64:(h + 1) * 64], rhs=cet[:, j, 0:L], start=False, stop=True))
                    return r
                OP("pe", yfn, reads=[kx, kmt, "hTb", kce], writes=[kyp])
                post_y(tile, col0, L, [(hb, yp[:, 0:L], kyp, None, None)])

            hmode = "A" if sample else "all"
            pre = bc_issue(0, hmode)
            for hb in range(16):
                nxt = bc_issue(hb + 1, hmode) if hb + 1 < 16 else None
                head_pair(hb, hmode, None, pre)
                pre = nxt
        yield
        if sample:
            OP("dve", lambda e: [
                e.tensor_tensor(out=xwall[0:64, :].rearrange("p (h d) -> p h d", d=64),
                                in0=xtm[0:64, 0:DIN].rearrange("p (h d) -> p h d", d=64),
                                in1=dtt[0:64, WW].unsqueeze(2).to_broadcast([64, 32, 64]), op=ALU.mult)],
               reads=[kx, kd], writes=["xwall"])
            for b in range(16):
                OP("sp", lambda e, b=b: [e.dma_start(out=hT[:, :], in_=sssm_d[b])], writes=["hT"], dma=("hld", 1))
                OP("act", lambda e: [e.activation(out=hTb[:, :], in_=hT[:, :], func=AF.Copy)], reads=["hT"], writes=["hTb"])
                for half in range(2):
                    yp, kyp = ypss[half]
                    ypv = yp[:, :].rearrange("p (c t) -> p c t", t=64)
                    OP("pe", lambda e, b=b, half=half, ypv=ypv: [
                        e.matmul(ypv[(h % 2) * 64:(h % 2) * 64 + 64, (h // 2) % 8, 4 * b:4 * b + 4],
                                 lhsT=hTb[:, h * 64:(h + 1) * 64], rhs=Ce[:, h, 4 * b:4 * b + 4], start=True, stop=True)
                        for h in range(16 * half, 16 * half + 16)],
                       reads=["hTb", "Ce"], writes=[kyp])
                for g in range(4):
                    bm, kbm = BM.next()
                    OP("dve", lambda e, bm=bm, g=g, b=b: [
                        e.tensor_scalar(out=bm[:, :], in0=xtm[0:64, DIN + g * 128:DIN + (g + 1) * 128],
                                        scalar1=cf[0:64, CF["sm"] + b:CF["sm"] + b + 1], scalar2=None, op0=ALU.mult)],
                       reads=[kx, "cf"], writes=[kbm])
                    pst, kps_ = PS("mm")
                    OP("pe", lambda e, bm=bm, g=g, pst=pst: [
                        e.matmul(pst[:, :], lhsT=bm[:, :], rhs=xwall[0:64, g * 512:(g + 1) * 512], start=True, stop=True)],
                       reads=[kbm, "xwall"], writes=[kps_])
                    hv = hT[:, g * 512:(g + 1) * 512].rearrange("p (h d) -> p h d", d=64)
                    OP("dve", lambda e, hv=hv, g=g, b=b: [
                        e.tensor_tensor(out=hv, in0=hv, in1=edec[:, 8 * g:8 * g + 8, b:b + 1].to_broadcast([128, 8, 64]), op=ALU.mult)],
                       reads=["hT", "edec"], writes=["hT"])
                    OP("dve", lambda e, g=g, pst=pst: [
                        e.tensor_tensor(out=hT[:, g * 512:(g + 1) * 512], in0=hT[:, g * 512:(g + 1) * 512], in1=pst[:, :], op=ALU.add)],
                       reads=["hT", kps_], writes=["hT"])
                OP("sp", lambda e, b=b: [e.dma_start(out=ssms_d[b], in_=hT[:, :])], reads=["hT"], writes=["o_ssms"], dma=("ost", 1))
            yint = [PS("hd"), PS("hd")]
            for hb in range(16):
                head_pair(hb, "B", yint)
            post_y(tile, col0, L, [(c, ypss[c // 8][0][:, (c % 8) * 64:(c % 8) * 64 + 64], ypss[c // 8][1],
                                    yint[c // 8][0][:, (c % 8) * 64:(c % 8) * 64 + 64], yint[c // 8][1]) for c in range(16)])
        else:
            for g in range(4):
                xw, kxw = XW.next()
                OP("dve", lambda e, xw=xw, g=g: [
                    e.tensor_tensor(out=xw[0:L, :].rearrange("p (h d) -> p h d", d=64),
                                    in0=xtm[0:L, g * 512:(g + 1) * 512].rearrange("p (h d) -> p h d", d=64),
                                    in1=dtt[0:L, 192 + 8 * g:192 + 8 * g + 8].unsqueeze(2).to_broadcast([L, 8, 64]), op=ALU.mult)],
                   reads=[kx, kd], writes=[kxw])
                pst, kps_ = PS("mm")
                OP("pe", lambda e, xw=xw, g=g, pst=pst: [
                    e.matmul(pst[:, :], lhsT=xtm[0:L, DIN + g * 128:DIN + (g + 1) * 128], rhs=xw[0:L, :], start=True, stop=True)],
                   reads=[kx, kxw], writes=[kps_])
                hv = hT[:, g * 512:(g + 1) * 512].rearrange("p (h d) -> p h d", d=64)
                OP("dve", lambda e, hv=hv, g=g: [
                    e.tensor_tensor(out=hv, in0=hv, in1=dtt[:, 256 + 8 * g:256 + 8 * g + 8].unsqueeze(2).to_broadcast([128, 8, 64]),
                                    op=ALU.mult)],
                   reads=["hT", kd], writes=["hT"])
                OP("dve", lambda e, g=g, pst=pst: [
                    e.tensor_tensor(out=hT[:, g * 512:(g + 1) * 512], in0=hT[:, g * 512:(g + 1) * 512], in1=pst[:, :], op=ALU.add)],
                   reads=["hT", kps_], writes=["hT"])
                OP("act", lambda e, g=g: [e.activation(out=hTb[:, g * 512:(g + 1) * 512], in_=hT[:, g * 512:(g + 1) * 512], func=AF.Copy)],
                   reads=["hT"], writes=["hTb"])

    ygstate = {}

    def post_y(tile, col0, L, items):
        c1 = col0 + L
        for (c, yap, kyp, yap2, kyp2) in items:
            y1, ky1 = Y1.next()
            OP("dve", lambda e, y1=y1, c=c, yap=yap: [
                e.scalar_tensor_tensor(out=y1[:, 0:L], in0=xsT[:, c, col0:c1], scalar=pcol("dsk", c), in1=yap,
                                       op0=ALU.mult, op1=ALU.add)],
               reads=["xsT", kyp, "pv"], writes=[ky1])
            if yap2 is not None:
                OP("dve", lambda e, y1=y1, yap2=yap2: [e.tensor_tensor(out=y1[:, 0:L], in0=y1[:, 0:L], in1=yap2, op=ALU.add)],
                   reads=[ky1, kyp2], writes=[ky1])
            if c % 4 == 0:
                ygstate["yg"] = YG.next()
                ygstate["ss"] = PS("mm")
            yg, kyg = ygstate["yg"]
            pss, kss = ygstate["ss"]
            OP("dve", lambda e, y1=y1, yg=yg, c=c: [
                e.tensor_tensor(out=yg[:, c % 4, 0:L], in0=y1[:, 0:L], in1=act[:, c, col0:c1], op=ALU.mult)],
               reads=[ky1, "sz"], writes=[kyg])
            sq, ksq = SQ.next()
            OP("act", lambda e, sq=sq, yg=yg, c=c: [e.activation(out=sq[:, 0, 0:L], in_=yg[:, c % 4, 0:L], func=AF.Square)],
               reads=[kyg], writes=[ksq])
            OP("pe", lambda e, sq=sq, pss=pss, c=c: [
                e.matmul(pss[:, 0:L], lhsT=ones_bf, rhs=sq[:, 0, 0:L], start=(c % 4 == 0), stop=(c % 4 == 3))],
               reads=[ksq, "cbf"], writes=[kss])
            if c % 4 == 3:
                rs, krs = RS.next()
                OP("act", lambda e, rs=rs, pss=pss: [
                    e.activation(out=rs[:, 0:L], in_=pss[:, 0:L], func=AF.Ln, bias=pcol("eps"), scale=1.0 / 512)],
                   reads=[kss, "pv"], writes=[krs])
                OP("act", lambda e, rs=rs: [e.activation(out=rs[:, 0:L], in_=rs[:, 0:L], func=AF.Exp, scale=-0.5)],
                   reads=[krs], writes=[krs])
                OP("dve", lambda e, rs=rs, yg=yg, c=c: [
                    e.scalar_tensor_tensor(out=act[:, c - 3 + k, col0:c1], in0=yg[:, k, 0:L], scalar=pcol("ng", c - 3 + k),
                                           in1=rs[:, 0:L], op0=ALU.mult, op1=ALU.mult) for k in range(4)],
                   reads=[kyg, krs, "pv"], writes=["sz"])

    def mixer_out(tile):
        for blk in range(4):
            wb, kw = W_get("wout", (blk,), WSLOT)
            wv = wb[:, :].rearrange("p (k n) -> p k n", k=16)
            for jj in range(2):
                dm = blk * 2 + jj
                for (c0, c1) in tile.subs:
                    n = c1 - c0
                    po, kpo = PS("mm")
                    OP("pe", mm_group(po[:, 0:n], [(wv[:, k, jj * 128:(jj + 1) * 128], act[:, k, c0:c1]) for k in range(16)]),
                       reads=["sz", kw], writes=[kpo])
                    OP("dve", lambda e, po=po, dm=dm, c0=c0, c1=c1, n=n: [
                        e.tensor_tensor(out=xT[:, dm, c0:c1], in0=po[:, 0:n], in1=xT[:, dm, c0:c1], op=ALU.add)],
                       reads=[kpo, "xT"], writes=["xT"])

    def mixer0(tile, prefix):
        mixer_in(tile, prefix)
        barrier()
        gens = [ssd_chunk(tile, ci, False, prefix) for ci in range(tile.nch)]
        next(gens[0])
        next(gens[0])
        for ci in range(tile.nch):
            nxt = gens[ci + 1] if ci + 1 < tile.nch else None
            if nxt is not None:
                next(nxt)
            next(gens[ci])
            if nxt is not None:
                next(nxt)
            for _ in gens[ci]:
                pass
        if tile.last:
            OP("sp", lambda e: [e.dma_start(out=ssmp_d[:, :], in_=hT[:, :])], reads=["hT"], writes=["o_ssmp"], dma=("ost", 1))
            OP("sp", lambda e: [e.dma_start(out=convp_d[:, :, :], in_=hist[:, :, :])], reads=["hist"], writes=["o_convp"], dma=("ost", 1))
        if tile.sample:
            for _ in ssd_chunk(tile, None, True, False):
                pass
            OP("sp", lambda e: [e.dma_start(out=convs_d[:, :, :, :], in_=convs[:, :, :, :])], reads=["sconv"], writes=["o_convs"],
               dma=("ost", 1))
        barrier()
        if not prefix and not (os.environ.get("KSKIP") == "out" and tile.kind == "main"):
            mixer_out(tile)

    def kv_proj(tile):
        rmsnorm(tile, PV["kvn"])
        wb, kw = W_get("wkv", (0,), WSLOT)
        wv = wb[:, :].rearrange("p (k n) -> p k n", k=8)
        for m in range(2):
            for (c0, c1) in tile.subs:
                n = c1 - c0
                pk, kpk = PS("mm")
                OP("pe", mm_group(pk[:, 0:n], [(wv[:, k, m * 128:(m + 1) * 128], hn[:, k, c0:c1]) for k in range(8)]),
                   reads=["hn", kw], writes=[kpk])
                OP("act", lambda e, pk=pk, m=m, c0=c0, c1=c1, n=n: [
                    e.activation(out=kT[:, m, 128 + c0:128 + c1], in_=pk[:, 0:n], func=AF.Identity, bias=pcol("bk", m))],
                   reads=[kpk, "pv"], writes=["kT"])
        chunks = [(ci * 128, 128, 1 + ci) for ci in range(tile.nch)] + ([(tile.Tp, 64, 5)] if tile.sample else [])
        for (c0, L, slot) in chunks:
            pvv, kpv = PS("mm")
            OP("pe", mm_group(pvv[0:L, 0:256], [(hn[:, k, c0:c0 + L], wv[:, k, 256:512]) for k in range(8)]),
               reads=["hn", kw], writes=[kpv])
            OP("dve", lambda e, pvv=pvv, L=L, slot=slot: [
                e.tensor_tensor(out=vtm[0:L, slot, :], in0=pvv[0:L, 0:256], in1=prow[0:L, PR["bv"]:PR["bv"] + 256], op=ALU.add)],
               reads=[kpv, "prow"], writes=["vtm"])
            is_lastp = tile.last and slot == tile.nch
            if is_lastp or slot == 5:
                pkk, kpkk = PS("mm")
                OP("pe", mm_group(pkk[0:L, 0:256], [(hn[:, k, c0:c0 + L], wv[:, k, 0:256]) for k in range(8)]),
                   reads=["hn", kw], writes=[kpkk])
                OP("dve", lambda e, pkk=pkk, L=L: [
                    e.tensor_tensor(out=kvf[0:L, 0:256], in0=pkk[0:L, 0:256], in1=prow[0:L, PR["bkr"]:PR["bkr"] + 256], op=ALU.add)],
                   reads=[kpkk, "prow"], writes=["kvf"])
                OP("dve", lambda e, pvv=pvv, L=L: [
                    e.tensor_tensor(out=kvf[0:L, 256:512], in0=pvv[0:L, 0:256], in1=prow[0:L, PR["bv"]:PR["bv"] + 256], op=ALU.add)],
                   reads=[kpv, "prow"], writes=["kvf"])
                if is_lastp:
                    OP("sp", lambda e: [e.dma_start(out=kwp_d[:, :], in_=kvf[:, 0:256]),
                                        e.dma_start(out=vwp_d[:, :], in_=kvf[:, 256:512])],
                       reads=["kvf"], writes=["o_kvp"], dma=("ost", 2))
                else:
                    OP("sp", lambda e: (
                        [e.dma_start(out=kws_d[b, 124:128, :], in_=kvf[4 * b:4 * b + 4, 0:256]) for b in range(16)]
                        + [e.dma_start(out=vws_d[b, 124:128, :], in_=kvf[4 * b:4 * b + 4, 256:512]) for b in range(16)]
                        + [e.dma_start(out=kws_d[:, 0:124, :], in_=ck_d[:, 4:128, :]),
                           e.dma_start(out=vws_d[:, 0:124, :], in_=cv_d[:, 4:128, :])]),
                       reads=["kvf"], writes=["o_kvs"], dma=("ost", 34))

    def kv_carry(tile):
        n = tile.nch
        OP("dve", lambda e: [e.tensor_copy(out=kT[:, :, 0:128], in_=kT[:, :, n * 128:n * 128 + 128])], reads=["kT"], writes=["kT"])
        OP("dve", lambda e: [e.tensor_copy(out=vtm[:, 0, :], in_=vtm[:, n, :])], reads=["vtm"], writes=["vtm"])

    qT = act
    OT0 = 8

    def attn_batch(items, nq, segs_n, masks, extra_bias_first=None):
        nk = sum(segs_n)
        nb = len(items)
        psS = [PS("hd"), PS("hd")]
        sm, ksm = SM.next()
        pn, kpn = PN.next()
        pt, kpt = PT.next()
        st, kst = STAT.next()

        def sfn(e):
            r = []
            for j, it in enumerate(items):
                o = 0
                for si, n_ in enumerate(segs_n):
                    r.append(e.matmul(psS[j % 2][0][0:nq, (j // 2) * 256 + o:(j // 2) * 256 + o + n_], lhsT=it["q"], rhs=it["k"][si],
                                      start=True, stop=True))
                    o += n_
            return r
        AST = int(os.environ.get("KASTAGE", "99"))
        if os.environ.get("KSKIP3") != "sfn":
            OP("pe", sfn, reads=["qT", "kT", "qs", "ckt0", "ckt1"], writes=[psS[0][1], psS[1][1]])
        if AST < 1:
            return
        OP("dve", lambda e: [
            e.scalar_tensor_tensor(out=sm[0:nq, j, m0:m0 + ml], in0=psS[j % 2][0][0:nq, (j // 2) * 256 + m0:(j // 2) * 256 + m0 + ml],
                                   scalar=0.125, in1=map_, op0=ALU.mult, op1=ALU.add) for j in range(nb) for (m0, ml, map_) in masks],
           reads=[psS[0][1], psS[1][1], "cf"], writes=[ksm])
        OP("dve", lambda e: [e.memset(st[0:nq, 0:nb, 2:3], 0.0)], writes=[kst + "a"])
        if extra_bias_first is not None:
            OP("dve", lambda e: [
                e.tensor_scalar(out=sm[0:nq, 0:nb, 0:128], in0=sm[0:nq, 0:nb, 0:128], scalar1=extra_bias_first, scalar2=None, op0=ALU.add)],
               reads=[ksm, "hbias"], writes=[ksm])
        if AST < 2:
            return
        OP("dve", lambda e: [e.reduce_max(out=st[0:nq, 0:nb, 0:1], in_=sm[0:nq, 0:nb, 0:nk], axis=AX.X)], reads=[ksm], writes=[kst])
        OP("dve", lambda e: [
            e.tensor_scalar(out=st[0:nq, j, 1:2], in0=st[0:nq, j, 0:1], scalar1=it["sink"], scalar2=-1.0, op0=ALU.max, op1=ALU.mult)
            for j, it in enumerate(items)], reads=[kst, "prow", "cf"], writes=[kst])
        if AST < 3:
            return
        OP("act", lambda e: [
            e.activation(out=sm[0:nq, j, 0:nk], in_=sm[0:nq, j, 0:nk], func=AF.Exp, bias=st[0:nq, j, 1:2], accum_out=st[0:nq, j, 2:3])
            for j in range(nb)], reads=[ksm, kst], writes=[ksm, kst + "a"])
        OP("act", lambda e: [
            e.activation(out=st[0:nq, j, 3:4], in_=st[0:nq, j, 1:2], func=AF.Exp, bias=it["sink"]) for j, it in enumerate(items)],
           reads=[kst, "prow", "cf"], writes=[kst + "b"])
        OP("dve", lambda e: [e.tensor_tensor(out=st[0:nq, 0:nb, 4:5], in0=st[0:nq, 0:nb, 2:3], in1=st[0:nq, 0:nb, 3:4], op=ALU.add)],
           reads=[kst + "a", kst + "b"], writes=[kst + "c"])
        OP("dve", lambda e: [e.reciprocal(out=st[0:nq, 0:nb, 5:6], in_=st[0:nq, 0:nb, 4:5])], reads=[kst + "c"], writes=[kst + "d"])
        OP("dve", lambda e: [
            e.tensor_scalar(out=pn[0:nq, j, 0:nk], in0=sm[0:nq, j, 0:nk], scalar1=st[0:nq, j, 5:6], scalar2=None, op0=ALU.mult)
            for j in range(nb)], reads=[ksm, kst + "d"], writes=[kpn])
        if AST < 4:
            return
        ptp, kptp = PS("mm")
        ptv = ptp[:, :].bitcast(BF16)

        def tfn(e):
            r = []
            for j in range(nb):
                o = 0
                for si, n_ in enumerate(segs_n):
                    r.append(e.transpose(ptv[0:n_, j * 256 + si * 128:j * 256 + si * 128 + nq], pn[0:nq, j, o:o + n_], ident[0:nq, 0:nq]))
                    o += n_
            return r
        OP("pe", tfn, reads=[kpn, "cbf"], writes=[kptp])
        nmax = max(segs_n)

        def cfn(e):
            if len(set(segs_n)) == 1:
                return [e.activation(out=pt[0:nmax, 0:nb, :], in_=ptv[0:nmax, 0:nb * 256].rearrange("p (j t) -> p j t", t=256), func=AF.Copy)]
            r = []
            for si, n_ in enumerate(segs_n):
                r.append(e.activation(out=pt[0:n_, 0:nb, si * 128:si * 128 + nq],
                                      in_=ptv[0:n_, 0:nb * 256].rearrange("p (j t) -> p j t", t=256)[:, :, si * 128:si * 128 + nq],
                                      func=AF.Copy))
            return r
        OP("act", cfn, reads=[kptp], writes=[kpt])

        if AST < 5:
            return

        def pvfn(e):
            r = []
            for j, it in enumerate(items):
                for si, n_ in enumerate(segs_n):
                    r.append(e.matmul(it["out"], lhsT=it["v"][si], rhs=pt[0:n_, j, si * 128:si * 128 + nq],
                                      start=(si == 0), stop=(si == len(segs_n) - 1)))
            return r
        OP("pe", pvfn, reads=[kpt, "vtm", "cvt0", "cvt1"], writes=sorted(set(it["okey"] for it in items)))

    def attention(tile):
        rmsnorm(tile, PV["g"] + 32)
        for blk in range(2):
            if os.environ.get("KSKIP2") == "aq":
                continue
            wb, kw = W_get("wq", (blk,), WSLOT)
            wv = wb[:, :].rearrange("p (k n) -> p k n", k=8)
            for jj in range(4):
                c = blk * 4 + jj
                for (c0, c1) in tile.subs:
                    n = c1 - c0
                    pq, kpq = PS("mm")
                    OP("pe", mm_group(pq[:, 0:n], [(wv[:, k, jj * 128:(jj + 1) * 128], hn[:, k, c0:c1]) for k in range(8)]),
                       reads=["hn", kw], writes=[kpq])
                    OP("act", lambda e, pq=pq, c=c, c0=c0, c1=c1, n=n: [
                        e.activation(out=qT[:, c, c0:c1], in_=pq[:, 0:n], func=AF.Identity, bias=pcol("bq", c))],
                       reads=[kpq, "pv"], writes=["qT"])
        maskb = cf[:, CF["mb"]:CF["mb"] + 256]
        barrier()
        for ci in range(tile.nch):
            if os.environ.get("KSKIP") == "ablk":
                continue
            q0 = ci * 128
            oA = PS("acc")
            oB = PS("acc")
            obanks = [oA, oB]
            for c in range(8):
                for half in range(1):
                    pass
            for bi in range(4):
                items = []
                for cc in (2 * bi, 2 * bi + 1):
                    for e_ in range(2):
                        g = 2 * (cc // 4) + e_
                        pr = slice(e_ * 64, e_ * 64 + 64)
                        prk = slice(0, 128) if os.environ.get("KFULLK") else pr
                        ob, kob = obanks[cc // 4]
                        items.append(dict(
                            q=qT[prk, cc, q0:q0 + 128],
                            k=[kT[prk, cc // 4, q0:q0 + 128], kT[prk, cc // 4, q0 + 128:q0 + 256]],
                            v=[vtm[:, ci, g * 64:(g + 1) * 64], vtm[:, ci + 1, g * 64:(g + 1) * 64]],
                            sink=prow[:, PR["snk"] + 2 * cc + e_:PR["snk"] + 2 * cc + e_ + 1],
                            out=ob[pr, (cc % 4) * 128:(cc % 4) * 128 + 128], okey=kob))
                attn_batch(items, 128, [128, 128], [(0, 256, maskb)],
                           extra_bias_first=(hbias[:, 0:1] if (tile.first and ci == 0) else None))
            for hb_, (ob, kob) in enumerate(obanks):
                if os.environ.get("KSKIP3") == "evac":
                    continue
                OP("act", lambda e, ob=ob, hb_=hb_, q0=q0: [
                    e.activation(out=act[:, OT0 + 4 * hb_:OT0 + 4 * hb_ + 4, q0:q0 + 128],
                                 in_=ob[:, :].rearrange("p (c t) -> p c t", t=128), func=AF.Copy)],
                   reads=[kob], writes=["oT"])
        if tile.sample:
            Tp = tile.Tp
            OP("act", lambda e: [
                e.activation(out=qs[:, :, c, :], in_=qT[:, c, Tp:Tp + 64].rearrange("p (b t) -> p b t", t=4), func=AF.Copy)
                for c in range(8)], reads=["qT"], writes=["qs"])
            pvp, kpvp = PS("acc")
            pvv = pvp[:, :].rearrange("p (b c q) -> p b c q", b=16, c=2)
            maskC = cf[0:16, CF["ms"]:CF["ms"] + 128]
            for b in range(16):
                ckt, kck = CKT.next()
                cvt, kcv = CV.next()
                OP("pool", lambda e, ckt=ckt, b=b: [e.dma_start(out=ckt[:, :], in_=ckT_d[b])], writes=[kck], dma=(kck, 1))
                OP("pool", lambda e, cvt=cvt, b=b: [e.dma_start(out=cvt[:, :], in_=cv_d[b])], writes=[kcv], dma=(kcv, 1))
                items = []
                for g in range(4):
                    e_ = g % 2
                    pr = slice(e_ * 64, e_ * 64 + 64)
                    cc0 = 4 * (g // 2)
                    items.append(dict(
                        q=qs[pr, b, cc0:cc0 + 4, :].rearrange("p c t -> p (c t)"),
                        k=[ckt[pr, (g // 2) * 128:(g // 2) * 128 + 128], kT[pr, g // 2, 128 + Tp:128 + Tp + 64]],
                        v=[cvt[:, g * 64:(g + 1) * 64], vtm[0:64, 5, g * 64:(g + 1) * 64]],
                        sink=cf[0:16, CF["ss"] + g:CF["ss"] + g + 1],
                        out=pvv[pr, b, g // 2, :], okey=kpvp))
                mN = cf[0:16, CF["mn"] + 60 - 4 * b:CF["mn"] + 60 - 4 * b + 64]
                attn_batch(items, 16, [128, 64], [(0, 128, maskC), (128, 64, mN)])
            for cc in range(2):
                OP("act", lambda e, cc=cc: [
                    e.activation(out=act[:, OT0 + 4 * cc:OT0 + 4 * cc + 4, Tp:Tp + 64].rearrange("p i (b t) -> p b i t", t=4),
                                 in_=pvv[:, :, cc, :].rearrange("p b (i t) -> p b i t", t=4), func=AF.Copy)],
                   reads=[kpvp], writes=["oT"])
        barrier()
        for blk in range(2):
            if os.environ.get("KSKIP2") == "ao":
                continue
            wb, kw = W_get("wo", (blk,), WSLOT)
            wv = wb[:, :].rearrange("p (k n) -> p k n", k=8)
            for jj in range(4):
                dm = blk * 4 + jj
                for (c0, c1) in tile.subs:
                    if tile.first and c1 <= 128:
                        pass
                    n = c1 - c0
                    po, kpo = PS("mm")
                    OP("pe", mm_group(po[:, 0:n], [(wv[:, k, jj * 128:(jj + 1) * 128], act[:, OT0 + k, c0:c1]) for k in range(8)]),
                       reads=["oT", kw], writes=[kpo])
                    OP("dve", lambda e, po=po, dm=dm, c0=c0, c1=c1, n=n: [
                        e.scalar_tensor_tensor(out=xT[:, dm, c0:c1], in0=po[:, 0:n], scalar=pcol("bo", dm), in1=xT[:, dm, c0:c1],
                                               op0=ALU.add, op1=ALU.add)],
                       reads=[kpo, "xT", "pv"], writes=["xT"])

    def final_out(tile):
        for (c0, c1) in tile.subs:
            n = c1 - c0
            pss, kps = PS("mm")
            for cp in range(4):
                sq, ksq = SQ.next()
                OP("act", lambda e, sq=sq, cp=cp, c0=c0, c1=c1, n=n: [
                    e.activation(out=sq[:, :, 0:n], in_=xT[:, 2 * cp:2 * cp + 2, c0:c1], func=AF.Square)],
                   reads=["xT"], writes=[ksq])
                OP("pe", lambda e, sq=sq, cp=cp, pss=pss, n=n: [
                    e.matmul(pss[:, 0:n], lhsT=ones_bf, rhs=sq[:, j, 0:n], start=(cp == 0 and j == 0),
                             stop=(cp == 3 and j == 1)) for j in range(2)],
                   reads=[ksq, "cbf"], writes=[kps])
            rs, krs = RS.next()
            OP("act", lambda e, rs=rs, pss=pss, n=n: [
                e.activation(out=rs[:, 0:n], in_=pss[:, 0:n], func=AF.Ln, bias=pcol("eps"), scale=1.0 / D)],
               reads=[kps, "pv"], writes=[krs])
            OP("act", lambda e, rs=rs, n=n: [e.activation(out=rs[:, 0:n], in_=rs[:, 0:n], func=AF.Exp, scale=-0.5)],
               reads=[krs], writes=[krs])
            OP("dve", lambda e, rs=rs, c0=c0, c1=c1, n=n: [
                e.scalar_tensor_tensor(out=xT[:, c, c0:c1], in0=xT[:, c, c0:c1], scalar=pcol("fin", c), in1=rs[:, 0:n],
                                       op0=ALU.mult, op1=ALU.mult) for c in range(8)],
               reads=["xT", krs, "pv"], writes=["xT"])
        s0 = 0
        nout = tile.Tp
        OP("sp", lambda e: [e.dma_start(out=yT_d[:, :, tile.out0:tile.out0 + nout], in_=xT[:, :, s0:tile.Tp])],
           reads=["xT"], writes=["o_y"], dma=("ost", 1))
        if tile.sample:
            OP("sp", lambda e: [e.dma_start(out=ysT_d[:, :, :], in_=xT[:, :, tile.Tp:tile.T])], reads=["xT"], writes=["o_ys"],
               dma=("ost", 1))

    def prologue():
        OP("sp", lambda e: [e.dma_start(out=pv[:, :], in_=pv_d[:, :]), e.dma_start(out=prow[:, :], in_=prow_d[:, :]),
                            e.dma_start(out=cf[:, :], in_=cf_d[:, :]), e.dma_start(out=cbf[:, :], in_=cbf_d[:, :]),
                            e.dma_start(out=cmask[:, :], in_=cmask_d[:, :]), e.dma_start(out=sconv[:, :, :, :], in_=sconv_d[:, :, :, :])],
           writes=["pv", "prow", "cf", "cbf", "cmask", "sconv"], dma=("cld", 6))
        OP("pool", lambda e: [e.dma_start(out=wdt[:, :, :], in_=wdt_d[:, :].rearrange("p (k n) -> p k n", k=8))], writes=["wdt"],
           dma=("wdtl", 1))
        OP("act", lambda e: [e.activation(out=abc[:, :], in_=prow[:, PR["alog"]:PR["alog"] + 32], func=AF.Exp)], reads=["prow"], writes=["abc"])
        OP("dve", lambda e: [e.tensor_scalar(out=abc[:, :], in0=abc[:, :], scalar1=-1.0, scalar2=None, op0=ALU.mult)], reads=["abc"], writes=["abc"])
        OP("dve", lambda e: [e.tensor_scalar(out=hbias[:, :], in0=cmask[:, :], scalar1=-1.0, scalar2=-NEG, op0=ALU.add, op1=ALU.mult)],
           reads=["cmask"], writes=["hbias"])
        OP("dve", lambda e: [e.memset(hT[:, :], 0.0)], writes=["hT"])
        OP("dve", lambda e: [e.memset(hTb[:, :], 0.0)], writes=["hTb"])
        OP("dve", lambda e: [e.memset(hist[:, :, :], 0.0)], writes=["hist"])
        OP("dve", lambda e: [e.memset(kT[:, :, :], 0.0)], writes=["kT"])
        OP("dve", lambda e: [e.memset(vtm[:, :, :], 0.0)], writes=["vtm"])

    import os
    KSTOP = int(os.environ.get("KSTOP", "100000"))

    def program():
        ph = [0]

        def step():
            ph[0] += 1
            return ph[0] > KSTOP

        def finish(dump=None):
            if dump is not None and KSTOP < 100000:
                t = dump
                OP("sp", lambda e: [e.dma_start(out=yT_d[:, :, 0:t.Tp], in_=xT[:, :, 0:t.Tp])], reads=["xT"], writes=["o_y"], dma=("ost", 1))
            OP("sp", lambda e: [], reads=["o_y", "o_ys", "o_ssmp", "o_convp", "o_convs", "o_ssms", "o_kvp", "o_kvs"])
            if not wstate["dry"]:
                last = {}
                for i_, o_ in enumerate(S.ops[:-1]):
                    last[o_.eng if o_.dma is None else o_.dma[0]] = i_
                for i_ in last.values():
                    S.ops[-1].deps.setdefault(i_, 2)

        prologue()
        if step(): return finish()
        for tile in tiles:
            load_x(tile)
            if step(): return finish(tile)
            ffn(tile, 0)
            barrier()
            if step(): return finish(tile)
            if tile.kind == "pre":
                mixer0(tile, True)
                barrier()
                if step(): return finish(tile)
                continue
            mixer0(tile, False)
            barrier()
            if step(): return finish(tile)
            ffn(tile, 1)
            barrier()
            if step(): return finish(tile)
            kv_proj(tile)
            barrier()
            if step(): return finish(tile)
            if tile.kind == "prefull":
                kv_carry(tile)
                OP("dve", lambda e: [e.tensor_scalar(out=hT[:, :], in0=hT[:, :], scalar1=cmask[:, 0:1], scalar2=None, op0=ALU.mult)],
                   reads=["hT", "cmask"], writes=["hT"])
                OP("act", lambda e: [e.activation(out=hTb[:, :], in_=hT[:, :], func=AF.Copy)], reads=["hT"], writes=["hTb"])
                continue
            ffn(tile, 2)
            barrier()
            if step(): return finish(tile)
            attention(tile)
            barrier()
            if step(): return finish(tile)
            kv_carry(tile)
            ffn(tile, 3)
            barrier()
            if step(): return finish(tile)
            final_out(tile)
            barrier()
            if step(): return finish()
        finish()

    program()
    wstate["dry"] = False
    wstate["i"] = 0
    for k in pctr:
        pctr[k] = 0
    for r_ in (XTM, SEG, MTB, EBC, Y1, YG, SQ, RS, SG, XIN, XINS, CACC, DTT, AG3, TMPF, ACF, AC3, CEP, XW, BM, SM, PN, PT, STAT, CKT, CV):
        r_.i = 0
    program()
    S.finalize()

    with ExitStack() as es2:
        semh = {n: es2.enter_context(nc.semaphore(f"s_{n}")) for n in S.semnames}
        for n in ("pe", "act", "dve", "pool"):
            if n not in semh:
                semh[n] = es2.enter_context(nc.semaphore(f"s_{n}"))
        with nc.Block() as block:
            @block.sync
            def _(e):
                S.emit_engine("sp", e, semh)

            @block.gpsimd
            def _(e):
                S.emit_engine("pool", e, semh)

            @block.tensor
            def _(e):
                S.emit_engine("pe", e, semh)

            @block.scalar
            def _(e):
                S.emit_engine("act", e, semh)

            @block.vector
            def _(e):
                S.emit_engine("dve", e, semh)
    es.close()
    return nc, len(S.ops)


def tile_w(Wm, nb):
    K, N = Wm.shape
    a = Wm.reshape(K // 128, 128, N // nb, nb)
    return np.ascontiguousarray(a.transpose(2, 1, 0, 3)).reshape(N // nb, 128, (K // 128) * nb)


def pad_last(a, n):
    if a.shape[-1] == n:
        return a
    out = np.zeros(a.shape[:-1] + (n,), a.dtype)
    out[..., :a.shape[-1]] = a
    return out


def host_consts():
    cfa = np.zeros((128, NCF), np.float32)
    i = np.arange(128)
    um = (i[:, None] <= i[None, :]).astype(np.float32)
    cfa[:, CF["um"]:CF["um"] + 128] = um
    usf = (i[:, None] > i[None, :]).astype(np.float32)
    j = np.arange(64)
    same = (j[:, None] // 4) == (j[None, :] // 4)
    cfa[:64, CF["ums"]:CF["ums"] + 64] = (same & (j[:, None] <= j[None, :])).astype(np.float32)
    ussf = (same & (j[:, None] > j[None, :])).astype(np.float32)
    mb = np.full((128, 256), NEG, np.float32)
    mb[:, :128][i[None, :] > i[:, None]] = 0.0
    mb[:, 128:][i[None, :] <= i[:, None]] = 0.0
    cfa[:, CF["mb"]:CF["mb"] + 256] = mb
    ms = np.full((16, 128), NEG, np.float32)
    mn = np.full((16, 124), NEG, np.float32)
    for r in range(16):
        t = r % 4
        ms[r, t + 1:128] = 0.0
        mn[r, 60:60 + t + 1] = 0.0
    cfa[:16, CF["ms"]:CF["ms"] + 128] = ms
    cfa[:16, CF["mn"]:CF["mn"] + 124] = mn
    cfa[:64, CF["sm"]:CF["sm"] + 16] = (j[:, None] // 4 == np.arange(16)[None, :]).astype(np.float32)
    cb = np.zeros((128, NCB), np.float32)
    cb[:, :128] = np.eye(128)
    cb[:, 128:256] = 1.0
    cb[:, CB["um"]:CB["um"] + 128] = cfa[:, CF["um"]:CF["um"] + 128]
    cb[:, CB["us"]:CB["us"] + 128] = usf
    cb[:64, CB["ums"]:CB["ums"] + 64] = cfa[:64, CF["ums"]:CF["ums"] + 64]
    cb[:64, CB["uss"]:CB["uss"] + 64] = ussf
    sel = np.zeros((32, 32, 128), np.float32)
    for h in range(32):
        sel[h, h, :] = 1.0
    cb[:32, CB["sel"]:CB["sel"] + 4096] = sel.reshape(32, 4096)
    return cfa, cb.astype(ml_dtypes.bfloat16)


def host_weights(p):
    f32 = np.float32
    out = {}
    wgu = np.zeros((4, 11, 128, WSLOT), f32)
    wdn = np.zeros((4, 8, 128, NF * 128), f32)
    for fi in range(4):
        l, i = fi // 2, fi % 2
        g = tile_w(np.asarray(p["ffn_w_gate"][l, i]), 256)
        u = tile_w(np.asarray(p["ffn_w_up"][l, i]), 256)
        wgu[fi] = np.concatenate([g, u], axis=2)
        wdn[fi] = tile_w(np.asarray(p["ffn_w_down"][l, i]), 128)
    out["wgu"], out["wdn"] = wgu, wdn
    win = np.asarray(p["ssm_w_in"][0])
    out["winz"] = tile_w(win[:, 0:2048], 512)
    out["winx"] = tile_w(win[:, 2048:5120], 512)
    out["wdt"] = np.ascontiguousarray(tile_w(win[:, 5120:5152], 32)[0])
    out["wout"] = tile_w(np.asarray(p["ssm_w_out"][0]), 256)
    out["wkv"] = tile_w(np.asarray(p["attn_w_kv"]), 512)
    perm = np.concatenate([np.arange(QH[c][e] * 64, QH[c][e] * 64 + 64) for c in range(8) for e in range(2)])
    out["wq"] = tile_w(np.asarray(p["attn_w_q"][0])[:, perm], 512)
    out["wo"] = tile_w(np.asarray(p["attn_w_o"][0])[perm, :], 512)
    pvh = np.zeros((128, NPV), f32)
    fm = lambda v: np.asarray(v, f32).reshape(-1, 128).T
    ng = np.asarray(p["norm_gain"])
    for l in range(2):
        for i in range(3):
            pvh[:, PV["g"] + (l * 3 + i) * 8:PV["g"] + (l * 3 + i) * 8 + 8] = fm(ng[l, i])
    pvh[:, PV["kvn"]:PV["kvn"] + 8] = fm(p["kv_norm"])
    pvh[:, PV["fin"]:PV["fin"] + 8] = fm(p["final_norm"])
    cw = np.asarray(p["ssm_conv_w"][0])
    for k in range(4):
        pvh[:, PV["cw"] + k * 24:PV["cw"] + k * 24 + 24] = fm(cw[k])
    pvh[:, PV["cb"]:PV["cb"] + 24] = fm(p["ssm_conv_b"][0])
    pvh[:, PV["dsk"]:PV["dsk"] + 16] = fm(np.repeat(np.asarray(p["ssm_d"][0]), 64))
    pvh[:, PV["ng"]:PV["ng"] + 16] = fm(p["ssm_norm"][0])
    pvh[:, PV["bq"]:PV["bq"] + 8] = fm(np.asarray(p["attn_b_q"][0])[perm])
    bkv = np.asarray(p["attn_b_kv"], f32)
    pvh[:, PV["bk"]:PV["bk"] + 2] = fm(bkv[:256])
    pvh[:, PV["bo"]:PV["bo"] + 8] = fm(p["attn_b_o"][0])
    pvh[:, PV["one"]] = 1.0
    pvh[:, PV["eps"]] = EPS
    out["pv"] = pvh
    pr = np.zeros((128, NPR), f32)
    pr[:, PR["dtb"]:PR["dtb"] + 32] = np.asarray(p["ssm_dt_bias"][0])[None, :]
    pr[:, PR["alog"]:PR["alog"] + 32] = np.asarray(p["ssm_a_log"][0])[None, :]
    pr[:, PR["bv"]:PR["bv"] + 256] = bkv[None, 256:]
    pr[:, PR["bkr"]:PR["bkr"] + 256] = bkv[None, :256]
    sinks = np.asarray(p["attn_sinks"][0], f32)
    pr[:, PR["snk"]:PR["snk"] + 16] = np.array([sinks[QH[c][e]] for c in range(8) for e in range(2)], f32)[None, :]
    out["prow"] = pr
    cfa, cb = host_consts()
    for g in range(4):
        for r in range(16):
            cfa[r, CF["ss"] + g] = sinks[QH[4 * (g // 2) + r // 4][g % 2]]
    out["cf"], out["cbf"] = cfa, cb
    return out


_PROG = {}


def run(inputs, seq, npre, nmain):
    f32 = np.float32
    key = (npre, nmain)
    if key not in _PROG:
        _PROG[key] = build_program(npre, nmain)
    nc, nops = _PROG[key]
    w = host_weights(inputs)
    xp = np.asarray(inputs["x_prompt"], f32)
    xs = np.asarray(inputs["x_sample"], f32)
    sconv = np.asarray(inputs["state_conv"], f32)[0]
    sssm = np.asarray(inputs["state_ssm"], f32)[0]
    ck = np.asarray(inputs["cache_k_win"], f32)
    cv = np.asarray(inputs["cache_v_win"], f32)
    half = seq // 2
    in_maps = []
    for c in range(8):
        s, hf = c // 2, c % 2
        m = dict(w)
        if hf == 0:
            m["xpre"] = np.zeros((D, npre * 128), f32)
        else:
            m["xpre"] = np.ascontiguousarray(xp[s, 0:half].T)
        m["xmain"] = np.ascontiguousarray(xp[s, hf * half:(hf + 1) * half].T)
        m["cmask"] = np.full((128, 1), float(hf), f32)
        b0 = 16 * c
        m["xsmp"] = np.ascontiguousarray(xs[b0:b0 + 16].reshape(64, D).T)
        m["sconv"] = np.ascontiguousarray(sconv[b0:b0 + 16].reshape(16, 3, 24, 128).transpose(3, 2, 0, 1))
        m["sssm"] = np.ascontiguousarray(sssm[b0:b0 + 16].reshape(16, DIN, 128).transpose(0, 2, 1))
        kk = ck[b0:b0 + 16].reshape(16, 128, 2, 2, 64)
        m["ckT"] = np.ascontiguousarray(kk.transpose(0, 3, 4, 2, 1)).reshape(16, 128, 256)
        m["ck"] = np.ascontiguousarray(ck[b0:b0 + 16].reshape(16, 128, 256))
        m["cv"] = np.ascontiguousarray(cv[b0:b0 + 16].reshape(16, 128, 256))
        in_maps.append(m)
    if os.environ.get("KTRACE"):
        res = run_bass_kernel_spmd(nc, in_maps, core_ids=list(range(8)), trace=True)
        print("EXEC_TIME_NS", res.exec_time_ns)
    else:
        res = run_bass_kernel_spmd(nc, in_maps, core_ids=list(range(8)))
    R = res.results
    B = xp.shape[0]
    y_p = np.zeros((B, seq, D), f32)
    conv_p = np.zeros((1, B, 3, 3072), f32)
    ssm_p = np.zeros((1, B, 32, 64, 128), f32)
    k_p = np.zeros((B, 128, 4, 64), f32)
    v_p = np.zeros((B, 128, 4, 64), f32)
    y_s = np.zeros((128, 4, D), f32)
    conv_s = np.zeros((1, 128, 3, 3072), f32)
    ssm_s = np.zeros((1, 128, 32, 64, 128), f32)
    k_s = np.zeros((128, 128, 4, 64), f32)
    v_s = np.zeros((128, 128, 4, 64), f32)
    for c in range(8):
        s, hf = c // 2, c % 2
        r = R[c]
        y_p[s, hf * half:(hf + 1) * half] = r["yT"].T
        b0 = 16 * c
        y_s[b0:b0 + 16] = r["ysT"].T.reshape(16, 4, D)
        conv_s[0, b0:b0 + 16] = r["convs"].transpose(2, 3, 1, 0).reshape(16, 3, 3072)
        ssm_s[0, b0:b0 + 16] = r["ssms"].transpose(0, 2, 1).reshape(16, 32, 64, 128)
        k_s[b0:b0 + 16] = r["kws"].reshape(16, 128, 4, 64)
        v_s[b0:b0 + 16] = r["vws"].reshape(16, 128, 4, 64)
        if hf == 1:
            conv_p[0, s] = r["convp"].transpose(2, 1, 0).reshape(3, 3072)
            ssm_p[0, s] = r["ssmp"].T.reshape(32, 64, 128)
            k_p[s] = r["kwp"].reshape(128, 4, 64)
            v_p[s] = r["vwp"].reshape(128, 4, 64)
    return (y_p, y_s, conv_p, ssm_p, k_p, v_p, conv_s, ssm_s, k_s, v_s)


def kernel(**inputs):
    seq = int(np.asarray(inputs["x_prompt"]).shape[1])
    nchunks = seq // 128
    return run(inputs, seq, nchunks // 2, nchunks // 2)
```

```python
import os
import numpy as np
import ml_dtypes
from contextlib import ExitStack
import concourse.bass as bass
import concourse.mybir as mybir
from concourse.bass_utils import run_bass_kernel_spmd

F32, BF16 = mybir.dt.float32, mybir.dt.bfloat16
AF = mybir.ActivationFunctionType
ALU = mybir.AluOpType
AX = mybir.AxisListType

D = 1024
DFF = 2816
NF = 22
DIN = 2048
EPS = 1e-6
NEG = -30000.0
WSLOT = 4096
NWS = 3

QH = [[c, c + 4] if c < 4 else [c + 4, c + 8] for c in range(8)]


class Op:
    __slots__ = ("eng", "fn", "deps", "dma", "signal", "value")

    def __init__(self, eng, fn, deps, dma):
        self.eng, self.fn, self.deps, self.dma = eng, fn, deps, dma
        self.signal = False
        self.value = 0


class Sched:
    def __init__(self):
        self.ops = []
        self.lw = {}
        self.rd = {}

    def op(self, eng, fn, reads=(), writes=(), dma=None):
        idx = len(self.ops)
        deps = {}
        for k in reads:
            w = self.lw.get(k)
            if w is not None:
                deps[w] = 2
        for k in writes:
            w = self.lw.get(k)
            if w is not None:
                deps.setdefault(w, 1)
            r = self.rd.get(k)
            if r:
                for e_, i_ in r[0].items():
                    deps.setdefault(i_, 1)
                for i_ in r[1]:
                    deps.setdefault(i_, 1)
        for k in writes:
            self.lw[k] = idx
            self.rd[k] = [{}, []]
        for k in reads:
            r = self.rd.setdefault(k, [{}, []])
            if dma is None:
                r[0][eng] = idx
            else:
                r[1].append(idx)
        self.ops.append(Op(eng, fn, deps, dma))
        return idx

    def finalize(self):
        ops = self.ops
        for o in ops:
            need = []
            for d, kind in o.deps.items():
                p = ops[d]
                if p.dma is not None:
                    need.append(d)
                elif o.dma is None and p.eng == o.eng:
                    if o.eng != "pe" and kind == 2:
                        need.append(d)
                else:
                    need.append(d)
            o.deps = need
            for d in need:
                ops[d].signal = True
        cnt = {}
        for o in ops:
            if o.dma is not None:
                s, n = o.dma
                cnt[s] = cnt.get(s, 0) + 16 * n
                o.value = cnt[s]
            elif o.signal:
                cnt[o.eng] = cnt.get(o.eng, 0) + 1
                o.value = cnt[o.eng]
        self.semnames = sorted(cnt.keys())

    def emit_engine(self, engname, e, semh):
        ops = self.ops
        waited = {}
        for o in ops:
            if o.eng != engname:
                continue
            for d in o.deps:
                p = ops[d]
                s = p.dma[0] if p.dma is not None else p.eng
                if waited.get(s, 0) < p.value:
                    e.wait_ge(semh[s], p.value)
                    waited[s] = p.value
            ins = o.fn(e)
            if o.dma is not None:
                assert len(ins) == o.dma[1], (len(ins), o.dma)
                for i_ in ins:
                    i_.then_inc(semh[o.dma[0]], 16)
            elif o.signal:
                ins[-1].then_inc(semh[engname], 1)


class Rot:
    def __init__(self, tensors, name):
        self.t = tensors
        self.n = len(tensors)
        self.i = 0
        self.name = name

    def next(self):
        j = self.i % self.n
        self.i += 1
        return self.t[j], f"{self.name}{j}"


def subtiles(T):
    out = []
    c = 0
    while c < T:
        n = min(512, T - c)
        out.append((c, c + n))
        c += n
    return out


class Tile:
    def __init__(self, kind, src, col0, nch, sample=False, first=False, last=False, out0=0):
        self.kind = kind
        self.src = src
        self.col0 = col0
        self.nch = nch
        self.Tp = nch * 128
        self.sample = sample
        self.T = self.Tp + (64 if sample else 0)
        self.first = first
        self.last = last
        self.out0 = out0
        self.subs = subtiles(self.Tp) + ([(self.Tp, self.Tp + 64)] if sample else [])


def make_tiles(npre, nmain, tch=4):
    def split(n):
        k = -(-n // tch)
        base, rem = divmod(n, k)
        return [base + 1] * rem + [base] * (k - rem)
    tiles = []
    sp = split(npre)
    c = 0
    for i, n in enumerate(sp):
        tiles.append(Tile("prefull" if i == len(sp) - 1 else "pre", "xpre", c * 128, n))
        c += n
    sp = split(nmain)
    c = 0
    for i, n in enumerate(sp):
        tiles.append(Tile("main", "xmain", c * 128, n, sample=(i == len(sp) - 1), first=(i == 0),
                          last=(i == len(sp) - 1), out0=c * 128))
        c += n
    return tiles


PV = {}
_o = 0
for _n, _w in [("g", 48), ("kvn", 8), ("fin", 8), ("cw", 96), ("cb", 24), ("dsk", 16), ("ng", 16),
               ("bq", 8), ("bk", 2), ("bo", 8), ("one", 1), ("eps", 1)]:
    PV[_n] = _o
    _o += _w
NPV = _o
PR = {"dtb": 0, "alog": 32, "bv": 64, "bkr": 320, "snk": 576}
NPR = 592
CF = {"um": 0, "ums": 128, "mb": 192, "ms": 448, "mn": 576, "sm": 700, "ss": 716}
NCF = 720
CB = {"id": 0, "on": 128, "um": 256, "us": 384, "ums": 512, "uss": 576, "sel": 640}
NCB = 640 + 4096


def build_program(npre, nmain):
    tiles = make_tiles(npre, nmain)
    TMAX = max(t.T for t in tiles)
    nc = bass.Bass("TRN2", target_bir_lowering=False)
    S = Sched()

    def din(name, shape, dt=F32):
        return nc.dram_tensor(name, list(shape), dt, kind="ExternalInput").ap()

    def dout(name, shape):
        return nc.dram_tensor(name, list(shape), F32, kind="ExternalOutput").ap()

    NOUT = nmain * 128
    dr = {}
    dr["xpre"] = din("xpre", [D, npre * 128]).rearrange("(c p) t -> p c t", p=128)
    dr["xmain"] = din("xmain", [D, nmain * 128]).rearrange("(c p) t -> p c t", p=128)
    xsmp_d = din("xsmp", [D, 64]).rearrange("(c p) t -> p c t", p=128)
    wd = {
        "wgu": din("wgu", [4, 11, 128, WSLOT]),
        "wdn": din("wdn", [4, 8, 128, NF * 128]),
        "winz": din("winz", [4, 128, WSLOT]),
        "winx": din("winx", [6, 128, WSLOT]),
        "wout": din("wout", [4, 128, WSLOT]),
        "wkv": din("wkv", [1, 128, WSLOT]),
        "wq": din("wq", [2, 128, WSLOT]),
        "wo": din("wo", [2, 128, WSLOT]),
    }
    wdt_d = din("wdt", [128, 256])
    pv_d = din("pv", [128, NPV])
    prow_d = din("prow", [128, NPR])
    cf_d = din("cf", [128, NCF])
    cbf_d = din("cbf", [128, NCB], BF16)
    cmask_d = din("cmask", [128, 1])
    sconv_d = din("sconv", [128, 24, 16, 3])
    sssm_d = din("sssm", [16, 128, DIN])
    ckT_d = din("ckT", [16, 128, 256])
    ck_d = din("ck", [16, 128, 256])
    cv_d = din("cv", [16, 128, 256])
    yT_d = dout("yT", [D, NOUT]).rearrange("(c p) t -> p c t", p=128)
    ysT_d = dout("ysT", [D, 64]).rearrange("(c p) t -> p c t", p=128)
    convp_d = dout("convp", [128, 24, 3])
    ssmp_d = dout("ssmp", [128, DIN])
    kwp_d = dout("kwp", [128, 256])
    vwp_d = dout("vwp", [128, 256])
    convs_d = dout("convs", [128, 24, 16, 3])
    ssms_d = dout("ssms", [16, 128, DIN])
    kws_d = dout("kws", [16, 128, 256])
    vws_d = dout("vws", [16, 128, 256])

    es = ExitStack()

    def sb(name, shape, dt):
        return es.enter_context(nc.sbuf_tensor(name, list(shape), dt))

    xT = sb("xT", [128, 8, TMAX], F32)
    hn = sb("hn", [128, 8, TMAX], BF16)
    act = sb("act", [128, 40, TMAX], BF16)
    wsl = [sb(f"wsl{i}", [128, WSLOT], BF16) for i in range(NWS)]
    xsT = act[:, 16:40, :]
    XTM = Rot([sb(f"xtm{i}", [128, 2560], BF16) for i in range(2)], "xtm")
    hT = sb("hT", [128, DIN], F32)
    hTb = sb("hTb", [128, DIN], BF16)
    Ce = sb("Ce", [128, 32, 64], BF16)
    CEP = Rot([sb(f"cep{i}", [128, 2, 128], BF16) for i in range(2)], "cep")
    Gm = sb("Gm", [128, 4, 128], F32)
    SEG = Rot([sb(f"seg{i}", [128, 2, 128], F32) for i in range(2)], "seg")
    MTB = Rot([sb(f"mtb{i}", [128, 2, 128], BF16) for i in range(2)], "mtb")
    EBC = Rot([sb(f"ebc{i}", [128, 2, 128], F32) for i in range(2)], "ebc")
    edec = sb("edec", [128, 32, 16], F32)
    Y1 = Rot([sb(f"y1{i}", [128, 128], F32) for i in range(2)], "y1")
    YG = Rot([sb(f"yg{i}", [128, 4, 128], F32) for i in range(1)], "yg")
    SQ = Rot([sb(f"sq{i}", [128, 2, 512], BF16) for i in range(2)], "sq")
    RS = Rot([sb(f"rs{i}", [128, 512], F32) for i in range(1)], "rs")
    SG = Rot([sb(f"sg{i}", [128, 512], F32) for i in range(2)], "sg")
    XIN = Rot([sb(f"xin{i}", [128, 3 + TMAX], F32) for i in range(2)], "xin")
    XINS = Rot([sb(f"xins{i}", [128, 16, 7], F32) for i in range(2)], "xins")
    CACC = Rot([sb(f"cacc{i}", [128, TMAX], F32) for i in range(1)], "cacc")
    hist = sb("hist", [128, 24, 3], F32)
    sconv = sb("sconv_t", [128, 24, 16, 3], F32)
    convs = sconv
    DTT = Rot([sb(f"dtt{i}", [128, 320], F32) for i in range(2)], "dtt")
    AG3 = Rot([sb(f"ag3{i}", [128, 3, 32], BF16) for i in range(2)], "ag3")
    TMPF = Rot([sb(f"tmpf{i}", [128, 2, 32], F32) for i in range(2)], "tmpf")
    ACF = Rot([sb(f"acf{i}", [32, 3, 128], F32) for i in range(1)], "acf")
    AC3 = Rot([sb(f"ac3{i}", [32, 3, 128], BF16) for i in range(2)], "ac3")
    XW = Rot([sb(f"xw{i}", [128, 512], BF16) for i in range(2)], "xw")
    xwall = sb("xwall", [64, DIN], BF16)
    BM = Rot([sb(f"bm{i}", [64, 128], BF16) for i in range(2)], "bm")
    kT = sb("kT", [128, 2, 128 + TMAX], BF16)
    vtm = sb("vtm", [128, 6, 256], BF16)
    kvf = sb("kvf", [128, 512], F32)
    SM = Rot([sb(f"sm{i}", [128, 4, 256], F32) for i in range(1)], "sm")
    PN = Rot([sb(f"pn{i}", [128, 4, 256], BF16) for i in range(1)], "pn")
    PT = Rot([sb(f"pt{i}", [128, 4, 256], BF16) for i in range(1)], "pt")
    STAT = Rot([sb(f"stat{i}", [128, 4, 8], F32) for i in range(4)], "stat")
    qs = sb("qs", [128, 16, 8, 4], BF16)
    CKT = Rot([sb(f"ckt{i}", [128, 256], BF16) for i in range(2)], "ckt")
    CV = Rot([sb(f"cvt{i}", [128, 256], BF16) for i in range(2)], "cvt")
    wdt = sb("wdt_t", [128, 8, 32], BF16)
    pv = sb("pv_t", [128, NPV], F32)
    prow = sb("prow_t", [128, NPR], F32)
    abc = sb("abc", [128, 32], F32)
    cf = sb("cf_t", [128, NCF], F32)
    cbf = sb("cbf_t", [128, NCB], BF16)
    cmask = sb("cmask_t", [128, 1], F32)
    hbias = sb("hbias", [128, 1], F32)
    ps = [es.enter_context(nc.psum_tensor(f"ps{i}", [128, 512], F32)) for i in range(8)]
    pools = {"mm": [0, 1, 2, 3], "hd": [4, 5], "acc": [6, 7], "hd2": [2, 3], "pt": [0, 1]}
    pctr = {k_: 0 for k_ in pools}

    def PS(pool):
        l = pools[pool]
        i = l[pctr[pool] % len(l)]
        pctr[pool] += 1
        return ps[i], f"ps{i}"

    ident = cbf[:, 0:128]
    ones_bf = cbf[:, 128:256]
    pcol = lambda name, j=0: pv[:, PV[name] + j:PV[name] + j + 1]

    wseq = []
    wstate = {"dry": True, "i": 0, "issued": 0}

    def wdram(name, idx):
        a = wd[name]
        return a[idx[0], idx[1]] if len(idx) == 2 else a[idx[0]]

    def W_issue(upto):
        while wstate["issued"] <= min(upto, len(wseq) - 1):
            k = wstate["issued"]
            name, idx, nel = wseq[k]
            slot = wsl[k % NWS]
            src = wdram(name, idx)
            S.op("pool", (lambda e, slot=slot, src=src, nel=nel: [e.dma_start(out=slot[:, 0:nel], in_=src[:, 0:nel])]),
                 writes=[f"wsl{k % NWS}"], dma=(f"w{k % NWS}", 1))
            wstate["issued"] += 1

    def W_get(name, idx, nel):
        k = wstate["i"]
        wstate["i"] += 1
        if wstate["dry"]:
            wseq.append((name, idx, nel))
            return wsl[k % NWS], f"wsl{k % NWS}"
        assert wseq[k] == (name, idx, nel)
        W_issue(k + NWS - 1)
        return wsl[k % NWS], f"wsl{k % NWS}"

    def OP(eng, fn, reads=(), writes=(), dma=None):
        if wstate["dry"]:
            return
        S.op(eng, fn, reads, writes, dma)

    def barrier():
        if wstate["dry"] or os.environ.get("KNOBAR"):
            return
        last = {}
        for i_, o_ in enumerate(S.ops):
            if o_.dma is None and o_.eng in ("pe", "act", "dve"):
                last[o_.eng] = i_
        for eng in ("pe", "act", "dve"):
            S.op(eng, lambda e: [e.nop()])
            for en2, i_ in last.items():
                if en2 != eng:
                    S.ops[-1].deps[i_] = 2

    def mm_group(out_ap, pairs):
        def fn(e):
            r = []
            n = len(pairs)
            for i, (l, rr) in enumerate(pairs):
                r.append(e.matmul(out_ap, lhsT=l, rhs=rr, start=(i == 0), stop=(i == n - 1)))
            return r
        return fn

    def rmsnorm(tile, gbase, ndim=D):
        for (c0, c1) in tile.subs:
            n = c1 - c0
            pss, kps = PS("mm")
            for cp in range(4):
                sq, ksq = SQ.next()
                OP("act", lambda e, sq=sq, cp=cp, c0=c0, c1=c1, n=n: [
                    e.activation(out=sq[:, :, 0:n], in_=xT[:, 2 * cp:2 * cp + 2, c0:c1], func=AF.Square)],
                   reads=["xT"], writes=[ksq])
                OP("pe", lambda e, sq=sq, cp=cp, pss=pss, n=n: [
                    e.matmul(pss[:, 0:n], lhsT=ones_bf, rhs=sq[:, j, 0:n], start=(cp == 0 and j == 0),
                             stop=(cp == 3 and j == 1)) for j in range(2)],
                   reads=[ksq, "cbf"], writes=[kps])
            rs, krs = RS.next()
            OP("act", lambda e, rs=rs, pss=pss, n=n: [
                e.activation(out=rs[:, 0:n], in_=pss[:, 0:n], func=AF.Ln, bias=pcol("eps"), scale=1.0 / ndim)],
               reads=[kps, "pv"], writes=[krs])
            OP("act", lambda e, rs=rs, n=n: [
                e.activation(out=rs[:, 0:n], in_=rs[:, 0:n], func=AF.Exp, scale=-0.5)],
               reads=[krs], writes=[krs])
            OP("dve", lambda e, rs=rs, c0=c0, c1=c1, n=n: [
                e.scalar_tensor_tensor(out=hn[:, c, c0:c1], in0=xT[:, c, c0:c1], scalar=pv[:, gbase + c:gbase + c + 1],
                                       in1=rs[:, 0:n], op0=ALU.mult, op1=ALU.mult) for c in range(8)],
               reads=["xT", krs, "pv"], writes=["hn"])

    def ffn(tile, fi):
        rmsnorm(tile, PV["g"] + [0, 16, 24, 40][fi])
        for blk in range(11):
            wb, kw = W_get("wgu", (fi, blk), WSLOT)
            wv = wb[:, :].rearrange("p (a k n) -> p a k n", a=2, k=8)
            for jj in range(2):
                j = blk * 2 + jj
                for (c0, c1) in tile.subs:
                    n = c1 - c0
                    pg, kpg = PS("mm")
                    pu, kpu = PS("mm")
                    OP("pe", lambda e, wv=wv, jj=jj, c0=c0, c1=c1, n=n, pg=pg, pu=pu: (
                        mm_group(pg[:, 0:n], [(wv[:, 0, k, jj * 128:(jj + 1) * 128], hn[:, k, c0:c1]) for k in range(8)])(e)
                        + mm_group(pu[:, 0:n], [(wv[:, 1, k, jj * 128:(jj + 1) * 128], hn[:, k, c0:c1]) for k in range(8)])(e)),
                       reads=["hn", kw], writes=[kpg, kpu])
                    sg, ksg = SG.next()
                    OP("act", lambda e, sg=sg, pg=pg, n=n: [e.activation(out=sg[:, 0:n], in_=pg[:, 0:n], func=AF.Silu)],
                       reads=[kpg], writes=[ksg])
                    OP("dve", lambda e, sg=sg, pu=pu, j=j, c0=c0, c1=c1, n=n: [
                        e.tensor_tensor(out=act[:, j, c0:c1], in0=sg[:, 0:n], in1=pu[:, 0:n], op=ALU.mult)],
                       reads=[ksg, kpu], writes=["act"])
        for blk in range(8):
            wb, kw = W_get("wdn", (fi, blk), NF * 128)
            wv = wb[:, 0:NF * 128].rearrange("p (j n) -> p j n", j=NF)
            for (c0, c1) in tile.subs:
                n = c1 - c0
                pd, kpd = PS("mm")
                OP("pe", mm_group(pd[:, 0:n], [(wv[:, j, :], act[:, j, c0:c1]) for j in range(NF)]),
                   reads=["act", kw], writes=[kpd])
                OP("dve", lambda e, pd=pd, blk=blk, c0=c0, c1=c1, n=n: [
                    e.scalar_tensor_tensor(out=xT[:, blk, c0:c1], in0=pd[:, 0:n], scalar=0.5, in1=xT[:, blk, c0:c1],
                                           op0=ALU.mult, op1=ALU.add)],
                   reads=[kpd, "xT"], writes=["xT"])

    def load_x(tile):
        src = dr[tile.src]
        OP("sp", lambda e: [e.dma_start(out=xT[:, :, 0:tile.Tp], in_=src[:, :, tile.col0:tile.col0 + tile.Tp])],
           writes=["xT"], dma=("xld", 1))
        if tile.sample:
            OP("sp", lambda e: [e.dma_start(out=xT[:, :, tile.Tp:tile.T], in_=xsmp_d[:, :, :])],
               writes=["xT"], dma=("xld", 1))

    def conv_chunk(tile, m, pre_only):
        pass

    def mixer_in(tile, prefix):
        rmsnorm(tile, PV["g"] + 8)
        Tp = tile.Tp
        if not prefix:
            for blk in range(4):
                wb, kw = W_get("winz", (blk,), WSLOT)
                wv = wb[:, :].rearrange("p (k n) -> p k n", k=8)
                for jj in range(4):
                    m = blk * 4 + jj
                    for (c0, c1) in tile.subs:
                        n = c1 - c0
                        pz, kpz = PS("mm")
                        OP("pe", mm_group(pz[:, 0:n], [(wv[:, k, jj * 128:(jj + 1) * 128], hn[:, k, c0:c1]) for k in range(8)]),
                           reads=["hn", kw], writes=[kpz])
                        OP("act", lambda e, pz=pz, m=m, c0=c0, c1=c1, n=n: [
                            e.activation(out=act[:, m, c0:c1], in_=pz[:, 0:n], func=AF.Silu)],
                           reads=[kpz], writes=["sz"])
        nblk = 6
        for blk in range(nblk):
            wb, kw = W_get("winx", (blk,), WSLOT)
            wv = wb[:, :].rearrange("p (k n) -> p k n", k=8)
            for jj in range(4):
                m = blk * 4 + jj
                xin, kxin = XIN.next()
                xins, kxins = XINS.next()
                OP("dve", lambda e, xin=xin, m=m: [e.tensor_copy(out=xin[:, 0:3], in_=hist[:, m, :])],
                   reads=["hist"], writes=[kxin])
                for (c0, c1) in tile.subs:
                    n = c1 - c0
                    px, kpx = PS("mm")
                    OP("pe", mm_group(px[:, 0:n], [(wv[:, k, jj * 128:(jj + 1) * 128], hn[:, k, c0:c1]) for k in range(8)]),
                       reads=["hn", kw], writes=[kpx])
                    if c0 < Tp:
                        OP("act", lambda e, px=px, xin=xin, c0=c0, c1=c1, n=n: [
                            e.activation(out=xin[:, 3 + c0:3 + c1], in_=px[:, 0:n], func=AF.Copy)],
                           reads=[kpx], writes=[kxin])
                    else:
                        OP("act", lambda e, px=px, xins=xins: [
                            e.activation(out=xins[:, :, 3:7], in_=px[:, 0:64].rearrange("p (b t) -> p b t", t=4), func=AF.Copy)],
                           reads=[kpx], writes=[kxins])
                cacc, kc = CACC.next()
                wc = lambda k, m=m: pv[:, PV["cw"] + k * 24 + m:PV["cw"] + k * 24 + m + 1]
                OP("dve", lambda e, xin=xin, cacc=cacc, m=m, wc=wc: [
                    e.tensor_scalar(out=cacc[:, 0:Tp], in0=xin[:, 3:3 + Tp], scalar1=wc(3), scalar2=pcol("cb", m),
                                    op0=ALU.mult, op1=ALU.add)],
                   reads=[kxin, "pv"], writes=[kc])
                for k in range(3):
                    OP("dve", lambda e, xin=xin, cacc=cacc, k=k, wc=wc: [
                        e.scalar_tensor_tensor(out=cacc[:, 0:Tp], in0=xin[:, k:k + Tp], scalar=wc(k), in1=cacc[:, 0:Tp],
                                               op0=ALU.mult, op1=ALU.add)],
                       reads=[kxin, kc, "pv"], writes=[kc])
                OP("act", lambda e, cacc=cacc, m=m: [e.activation(out=xsT[:, m, 0:Tp], in_=cacc[:, 0:Tp], func=AF.Silu)],
                   reads=[kc], writes=["xsT"])
                OP("dve", lambda e, xin=xin, m=m: [e.tensor_copy(out=hist[:, m, :], in_=xin[:, Tp:Tp + 3])],
                   reads=[kxin], writes=["hist"])
                if tile.sample:
                    OP("dve", lambda e, xins=xins, m=m: [e.tensor_copy(out=xins[:, :, 0:3], in_=sconv[:, m, :, :])],
                       reads=["sconv"], writes=[kxins])
                    cacc, kc = CACC.next()
                    cv3 = lambda cacc=cacc: cacc[:, 0:64].rearrange("p (b t) -> p b t", t=4)
                    OP("dve", lambda e, xins=xins, cv3=cv3, m=m, wc=wc: [
                        e.tensor_scalar(out=cv3(), in0=xins[:, :, 3:7], scalar1=wc(3), scalar2=pcol("cb", m),
                                        op0=ALU.mult, op1=ALU.add)],
                       reads=[kxins, "pv"], writes=[kc])
                    for k in range(3):
                        OP("dve", lambda e, xins=xins, cv3=cv3, k=k, wc=wc: [
                            e.scalar_tensor_tensor(out=cv3(), in0=xins[:, :, k:k + 4], scalar=wc(k), in1=cv3(),
                                                   op0=ALU.mult, op1=ALU.add)],
                           reads=[kxins, kc, "pv"], writes=[kc])
                    OP("act", lambda e, cacc=cacc, m=m: [e.activation(out=xsT[:, m, Tp:Tp + 64], in_=cacc[:, 0:64], func=AF.Silu)],
                       reads=[kc], writes=["xsT"])
                    OP("dve", lambda e, xins=xins, m=m: [e.tensor_copy(out=convs[:, m, :, :], in_=xins[:, :, 4:7])],
                       reads=[kxins], writes=["sconv"])

    def ssd_chunk(tile, ci, sample, prefix):
        L = 64 if sample else 128
        col0 = tile.Tp if sample else ci * 128
        c1 = col0 + L
        um = cf[0:L, CF["ums"]:CF["ums"] + L] if sample else cf[0:L, CF["um"]:CF["um"] + L]
        xtm, kx = XTM.next()
        for grp in range(3):
            ms = list(range(grp * 8, min(grp * 8 + 8, 20)))
            pb, kpb = PS("mm")
            pbv = pb[:, :].bitcast(BF16)
            OP("pe", lambda e, ms=ms, pbv=pbv: [
                e.transpose(pbv[0:L, j * 128:(j + 1) * 128], xsT[:, m, col0:c1], ident) for j, m in enumerate(ms)],
               reads=["xsT", "cbf"], writes=[kpb])
            eng = "act" if grp % 2 == 0 else "dve"
            w = len(ms) * 128
            if eng == "act":
                OP("act", lambda e, pbv=pbv, grp=grp, w=w: [
                    e.activation(out=xtm[0:L, grp * 1024:grp * 1024 + w], in_=pbv[0:L, 0:w], func=AF.Copy)],
                   reads=[kpb], writes=[kx])
            else:
                OP("dve", lambda e, pbv=pbv, grp=grp, w=w: [
                    e.tensor_copy(out=xtm[0:L, grp * 1024:grp * 1024 + w], in_=pbv[0:L, 0:w])],
                   reads=[kpb], writes=[kx])
        dtt, kd = DTT.next()
        pdt, kpd = PS("mm")
        OP("pe", mm_group(pdt[0:L, 0:32], [(hn[:, k, col0:c1], wdt[:, k, :]) for k in range(8)]),
           reads=["hn", "wdt"], writes=[kpd])
        XB, AXc, EX, LN1, DT, AG, WW, NAC = [slice(32 * i, 32 * i + 32) for i in range(8)]
        OP("dve", lambda e: [e.tensor_tensor(out=dtt[0:L, XB], in0=pdt[0:L, 0:32], in1=prow[0:L, PR["dtb"]:PR["dtb"] + 32], op=ALU.add)],
           reads=[kpd, "prow"], writes=[kd])
        OP("act", lambda e: [e.activation(out=dtt[0:L, AXc], in_=dtt[0:L, XB], func=AF.Abs)],
           reads=[kd], writes=[kd])
        OP("act", lambda e: [e.activation(out=dtt[0:L, EX], in_=dtt[0:L, AXc], func=AF.Exp, scale=-1.0)],
           reads=[kd], writes=[kd])
        OP("act", lambda e: [e.activation(out=dtt[0:L, LN1], in_=dtt[0:L, EX], func=AF.Ln, bias=pv[0:L, PV["one"]:PV["one"] + 1])],
           reads=[kd, "pv"], writes=[kd])
        OP("dve", lambda e: [e.scalar_tensor_tensor(out=dtt[0:L, DT], in0=dtt[0:L, XB], scalar=0.0, in1=dtt[0:L, LN1],
                                                    op0=ALU.max, op1=ALU.add)],
           reads=[kd], writes=[kd])
        OP("dve", lambda e: [e.tensor_tensor(out=dtt[0:L, AG], in0=dtt[0:L, DT], in1=abc[0:L, :], op=ALU.mult)],
           reads=[kd, "abc"], writes=[kd])
        ag3, kag = AG3.next()
        tmpf, ktf = TMPF.next()

        def split3(src, dst3, tmp, rk, wk, tk, P, n):
            OP("dve", lambda e: [e.tensor_copy(out=dst3[0:P, 0, 0:n], in_=src)], reads=[rk], writes=[wk])
            OP("dve", lambda e: [e.tensor_tensor(out=tmp[0:P, 0, 0:n], in0=src, in1=dst3[0:P, 0, 0:n], op=ALU.subtract)],
               reads=[rk, wk], writes=[tk])
            OP("dve", lambda e: [e.tensor_copy(out=dst3[0:P, 1, 0:n], in_=tmp[0:P, 0, 0:n])], reads=[tk], writes=[wk])
            OP("dve", lambda e: [e.tensor_tensor(out=tmp[0:P, 1, 0:n], in0=tmp[0:P, 0, 0:n], in1=dst3[0:P, 1, 0:n], op=ALU.subtract)],
               reads=[tk, wk], writes=[tk])
            OP("dve", lambda e: [e.tensor_copy(out=dst3[0:P, 2, 0:n], in_=tmp[0:P, 1, 0:n])], reads=[tk], writes=[wk])
        split3(dtt[0:L, AG], ag3, tmpf, kd, kag, ktf, L, 32)
        umb = cbf[0:L, CB["ums"]:CB["ums"] + L] if sample else cbf[0:L, CB["um"]:CB["um"] + L]
        usb = cbf[0:L, CB["uss"]:CB["uss"] + L] if sample else cbf[0:L, CB["us"]:CB["us"] + L]
        yield
        pc, kpc = PS("mm")
        def cums(e):
            r = []
            for i in range(3):
                r.append(e.matmul(pc[0:L, 0:32], lhsT=usb, rhs=ag3[0:L, i, :], start=(i == 0), stop=(i == 2)))
            for i in range(3):
                r.append(e.matmul(pc[0:L, 32:64], lhsT=umb, rhs=ag3[0:L, i, :], start=(i == 0), stop=(i == 2)))
            for i in range(3):
                r.append(e.matmul(pc[0:32, 64:64 + L], lhsT=ag3[0:L, i, :], rhs=umb, start=(i == 0), stop=(i == 2)))
            if not sample:
                for i in range(3):
                    r.append(e.matmul(pc[:, 192:224], lhsT=ones_bf[0:L, :], rhs=ag3[0:L, i, :], start=(i == 0), stop=(i == 2)))
            return r
        OP("pe", cums, reads=[kag, "cbf"], writes=[kpc])
        OP("act", lambda e: [e.activation(out=dtt[0:L, WW], in_=pc[0:L, 0:32], func=AF.Exp)], reads=[kpc], writes=[kd])
        OP("dve", lambda e: [e.tensor_tensor(out=dtt[0:L, WW], in0=dtt[0:L, WW], in1=dtt[0:L, DT], op=ALU.mult)],
           reads=[kd], writes=[kd])
        OP("dve", lambda e: [e.tensor_scalar(out=dtt[0:L, NAC], in0=pc[0:L, 32:64], scalar1=-1.0, scalar2=None, op0=ALU.mult)],
           reads=[kpc], writes=[kd])
        ETOT = slice(256, 288)
        if not sample:
            OP("act", lambda e: [e.activation(out=dtt[:, ETOT], in_=pc[:, 192:224], func=AF.Exp)], reads=[kpc], writes=[kd])
        if not prefix:
            acf, kacf = ACF.next()
            ac3, kac3 = AC3.next()
            OP("act", lambda e: [e.activation(out=acf[0:32, 0, 0:L], in_=pc[0:32, 64:64 + L], func=AF.Copy)], reads=[kpc], writes=[kacf])
            split3(acf[0:32, 0, 0:L], ac3, acf[:, 1:3, :], kacf, kac3, kacf + "t", 32, L)
            for g in range(4):
                pg, kpg = PS("mm")
                OP("pe", lambda e, pg=pg, g=g: [e.matmul(pg[0:L, 0:L], lhsT=xsT[:, 16 + g, col0:c1], rhs=xsT[:, 20 + g, col0:c1],
                                                         start=True, stop=True)],
                   reads=["xsT"], writes=[kpg])
                OP("dve", lambda e, pg=pg, g=g: [e.tensor_tensor(out=Gm[0:L, g, 0:L], in0=pg[0:L, 0:L], in1=um, op=ALU.mult)],
                   reads=[kpg, "cf"], writes=["Gm"])
        yield
        if not prefix:
            ypss = []
            if sample:
                ypss = [PS("acc"), PS("acc")]
            def bc_issue(hb, mode):
                heads = [2 * hb, 2 * hb + 1]
                pbc, kpbc = PS("mm" if mode == "B" else "hd")
                OP("pe", lambda e, pbc=pbc, heads=heads: [
                    e.matmul(pbc[:, j * L:(j + 1) * L], lhsT=cbf[0:32, CB["sel"] + h * 128:CB["sel"] + (h + 1) * 128],
                             rhs=ac3[0:32, i, 0:L], start=(i == 0), stop=(i == 2)) for j, h in enumerate(heads) for i in range(3)],
                   reads=[kac3, "cbf"], writes=[kpbc])
                return pbc, kpbc

            def head_pair(hb, mode, yint=None, pre=None):
                g = hb // 4
                heads = [2 * hb, 2 * hb + 1]
                pbc, kpbc = pre if pre is not None else bc_issue(hb, mode)
                if mode != "A":
                    seg, ksg = SEG.next()
                    OP("dve", lambda e, pbc=pbc, heads=heads, seg=seg: [
                        e.tensor_scalar(out=seg[0:L, j, 0:L], in0=pbc[0:L, j * L:(j + 1) * L], scalar1=dtt[0:L, 224 + h:225 + h],
                                        scalar2=0.0, op0=ALU.add, op1=ALU.min) for j, h in enumerate(heads)],
                       reads=[kpbc, kd], writes=[ksg])
                    OP("act", lambda e, seg=seg: [e.activation(out=seg[0:L, :, 0:L], in_=seg[0:L, :, 0:L], func=AF.Exp)],
                       reads=[ksg], writes=[ksg])
                    mtb, kmt = MTB.next()
                    OP("dve", lambda e, seg=seg, mtb=mtb, heads=heads, g=g: [
                        e.scalar_tensor_tensor(out=mtb[0:L, j, 0:L], in0=seg[0:L, j, 0:L], scalar=dtt[0:L, 128 + h:129 + h],
                                               in1=Gm[0:L, g, 0:L], op0=ALU.mult, op1=ALU.mult) for j, h in enumerate(heads)],
                       reads=[ksg, kd, "Gm"], writes=[kmt])
                if mode == "B":
                    yp, kyp = yint[hb // 8]
                    ypv = yp[:, :].rearrange("p (c t) -> p c t", t=64)
                    OP("pe", lambda e, mtb=mtb, heads=heads, ypv=ypv, hb=hb: [
                        e.matmul(ypv[(h % 2) * 64:(h % 2) * 64 + 64, hb % 8, :], lhsT=xtm[0:L, h * 64:(h + 1) * 64],
                                 rhs=mtb[0:L, j, 0:L], start=True, stop=True) for j, h in enumerate(heads)],
                       reads=[kx, kmt], writes=[kyp])
                    return
                ebc, keb = EBC.next()
                OP("act", lambda e, pbc=pbc, ebc=ebc: [
                    e.activation(out=ebc[:, :, 0:L], in_=pbc[:, 0:2 * L].rearrange("p (j t) -> p j t", j=2), func=AF.Exp)],
                   reads=[kpbc], writes=[keb])
                if mode == "A":
                    cet, kce, ceo = Ce, "Ce", 2 * hb
                else:
                    cet, kce = CEP.next()
                    ceo = 0
                OP("dve", lambda e, ebc=ebc, hb=hb, g=g, cet=cet, ceo=ceo: [
                    e.tensor_tensor(out=cet[:, ceo:ceo + 2, 0:L], in0=ebc[:, :, 0:L],
                                    in1=xsT[:, 20 + g, col0:c1].unsqueeze(1).to_broadcast([128, 2, L]), op=ALU.mult)],
                   reads=[keb, "xsT"], writes=[kce])
                if mode == "A":
                    OP("dve", lambda e, ebc=ebc, hb=hb: [
                        e.tensor_copy(out=edec[:, 2 * hb:2 * hb + 2, :],
                                      in_=ebc[:, :, 0:64].rearrange("p j (b t) -> p j b t", t=4)[:, :, :, 3])],
                       reads=[keb], writes=["edec"])
                    return
                yp, kyp = PS("acc")
                def yfn(e, mtb=mtb, heads=heads, yp=yp, cet=cet):
                    r = []
                    for j, h in enumerate(heads):
                        o = yp[j * 64:j * 64 + 64, 0:L]
                        r.append(e.matmul(o, lhsT=xtm[0:L, h * 64:(h + 1) * 64], rhs=mtb[0:L, j, 0:L], start=True, stop=False))
                        r.append(e.matmul(o, lhsT=hTb[:, h * 64:(h + 1) * 64], rhs=cet[:, j, 0:L], start=False, stop=True))
                    return r
                OP("pe", yfn, reads=[kx, kmt, "hTb", kce], writes=[kyp])
                post_y(tile, col0, L, [(hb, yp[:, 0:L], kyp, None, None)])

            hmode = "A" if sample else "all"
            pre = bc_issue(0, hmode)
            for hb in range(16):
                nxt = bc_issue(hb + 1, hmode) if hb + 1 < 16 else None
                head_pair(hb, hmode, None, pre)
                pre = nxt
        yield
        if sample:
            OP("dve", lambda e: [
                e.tensor_tensor(out=xwall[0:64, :].rearrange("p (h d) -> p h d", d=64),
                                in0=xtm[0:64, 0:DIN].rearrange("p (h d) -> p h d", d=64),
                                in1=dtt[0:64, WW].unsqueeze(2).to_broadcast([64, 32, 64]), op=ALU.mult)],
               reads=[kx, kd], writes=["xwall"])
            for b in range(16):
                OP("sp", lambda e, b=b: [e.dma_start(out=hT[:, :], in_=sssm_d[b])], writes=["hT"], dma=("hld", 1))
                OP("act", lambda e: [e.activation(out=hTb[:, :], in_=hT[:, :], func=AF.Copy)], reads=["hT"], writes=["hTb"])
                for half in range(2):
                    yp, kyp = ypss[half]
                    ypv = yp[:, :].rearrange("p (c t) -> p c t", t=64)
                    OP("pe", lambda e, b=b, half=half, ypv=ypv: [
                        e.matmul(ypv[(h % 2) * 64:(h % 2) * 64 + 64, (h // 2) % 8, 4 * b:4 * b + 4],
                                 lhsT=hTb[:, h * 64:(h + 1) * 64], rhs=Ce[:, h, 4 * b:4 * b + 4], start=True, stop=True)
                        for h in range(16 * half, 16 * half + 16)],
                       reads=["hTb", "Ce"], writes=[kyp])
                for g in range(4):
                    bm, kbm = BM.next()
                    OP("dve", lambda e, bm=bm, g=g, b=b: [
                        e.tensor_scalar(out=bm[:, :], in0=xtm[0:64, DIN + g * 128:DIN + (g + 1) * 128],
                                        scalar1=cf[0:64, CF["sm"] + b:CF["sm"] + b + 1], scalar2=None, op0=ALU.mult)],
                       reads=[kx, "cf"], writes=[kbm])
                    pst, kps_ = PS("mm")
                    OP("pe", lambda e, bm=bm, g=g, pst=pst: [
                        e.matmul(pst[:, :], lhsT=bm[:, :], rhs=xwall[0:64, g * 512:(g + 1) * 512], start=True, stop=True)],
                       reads=[kbm, "xwall"], writes=[kps_])
                    hv = hT[:, g * 512:(g + 1) * 512].rearrange("p (h d) -> p h d", d=64)
                    OP("dve", lambda e, hv=hv, g=g, b=b: [
                        e.tensor_tensor(out=hv, in0=hv, in1=edec[:, 8 * g:8 * g + 8, b:b + 1].to_broadcast([128, 8, 64]), op=ALU.mult)],
                       reads=["hT", "edec"], writes=["hT"])
                    OP("dve", lambda e, g=g, pst=pst: [
                        e.tensor_tensor(out=hT[:, g * 512:(g + 1) * 512], in0=hT[:, g * 512:(g + 1) * 512], in1=pst[:, :], op=ALU.add)],
                       reads=["hT", kps_], writes=["hT"])
                OP("sp", lambda e, b=b: [e.dma_start(out=ssms_d[b], in_=hT[:, :])], reads=["hT"], writes=["o_ssms"], dma=("ost", 1))
            yint = [PS("hd"), PS("hd")]
            for hb in range(16):
                head_pair(hb, "B", yint)
            post_y(tile, col0, L, [(c, ypss[c // 8][0][:, (c % 8) * 64:(c % 8) * 64 + 64], ypss[c // 8][1],
                                    yint[c // 8][0][:, (c % 8) * 64:(c % 8) * 64 + 64], yint[c // 8][1]) for c in range(16)])
        else:
            for g in range(4):
                xw, kxw = XW.next()
                OP("dve", lambda e, xw=xw, g=g: [
                    e.tensor_tensor(out=xw[0:L, :].rearrange("p (h d) -> p h d", d=64),
                                    in0=xtm[0:L, g * 512:(g + 1) * 512].rearrange("p (h d) -> p h d", d=64),
                                    in1=dtt[0:L, 192 + 8 * g:192 + 8 * g + 8].unsqueeze(2).to_broadcast([L, 8, 64]), op=ALU.mult)],
                   reads=[kx, kd], writes=[kxw])
                pst, kps_ = PS("mm")
                OP("pe", lambda e, xw=xw, g=g, pst=pst: [
                    e.matmul(pst[:, :], lhsT=xtm[0:L, DIN + g * 128:DIN + (g + 1) * 128], rhs=xw[0:L, :], start=True, stop=True)],
                   reads=[kx, kxw], writes=[kps_])
                hv = hT[:, g * 512:(g + 1) * 512].rearrange("p (h d) -> p h d", d=64)
                OP("dve", lambda e, hv=hv, g=g: [
                    e.tensor_tensor(out=hv, in0=hv, in1=dtt[:, 256 + 8 * g:256 + 8 * g + 8].unsqueeze(2).to_broadcast([128, 8, 64]),
                                    op=ALU.mult)],
                   reads=["hT", kd], writes=["hT"])
                OP("dve", lambda e, g=g, pst=pst: [
                    e.tensor_tensor(out=hT[:, g * 512:(g + 1) * 512], in0=hT[:, g * 512:(g + 1) * 512], in1=pst[:, :], op=ALU.add)],
                   reads=["hT", kps_], writes=["hT"])
                OP("act", lambda e, g=g: [e.activation(out=hTb[:, g * 512:(g + 1) * 512], in_=hT[:, g * 512:(g + 1) * 512], func=AF.Copy)],
                   reads=["hT"], writes=["hTb"])

    ygstate = {}

    def post_y(tile, col0, L, items):
        c1 = col0 + L
        for (c, yap, kyp, yap2, kyp2) in items:
            y1, ky1 = Y1.next()
            OP("dve", lambda e, y1=y1, c=c, yap=yap: [
                e.scalar_tensor_tensor(out=y1[:, 0:L], in0=xsT[:, c, col0:c1], scalar=pcol("dsk", c), in1=yap,
                                       op0=ALU.mult, op1=ALU.add)],
               reads=["xsT", kyp, "pv"], writes=[ky1])
            if yap2 is not None:
                OP("dve", lambda e, y1=y1, yap2=yap2: [e.tensor_tensor(out=y1[:, 0:L], in0=y1[:, 0:L], in1=yap2, op=ALU.add)],
                   reads=[ky1, kyp2], writes=[ky1])
            if c % 4 == 0:
                ygstate["yg"] = YG.next()
                ygstate["ss"] = PS("mm")
            yg, kyg = ygstate["yg"]
            pss, kss = ygstate["ss"]
            OP("dve", lambda e, y1=y1, yg=yg, c=c: [
                e.tensor_tensor(out=yg[:, c % 4, 0:L], in0=y1[:, 0:L], in1=act[:, c, col0:c1], op=ALU.mult)],
               reads=[ky1, "sz"], writes=[kyg])
            sq, ksq = SQ.next()
            OP("act", lambda e, sq=sq, yg=yg, c=c: [e.activation(out=sq[:, 0, 0:L], in_=yg[:, c % 4, 0:L], func=AF.Square)],
               reads=[kyg], writes=[ksq])

            def pss_op(sq=sq, ksq=ksq, pss=pss, kss=kss, c=c):
                OP("pe", lambda e: [
                    e.matmul(pss[:, 0:L], lhsT=ones_bf, rhs=sq[:, 0, 0:L], start=(c % 4 == 0), stop=(c % 4 == 3))],
                   reads=[ksq, "cbf"], writes=[kss])
            if ygstate.get("pend") is not None:
                ygstate.pop("pend")()
            if c % 4 == 3:
                pss_op()
            else:
                ygstate["pend"] = pss_op
            if c % 4 == 3:
                rs, krs = RS.next()
                OP("act", lambda e, rs=rs, pss=pss: [
                    e.activation(out=rs[:, 0:L], in_=pss[:, 0:L], func=AF.Ln, bias=pcol("eps"), scale=1.0 / 512)],
                   reads=[kss, "pv"], writes=[krs])
                OP("act", lambda e, rs=rs: [e.activation(out=rs[:, 0:L], in_=rs[:, 0:L], func=AF.Exp, scale=-0.5)],
                   reads=[krs], writes=[krs])
                OP("dve", lambda e, rs=rs, yg=yg, c=c: [
                    e.scalar_tensor_tensor(out=act[:, c - 3 + k, col0:c1], in0=yg[:, k, 0:L], scalar=pcol("ng", c - 3 + k),
                                           in1=rs[:, 0:L], op0=ALU.mult, op1=ALU.mult) for k in range(4)],
                   reads=[kyg, krs, "pv"], writes=["sz"])

    def mixer_out(tile):
        for blk in range(4):
            wb, kw = W_get("wout", (blk,), WSLOT)
            wv = wb[:, :].rearrange("p (k n) -> p k n", k=16)
            for jj in range(2):
                dm = blk * 2 + jj
                for (c0, c1) in tile.subs:
                    n = c1 - c0
                    po, kpo = PS("mm")
                    OP("pe", mm_group(po[:, 0:n], [(wv[:, k, jj * 128:(jj + 1) * 128], act[:, k, c0:c1]) for k in range(16)]),
                       reads=["sz", kw], writes=[kpo])
                    OP("dve", lambda e, po=po, dm=dm, c0=c0, c1=c1, n=n: [
                        e.tensor_tensor(out=xT[:, dm, c0:c1], in0=po[:, 0:n], in1=xT[:, dm, c0:c1], op=ALU.add)],
                       reads=[kpo, "xT"], writes=["xT"])

    def mixer0(tile, prefix):
        mixer_in(tile, prefix)
        barrier()
        gens = [ssd_chunk(tile, ci, False, prefix) for ci in range(tile.nch)]
        next(gens[0])
        next(gens[0])
        for ci in range(tile.nch):
            nxt = gens[ci + 1] if ci + 1 < tile.nch else None
            if nxt is not None:
                next(nxt)
            next(gens[ci])
            if nxt is not None:
                next(nxt)
            for _ in gens[ci]:
                pass
        if tile.last:
            OP("sp", lambda e: [e.dma_start(out=ssmp_d[:, :], in_=hT[:, :])], reads=["hT"], writes=["o_ssmp"], dma=("ost", 1))
            OP("sp", lambda e: [e.dma_start(out=convp_d[:, :, :], in_=hist[:, :, :])], reads=["hist"], writes=["o_convp"], dma=("ost", 1))
        if tile.sample:
            for _ in ssd_chunk(tile, None, True, False):
                pass
            OP("sp", lambda e: [e.dma_start(out=convs_d[:, :, :, :], in_=convs[:, :, :, :])], reads=["sconv"], writes=["o_convs"],
               dma=("ost", 1))
        barrier()
        if not prefix and not (os.environ.get("KSKIP") == "out" and tile.kind == "main"):
            mixer_out(tile)

    def kv_proj(tile):
        rmsnorm(tile, PV["kvn"])
        wb, kw = W_get("wkv", (0,), WSLOT)
        wv = wb[:, :].rearrange("p (k n) -> p k n", k=8)
        for m in range(2):
            for (c0, c1) in tile.subs:
                n = c1 - c0
                pk, kpk = PS("mm")
                OP("pe", mm_group(pk[:, 0:n], [(wv[:, k, m * 128:(m + 1) * 128], hn[:, k, c0:c1]) for k in range(8)]),
                   reads=["hn", kw], writes=[kpk])
                OP("act", lambda e, pk=pk, m=m, c0=c0, c1=c1, n=n: [
                    e.activation(out=kT[:, m, 128 + c0:128 + c1], in_=pk[:, 0:n], func=AF.Identity, bias=pcol("bk", m))],
                   reads=[kpk, "pv"], writes=["kT"])
        chunks = [(ci * 128, 128, 1 + ci) for ci in range(tile.nch)] + ([(tile.Tp, 64, 5)] if tile.sample else [])
        for (c0, L, slot) in chunks:
            pvv, kpv = PS("mm")
            OP("pe", mm_group(pvv[0:L, 0:256], [(hn[:, k, c0:c0 + L], wv[:, k, 256:512]) for k in range(8)]),
               reads=["hn", kw], writes=[kpv])
            OP("dve", lambda e, pvv=pvv, L=L, slot=slot: [
                e.tensor_tensor(out=vtm[0:L, slot, :], in0=pvv[0:L, 0:256], in1=prow[0:L, PR["bv"]:PR["bv"] + 256], op=ALU.add)],
               reads=[kpv, "prow"], writes=["vtm"])
            is_lastp = tile.last and slot == tile.nch
            if is_lastp or slot == 5:
                pkk, kpkk = PS("mm")
                OP("pe", mm_group(pkk[0:L, 0:256], [(hn[:, k, c0:c0 + L], wv[:, k, 0:256]) for k in range(8)]),
                   reads=["hn", kw], writes=[kpkk])
                OP("dve", lambda e, pkk=pkk, L=L: [
                    e.tensor_tensor(out=kvf[0:L, 0:256], in0=pkk[0:L, 0:256], in1=prow[0:L, PR["bkr"]:PR["bkr"] + 256], op=ALU.add)],
                   reads=[kpkk, "prow"], writes=["kvf"])
                OP("dve", lambda e, pvv=pvv, L=L: [
                    e.tensor_tensor(out=kvf[0:L, 256:512], in0=pvv[0:L, 0:256], in1=prow[0:L, PR["bv"]:PR["bv"] + 256], op=ALU.add)],
                   reads=[kpv, "prow"], writes=["kvf"])
                if is_lastp:
                    OP("sp", lambda e: [e.dma_start(out=kwp_d[:, :], in_=kvf[:, 0:256]),
                                        e.dma_start(out=vwp_d[:, :], in_=kvf[:, 256:512])],
                       reads=["kvf"], writes=["o_kvp"], dma=("ost", 2))
                else:
                    OP("sp", lambda e: (
                        [e.dma_start(out=kws_d[b, 124:128, :], in_=kvf[4 * b:4 * b + 4, 0:256]) for b in range(16)]
                        + [e.dma_start(out=vws_d[b, 124:128, :], in_=kvf[4 * b:4 * b + 4, 256:512]) for b in range(16)]
                        + [e.dma_start(out=kws_d[:, 0:124, :], in_=ck_d[:, 4:128, :]),
                           e.dma_start(out=vws_d[:, 0:124, :], in_=cv_d[:, 4:128, :])]),
                       reads=["kvf"], writes=["o_kvs"], dma=("ost", 34))

    def kv_carry(tile):
        n = tile.nch
        OP("dve", lambda e: [e.tensor_copy(out=kT[:, :, 0:128], in_=kT[:, :, n * 128:n * 128 + 128])], reads=["kT"], writes=["kT"])
        OP("dve", lambda e: [e.tensor_copy(out=vtm[:, 0, :], in_=vtm[:, n, :])], reads=["vtm"], writes=["vtm"])

    qT = act
    OT0 = 8

    def attn_batch(items, nq, segs_n, masks, extra_bias_first=None, spool="hd", ptpool="mm", stage=None, ctx=None):
        nk = sum(segs_n)
        nb = len(items)
        if ctx is not None:
            psS = ctx
        else:
            psS = [PS(spool), PS(spool)]

        def sfn(e):
            r = []
            for j, it in enumerate(items):
                o = 0
                for si, n_ in enumerate(segs_n):
                    r.append(e.matmul(psS[j % 2][0][0:nq, (j // 2) * 256 + o:(j // 2) * 256 + o + n_], lhsT=it["q"], rhs=it["k"][si],
                                      start=True, stop=True))
                    o += n_
            return r
        AST = 99
        if stage != "rest":
            OP("pe", sfn, reads=["qT", "kT", "qs", "ckt0", "ckt1"], writes=[psS[0][1], psS[1][1]])
        if stage == "scores":
            return psS
        sm, ksm = SM.next()
        pn, kpn = PN.next()
        pt, kpt = PT.next()
        st, kst = STAT.next()
        OP("dve", lambda e: [
            e.scalar_tensor_tensor(out=sm[0:nq, j, m0:m0 + ml], in0=psS[j % 2][0][0:nq, (j // 2) * 256 + m0:(j // 2) * 256 + m0 + ml],
                                   scalar=0.125, in1=map_, op0=ALU.mult, op1=ALU.add) for j in range(nb) for (m0, ml, map_) in masks],
           reads=[psS[0][1], psS[1][1], "cf"], writes=[ksm])
        OP("dve", lambda e: [e.memset(st[0:nq, 0:nb, 2:3], 0.0)], writes=[kst + "a"])
        if extra_bias_first is not None:
            OP("dve", lambda e: [
                e.tensor_scalar(out=sm[0:nq, 0:nb, 0:128], in0=sm[0:nq, 0:nb, 0:128], scalar1=extra_bias_first, scalar2=None, op0=ALU.add)],
               reads=[ksm, "hbias"], writes=[ksm])
        if AST < 2:
            return
        OP("dve", lambda e: [e.reduce_max(out=st[0:nq, 0:nb, 0:1], in_=sm[0:nq, 0:nb, 0:nk], axis=AX.X)], reads=[ksm], writes=[kst])
        OP("dve", lambda e: [
            e.tensor_scalar(out=st[0:nq, j, 1:2], in0=st[0:nq, j, 0:1], scalar1=it["sink"], scalar2=-1.0, op0=ALU.max, op1=ALU.mult)
            for j, it in enumerate(items)], reads=[kst, "prow", "cf"], writes=[kst])
        if AST < 3:
            return
        OP("act", lambda e: [
            e.activation(out=sm[0:nq, j, 0:nk], in_=sm[0:nq, j, 0:nk], func=AF.Exp, bias=st[0:nq, j, 1:2], accum_out=st[0:nq, j, 2:3])
            for j in range(nb)], reads=[ksm, kst], writes=[ksm, kst + "a"])
        OP("act", lambda e: [
            e.activation(out=st[0:nq, j, 3:4], in_=st[0:nq, j, 1:2], func=AF.Exp, bias=it["sink"]) for j, it in enumerate(items)],
           reads=[kst, "prow", "cf"], writes=[kst + "b"])
        OP("dve", lambda e: [e.tensor_tensor(out=st[0:nq, 0:nb, 4:5], in0=st[0:nq, 0:nb, 2:3], in1=st[0:nq, 0:nb, 3:4], op=ALU.add)],
           reads=[kst + "a", kst + "b"], writes=[kst + "c"])
        OP("dve", lambda e: [e.reciprocal(out=st[0:nq, 0:nb, 5:6], in_=st[0:nq, 0:nb, 4:5])], reads=[kst + "c"], writes=[kst + "d"])
        OP("dve", lambda e: [
            e.tensor_scalar(out=pn[0:nq, j, 0:nk], in0=sm[0:nq, j, 0:nk], scalar1=st[0:nq, j, 5:6], scalar2=None, op0=ALU.mult)
            for j in range(nb)], reads=[ksm, kst + "d"], writes=[kpn])
        if AST < 4:
            return
        ptp, kptp = PS(ptpool)
        ptv = ptp[:, :].bitcast(BF16)

        def tfn(e):
            r = []
            for j in range(nb):
                o = 0
                for si, n_ in enumerate(segs_n):
                    r.append(e.transpose(ptv[0:n_, j * 256 + si * 128:j * 256 + si * 128 + nq], pn[0:nq, j, o:o + n_], ident[0:nq, 0:nq]))
                    o += n_
            return r
        OP("pe", tfn, reads=[kpn, "cbf"], writes=[kptp])
        nmax = max(segs_n)

        def cfn(e):
            if len(set(segs_n)) == 1:
                return [e.activation(out=pt[0:nmax, 0:nb, :], in_=ptv[0:nmax, 0:nb * 256].rearrange("p (j t) -> p j t", t=256), func=AF.Copy)]
            r = []
            for si, n_ in enumerate(segs_n):
                r.append(e.activation(out=pt[0:n_, 0:nb, si * 128:si * 128 + nq],
                                      in_=ptv[0:n_, 0:nb * 256].rearrange("p (j t) -> p j t", t=256)[:, :, si * 128:si * 128 + nq],
                                      func=AF.Copy))
            return r
        OP("act", cfn, reads=[kptp], writes=[kpt])

        if AST < 5:
            return

        def pvfn(e):
            r = []
            for j, it in enumerate(items):
                for si, n_ in enumerate(segs_n):
                    r.append(e.matmul(it["out"], lhsT=it["v"][si], rhs=pt[0:n_, j, si * 128:si * 128 + nq],
                                      start=(si == 0), stop=(si == len(segs_n) - 1)))
            return r
        OP("pe", pvfn, reads=[kpt, "vtm", "cvt0", "cvt1"], writes=sorted(set(it["okey"] for it in items)))

    def attention(tile):
        rmsnorm(tile, PV["g"] + 32)
        for blk in range(2):
            if os.environ.get("KSKIP2") == "aq":
                continue
            wb, kw = W_get("wq", (blk,), WSLOT)
            wv = wb[:, :].rearrange("p (k n) -> p k n", k=8)
            for jj in range(4):
                c = blk * 4 + jj
                for (c0, c1) in tile.subs:
                    n = c1 - c0
                    pq, kpq = PS("mm")
                    OP("pe", mm_group(pq[:, 0:n], [(wv[:, k, jj * 128:(jj + 1) * 128], hn[:, k, c0:c1]) for k in range(8)]),
                       reads=["hn", kw], writes=[kpq])
                    OP("act", lambda e, pq=pq, c=c, c0=c0, c1=c1, n=n: [
                        e.activation(out=qT[:, c, c0:c1], in_=pq[:, 0:n], func=AF.Identity, bias=pcol("bq", c))],
                       reads=[kpq, "pv"], writes=["qT"])
        maskb = cf[:, CF["mb"]:CF["mb"] + 256]
        barrier()
        for ci in range(tile.nch):
            if os.environ.get("KSKIP") == "ablk":
                continue
            q0 = ci * 128
            oA = PS("acc")
            oB = PS("acc")
            obanks = [oA, oB]
            for c in range(8):
                for half in range(1):
                    pass
            batches = []
            for bi in range(4):
                items = []
                for cc in (2 * bi, 2 * bi + 1):
                    for e_ in range(2):
                        g = 2 * (cc // 4) + e_
                        pr = slice(e_ * 64, e_ * 64 + 64)
                        ob, kob = obanks[cc // 4]
                        items.append(dict(
                            q=qT[pr, cc, q0:q0 + 128],
                            k=[kT[pr, cc // 4, q0:q0 + 128], kT[pr, cc // 4, q0 + 128:q0 + 256]],
                            v=[vtm[:, ci, g * 64:(g + 1) * 64], vtm[:, ci + 1, g * 64:(g + 1) * 64]],
                            sink=prow[:, PR["snk"] + 2 * cc + e_:PR["snk"] + 2 * cc + e_ + 1],
                            out=ob[pr, (cc % 4) * 128:(cc % 4) * 128 + 128], okey=kob))
                batches.append(items)
            xb = (hbias[:, 0:1] if (tile.first and ci == 0) else None)
            spools = ["hd", "hd2"]
            ctxs = [None] * 4
            ctxs[0] = attn_batch(batches[0], 128, [128, 128], [(0, 256, maskb)], spool=spools[0], stage="scores")
            for bi in range(4):
                if bi + 1 < 4:
                    ctxs[bi + 1] = attn_batch(batches[bi + 1], 128, [128, 128], [(0, 256, maskb)], spool=spools[(bi + 1) % 2], stage="scores")
                attn_batch(batches[bi], 128, [128, 128], [(0, 256, maskb)], extra_bias_first=xb, ptpool="pt", stage="rest", ctx=ctxs[bi])
            for hb_, (ob, kob) in enumerate(obanks):
                if os.environ.get("KSKIP3") == "evac":
                    continue
                OP("act", lambda e, ob=ob, hb_=hb_, q0=q0: [
                    e.activation(out=act[:, OT0 + 4 * hb_:OT0 + 4 * hb_ + 4, q0:q0 + 128],
                                 in_=ob[:, :].rearrange("p (c t) -> p c t", t=128), func=AF.Copy)],
                   reads=[kob], writes=["oT"])
        if tile.sample:
            Tp = tile.Tp
            OP("act", lambda e: [
                e.activation(out=qs[:, :, c, :], in_=qT[:, c, Tp:Tp + 64].rearrange("p (b t) -> p b t", t=4), func=AF.Copy)
                for c in range(8)], reads=["qT"], writes=["qs"])
            pvp, kpvp = PS("acc")
            pvv = pvp[:, :].rearrange("p (b c q) -> p b c q", b=16, c=2)
            maskC = cf[0:16, CF["ms"]:CF["ms"] + 128]
            for b in range(16):
                ckt, kck = CKT.next()
                cvt, kcv = CV.next()
                OP("pool", lambda e, ckt=ckt, b=b: [e.dma_start(out=ckt[:, :], in_=ckT_d[b])], writes=[kck], dma=(kck, 1))
                OP("pool", lambda e, cvt=cvt, b=b: [e.dma_start(out=cvt[:, :], in_=cv_d[b])], writes=[kcv], dma=(kcv, 1))
                items = []
                for g in range(4):
                    e_ = g % 2
                    pr = slice(e_ * 64, e_ * 64 + 64)
                    cc0 = 4 * (g // 2)
                    items.append(dict(
                        q=qs[pr, b, cc0:cc0 + 4, :].rearrange("p c t -> p (c t)"),
                        k=[ckt[pr, (g // 2) * 128:(g // 2) * 128 + 128], kT[pr, g // 2, 128 + Tp:128 + Tp + 64]],
                        v=[cvt[:, g * 64:(g + 1) * 64], vtm[0:64, 5, g * 64:(g + 1) * 64]],
                        sink=cf[0:16, CF["ss"] + g:CF["ss"] + g + 1],
                        out=pvv[pr, b, g // 2, :], okey=kpvp))
                mN = cf[0:16, CF["mn"] + 60 - 4 * b:CF["mn"] + 60 - 4 * b + 64]
                attn_batch(items, 16, [128, 64], [(0, 128, maskC), (128, 64, mN)])
            for cc in range(2):
                OP("act", lambda e, cc=cc: [
                    e.activation(out=act[:, OT0 + 4 * cc:OT0 + 4 * cc + 4, Tp:Tp + 64].rearrange("p i (b t) -> p b i t", t=4),
                                 in_=pvv[:, :, cc, :].rearrange("p b (i t) -> p b i t", t=4), func=AF.Copy)],
                   reads=[kpvp], writes=["oT"])
        barrier()
        for blk in range(2):
            if os.environ.get("KSKIP2") == "ao":
                continue
            wb, kw = W_get("wo", (blk,), WSLOT)
            wv = wb[:, :].rearrange("p (k n) -> p k n", k=8)
            for jj in range(4):
                dm = blk * 4 + jj
                for (c0, c1) in tile.subs:
                    if tile.first and c1 <= 128:
                        pass
                    n = c1 - c0
                    po, kpo = PS("mm")
                    OP("pe", mm_group(po[:, 0:n], [(wv[:, k, jj * 128:(jj + 1) * 128], act[:, OT0 + k, c0:c1]) for k in range(8)]),
                       reads=["oT", kw], writes=[kpo])
                    OP("dve", lambda e, po=po, dm=dm, c0=c0, c1=c1, n=n: [
                        e.scalar_tensor_tensor(out=xT[:, dm, c0:c1], in0=po[:, 0:n], scalar=pcol("bo", dm), in1=xT[:, dm, c0:c1],
                                               op0=ALU.add, op1=ALU.add)],
                       reads=[kpo, "xT", "pv"], writes=["xT"])

    def final_out(tile):
        for (c0, c1) in tile.subs:
            n = c1 - c0
            pss, kps = PS("mm")
            for cp in range(4):
                sq, ksq = SQ.next()
                OP("act", lambda e, sq=sq, cp=cp, c0=c0, c1=c1, n=n: [
                    e.activation(out=sq[:, :, 0:n], in_=xT[:, 2 * cp:2 * cp + 2, c0:c1], func=AF.Square)],
                   reads=["xT"], writes=[ksq])
                OP("pe", lambda e, sq=sq, cp=cp, pss=pss, n=n: [
                    e.matmul(pss[:, 0:n], lhsT=ones_bf, rhs=sq[:, j, 0:n], start=(cp == 0 and j == 0),
                             stop=(cp == 3 and j == 1)) for j in range(2)],
                   reads=[ksq, "cbf"], writes=[kps])
            rs, krs = RS.next()
            OP("act", lambda e, rs=rs, pss=pss, n=n: [
                e.activation(out=rs[:, 0:n], in_=pss[:, 0:n], func=AF.Ln, bias=pcol("eps"), scale=1.0 / D)],
               reads=[kps, "pv"], writes=[krs])
            OP("act", lambda e, rs=rs, n=n: [e.activation(out=rs[:, 0:n], in_=rs[:, 0:n], func=AF.Exp, scale=-0.5)],
               reads=[krs], writes=[krs])
            OP("dve", lambda e, rs=rs, c0=c0, c1=c1, n=n: [
                e.scalar_tensor_tensor(out=xT[:, c, c0:c1], in0=xT[:, c, c0:c1], scalar=pcol("fin", c), in1=rs[:, 0:n],
                                       op0=ALU.mult, op1=ALU.mult) for c in range(8)],
               reads=["xT", krs, "pv"], writes=["xT"])
        s0 = 0
        nout = tile.Tp
        OP("sp", lambda e: [e.dma_start(out=yT_d[:, :, tile.out0:tile.out0 + nout], in_=xT[:, :, s0:tile.Tp])],
           reads=["xT"], writes=["o_y"], dma=("ost", 1))
        if tile.sample:
            OP("sp", lambda e: [e.dma_start(out=ysT_d[:, :, :], in_=xT[:, :, tile.Tp:tile.T])], reads=["xT"], writes=["o_ys"],
               dma=("ost", 1))

    def prologue():
        OP("sp", lambda e: [e.dma_start(out=pv[:, :], in_=pv_d[:, :]), e.dma_start(out=prow[:, :], in_=prow_d[:, :]),
                            e.dma_start(out=cf[:, :], in_=cf_d[:, :]), e.dma_start(out=cbf[:, :], in_=cbf_d[:, :]),
                            e.dma_start(out=cmask[:, :], in_=cmask_d[:, :]), e.dma_start(out=sconv[:, :, :, :], in_=sconv_d[:, :, :, :])],
           writes=["pv", "prow", "cf", "cbf", "cmask", "sconv"], dma=("cld", 6))
        OP("pool", lambda e: [e.dma_start(out=wdt[:, :, :], in_=wdt_d[:, :].rearrange("p (k n) -> p k n", k=8))], writes=["wdt"],
           dma=("wdtl", 1))
        OP("act", lambda e: [e.activation(out=abc[:, :], in_=prow[:, PR["alog"]:PR["alog"] + 32], func=AF.Exp)], reads=["prow"], writes=["abc"])
        OP("dve", lambda e: [e.tensor_scalar(out=abc[:, :], in0=abc[:, :], scalar1=-1.0, scalar2=None, op0=ALU.mult)], reads=["abc"], writes=["abc"])
        OP("dve", lambda e: [e.tensor_scalar(out=hbias[:, :], in0=cmask[:, :], scalar1=-1.0, scalar2=-NEG, op0=ALU.add, op1=ALU.mult)],
           reads=["cmask"], writes=["hbias"])
        OP("dve", lambda e: [e.memset(hT[:, :], 0.0)], writes=["hT"])
        OP("dve", lambda e: [e.memset(hTb[:, :], 0.0)], writes=["hTb"])
        OP("dve", lambda e: [e.memset(hist[:, :, :], 0.0)], writes=["hist"])
        OP("dve", lambda e: [e.memset(kT[:, :, :], 0.0)], writes=["kT"])
        OP("dve", lambda e: [e.memset(vtm[:, :, :], 0.0)], writes=["vtm"])

    import os
    KSTOP = int(os.environ.get("KSTOP", "100000"))

    def program():
        ph = [0]

        def step():
            ph[0] += 1
            return ph[0] > KSTOP

        def finish(dump=None):
            if dump is not None and KSTOP < 100000:
                t = dump
                OP("sp", lambda e: [e.dma_start(out=yT_d[:, :, 0:t.Tp], in_=xT[:, :, 0:t.Tp])], reads=["xT"], writes=["o_y"], dma=("ost", 1))
            OP("sp", lambda e: [], reads=["o_y", "o_ys", "o_ssmp", "o_convp", "o_convs", "o_ssms", "o_kvp", "o_kvs"])
            if not wstate["dry"]:
                last = {}
                for i_, o_ in enumerate(S.ops[:-1]):
                    last[o_.eng if o_.dma is None else o_.dma[0]] = i_
                for i_ in last.values():
                    S.ops[-1].deps.setdefault(i_, 2)

        prologue()
        if step(): return finish()
        for tile in tiles:
            load_x(tile)
            if step(): return finish(tile)
            ffn(tile, 0)
            barrier()
            if step(): return finish(tile)
            if tile.kind == "pre":
                mixer0(tile, True)
                barrier()
                if step(): return finish(tile)
                continue
            mixer0(tile, False)
            barrier()
            if step(): return finish(tile)
            ffn(tile, 1)
            barrier()
            if step(): return finish(tile)
            kv_proj(tile)
            barrier()
            if step(): return finish(tile)
            if tile.kind == "prefull":
                kv_carry(tile)
                OP("dve", lambda e: [e.tensor_scalar(out=hT[:, :], in0=hT[:, :], scalar1=cmask[:, 0:1], scalar2=None, op0=ALU.mult)],
                   reads=["hT", "cmask"], writes=["hT"])
                OP("act", lambda e: [e.activation(out=hTb[:, :], in_=hT[:, :], func=AF.Copy)], reads=["hT"], writes=["hTb"])
                continue
            ffn(tile, 2)
            barrier()
            if step(): return finish(tile)
            attention(tile)
            barrier()
            if step(): return finish(tile)
            kv_carry(tile)
            ffn(tile, 3)
            barrier()
            if step(): return finish(tile)
            final_out(tile)
            barrier()
            if step(): return finish()
        finish()

    program()
    wstate["dry"] = False
    wstate["i"] = 0
    for k in pctr:
        pctr[k] = 0
    for r_ in (XTM, SEG, MTB, EBC, Y1, YG, SQ, RS, SG, XIN, XINS, CACC, DTT, AG3, TMPF, ACF, AC3, CEP, XW, BM, SM, PN, PT, STAT, CKT, CV):
        r_.i = 0
    program()
    S.finalize()

    with ExitStack() as es2:
        semh = {n: es2.enter_context(nc.semaphore(f"s_{n}")) for n in S.semnames}
        for n in ("pe", "act", "dve", "pool"):
            if n not in semh:
                semh[n] = es2.enter_context(nc.semaphore(f"s_{n}"))
        with nc.Block() as block:
            @block.sync
            def _(e):
                S.emit_engine("sp", e, semh)

            @block.gpsimd
            def _(e):
                S.emit_engine("pool", e, semh)

            @block.tensor
            def _(e):
                S.emit_engine("pe", e, semh)

            @block.scalar
            def _(e):
                S.emit_engine("act", e, semh)

            @block.vector
            def _(e):
                S.emit_engine("dve", e, semh)
    es.close()
    return nc, len(S.ops)


def tile_w(Wm, nb):
    K, N = Wm.shape
    a = Wm.reshape(K // 128, 128, N // nb, nb)
    return np.ascontiguousarray(a.transpose(2, 1, 0, 3)).reshape(N // nb, 128, (K // 128) * nb)


def pad_last(a, n):
    if a.shape[-1] == n:
        return a
    out = np.zeros(a.shape[:-1] + (n,), a.dtype)
    out[..., :a.shape[-1]] = a
    return out


def host_consts():
    cfa = np.zeros((128, NCF), np.float32)
    i = np.arange(128)
    um = (i[:, None] <= i[None, :]).astype(np.float32)
    cfa[:, CF["um"]:CF["um"] + 128] = um
    usf = (i[:, None] > i[None, :]).astype(np.float32)
    j = np.arange(64)
    same = (j[:, None] // 4) == (j[None, :] // 4)
    cfa[:64, CF["ums"]:CF["ums"] + 64] = (same & (j[:, None] <= j[None, :])).astype(np.float32)
    ussf = (same & (j[:, None] > j[None, :])).astype(np.float32)
    mb = np.full((128, 256), NEG, np.float32)
    mb[:, :128][i[None, :] > i[:, None]] = 0.0
    mb[:, 128:][i[None, :] <= i[:, None]] = 0.0
    cfa[:, CF["mb"]:CF["mb"] + 256] = mb
    ms = np.full((16, 128), NEG, np.float32)
    mn = np.full((16, 124), NEG, np.float32)
    for r in range(16):
        t = r % 4
        ms[r, t + 1:128] = 0.0
        mn[r, 60:60 + t + 1] = 0.0
    cfa[:16, CF["ms"]:CF["ms"] + 128] = ms
    cfa[:16, CF["mn"]:CF["mn"] + 124] = mn
    cfa[:64, CF["sm"]:CF["sm"] + 16] = (j[:, None] // 4 == np.arange(16)[None, :]).astype(np.float32)
    cb = np.zeros((128, NCB), np.float32)
    cb[:, :128] = np.eye(128)
    cb[:, 128:256] = 1.0
    cb[:, CB["um"]:CB["um"] + 128] = cfa[:, CF["um"]:CF["um"] + 128]
    cb[:, CB["us"]:CB["us"] + 128] = usf
    cb[:64, CB["ums"]:CB["ums"] + 64] = cfa[:64, CF["ums"]:CF["ums"] + 64]
    cb[:64, CB["uss"]:CB["uss"] + 64] = ussf
    sel = np.zeros((32, 32, 128), np.float32)
    for h in range(32):
        sel[h, h, :] = 1.0
    cb[:32, CB["sel"]:CB["sel"] + 4096] = sel.reshape(32, 4096)
    return cfa, cb.astype(ml_dtypes.bfloat16)


def host_weights(p):
    f32 = np.float32
    out = {}
    wgu = np.zeros((4, 11, 128, WSLOT), f32)
    wdn = np.zeros((4, 8, 128, NF * 128), f32)
    for fi in range(4):
        l, i = fi // 2, fi % 2
        g = tile_w(np.asarray(p["ffn_w_gate"][l, i]), 256)
        u = tile_w(np.asarray(p["ffn_w_up"][l, i]), 256)
        wgu[fi] = np.concatenate([g, u], axis=2)
        wdn[fi] = tile_w(np.asarray(p["ffn_w_down"][l, i]), 128)
    out["wgu"], out["wdn"] = wgu, wdn
    win = np.asarray(p["ssm_w_in"][0])
    out["winz"] = tile_w(win[:, 0:2048], 512)
    out["winx"] = tile_w(win[:, 2048:5120], 512)
    out["wdt"] = np.ascontiguousarray(tile_w(win[:, 5120:5152], 32)[0])
    out["wout"] = tile_w(np.asarray(p["ssm_w_out"][0]), 256)
    out["wkv"] = tile_w(np.asarray(p["attn_w_kv"]), 512)
    perm = np.concatenate([np.arange(QH[c][e] * 64, QH[c][e] * 64 + 64) for c in range(8) for e in range(2)])
    out["wq"] = tile_w(np.asarray(p["attn_w_q"][0])[:, perm], 512)
    out["wo"] = tile_w(np.asarray(p["attn_w_o"][0])[perm, :], 512)
    pvh = np.zeros((128, NPV), f32)
    fm = lambda v: np.asarray(v, f32).reshape(-1, 128).T
    ng = np.asarray(p["norm_gain"])
    for l in range(2):
        for i in range(3):
            pvh[:, PV["g"] + (l * 3 + i) * 8:PV["g"] + (l * 3 + i) * 8 + 8] = fm(ng[l, i])
    pvh[:, PV["kvn"]:PV["kvn"] + 8] = fm(p["kv_norm"])
    pvh[:, PV["fin"]:PV["fin"] + 8] = fm(p["final_norm"])
    cw = np.asarray(p["ssm_conv_w"][0])
    for k in range(4):
        pvh[:, PV["cw"] + k * 24:PV["cw"] + k * 24 + 24] = fm(cw[k])
    pvh[:, PV["cb"]:PV["cb"] + 24] = fm(p["ssm_conv_b"][0])
    pvh[:, PV["dsk"]:PV["dsk"] + 16] = fm(np.repeat(np.asarray(p["ssm_d"][0]), 64))
    pvh[:, PV["ng"]:PV["ng"] + 16] = fm(p["ssm_norm"][0])
    pvh[:, PV["bq"]:PV["bq"] + 8] = fm(np.asarray(p["attn_b_q"][0])[perm])
    bkv = np.asarray(p["attn_b_kv"], f32)
    pvh[:, PV["bk"]:PV["bk"] + 2] = fm(bkv[:256])
    pvh[:, PV["bo"]:PV["bo"] + 8] = fm(p["attn_b_o"][0])
    pvh[:, PV["one"]] = 1.0
    pvh[:, PV["eps"]] = EPS
    out["pv"] = pvh
    pr = np.zeros((128, NPR), f32)
    pr[:, PR["dtb"]:PR["dtb"] + 32] = np.asarray(p["ssm_dt_bias"][0])[None, :]
    pr[:, PR["alog"]:PR["alog"] + 32] = np.asarray(p["ssm_a_log"][0])[None, :]
    pr[:, PR["bv"]:PR["bv"] + 256] = bkv[None, 256:]
    pr[:, PR["bkr"]:PR["bkr"] + 256] = bkv[None, :256]
    sinks = np.asarray(p["attn_sinks"][0], f32)
    pr[:, PR["snk"]:PR["snk"] + 16] = np.array([sinks[QH[c][e]] for c in range(8) for e in range(2)], f32)[None, :]
    out["prow"] = pr
    cfa, cb = host_consts()
    for g in range(4):
        for r in range(16):
            cfa[r, CF["ss"] + g] = sinks[QH[4 * (g // 2) + r // 4][g % 2]]
    out["cf"], out["cbf"] = cfa, cb
    return out


_PROG = {}


def run(inputs, seq, npre, nmain):
    f32 = np.float32
    key = (npre, nmain)
    if key not in _PROG:
        _PROG[key] = build_program(npre, nmain)
    nc, nops = _PROG[key]
    w = host_weights(inputs)
    xp = np.asarray(inputs["x_prompt"], f32)
    xs = np.asarray(inputs["x_sample"], f32)
    sconv = np.asarray(inputs["state_conv"], f32)[0]
    sssm = np.asarray(inputs["state_ssm"], f32)[0]
    ck = np.asarray(inputs["cache_k_win"], f32)
    cv = np.asarray(inputs["cache_v_win"], f32)
    half = seq // 2
    in_maps = []
    for c in range(8):
        s, hf = c // 2, c % 2
        m = dict(w)
        if hf == 0:
            m["xpre"] = np.zeros((D, npre * 128), f32)
        else:
            m["xpre"] = np.ascontiguousarray(xp[s, 0:half].T)
        m["xmain"] = np.ascontiguousarray(xp[s, hf * half:(hf + 1) * half].T)
        m["cmask"] = np.full((128, 1), float(hf), f32)
        b0 = 16 * c
        m["xsmp"] = np.ascontiguousarray(xs[b0:b0 + 16].reshape(64, D).T)
        m["sconv"] = np.ascontiguousarray(sconv[b0:b0 + 16].reshape(16, 3, 24, 128).transpose(3, 2, 0, 1))
        m["sssm"] = np.ascontiguousarray(sssm[b0:b0 + 16].reshape(16, DIN, 128).transpose(0, 2, 1))
        kk = ck[b0:b0 + 16].reshape(16, 128, 2, 2, 64)
        m["ckT"] = np.ascontiguousarray(kk.transpose(0, 3, 4, 2, 1)).reshape(16, 128, 256)
        m["ck"] = np.ascontiguousarray(ck[b0:b0 + 16].reshape(16, 128, 256))
        m["cv"] = np.ascontiguousarray(cv[b0:b0 + 16].reshape(16, 128, 256))
        in_maps.append(m)
    if os.environ.get("KTRACE"):
        res = run_bass_kernel_spmd(nc, in_maps, core_ids=list(range(8)), trace=True)
        print("EXEC_TIME_NS", res.exec_time_ns)
    else:
        res = run_bass_kernel_spmd(nc, in_maps, core_ids=list(range(8)))
    R = res.results
    B = xp.shape[0]
    y_p = np.zeros((B, seq, D), f32)
    conv_p = np.zeros((1, B, 3, 3072), f32)
    ssm_p = np.zeros((1, B, 32, 64, 128), f32)
    k_p = np.zeros((B, 128, 4, 64), f32)
    v_p = np.zeros((B, 128, 4, 64), f32)
    y_s = np.zeros((128, 4, D), f32)
    conv_s = np.zeros((1, 128, 3, 3072), f32)
    ssm_s = np.zeros((1, 128, 32, 64, 128), f32)
    k_s = np.zeros((128, 128, 4, 64), f32)
    v_s = np.zeros((128, 128, 4, 64), f32)
    for c in range(8):
        s, hf = c // 2, c % 2
        r = R[c]
        y_p[s, hf * half:(hf + 1) * half] = r["yT"].T
        b0 = 16 * c
        y_s[b0:b0 + 16] = r["ysT"].T.reshape(16, 4, D)
        conv_s[0, b0:b0 + 16] = r["convs"].transpose(2, 3, 1, 0).reshape(16, 3, 3072)
        ssm_s[0, b0:b0 + 16] = r["ssms"].transpose(0, 2, 1).reshape(16, 32, 64, 128)
        k_s[b0:b0 + 16] = r["kws"].reshape(16, 128, 4, 64)
        v_s[b0:b0 + 16] = r["vws"].reshape(16, 128, 4, 64)
        if hf == 1:
            conv_p[0, s] = r["convp"].transpose(2, 1, 0).reshape(3, 3072)
            ssm_p[0, s] = r["ssmp"].T.reshape(32, 64, 128)
            k_p[s] = r["kwp"].reshape(128, 4, 64)
            v_p[s] = r["vwp"].reshape(128, 4, 64)
    return (y_p, y_s, conv_p, ssm_p, k_p, v_p, conv_s, ssm_s, k_s, v_s)


def kernel(**inputs):
    seq = int(np.asarray(inputs["x_prompt"]).shape[1])
    nchunks = seq // 128
    return run(inputs, seq, nchunks // 2, nchunks // 2)
```

```python
import os
import numpy as np
import ml_dtypes
from contextlib import ExitStack
import concourse.bass as bass
import concourse.mybir as mybir
from concourse.bass_utils import run_bass_kernel_spmd

F32, BF16 = mybir.dt.float32, mybir.dt.bfloat16
AF = mybir.ActivationFunctionType
ALU = mybir.AluOpType
AX = mybir.AxisListType

D = 1024
DFF = 2816
NF = 22
DIN = 2048
EPS = 1e-6
NEG = -30000.0
WSLOT = 4096
NWS = 3

QH = [[c, c + 4] if c < 4 else [c + 4, c + 8] for c in range(8)]


class Op:
    __slots__ = ("eng", "fn", "deps", "dma", "signal", "value")

    def __init__(self, eng, fn, deps, dma):
        self.eng, self.fn, self.deps, self.dma = eng, fn, deps, dma
        self.signal = False
        self.value = 0


class Sched:
    def __init__(self):
        self.ops = []
        self.lw = {}
        self.rd = {}

    def op(self, eng, fn, reads=(), writes=(), dma=None):
        idx = len(self.ops)
        deps = {}
        for k in reads:
            w = self.lw.get(k)
            if w is not None:
                deps[w] = 2
        for k in writes:
            w = self.lw.get(k)
            if w is not None:
                deps.setdefault(w, 1)
            r = self.rd.get(k)
            if r:
                for e_, i_ in r[0].items():
                    deps.setdefault(i_, 1)
                for i_ in r[1]:
                    deps.setdefault(i_, 1)
        for k in writes:
            self.lw[k] = idx
            self.rd[k] = [{}, []]
        for k in reads:
            r = self.rd.setdefault(k, [{}, []])
            if dma is None:
                r[0][eng] = idx
            else:
                r[1].append(idx)
        self.ops.append(Op(eng, fn, deps, dma))
        return idx

    def finalize(self):
        ops = self.ops
        for o in ops:
            need = []
            for d, kind in o.deps.items():
                p = ops[d]
                if p.dma is not None:
                    need.append(d)
                elif o.dma is None and p.eng == o.eng:
                    if o.eng != "pe" and kind == 2:
                        need.append(d)
                else:
                    need.append(d)
            o.deps = need
            for d in need:
                ops[d].signal = True
        cnt = {}
        for o in ops:
            if o.dma is not None:
                s, n = o.dma
                cnt[s] = cnt.get(s, 0) + 16 * n
                o.value = cnt[s]
            elif o.signal:
                cnt[o.eng] = cnt.get(o.eng, 0) + 1
                o.value = cnt[o.eng]
        self.semnames = sorted(cnt.keys())

    def emit_engine(self, engname, e, semh):
        ops = self.ops
        waited = {}
        for o in ops:
            if o.eng != engname:
                continue
            for d in o.deps:
                p = ops[d]
                s = p.dma[0] if p.dma is not None else p.eng
                if waited.get(s, 0) < p.value:
                    e.wait_ge(semh[s], p.value)
                    waited[s] = p.value
            ins = o.fn(e)
            if o.dma is not None:
                assert len(ins) == o.dma[1], (len(ins), o.dma)
                for i_ in ins:
                    i_.then_inc(semh[o.dma[0]], 16)
            elif o.signal:
                ins[-1].then_inc(semh[engname], 1)


class Rot:
    def __init__(self, tensors, name):
        self.t = tensors
        self.n = len(tensors)
        self.i = 0
        self.name = name

    def next(self):
        j = self.i % self.n
        self.i += 1
        return self.t[j], f"{self.name}{j}"


def subtiles(T):
    out = []
    c = 0
    while c < T:
        n = min(512, T - c)
        out.append((c, c + n))
        c += n
    return out


class Tile:
    def __init__(self, kind, src, col0, nch, sample=False, first=False, last=False, out0=0):
        self.kind = kind
        self.src = src
        self.col0 = col0
        self.nch = nch
        self.Tp = nch * 128
        self.sample = sample
        self.T = self.Tp + (64 if sample else 0)
        self.first = first
        self.last = last
        self.out0 = out0
        self.subs = subtiles(self.Tp) + ([(self.Tp, self.Tp + 64)] if sample else [])


def make_tiles(npre, nmain, tch=4):
    def split(n):
        k = -(-n // tch)
        base, rem = divmod(n, k)
        return [base + 1] * rem + [base] * (k - rem)
    tiles = []
    sp = split(npre)
    c = 0
    for i, n in enumerate(sp):
        tiles.append(Tile("prefull" if i == len(sp) - 1 else "pre", "xpre", c * 128, n))
        c += n
    sp = split(nmain)
    c = 0
    for i, n in enumerate(sp):
        tiles.append(Tile("main", "xmain", c * 128, n, sample=(i == len(sp) - 1), first=(i == 0),
                          last=(i == len(sp) - 1), out0=c * 128))
        c += n
    return tiles


PV = {}
_o = 0
for _n, _w in [("g", 48), ("kvn", 8), ("fin", 8), ("cw", 96), ("cb", 24), ("dsk", 16), ("ng", 16),
               ("bq", 8), ("bk", 2), ("bo", 8), ("one", 1), ("eps", 1)]:
    PV[_n] = _o
    _o += _w
NPV = _o
PR = {"dtb": 0, "alog": 32, "bv": 64, "bkr": 320, "snk": 576}
NPR = 592
CF = {"um": 0, "ums": 128, "mb": 192, "ms": 448, "mn": 576, "sm": 700, "ss": 716}
NCF = 720
CB = {"id": 0, "on": 128, "um": 256, "us": 384, "ums": 512, "uss": 576, "sel": 640}
NCB = 640 + 4096


def build_program(npre, nmain):
    tiles = make_tiles(npre, nmain)
    TMAX = max(t.T for t in tiles)
    nc = bass.Bass("TRN2", target_bir_lowering=False)
    S = Sched()

    def din(name, shape, dt=F32):
        return nc.dram_tensor(name, list(shape), dt, kind="ExternalInput").ap()

    def dout(name, shape):
        return nc.dram_tensor(name, list(shape), F32, kind="ExternalOutput").ap()

    NOUT = nmain * 128
    dr = {}
    dr["xpre"] = din("xpre", [D, npre * 128]).rearrange("(c p) t -> p c t", p=128)
    dr["xmain"] = din("xmain", [D, nmain * 128]).rearrange("(c p) t -> p c t", p=128)
    xsmp_d = din("xsmp", [D, 64]).rearrange("(c p) t -> p c t", p=128)
    wd = {
        "wgu": din("wgu", [4, 11, 128, WSLOT]),
        "wdn": din("wdn", [4, 8, 128, NF * 128]),
        "winz": din("winz", [4, 128, WSLOT]),
        "winx": din("winx", [6, 128, WSLOT]),
        "wout": din("wout", [4, 128, WSLOT]),
        "wkv": din("wkv", [1, 128, WSLOT]),
        "wq": din("wq", [2, 128, WSLOT]),
        "wo": din("wo", [2, 128, WSLOT]),
    }
    wdt_d = din("wdt", [128, 256])
    pv_d = din("pv", [128, NPV])
    prow_d = din("prow", [128, NPR])
    cf_d = din("cf", [128, NCF])
    cbf_d = din("cbf", [128, NCB], BF16)
    cmask_d = din("cmask", [128, 1])
    sconv_d = din("sconv", [128, 24, 16, 3])
    sssm_d = din("sssm", [16, 128, DIN])
    ckT_d = din("ckT", [16, 128, 256])
    ck_d = din("ck", [16, 128, 256])
    cv_d = din("cv", [16, 128, 256])
    yT_d = dout("yT", [D, NOUT]).rearrange("(c p) t -> p c t", p=128)
    ysT_d = dout("ysT", [D, 64]).rearrange("(c p) t -> p c t", p=128)
    convp_d = dout("convp", [128, 24, 3])
    ssmp_d = dout("ssmp", [128, DIN])
    kwp_d = dout("kwp", [128, 256])
    vwp_d = dout("vwp", [128, 256])
    convs_d = dout("convs", [128, 24, 16, 3])
    ssms_d = dout("ssms", [16, 128, DIN])
    kws_d = dout("kws", [16, 128, 256])
    vws_d = dout("vws", [16, 128, 256])

    es = ExitStack()

    def sb(name, shape, dt):
        return es.enter_context(nc.sbuf_tensor(name, list(shape), dt))

    xT = sb("xT", [128, 8, TMAX], F32)
    hn = sb("hn", [128, 8, TMAX], BF16)
    act = sb("act", [128, 40, TMAX], BF16)
    wsl = [sb(f"wsl{i}", [128, WSLOT], BF16) for i in range(NWS)]
    xsT = act[:, 16:40, :]
    XTM = Rot([sb(f"xtm{i}", [128, 2560], BF16) for i in range(2)], "xtm")
    hT = sb("hT", [128, DIN], F32)
    hTb = sb("hTb", [128, DIN], BF16)
    Ce = sb("Ce", [128, 32, 64], BF16)
    CEP = Rot([sb(f"cep{i}", [128, 2, 128], BF16) for i in range(2)], "cep")
    Gm = sb("Gm", [128, 4, 128], F32)
    SEG = Rot([sb(f"seg{i}", [128, 2, 128], F32) for i in range(2)], "seg")
    MTB = Rot([sb(f"mtb{i}", [128, 2, 128], BF16) for i in range(2)], "mtb")
    EBC = Rot([sb(f"ebc{i}", [128, 2, 128], F32) for i in range(2)], "ebc")
    edec = sb("edec", [128, 32, 16], F32)
    Y1 = Rot([sb(f"y1{i}", [128, 128], F32) for i in range(2)], "y1")
    YG = Rot([sb(f"yg{i}", [128, 4, 128], F32) for i in range(1)], "yg")
    SQ = Rot([sb(f"sq{i}", [128, 2, 512], BF16) for i in range(2)], "sq")
    RS = Rot([sb(f"rs{i}", [128, 512], F32) for i in range(1)], "rs")
    SG = Rot([sb(f"sg{i}", [128, 512], F32) for i in range(2)], "sg")
    XIN = Rot([sb(f"xin{i}", [128, 3 + TMAX], F32) for i in range(2)], "xin")
    XINS = Rot([sb(f"xins{i}", [128, 16, 7], F32) for i in range(2)], "xins")
    CACC = Rot([sb(f"cacc{i}", [128, TMAX], F32) for i in range(1)], "cacc")
    hist = sb("hist", [128, 24, 3], F32)
    sconv = sb("sconv_t", [128, 24, 16, 3], F32)
    convs = sconv
    DTT = Rot([sb(f"dtt{i}", [128, 320], F32) for i in range(2)], "dtt")
    AG3 = Rot([sb(f"ag3{i}", [128, 3, 32], BF16) for i in range(2)], "ag3")
    TMPF = Rot([sb(f"tmpf{i}", [128, 2, 32], F32) for i in range(2)], "tmpf")
    ACF = Rot([sb(f"acf{i}", [32, 3, 128], F32) for i in range(1)], "acf")
    AC3 = Rot([sb(f"ac3{i}", [32, 3, 128], BF16) for i in range(2)], "ac3")
    XW = Rot([sb(f"xw{i}", [128, 512], BF16) for i in range(2)], "xw")
    xwall = sb("xwall", [64, DIN], BF16)
    BM = Rot([sb(f"bm{i}", [64, 128], BF16) for i in range(2)], "bm")
    kT = sb("kT", [128, 2, 128 + TMAX], BF16)
    vtm = sb("vtm", [128, 6, 256], BF16)
    kvf = sb("kvf", [128, 512], F32)
    SM = Rot([sb(f"sm{i}", [128, 4, 256], F32) for i in range(1)], "sm")
    PN = Rot([sb(f"pn{i}", [128, 4, 256], BF16) for i in range(1)], "pn")
    PT = Rot([sb(f"pt{i}", [128, 4, 256], BF16) for i in range(1)], "pt")
    STAT = Rot([sb(f"stat{i}", [128, 4, 8], F32) for i in range(4)], "stat")
    qs = sb("qs", [128, 16, 8, 4], BF16)
    CKT = Rot([sb(f"ckt{i}", [128, 256], BF16) for i in range(2)], "ckt")
    CV = Rot([sb(f"cvt{i}", [128, 256], BF16) for i in range(2)], "cvt")
    wdt = sb("wdt_t", [128, 8, 32], BF16)
    pv = sb("pv_t", [128, NPV], F32)
    prow = sb("prow_t", [128, NPR], F32)
    abc = sb("abc", [128, 32], F32)
    cf = sb("cf_t", [128, NCF], F32)
    cbf = sb("cbf_t", [128, NCB], BF16)
    cmask = sb("cmask_t", [128, 1], F32)
    hbias = sb("hbias", [128, 1], F32)
    ps = [es.enter_context(nc.psum_tensor(f"ps{i}", [128, 512], F32)) for i in range(8)]
    pools = {"mm": [0, 1, 2, 3], "hd": [4, 5], "acc": [6, 7], "hd2": [2, 3], "pt": [0, 1]}
    pctr = {k_: 0 for k_ in pools}

    def PS(pool):
        l = pools[pool]
        i = l[pctr[pool] % len(l)]
        pctr[pool] += 1
        return ps[i], f"ps{i}"

    ident = cbf[:, 0:128]
    ones_bf = cbf[:, 128:256]
    pcol = lambda name, j=0: pv[:, PV[name] + j:PV[name] + j + 1]

    wseq = []
    wstate = {"dry": True, "i": 0, "issued": 0}

    def wdram(name, idx):
        a = wd[name]
        return a[idx[0], idx[1]] if len(idx) == 2 else a[idx[0]]

    def W_issue(upto):
        while wstate["issued"] <= min(upto, len(wseq) - 1):
            k = wstate["issued"]
            name, idx, nel = wseq[k]
            slot = wsl[k % NWS]
            src = wdram(name, idx)
            S.op("pool", (lambda e, slot=slot, src=src, nel=nel: [e.dma_start(out=slot[:, 0:nel], in_=src[:, 0:nel])]),
                 writes=[f"wsl{k % NWS}"], dma=(f"w{k % NWS}", 1))
            wstate["issued"] += 1

    def W_get(name, idx, nel):
        k = wstate["i"]
        wstate["i"] += 1
        if wstate["dry"]:
            wseq.append((name, idx, nel))
            return wsl[k % NWS], f"wsl{k % NWS}"
        assert wseq[k] == (name, idx, nel)
        W_issue(k + NWS - 1)
        return wsl[k % NWS], f"wsl{k % NWS}"

    def OP(eng, fn, reads=(), writes=(), dma=None):
        if wstate["dry"]:
            return
        S.op(eng, fn, reads, writes, dma)

    def barrier():
        if wstate["dry"] or os.environ.get("KNOBAR"):
            return
        last = {}
        for i_, o_ in enumerate(S.ops):
            if o_.dma is None and o_.eng in ("pe", "act", "dve"):
                last[o_.eng] = i_
        for eng in ("pe", "act", "dve"):
            S.op(eng, lambda e: [e.nop()])
            for en2, i_ in last.items():
                if en2 != eng:
                    S.ops[-1].deps[i_] = 2

    def mm_group(out_ap, pairs):
        def fn(e):
            r = []
            n = len(pairs)
            for i, (l, rr) in enumerate(pairs):
                r.append(e.matmul(out_ap, lhsT=l, rhs=rr, start=(i == 0), stop=(i == n - 1)))
            return r
        return fn

    def rmsnorm(tile, gbase, ndim=D):
        for (c0, c1) in tile.subs:
            n = c1 - c0
            pss, kps = PS("mm")
            for cp in range(4):
                sq, ksq = SQ.next()
                OP("act", lambda e, sq=sq, cp=cp, c0=c0, c1=c1, n=n: [
                    e.activation(out=sq[:, :, 0:n], in_=xT[:, 2 * cp:2 * cp + 2, c0:c1], func=AF.Square)],
                   reads=["xT"], writes=[ksq])
                OP("pe", lambda e, sq=sq, cp=cp, pss=pss, n=n: [
                    e.matmul(pss[:, 0:n], lhsT=ones_bf, rhs=sq[:, j, 0:n], start=(cp == 0 and j == 0),
                             stop=(cp == 3 and j == 1)) for j in range(2)],
                   reads=[ksq, "cbf"], writes=[kps])
            rs, krs = RS.next()
            OP("act", lambda e, rs=rs, pss=pss, n=n: [
                e.activation(out=rs[:, 0:n], in_=pss[:, 0:n], func=AF.Ln, bias=pcol("eps"), scale=1.0 / ndim)],
               reads=[kps, "pv"], writes=[krs])
            OP("act", lambda e, rs=rs, n=n: [
                e.activation(out=rs[:, 0:n], in_=rs[:, 0:n], func=AF.Exp, scale=-0.5)],
               reads=[krs], writes=[krs])
            OP("dve", lambda e, rs=rs, c0=c0, c1=c1, n=n: [
                e.scalar_tensor_tensor(out=hn[:, c, c0:c1], in0=xT[:, c, c0:c1], scalar=pv[:, gbase + c:gbase + c + 1],
                                       in1=rs[:, 0:n], op0=ALU.mult, op1=ALU.mult) for c in range(8)],
               reads=["xT", krs, "pv"], writes=["hn"])

    def ffn(tile, fi):
        rmsnorm(tile, PV["g"] + [0, 16, 24, 40][fi])
        for blk in range(11):
            wb, kw = W_get("wgu", (fi, blk), WSLOT)
            wv = wb[:, :].rearrange("p (a k n) -> p a k n", a=2, k=8)
            for jj in range(2):
                j = blk * 2 + jj
                for (c0, c1) in tile.subs:
                    n = c1 - c0
                    pg, kpg = PS("mm")
                    pu, kpu = PS("mm")
                    OP("pe", lambda e, wv=wv, jj=jj, c0=c0, c1=c1, n=n, pg=pg, pu=pu: (
                        mm_group(pg[:, 0:n], [(wv[:, 0, k, jj * 128:(jj + 1) * 128], hn[:, k, c0:c1]) for k in range(8)])(e)
                        + mm_group(pu[:, 0:n], [(wv[:, 1, k, jj * 128:(jj + 1) * 128], hn[:, k, c0:c1]) for k in range(8)])(e)),
                       reads=["hn", kw], writes=[kpg, kpu])
                    sg, ksg = SG.next()
                    OP("act", lambda e, sg=sg, pg=pg, n=n: [e.activation(out=sg[:, 0:n], in_=pg[:, 0:n], func=AF.Silu)],
                       reads=[kpg], writes=[ksg])
                    OP("dve", lambda e, sg=sg, pu=pu, j=j, c0=c0, c1=c1, n=n: [
                        e.tensor_tensor(out=act[:, j, c0:c1], in0=sg[:, 0:n], in1=pu[:, 0:n], op=ALU.mult)],
                       reads=[ksg, kpu], writes=["act"])
        for blk in range(8):
            wb, kw = W_get("wdn", (fi, blk), NF * 128)
            wv = wb[:, 0:NF * 128].rearrange("p (j n) -> p j n", j=NF)
            for (c0, c1) in tile.subs:
                n = c1 - c0
                pd, kpd = PS("mm")
                OP("pe", mm_group(pd[:, 0:n], [(wv[:, j, :], act[:, j, c0:c1]) for j in range(NF)]),
                   reads=["act", kw], writes=[kpd])
                OP("dve", lambda e, pd=pd, blk=blk, c0=c0, c1=c1, n=n: [
                    e.scalar_tensor_tensor(out=xT[:, blk, c0:c1], in0=pd[:, 0:n], scalar=0.5, in1=xT[:, blk, c0:c1],
                                           op0=ALU.mult, op1=ALU.add)],
                   reads=[kpd, "xT"], writes=["xT"])

    def load_x(tile):
        src = dr[tile.src]
        OP("sp", lambda e: [e.dma_start(out=xT[:, :, 0:tile.Tp], in_=src[:, :, tile.col0:tile.col0 + tile.Tp])],
           writes=["xT"], dma=("xld", 1))
        if tile.sample:
            OP("sp", lambda e: [e.dma_start(out=xT[:, :, tile.Tp:tile.T], in_=xsmp_d[:, :, :])],
               writes=["xT"], dma=("xld", 1))

    def conv_chunk(tile, m, pre_only):
        pass

    def mixer_in(tile, prefix):
        rmsnorm(tile, PV["g"] + 8)
        Tp = tile.Tp
        if not prefix:
            for blk in range(4):
                wb, kw = W_get("winz", (blk,), WSLOT)
                wv = wb[:, :].rearrange("p (k n) -> p k n", k=8)
                for jj in range(4):
                    m = blk * 4 + jj
                    for (c0, c1) in tile.subs:
                        n = c1 - c0
                        pz, kpz = PS("mm")
                        OP("pe", mm_group(pz[:, 0:n], [(wv[:, k, jj * 128:(jj + 1) * 128], hn[:, k, c0:c1]) for k in range(8)]),
                           reads=["hn", kw], writes=[kpz])
                        OP("act", lambda e, pz=pz, m=m, c0=c0, c1=c1, n=n: [
                            e.activation(out=act[:, m, c0:c1], in_=pz[:, 0:n], func=AF.Silu)],
                           reads=[kpz], writes=["sz"])
        nblk = 6
        for blk in range(nblk):
            wb, kw = W_get("winx", (blk,), WSLOT)
            wv = wb[:, :].rearrange("p (k n) -> p k n", k=8)
            for jj in range(4):
                m = blk * 4 + jj
                xin, kxin = XIN.next()
                xins, kxins = XINS.next()
                OP("dve", lambda e, xin=xin, m=m: [e.tensor_copy(out=xin[:, 0:3], in_=hist[:, m, :])],
                   reads=["hist"], writes=[kxin])
                for (c0, c1) in tile.subs:
                    n = c1 - c0
                    px, kpx = PS("mm")
                    OP("pe", mm_group(px[:, 0:n], [(wv[:, k, jj * 128:(jj + 1) * 128], hn[:, k, c0:c1]) for k in range(8)]),
                       reads=["hn", kw], writes=[kpx])
                    if c0 < Tp:
                        if c0 == 0:
                            cacc, kc = CACC.next()
                        OP("act", lambda e, px=px, xin=xin, c0=c0, c1=c1, n=n, cacc=cacc, m=m: [
                            e.activation(out=xin[:, 3 + c0:3 + c1], in_=px[:, 0:n], func=AF.Copy),
                            e.activation(out=cacc[:, c0:c1], in_=px[:, 0:n], func=AF.Identity,
                                         scale=pv[:, PV["cw"] + 3 * 24 + m:PV["cw"] + 3 * 24 + m + 1], bias=pcol("cb", m))],
                           reads=[kpx, "pv"], writes=[kxin, kc])
                    else:
                        OP("act", lambda e, px=px, xins=xins: [
                            e.activation(out=xins[:, :, 3:7], in_=px[:, 0:64].rearrange("p (b t) -> p b t", t=4), func=AF.Copy)],
                           reads=[kpx], writes=[kxins])
                wc = lambda k, m=m: pv[:, PV["cw"] + k * 24 + m:PV["cw"] + k * 24 + m + 1]
                for k in range(3):
                    OP("dve", lambda e, xin=xin, cacc=cacc, k=k, wc=wc: [
                        e.scalar_tensor_tensor(out=cacc[:, 0:Tp], in0=xin[:, k:k + Tp], scalar=wc(k), in1=cacc[:, 0:Tp],
                                               op0=ALU.mult, op1=ALU.add)],
                       reads=[kxin, kc, "pv"], writes=[kc])
                OP("act", lambda e, cacc=cacc, m=m: [e.activation(out=xsT[:, m, 0:Tp], in_=cacc[:, 0:Tp], func=AF.Silu)],
                   reads=[kc], writes=["xsT"])
                OP("dve", lambda e, xin=xin, m=m: [e.tensor_copy(out=hist[:, m, :], in_=xin[:, Tp:Tp + 3])],
                   reads=[kxin], writes=["hist"])
                if tile.sample:
                    OP("dve", lambda e, xins=xins, m=m: [e.tensor_copy(out=xins[:, :, 0:3], in_=sconv[:, m, :, :])],
                       reads=["sconv"], writes=[kxins])
                    cacc, kc = CACC.next()
                    cv3 = lambda cacc=cacc: cacc[:, 0:64].rearrange("p (b t) -> p b t", t=4)
                    OP("dve", lambda e, xins=xins, cv3=cv3, m=m, wc=wc: [
                        e.tensor_scalar(out=cv3(), in0=xins[:, :, 3:7], scalar1=wc(3), scalar2=pcol("cb", m),
                                        op0=ALU.mult, op1=ALU.add)],
                       reads=[kxins, "pv"], writes=[kc])
                    for k in range(3):
                        OP("dve", lambda e, xins=xins, cv3=cv3, k=k, wc=wc: [
                            e.scalar_tensor_tensor(out=cv3(), in0=xins[:, :, k:k + 4], scalar=wc(k), in1=cv3(),
                                                   op0=ALU.mult, op1=ALU.add)],
                           reads=[kxins, kc, "pv"], writes=[kc])
                    OP("act", lambda e, cacc=cacc, m=m: [e.activation(out=xsT[:, m, Tp:Tp + 64], in_=cacc[:, 0:64], func=AF.Silu)],
                       reads=[kc], writes=["xsT"])
                    OP("dve", lambda e, xins=xins, m=m: [e.tensor_copy(out=convs[:, m, :, :], in_=xins[:, :, 4:7])],
                       reads=[kxins], writes=["sconv"])

    def ssd_chunk(tile, ci, sample, prefix):
        L = 64 if sample else 128
        col0 = tile.Tp if sample else ci * 128
        c1 = col0 + L
        um = cf[0:L, CF["ums"]:CF["ums"] + L] if sample else cf[0:L, CF["um"]:CF["um"] + L]
        xtm, kx = XTM.next()
        for grp in range(3):
            ms = list(range(grp * 8, min(grp * 8 + 8, 20)))
            pb, kpb = PS("mm")
            pbv = pb[:, :].bitcast(BF16)
            OP("pe", lambda e, ms=ms, pbv=pbv: [
                e.transpose(pbv[0:L, j * 128:(j + 1) * 128], xsT[:, m, col0:c1], ident) for j, m in enumerate(ms)],
               reads=["xsT", "cbf"], writes=[kpb])
            eng = "act" if grp % 2 == 0 else "dve"
            w = len(ms) * 128
            if eng == "act":
                OP("act", lambda e, pbv=pbv, grp=grp, w=w: [
                    e.activation(out=xtm[0:L, grp * 1024:grp * 1024 + w], in_=pbv[0:L, 0:w], func=AF.Copy)],
                   reads=[kpb], writes=[kx])
            else:
                OP("dve", lambda e, pbv=pbv, grp=grp, w=w: [
                    e.tensor_copy(out=xtm[0:L, grp * 1024:grp * 1024 + w], in_=pbv[0:L, 0:w])],
                   reads=[kpb], writes=[kx])
        dtt, kd = DTT.next()
        pdt, kpd = PS("mm")
        OP("pe", mm_group(pdt[0:L, 0:32], [(hn[:, k, col0:c1], wdt[:, k, :]) for k in range(8)]),
           reads=["hn", "wdt"], writes=[kpd])
        XB, AXc, EX, LN1, DT, AG, WW, NAC = [slice(32 * i, 32 * i + 32) for i in range(8)]
        OP("dve", lambda e: [e.tensor_tensor(out=dtt[0:L, XB], in0=pdt[0:L, 0:32], in1=prow[0:L, PR["dtb"]:PR["dtb"] + 32], op=ALU.add)],
           reads=[kpd, "prow"], writes=[kd])
        OP("act", lambda e: [e.activation(out=dtt[0:L, AXc], in_=dtt[0:L, XB], func=AF.Abs)],
           reads=[kd], writes=[kd])
        OP("act", lambda e: [e.activation(out=dtt[0:L, EX], in_=dtt[0:L, AXc], func=AF.Exp, scale=-1.0)],
           reads=[kd], writes=[kd])
        OP("act", lambda e: [e.activation(out=dtt[0:L, LN1], in_=dtt[0:L, EX], func=AF.Ln, bias=pv[0:L, PV["one"]:PV["one"] + 1])],
           reads=[kd, "pv"], writes=[kd])
        OP("dve", lambda e: [e.scalar_tensor_tensor(out=dtt[0:L, DT], in0=dtt[0:L, XB], scalar=0.0, in1=dtt[0:L, LN1],
                                                    op0=ALU.max, op1=ALU.add)],
           reads=[kd], writes=[kd])
        OP("dve", lambda e: [e.tensor_tensor(out=dtt[0:L, AG], in0=dtt[0:L, DT], in1=abc[0:L, :], op=ALU.mult)],
           reads=[kd, "abc"], writes=[kd])
        ag3, kag = AG3.next()
        tmpf, ktf = TMPF.next()

        def split3(src, dst3, tmp, rk, wk, tk, P, n):
            OP("dve", lambda e: [e.tensor_copy(out=dst3[0:P, 0, 0:n], in_=src)], reads=[rk], writes=[wk])
            OP("dve", lambda e: [e.tensor_tensor(out=tmp[0:P, 0, 0:n], in0=src, in1=dst3[0:P, 0, 0:n], op=ALU.subtract)],
               reads=[rk, wk], writes=[tk])
            OP("dve", lambda e: [e.tensor_copy(out=dst3[0:P, 1, 0:n], in_=tmp[0:P, 0, 0:n])], reads=[tk], writes=[wk])
            OP("dve", lambda e: [e.tensor_tensor(out=tmp[0:P, 1, 0:n], in0=tmp[0:P, 0, 0:n], in1=dst3[0:P, 1, 0:n], op=ALU.subtract)],
               reads=[tk, wk], writes=[tk])
            OP("dve", lambda e: [e.tensor_copy(out=dst3[0:P, 2, 0:n], in_=tmp[0:P, 1, 0:n])], reads=[tk], writes=[wk])
        split3(dtt[0:L, AG], ag3, tmpf, kd, kag, ktf, L, 32)
        umb = cbf[0:L, CB["ums"]:CB["ums"] + L] if sample else cbf[0:L, CB["um"]:CB["um"] + L]
        usb = cbf[0:L, CB["uss"]:CB["uss"] + L] if sample else cbf[0:L, CB["us"]:CB["us"] + L]
        yield
        pc, kpc = PS("mm")
        def cums(e):
            r = []
            for i in range(3):
                r.append(e.matmul(pc[0:L, 0:32], lhsT=usb, rhs=ag3[0:L, i, :], start=(i == 0), stop=(i == 2)))
            for i in range(3):
                r.append(e.matmul(pc[0:L, 32:64], lhsT=umb, rhs=ag3[0:L, i, :], start=(i == 0), stop=(i == 2)))
            for i in range(3):
                r.append(e.matmul(pc[0:32, 64:64 + L], lhsT=ag3[0:L, i, :], rhs=umb, start=(i == 0), stop=(i == 2)))
            if not sample:
                for i in range(3):
                    r.append(e.matmul(pc[:, 192:224], lhsT=ones_bf[0:L, :], rhs=ag3[0:L, i, :], start=(i == 0), stop=(i == 2)))
            return r
        OP("pe", cums, reads=[kag, "cbf"], writes=[kpc])
        OP("act", lambda e: [e.activation(out=dtt[0:L, WW], in_=pc[0:L, 0:32], func=AF.Exp)], reads=[kpc], writes=[kd])
        OP("dve", lambda e: [e.tensor_tensor(out=dtt[0:L, WW], in0=dtt[0:L, WW], in1=dtt[0:L, DT], op=ALU.mult)],
           reads=[kd], writes=[kd])
        OP("dve", lambda e: [e.tensor_scalar(out=dtt[0:L, NAC], in0=pc[0:L, 32:64], scalar1=-1.0, scalar2=None, op0=ALU.mult)],
           reads=[kpc], writes=[kd])
        ETOT = slice(256, 288)
        if not sample:
            OP("act", lambda e: [e.activation(out=dtt[:, ETOT], in_=pc[:, 192:224], func=AF.Exp)], reads=[kpc], writes=[kd])
        if not prefix:
            acf, kacf = ACF.next()
            ac3, kac3 = AC3.next()
            OP("act", lambda e: [e.activation(out=acf[0:32, 0, 0:L], in_=pc[0:32, 64:64 + L], func=AF.Copy)], reads=[kpc], writes=[kacf])
            split3(acf[0:32, 0, 0:L], ac3, acf[:, 1:3, :], kacf, kac3, kacf + "t", 32, L)
            for g in range(4):
                pg, kpg = PS("mm")
                OP("pe", lambda e, pg=pg, g=g: [e.matmul(pg[0:L, 0:L], lhsT=xsT[:, 16 + g, col0:c1], rhs=xsT[:, 20 + g, col0:c1],
                                                         start=True, stop=True)],
                   reads=["xsT"], writes=[kpg])
                OP("dve", lambda e, pg=pg, g=g: [e.tensor_tensor(out=Gm[0:L, g, 0:L], in0=pg[0:L, 0:L], in1=um, op=ALU.mult)],
                   reads=[kpg, "cf"], writes=["Gm"])
        yield
        if not prefix:
            ypss = []
            if sample:
                ypss = [PS("acc"), PS("acc")]
            def bc_issue(hb, mode):
                heads = [2 * hb, 2 * hb + 1]
                pbc, kpbc = PS("mm" if mode == "B" else "hd")
                OP("pe", lambda e, pbc=pbc, heads=heads: [
                    e.matmul(pbc[:, j * L:(j + 1) * L], lhsT=cbf[0:32, CB["sel"] + h * 128:CB["sel"] + (h + 1) * 128],
                             rhs=ac3[0:32, i, 0:L], start=(i == 0), stop=(i == 2)) for j, h in enumerate(heads) for i in range(3)],
                   reads=[kac3, "cbf"], writes=[kpbc])
                return pbc, kpbc

            def head_pair(hb, mode, yint=None, pre=None):
                g = hb // 4
                heads = [2 * hb, 2 * hb + 1]
                pbc, kpbc = pre if pre is not None else bc_issue(hb, mode)
                if mode != "A":
                    seg, ksg = SEG.next()
                    OP("dve", lambda e, pbc=pbc, heads=heads, seg=seg: [
                        e.tensor_scalar(out=seg[0:L, j, 0:L], in0=pbc[0:L, j * L:(j + 1) * L], scalar1=dtt[0:L, 224 + h:225 + h],
                                        scalar2=0.0, op0=ALU.add, op1=ALU.min) for j, h in enumerate(heads)],
                       reads=[kpbc, kd], writes=[ksg])
                    OP("act", lambda e, seg=seg: [e.activation(out=seg[0:L, :, 0:L], in_=seg[0:L, :, 0:L], func=AF.Exp)],
                       reads=[ksg], writes=[ksg])
                    mtb, kmt = MTB.next()
                    OP("dve", lambda e, seg=seg, mtb=mtb, heads=heads, g=g: [
                        e.scalar_tensor_tensor(out=mtb[0:L, j, 0:L], in0=seg[0:L, j, 0:L], scalar=dtt[0:L, 128 + h:129 + h],
                                               in1=Gm[0:L, g, 0:L], op0=ALU.mult, op1=ALU.mult) for j, h in enumerate(heads)],
                       reads=[ksg, kd, "Gm"], writes=[kmt])
                if mode == "B":
                    yp, kyp = yint[hb // 8]
                    ypv = yp[:, :].rearrange("p (c t) -> p c t", t=64)
                    OP("pe", lambda e, mtb=mtb, heads=heads, ypv=ypv, hb=hb: [
                        e.matmul(ypv[(h % 2) * 64:(h % 2) * 64 + 64, hb % 8, :], lhsT=xtm[0:L, h * 64:(h + 1) * 64],
                                 rhs=mtb[0:L, j, 0:L], start=True, stop=True) for j, h in enumerate(heads)],
                       reads=[kx, kmt], writes=[kyp])
                    return
                ebc, keb = EBC.next()
                OP("act", lambda e, pbc=pbc, ebc=ebc: [
                    e.activation(out=ebc[:, :, 0:L], in_=pbc[:, 0:2 * L].rearrange("p (j t) -> p j t", j=2), func=AF.Exp)],
                   reads=[kpbc], writes=[keb])
                if mode == "A":
                    cet, kce, ceo = Ce, "Ce", 2 * hb
                else:
                    cet, kce = CEP.next()
                    ceo = 0
                OP("dve", lambda e, ebc=ebc, hb=hb, g=g, cet=cet, ceo=ceo: [
                    e.tensor_tensor(out=cet[:, ceo:ceo + 2, 0:L], in0=ebc[:, :, 0:L],
                                    in1=xsT[:, 20 + g, col0:c1].unsqueeze(1).to_broadcast([128, 2, L]), op=ALU.mult)],
                   reads=[keb, "xsT"], writes=[kce])
                if mode == "A":
                    OP("dve", lambda e, ebc=ebc, hb=hb: [
                        e.tensor_copy(out=edec[:, 2 * hb:2 * hb + 2, :],
                                      in_=ebc[:, :, 0:64].rearrange("p j (b t) -> p j b t", t=4)[:, :, :, 3])],
                       reads=[keb], writes=["edec"])
                    return
                yp, kyp = PS("acc")
                def yfn(e, mtb=mtb, heads=heads, yp=yp, cet=cet):
                    r = []
                    for j, h in enumerate(heads):
                        o = yp[j * 64:j * 64 + 64, 0:L]
                        r.append(e.matmul(o, lhsT=xtm[0:L, h * 64:(h + 1) * 64], rhs=mtb[0:L, j, 0:L], start=True, stop=False))
                        r.append(e.matmul(o, lhsT=hTb[:, h * 64:(h + 1) * 64], rhs=cet[:, j, 0:L], start=False, stop=True))
                    return r
                OP("pe", yfn, reads=[kx, kmt, "hTb", kce], writes=[kyp])
                post_y(tile, col0, L, [(hb, yp[:, 0:L], kyp, None, None)])

            hmode = "A" if sample else "all"
            pre = bc_issue(0, hmode)
            for hb in range(16):
                nxt = bc_issue(hb + 1, hmode) if hb + 1 < 16 else None
                head_pair(hb, hmode, None, pre)
                pre = nxt
        yield
        if sample:
            OP("dve", lambda e: [
                e.tensor_tensor(out=xwall[0:64, :].rearrange("p (h d) -> p h d", d=64),
                                in0=xtm[0:64, 0:DIN].rearrange("p (h d) -> p h d", d=64),
                                in1=dtt[0:64, WW].unsqueeze(2).to_broadcast([64, 32, 64]), op=ALU.mult)],
               reads=[kx, kd], writes=["xwall"])
            for b in range(16):
                OP("sp", lambda e, b=b: [e.dma_start(out=hT[:, :], in_=sssm_d[b])], writes=["hT"], dma=("hld", 1))
                OP("act", lambda e: [e.activation(out=hTb[:, :], in_=hT[:, :], func=AF.Copy)], reads=["hT"], writes=["hTb"])
                for half in range(2):
                    yp, kyp = ypss[half]
                    ypv = yp[:, :].rearrange("p (c t) -> p c t", t=64)
                    OP("pe", lambda e, b=b, half=half, ypv=ypv: [
                        e.matmul(ypv[(h % 2) * 64:(h % 2) * 64 + 64, (h // 2) % 8, 4 * b:4 * b + 4],
                                 lhsT=hTb[:, h * 64:(h + 1) * 64], rhs=Ce[:, h, 4 * b:4 * b + 4], start=True, stop=True)
                        for h in range(16 * half, 16 * half + 16)],
                       reads=["hTb", "Ce"], writes=[kyp])
                for g in range(4):
                    bm, kbm = BM.next()
                    OP("dve", lambda e, bm=bm, g=g, b=b: [
                        e.tensor_scalar(out=bm[:, :], in0=xtm[0:64, DIN + g * 128:DIN + (g + 1) * 128],
                                        scalar1=cf[0:64, CF["sm"] + b:CF["sm"] + b + 1], scalar2=None, op0=ALU.mult)],
                       reads=[kx, "cf"], writes=[kbm])
                    pst, kps_ = PS("mm")
                    OP("pe", lambda e, bm=bm, g=g, pst=pst: [
                        e.matmul(pst[:, :], lhsT=bm[:, :], rhs=xwall[0:64, g * 512:(g + 1) * 512], start=True, stop=True)],
                       reads=[kbm, "xwall"], writes=[kps_])
                    hv = hT[:, g * 512:(g + 1) * 512].rearrange("p (h d) -> p h d", d=64)
                    OP("dve", lambda e, hv=hv, g=g, b=b: [
                        e.tensor_tensor(out=hv, in0=hv, in1=edec[:, 8 * g:8 * g + 8, b:b + 1].to_broadcast([128, 8, 64]), op=ALU.mult)],
                       reads=["hT", "edec"], writes=["hT"])
                    OP("dve", lambda e, g=g, pst=pst: [
                        e.tensor_tensor(out=hT[:, g * 512:(g + 1) * 512], in0=hT[:, g * 512:(g + 1) * 512], in1=pst[:, :], op=ALU.add)],
                       reads=["hT", kps_], writes=["hT"])
                OP("sp", lambda e, b=b: [e.dma_start(out=ssms_d[b], in_=hT[:, :])], reads=["hT"], writes=["o_ssms"], dma=("ost", 1))
            yint = [PS("hd"), PS("hd")]
            for hb in range(16):
                head_pair(hb, "B", yint)
            post_y(tile, col0, L, [(c, ypss[c // 8][0][:, (c % 8) * 64:(c % 8) * 64 + 64], ypss[c // 8][1],
                                    yint[c // 8][0][:, (c % 8) * 64:(c % 8) * 64 + 64], yint[c // 8][1]) for c in range(16)])
        else:
            for g in range(4):
                xw, kxw = XW.next()
                OP("dve", lambda e, xw=xw, g=g: [
                    e.tensor_tensor(out=xw[0:L, :].rearrange("p (h d) -> p h d", d=64),
                                    in0=xtm[0:L, g * 512:(g + 1) * 512].rearrange("p (h d) -> p h d", d=64),
                                    in1=dtt[0:L, 192 + 8 * g:192 + 8 * g + 8].unsqueeze(2).to_broadcast([L, 8, 64]), op=ALU.mult)],
                   reads=[kx, kd], writes=[kxw])
                pst, kps_ = PS("mm")
                OP("pe", lambda e, xw=xw, g=g, pst=pst: [
                    e.matmul(pst[:, :], lhsT=xtm[0:L, DIN + g * 128:DIN + (g + 1) * 128], rhs=xw[0:L, :], start=True, stop=True)],
                   reads=[kx, kxw], writes=[kps_])
                hv = hT[:, g * 512:(g + 1) * 512].rearrange("p (h d) -> p h d", d=64)
                OP("dve", lambda e, hv=hv, g=g: [
                    e.tensor_tensor(out=hv, in0=hv, in1=dtt[:, 256 + 8 * g:256 + 8 * g + 8].unsqueeze(2).to_broadcast([128, 8, 64]),
                                    op=ALU.mult)],
                   reads=["hT", kd], writes=["hT"])
                OP("dve", lambda e, g=g, pst=pst: [
                    e.tensor_tensor(out=hT[:, g * 512:(g + 1) * 512], in0=hT[:, g * 512:(g + 1) * 512], in1=pst[:, :], op=ALU.add)],
                   reads=["hT", kps_], writes=["hT"])
                OP("act", lambda e, g=g: [e.activation(out=hTb[:, g * 512:(g + 1) * 512], in_=hT[:, g * 512:(g + 1) * 512], func=AF.Copy)],
                   reads=["hT"], writes=["hTb"])

    ygstate = {}

    def post_y(tile, col0, L, items):
        c1 = col0 + L
        for (c, yap, kyp, yap2, kyp2) in items:
            y1, ky1 = Y1.next()
            OP("dve", lambda e, y1=y1, c=c, yap=yap: [
                e.scalar_tensor_tensor(out=y1[:, 0:L], in0=xsT[:, c, col0:c1], scalar=pcol("dsk", c), in1=yap,
                                       op0=ALU.mult, op1=ALU.add)],
               reads=["xsT", kyp, "pv"], writes=[ky1])
            if yap2 is not None:
                OP("dve", lambda e, y1=y1, yap2=yap2: [e.tensor_tensor(out=y1[:, 0:L], in0=y1[:, 0:L], in1=yap2, op=ALU.add)],
                   reads=[ky1, kyp2], writes=[ky1])
            if c % 4 == 0:
                ygstate["yg"] = YG.next()
                ygstate["ss"] = PS("mm")
            yg, kyg = ygstate["yg"]
            pss, kss = ygstate["ss"]
            OP("dve", lambda e, y1=y1, yg=yg, c=c: [
                e.tensor_tensor(out=yg[:, c % 4, 0:L], in0=y1[:, 0:L], in1=act[:, c, col0:c1], op=ALU.mult)],
               reads=[ky1, "sz"], writes=[kyg])
            sq, ksq = SQ.next()
            OP("act", lambda e, sq=sq, yg=yg, c=c: [e.activation(out=sq[:, 0, 0:L], in_=yg[:, c % 4, 0:L], func=AF.Square)],
               reads=[kyg], writes=[ksq])

            def pss_op(sq=sq, ksq=ksq, pss=pss, kss=kss, c=c):
                OP("pe", lambda e: [
                    e.matmul(pss[:, 0:L], lhsT=ones_bf, rhs=sq[:, 0, 0:L], start=(c % 4 == 0), stop=(c % 4 == 3))],
                   reads=[ksq, "cbf"], writes=[kss])
            if ygstate.get("pend") is not None:
                ygstate.pop("pend")()
            if c % 4 == 3:
                pss_op()
            else:
                ygstate["pend"] = pss_op
            if c % 4 == 3:
                rs, krs = RS.next()
                OP("act", lambda e, rs=rs, pss=pss: [
                    e.activation(out=rs[:, 0:L], in_=pss[:, 0:L], func=AF.Ln, bias=pcol("eps"), scale=1.0 / 512)],
                   reads=[kss, "pv"], writes=[krs])
                OP("act", lambda e, rs=rs: [e.activation(out=rs[:, 0:L], in_=rs[:, 0:L], func=AF.Exp, scale=-0.5)],
                   reads=[krs], writes=[krs])
                OP("dve", lambda e, rs=rs, yg=yg, c=c: [
                    e.scalar_tensor_tensor(out=act[:, c - 3 + k, col0:c1], in0=yg[:, k, 0:L], scalar=pcol("ng", c - 3 + k),
                                           in1=rs[:, 0:L], op0=ALU.mult, op1=ALU.mult) for k in range(4)],
                   reads=[kyg, krs, "pv"], writes=["sz"])

    def mixer_out(tile):
        for blk in range(4):
            wb, kw = W_get("wout", (blk,), WSLOT)
            wv = wb[:, :].rearrange("p (k n) -> p k n", k=16)
            for jj in range(2):
                dm = blk * 2 + jj
                for (c0, c1) in tile.subs:
                    n = c1 - c0
                    po, kpo = PS("mm")
                    OP("pe", mm_group(po[:, 0:n], [(wv[:, k, jj * 128:(jj + 1) * 128], act[:, k, c0:c1]) for k in range(16)]),
                       reads=["sz", kw], writes=[kpo])
                    OP("dve", lambda e, po=po, dm=dm, c0=c0, c1=c1, n=n: [
                        e.tensor_tensor(out=xT[:, dm, c0:c1], in0=po[:, 0:n], in1=xT[:, dm, c0:c1], op=ALU.add)],
                       reads=[kpo, "xT"], writes=["xT"])

    def mixer0(tile, prefix):
        mixer_in(tile, prefix)
        barrier()
        gens = [ssd_chunk(tile, ci, False, prefix) for ci in range(tile.nch)]
        next(gens[0])
        next(gens[0])
        for ci in range(tile.nch):
            nxt = gens[ci + 1] if ci + 1 < tile.nch else None
            if nxt is not None:
                next(nxt)
            next(gens[ci])
            if nxt is not None:
                next(nxt)
            for _ in gens[ci]:
                pass
        if tile.last:
            OP("sp", lambda e: [e.dma_start(out=ssmp_d[:, :], in_=hT[:, :])], reads=["hT"], writes=["o_ssmp"], dma=("ost", 1))
            OP("sp", lambda e: [e.dma_start(out=convp_d[:, :, :], in_=hist[:, :, :])], reads=["hist"], writes=["o_convp"], dma=("ost", 1))
        if tile.sample:
            for _ in ssd_chunk(tile, None, True, False):
                pass
            OP("sp", lambda e: [e.dma_start(out=convs_d[:, :, :, :], in_=convs[:, :, :, :])], reads=["sconv"], writes=["o_convs"],
               dma=("ost", 1))
        barrier()
        if not prefix and not (os.environ.get("KSKIP") == "out" and tile.kind == "main"):
            mixer_out(tile)

    def kv_proj(tile):
        rmsnorm(tile, PV["kvn"])
        wb, kw = W_get("wkv", (0,), WSLOT)
        wv = wb[:, :].rearrange("p (k n) -> p k n", k=8)
        for m in range(2):
            for (c0, c1) in tile.subs:
                n = c1 - c0
                pk, kpk = PS("mm")
                OP("pe", mm_group(pk[:, 0:n], [(wv[:, k, m * 128:(m + 1) * 128], hn[:, k, c0:c1]) for k in range(8)]),
                   reads=["hn", kw], writes=[kpk])
                OP("act", lambda e, pk=pk, m=m, c0=c0, c1=c1, n=n: [
                    e.activation(out=kT[:, m, 128 + c0:128 + c1], in_=pk[:, 0:n], func=AF.Identity, bias=pcol("bk", m))],
                   reads=[kpk, "pv"], writes=["kT"])
        chunks = [(ci * 128, 128, 1 + ci) for ci in range(tile.nch)] + ([(tile.Tp, 64, 5)] if tile.sample else [])
        for (c0, L, slot) in chunks:
            pvv, kpv = PS("mm")
            OP("pe", mm_group(pvv[0:L, 0:256], [(hn[:, k, c0:c0 + L], wv[:, k, 256:512]) for k in range(8)]),
               reads=["hn", kw], writes=[kpv])
            OP("dve", lambda e, pvv=pvv, L=L, slot=slot: [
                e.tensor_tensor(out=vtm[0:L, slot, :], in0=pvv[0:L, 0:256], in1=prow[0:L, PR["bv"]:PR["bv"] + 256], op=ALU.add)],
               reads=[kpv, "prow"], writes=["vtm"])
            is_lastp = tile.last and slot == tile.nch
            if is_lastp or slot == 5:
                pkk, kpkk = PS("mm")
                OP("pe", mm_group(pkk[0:L, 0:256], [(hn[:, k, c0:c0 + L], wv[:, k, 0:256]) for k in range(8)]),
                   reads=["hn", kw], writes=[kpkk])
                OP("dve", lambda e, pkk=pkk, L=L: [
                    e.tensor_tensor(out=kvf[0:L, 0:256], in0=pkk[0:L, 0:256], in1=prow[0:L, PR["bkr"]:PR["bkr"] + 256], op=ALU.add)],
                   reads=[kpkk, "prow"], writes=["kvf"])
                OP("dve", lambda e, pvv=pvv, L=L: [
                    e.tensor_tensor(out=kvf[0:L, 256:512], in0=pvv[0:L, 0:256], in1=prow[0:L, PR["bv"]:PR["bv"] + 256], op=ALU.add)],
                   reads=[kpv, "prow"], writes=["kvf"])
                if is_lastp:
                    OP("sp", lambda e: [e.dma_start(out=kwp_d[:, :], in_=kvf[:, 0:256]),
                                        e.dma_start(out=vwp_d[:, :], in_=kvf[:, 256:512])],
                       reads=["kvf"], writes=["o_kvp"], dma=("ost", 2))
                else:
                    OP("sp", lambda e: (
                        [e.dma_start(out=kws_d[b, 124:128, :], in_=kvf[4 * b:4 * b + 4, 0:256]) for b in range(16)]
                        + [e.dma_start(out=vws_d[b, 124:128, :], in_=kvf[4 * b:4 * b + 4, 256:512]) for b in range(16)]
                        + [e.dma_start(out=kws_d[:, 0:124, :], in_=ck_d[:, 4:128, :]),
                           e.dma_start(out=vws_d[:, 0:124, :], in_=cv_d[:, 4:128, :])]),
                       reads=["kvf"], writes=["o_kvs"], dma=("ost", 34))

    def kv_carry(tile):
        n = tile.nch
        OP("dve", lambda e: [e.tensor_copy(out=kT[:, :, 0:128], in_=kT[:, :, n * 128:n * 128 + 128])], reads=["kT"], writes=["kT"])
        OP("dve", lambda e: [e.tensor_copy(out=vtm[:, 0, :], in_=vtm[:, n, :])], reads=["vtm"], writes=["vtm"])

    qT = act
    OT0 = 8

    def attn_batch(items, nq, segs_n, masks, extra_bias_first=None, spool="hd", ptpool="mm", stage=None, ctx=None):
        nk = sum(segs_n)
        nb = len(items)
        if ctx is not None:
            psS = ctx
        else:
            psS = [PS(spool), PS(spool)]

        def sfn(e):
            r = []
            for j, it in enumerate(items):
                o = 0
                for si, n_ in enumerate(segs_n):
                    r.append(e.matmul(psS[j % 2][0][0:nq, (j // 2) * 256 + o:(j // 2) * 256 + o + n_], lhsT=it["q"], rhs=it["k"][si],
                                      start=True, stop=True))
                    o += n_
            return r
        AST = 99
        if stage != "rest":
            OP("pe", sfn, reads=["qT", "kT", "qs", "ckt0", "ckt1"], writes=[psS[0][1], psS[1][1]])
        if stage == "scores":
            return psS
        sm, ksm = SM.next()
        pn, kpn = PN.next()
        pt, kpt = PT.next()
        st, kst = STAT.next()
        OP("dve", lambda e: [
            e.scalar_tensor_tensor(out=sm[0:nq, j, m0:m0 + ml], in0=psS[j % 2][0][0:nq, (j // 2) * 256 + m0:(j // 2) * 256 + m0 + ml],
                                   scalar=0.125, in1=map_, op0=ALU.mult, op1=ALU.add) for j in range(nb) for (m0, ml, map_) in masks],
           reads=[psS[0][1], psS[1][1], "cf"], writes=[ksm])
        OP("dve", lambda e: [e.memset(st[0:nq, 0:nb, 2:3], 0.0)], writes=[kst + "a"])
        if extra_bias_first is not None:
            OP("dve", lambda e: [
                e.tensor_scalar(out=sm[0:nq, 0:nb, 0:128], in0=sm[0:nq, 0:nb, 0:128], scalar1=extra_bias_first, scalar2=None, op0=ALU.add)],
               reads=[ksm, "hbias"], writes=[ksm])
        if AST < 2:
            return
        OP("dve", lambda e: [e.reduce_max(out=st[0:nq, 0:nb, 0:1], in_=sm[0:nq, 0:nb, 0:nk], axis=AX.X)], reads=[ksm], writes=[kst])
        OP("dve", lambda e: [
            e.tensor_scalar(out=st[0:nq, j, 1:2], in0=st[0:nq, j, 0:1], scalar1=it["sink"], scalar2=-1.0, op0=ALU.max, op1=ALU.mult)
            for j, it in enumerate(items)], reads=[kst, "prow", "cf"], writes=[kst])
        if AST < 3:
            return
        OP("act", lambda e: [
            e.activation(out=sm[0:nq, j, 0:nk], in_=sm[0:nq, j, 0:nk], func=AF.Exp, bias=st[0:nq, j, 1:2], accum_out=st[0:nq, j, 2:3])
            for j in range(nb)], reads=[ksm, kst], writes=[ksm, kst + "a"])
        OP("act", lambda e: [
            e.activation(out=st[0:nq, j, 3:4], in_=st[0:nq, j, 1:2], func=AF.Exp, bias=it["sink"]) for j, it in enumerate(items)],
           reads=[kst, "prow", "cf"], writes=[kst + "b"])
        OP("dve", lambda e: [e.tensor_tensor(out=st[0:nq, 0:nb, 4:5], in0=st[0:nq, 0:nb, 2:3], in1=st[0:nq, 0:nb, 3:4], op=ALU.add)],
           reads=[kst + "a", kst + "b"], writes=[kst + "c"])
        OP("dve", lambda e: [e.reciprocal(out=st[0:nq, 0:nb, 5:6], in_=st[0:nq, 0:nb, 4:5])], reads=[kst + "c"], writes=[kst + "d"])
        OP("dve", lambda e: [
            e.tensor_scalar(out=pn[0:nq, j, 0:nk], in0=sm[0:nq, j, 0:nk], scalar1=st[0:nq, j, 5:6], scalar2=None, op0=ALU.mult)
            for j in range(nb)], reads=[ksm, kst + "d"], writes=[kpn])
        if AST < 4:
            return
        ptp, kptp = PS(ptpool)
        ptv = ptp[:, :].bitcast(BF16)

        def tfn(e):
            r = []
            for j in range(nb):
                o = 0
                for si, n_ in enumerate(segs_n):
                    r.append(e.transpose(ptv[0:n_, j * 256 + si * 128:j * 256 + si * 128 + nq], pn[0:nq, j, o:o + n_], ident[0:nq, 0:nq]))
                    o += n_
            return r
        OP("pe", tfn, reads=[kpn, "cbf"], writes=[kptp])
        nmax = max(segs_n)

        def cfn(e):
            if len(set(segs_n)) == 1:
                return [e.activation(out=pt[0:nmax, 0:nb, :], in_=ptv[0:nmax, 0:nb * 256].rearrange("p (j t) -> p j t", t=256), func=AF.Copy)]
            r = []
            for si, n_ in enumerate(segs_n):
                r.append(e.activation(out=pt[0:n_, 0:nb, si * 128:si * 128 + nq],
                                      in_=ptv[0:n_, 0:nb * 256].rearrange("p (j t) -> p j t", t=256)[:, :, si * 128:si * 128 + nq],
                                      func=AF.Copy))
            return r
        OP("act", cfn, reads=[kptp], writes=[kpt])

        if AST < 5:
            return

        def pvfn(e):
            r = []
            for j, it in enumerate(items):
                for si, n_ in enumerate(segs_n):
                    r.append(e.matmul(it["out"], lhsT=it["v"][si], rhs=pt[0:n_, j, si * 128:si * 128 + nq],
                                      start=(si == 0), stop=(si == len(segs_n) - 1)))
            return r
        OP("pe", pvfn, reads=[kpt, "vtm", "cvt0", "cvt1"], writes=sorted(set(it["okey"] for it in items)))

    def attention(tile):
        rmsnorm(tile, PV["g"] + 32)
        for blk in range(2):
            if os.environ.get("KSKIP2") == "aq":
                continue
            wb, kw = W_get("wq", (blk,), WSLOT)
            wv = wb[:, :].rearrange("p (k n) -> p k n", k=8)
            for jj in range(4):
                c = blk * 4 + jj
                for (c0, c1) in tile.subs:
                    n = c1 - c0
                    pq, kpq = PS("mm")
                    OP("pe", mm_group(pq[:, 0:n], [(wv[:, k, jj * 128:(jj + 1) * 128], hn[:, k, c0:c1]) for k in range(8)]),
                       reads=["hn", kw], writes=[kpq])
                    OP("act", lambda e, pq=pq, c=c, c0=c0, c1=c1, n=n: [
                        e.activation(out=qT[:, c, c0:c1], in_=pq[:, 0:n], func=AF.Identity, bias=pcol("bq", c))],
                       reads=[kpq, "pv"], writes=["qT"])
        maskb = cf[:, CF["mb"]:CF["mb"] + 256]
        barrier()
        for ci in range(tile.nch):
            if os.environ.get("KSKIP") == "ablk":
                continue
            q0 = ci * 128
            oA = PS("acc")
            oB = PS("acc")
            obanks = [oA, oB]
            for c in range(8):
                for half in range(1):
                    pass
            batches = []
            for bi in range(4):
                items = []
                for cc in (2 * bi, 2 * bi + 1):
                    for e_ in range(2):
                        g = 2 * (cc // 4) + e_
                        pr = slice(e_ * 64, e_ * 64 + 64)
                        ob, kob = obanks[cc // 4]
                        items.append(dict(
                            q=qT[pr, cc, q0:q0 + 128],
                            k=[kT[pr, cc // 4, q0:q0 + 128], kT[pr, cc // 4, q0 + 128:q0 + 256]],
                            v=[vtm[:, ci, g * 64:(g + 1) * 64], vtm[:, ci + 1, g * 64:(g + 1) * 64]],
                            sink=prow[:, PR["snk"] + 2 * cc + e_:PR["snk"] + 2 * cc + e_ + 1],
                            out=ob[pr, (cc % 4) * 128:(cc % 4) * 128 + 128], okey=kob))
                batches.append(items)
            xb = (hbias[:, 0:1] if (tile.first and ci == 0) else None)
            spools = ["hd", "hd2"]
            ctxs = [None] * 4
            ctxs[0] = attn_batch(batches[0], 128, [128, 128], [(0, 256, maskb)], spool=spools[0], stage="scores")
            for bi in range(4):
                if bi + 1 < 4:
                    ctxs[bi + 1] = attn_batch(batches[bi + 1], 128, [128, 128], [(0, 256, maskb)], spool=spools[(bi + 1) % 2], stage="scores")
                attn_batch(batches[bi], 128, [128, 128], [(0, 256, maskb)], extra_bias_first=xb, ptpool="pt", stage="rest", ctx=ctxs[bi])
            for hb_, (ob, kob) in enumerate(obanks):
                if os.environ.get("KSKIP3") == "evac":
                    continue
                OP("act", lambda e, ob=ob, hb_=hb_, q0=q0: [
                    e.activation(out=act[:, OT0 + 4 * hb_:OT0 + 4 * hb_ + 4, q0:q0 + 128],
                                 in_=ob[:, :].rearrange("p (c t) -> p c t", t=128), func=AF.Copy)],
                   reads=[kob], writes=["oT"])
        if tile.sample:
            Tp = tile.Tp
            OP("act", lambda e: [
                e.activation(out=qs[:, :, c, :], in_=qT[:, c, Tp:Tp + 64].rearrange("p (b t) -> p b t", t=4), func=AF.Copy)
                for c in range(8)], reads=["qT"], writes=["qs"])
            pvp, kpvp = PS("acc")
            pvv = pvp[:, :].rearrange("p (b c q) -> p b c q", b=16, c=2)
            maskC = cf[0:16, CF["ms"]:CF["ms"] + 128]
            for b in range(16):
                ckt, kck = CKT.next()
                cvt, kcv = CV.next()
                OP("pool", lambda e, ckt=ckt, b=b: [e.dma_start(out=ckt[:, :], in_=ckT_d[b])], writes=[kck], dma=(kck, 1))
                OP("pool", lambda e, cvt=cvt, b=b: [e.dma_start(out=cvt[:, :], in_=cv_d[b])], writes=[kcv], dma=(kcv, 1))
                items = []
                for g in range(4):
                    e_ = g % 2
                    pr = slice(e_ * 64, e_ * 64 + 64)
                    cc0 = 4 * (g // 2)
                    items.append(dict(
                        q=qs[pr, b, cc0:cc0 + 4, :].rearrange("p c t -> p (c t)"),
                        k=[ckt[pr, (g // 2) * 128:(g // 2) * 128 + 128], kT[pr, g // 2, 128 + Tp:128 + Tp + 64]],
                        v=[cvt[:, g * 64:(g + 1) * 64], vtm[0:64, 5, g * 64:(g + 1) * 64]],
                        sink=cf[0:16, CF["ss"] + g:CF["ss"] + g + 1],
                        out=pvv[pr, b, g // 2, :], okey=kpvp))
                mN = cf[0:16, CF["mn"] + 60 - 4 * b:CF["mn"] + 60 - 4 * b + 64]
                attn_batch(items, 16, [128, 64], [(0, 128, maskC), (128, 64, mN)])
            for cc in range(2):
                OP("act", lambda e, cc=cc: [
                    e.activation(out=act[:, OT0 + 4 * cc:OT0 + 4 * cc + 4, Tp:Tp + 64].rearrange("p i (b t) -> p b i t", t=4),
                                 in_=pvv[:, :, cc, :].rearrange("p b (i t) -> p b i t", t=4), func=AF.Copy)],
                   reads=[kpvp], writes=["oT"])
        barrier()
        for blk in range(2):
            if os.environ.get("KSKIP2") == "ao":
                continue
            wb, kw = W_get("wo", (blk,), WSLOT)
            wv = wb[:, :].rearrange("p (k n) -> p k n", k=8)
            for jj in range(4):
                dm = blk * 4 + jj
                for (c0, c1) in tile.subs:
                    if tile.first and c1 <= 128:
                        pass
                    n = c1 - c0
                    po, kpo = PS("mm")
                    OP("pe", mm_group(po[:, 0:n], [(wv[:, k, jj * 128:(jj + 1) * 128], act[:, OT0 + k, c0:c1]) for k in range(8)]),
                       reads=["oT", kw], writes=[kpo])
                    OP("dve", lambda e, po=po, dm=dm, c0=c0, c1=c1, n=n: [
                        e.scalar_tensor_tensor(out=xT[:, dm, c0:c1], in0=po[:, 0:n], scalar=pcol("bo", dm), in1=xT[:, dm, c0:c1],
                                               op0=ALU.add, op1=ALU.add)],
                       reads=[kpo, "xT", "pv"], writes=["xT"])

    def final_out(tile):
        for (c0, c1) in tile.subs:
            n = c1 - c0
            pss, kps = PS("mm")
            for cp in range(4):
                sq, ksq = SQ.next()
                OP("act", lambda e, sq=sq, cp=cp, c0=c0, c1=c1, n=n: [
                    e.activation(out=sq[:, :, 0:n], in_=xT[:, 2 * cp:2 * cp + 2, c0:c1], func=AF.Square)],
                   reads=["xT"], writes=[ksq])
                OP("pe", lambda e, sq=sq, cp=cp, pss=pss, n=n: [
                    e.matmul(pss[:, 0:n], lhsT=ones_bf, rhs=sq[:, j, 0:n], start=(cp == 0 and j == 0),
                             stop=(cp == 3 and j == 1)) for j in range(2)],
                   reads=[ksq, "cbf"], writes=[kps])
            rs, krs = RS.next()
            OP("act", lambda e, rs=rs, pss=pss, n=n: [
                e.activation(out=rs[:, 0:n], in_=pss[:, 0:n], func=AF.Ln, bias=pcol("eps"), scale=1.0 / D)],
               reads=[kps, "pv"], writes=[krs])
            OP("act", lambda e, rs=rs, n=n: [e.activation(out=rs[:, 0:n], in_=rs[:, 0:n], func=AF.Exp, scale=-0.5)],
               reads=[krs], writes=[krs])
            OP("dve", lambda e, rs=rs, c0=c0, c1=c1, n=n: [
                e.scalar_tensor_tensor(out=xT[:, c, c0:c1], in0=xT[:, c, c0:c1], scalar=pcol("fin", c), in1=rs[:, 0:n],
                                       op0=ALU.mult, op1=ALU.mult) for c in range(8)],
               reads=["xT", krs, "pv"], writes=["xT"])
        s0 = 0
        nout = tile.Tp
        OP("sp", lambda e: [e.dma_start(out=yT_d[:, :, tile.out0:tile.out0 + nout], in_=xT[:, :, s0:tile.Tp])],
           reads=["xT"], writes=["o_y"], dma=("ost", 1))
        if tile.sample:
            OP("sp", lambda e: [e.dma_start(out=ysT_d[:, :, :], in_=xT[:, :, tile.Tp:tile.T])], reads=["xT"], writes=["o_ys"],
               dma=("ost", 1))

    def prologue():
        OP("sp", lambda e: [e.dma_start(out=pv[:, :], in_=pv_d[:, :]), e.dma_start(out=prow[:, :], in_=prow_d[:, :]),
                            e.dma_start(out=cf[:, :], in_=cf_d[:, :]), e.dma_start(out=cbf[:, :], in_=cbf_d[:, :]),
                            e.dma_start(out=cmask[:, :], in_=cmask_d[:, :]), e.dma_start(out=sconv[:, :, :, :], in_=sconv_d[:, :, :, :])],
           writes=["pv", "prow", "cf", "cbf", "cmask", "sconv"], dma=("cld", 6))
        OP("pool", lambda e: [e.dma_start(out=wdt[:, :, :], in_=wdt_d[:, :].rearrange("p (k n) -> p k n", k=8))], writes=["wdt"],
           dma=("wdtl", 1))
        OP("act", lambda e: [e.activation(out=abc[:, :], in_=prow[:, PR["alog"]:PR["alog"] + 32], func=AF.Exp)], reads=["prow"], writes=["abc"])
        OP("dve", lambda e: [e.tensor_scalar(out=abc[:, :], in0=abc[:, :], scalar1=-1.0, scalar2=None, op0=ALU.mult)], reads=["abc"], writes=["abc"])
        OP("dve", lambda e: [e.tensor_scalar(out=hbias[:, :], in0=cmask[:, :], scalar1=-1.0, scalar2=-NEG, op0=ALU.add, op1=ALU.mult)],
           reads=["cmask"], writes=["hbias"])
        OP("dve", lambda e: [e.memset(hT[:, :], 0.0)], writes=["hT"])
        OP("dve", lambda e: [e.memset(hTb[:, :], 0.0)], writes=["hTb"])
        OP("dve", lambda e: [e.memset(hist[:, :, :], 0.0)], writes=["hist"])
        OP("dve", lambda e: [e.memset(kT[:, :, :], 0.0)], writes=["kT"])
        OP("dve", lambda e: [e.memset(vtm[:, :, :], 0.0)], writes=["vtm"])

    import os
    KSTOP = int(os.environ.get("KSTOP", "100000"))

    def program():
        ph = [0]

        def step():
            ph[0] += 1
            return ph[0] > KSTOP

        def finish(dump=None):
            if dump is not None and KSTOP < 100000:
                t = dump
                OP("sp", lambda e: [e.dma_start(out=yT_d[:, :, 0:t.Tp], in_=xT[:, :, 0:t.Tp])], reads=["xT"], writes=["o_y"], dma=("ost", 1))
            OP("sp", lambda e: [], reads=["o_y", "o_ys", "o_ssmp", "o_convp", "o_convs", "o_ssms", "o_kvp", "o_kvs"])
            if not wstate["dry"]:
                last = {}
                for i_, o_ in enumerate(S.ops[:-1]):
                    last[o_.eng if o_.dma is None else o_.dma[0]] = i_
                for i_ in last.values():
                    S.ops[-1].deps.setdefault(i_, 2)

        prologue()
        if step(): return finish()
        for tile in tiles:
            load_x(tile)
            if step(): return finish(tile)
            ffn(tile, 0)
            barrier()
            if step(): return finish(tile)
            if tile.kind == "pre":
                mixer0(tile, True)
                barrier()
                if step(): return finish(tile)
                continue
            mixer0(tile, False)
            barrier()
            if step(): return finish(tile)
            ffn(tile, 1)
            barrier()
            if step(): return finish(tile)
            kv_proj(tile)
            barrier()
            if step(): return finish(tile)
            if tile.kind == "prefull":
                kv_carry(tile)
                OP("dve", lambda e: [e.tensor_scalar(out=hT[:, :], in0=hT[:, :], scalar1=cmask[:, 0:1], scalar2=None, op0=ALU.mult)],
                   reads=["hT", "cmask"], writes=["hT"])
                OP("act", lambda e: [e.activation(out=hTb[:, :], in_=hT[:, :], func=AF.Copy)], reads=["hT"], writes=["hTb"])
                continue
            ffn(tile, 2)
            barrier()
            if step(): return finish(tile)
            attention(tile)
            barrier()
            if step(): return finish(tile)
            kv_carry(tile)
            ffn(tile, 3)
            barrier()
            if step(): return finish(tile)
            final_out(tile)
            barrier()
            if step(): return finish()
        finish()

    program()
    wstate["dry"] = False
    wstate["i"] = 0
    for k in pctr:
        pctr[k] = 0
    for r_ in (XTM, SEG, MTB, EBC, Y1, YG, SQ, RS, SG, XIN, XINS, CACC, DTT, AG3, TMPF, ACF, AC3, CEP, XW, BM, SM, PN, PT, STAT, CKT, CV):
        r_.i = 0
    program()
    S.finalize()

    with ExitStack() as es2:
        semh = {n: es2.enter_context(nc.semaphore(f"s_{n}")) for n in S.semnames}
        for n in ("pe", "act", "dve", "pool"):
            if n not in semh:
                semh[n] = es2.enter_context(nc.semaphore(f"s_{n}"))
        with nc.Block() as block:
            @block.sync
            def _(e):
                S.emit_engine("sp", e, semh)

            @block.gpsimd
            def _(e):
                S.emit_engine("pool", e, semh)

            @block.tensor
            def _(e):
                S.emit_engine("pe", e, semh)

            @block.scalar
            def _(e):
                S.emit_engine("act", e, semh)

            @block.vector
            def _(e):
                S.emit_engine("dve", e, semh)
    es.close()
    return nc, len(S.ops)


def tile_w(Wm, nb):
    K, N = Wm.shape
    a = Wm.reshape(K // 128, 128, N // nb, nb)
    return np.ascontiguousarray(a.transpose(2, 1, 0, 3)).reshape(N // nb, 128, (K // 128) * nb)


def pad_last(a, n):
    if a.shape[-1] == n:
        return a
    out = np.zeros(a.shape[:-1] + (n,), a.dtype)
    out[..., :a.shape[-1]] = a
    return out


def host_consts():
    cfa = np.zeros((128, NCF), np.float32)
    i = np.arange(128)
    um = (i[:, None] <= i[None, :]).astype(np.float32)
    cfa[:, CF["um"]:CF["um"] + 128] = um
    usf = (i[:, None] > i[None, :]).astype(np.float32)
    j = np.arange(64)
    same = (j[:, None] // 4) == (j[None, :] // 4)
    cfa[:64, CF["ums"]:CF["ums"] + 64] = (same & (j[:, None] <= j[None, :])).astype(np.float32)
    ussf = (same & (j[:, None] > j[None, :])).astype(np.float32)
    mb = np.full((128, 256), NEG, np.float32)
    mb[:, :128][i[None, :] > i[:, None]] = 0.0
    mb[:, 128:][i[None, :] <= i[:, None]] = 0.0
    cfa[:, CF["mb"]:CF["mb"] + 256] = mb
    ms = np.full((16, 128), NEG, np.float32)
    mn = np.full((16, 124), NEG, np.float32)
    for r in range(16):
        t = r % 4
        ms[r, t + 1:128] = 0.0
        mn[r, 60:60 + t + 1] = 0.0
    cfa[:16, CF["ms"]:CF["ms"] + 128] = ms
    cfa[:16, CF["mn"]:CF["mn"] + 124] = mn
    cfa[:64, CF["sm"]:CF["sm"] + 16] = (j[:, None] // 4 == np.arange(16)[None, :]).astype(np.float32)
    cb = np.zeros((128, NCB), np.float32)
    cb[:, :128] = np.eye(128)
    cb[:, 128:256] = 1.0
    cb[:, CB["um"]:CB["um"] + 128] = cfa[:, CF["um"]:CF["um"] + 128]
    cb[:, CB["us"]:CB["us"] + 128] = usf
    cb[:64, CB["ums"]:CB["ums"] + 64] = cfa[:64, CF["ums"]:CF["ums"] + 64]
    cb[:64, CB["uss"]:CB["uss"] + 64] = ussf
    sel = np.zeros((32, 32, 128), np.float32)
    for h in range(32):
        sel[h, h, :] = 1.0
    cb[:32, CB["sel"]:CB["sel"] + 4096] = sel.reshape(32, 4096)
    return cfa, cb.astype(ml_dtypes.bfloat16)


def host_weights(p):
    f32 = np.float32
    out = {}
    wgu = np.zeros((4, 11, 128, WSLOT), f32)
    wdn = np.zeros((4, 8, 128, NF * 128), f32)
    for fi in range(4):
        l, i = fi // 2, fi % 2
        g = tile_w(np.asarray(p["ffn_w_gate"][l, i]), 256)
        u = tile_w(np.asarray(p["ffn_w_up"][l, i]), 256)
        wgu[fi] = np.concatenate([g, u], axis=2)
        wdn[fi] = tile_w(np.asarray(p["ffn_w_down"][l, i]), 128)
    out["wgu"], out["wdn"] = wgu, wdn
    win = np.asarray(p["ssm_w_in"][0])
    out["winz"] = tile_w(win[:, 0:2048], 512)
    out["winx"] = tile_w(win[:, 2048:5120], 512)
    out["wdt"] = np.ascontiguousarray(tile_w(win[:, 5120:5152], 32)[0])
    out["wout"] = tile_w(np.asarray(p["ssm_w_out"][0]), 256)
    out["wkv"] = tile_w(np.asarray(p["attn_w_kv"]), 512)
    perm = np.concatenate([np.arange(QH[c][e] * 64, QH[c][e] * 64 + 64) for c in range(8) for e in range(2)])
    out["wq"] = tile_w(np.asarray(p["attn_w_q"][0])[:, perm], 512)
    out["wo"] = tile_w(np.asarray(p["attn_w_o"][0])[perm, :], 512)
    pvh = np.zeros((128, NPV), f32)
    fm = lambda v: np.asarray(v, f32).reshape(-1, 128).T
    ng = np.asarray(p["norm_gain"])
    for l in range(2):
        for i in range(3):
            pvh[:, PV["g"] + (l * 3 + i) * 8:PV["g"] + (l * 3 + i) * 8 + 8] = fm(ng[l, i])
    pvh[:, PV["kvn"]:PV["kvn"] + 8] = fm(p["kv_norm"])
    pvh[:, PV["fin"]:PV["fin"] + 8] = fm(p["final_norm"])
    cw = np.asarray(p["ssm_conv_w"][0])
    for k in range(4):
        pvh[:, PV["cw"] + k * 24:PV["cw"] + k * 24 + 24] = fm(cw[k])
    pvh[:, PV["cb"]:PV["cb"] + 24] = fm(p["ssm_conv_b"][0])
    pvh[:, PV["dsk"]:PV["dsk"] + 16] = fm(np.repeat(np.asarray(p["ssm_d"][0]), 64))
    pvh[:, PV["ng"]:PV["ng"] + 16] = fm(p["ssm_norm"][0])
    pvh[:, PV["bq"]:PV["bq"] + 8] = fm(np.asarray(p["attn_b_q"][0])[perm])
    bkv = np.asarray(p["attn_b_kv"], f32)
    pvh[:, PV["bk"]:PV["bk"] + 2] = fm(bkv[:256])
    pvh[:, PV["bo"]:PV["bo"] + 8] = fm(p["attn_b_o"][0])
    pvh[:, PV["one"]] = 1.0
    pvh[:, PV["eps"]] = EPS
    out["pv"] = pvh
    pr = np.zeros((128, NPR), f32)
    pr[:, PR["dtb"]:PR["dtb"] + 32] = np.asarray(p["ssm_dt_bias"][0])[None, :]
    pr[:, PR["alog"]:PR["alog"] + 32] = np.asarray(p["ssm_a_log"][0])[None, :]
    pr[:, PR["bv"]:PR["bv"] + 256] = bkv[None, 256:]
    pr[:, PR["bkr"]:PR["bkr"] + 256] = bkv[None, :256]
    sinks = np.asarray(p["attn_sinks"][0], f32)
    pr[:, PR["snk"]:PR["snk"] + 16] = np.array([sinks[QH[c][e]] for c in range(8) for e in range(2)], f32)[None, :]
    out["prow"] = pr
    cfa, cb = host_consts()
    for g in range(4):
        for r in range(16):
            cfa[r, CF["ss"] + g] = sinks[QH[4 * (g // 2) + r // 4][g % 2]]
    out["cf"], out["cbf"] = cfa, cb
    return out


_PROG = {}


def run(inputs, seq, npre, nmain):
    f32 = np.float32
    key = (npre, nmain)
    if key not in _PROG:
        _PROG[key] = build_program(npre, nmain)
    nc, nops = _PROG[key]
    w = host_weights(inputs)
    xp = np.asarray(inputs["x_prompt"], f32)
    xs = np.asarray(inputs["x_sample"], f32)
    sconv = np.asarray(inputs["state_conv"], f32)[0]
    sssm = np.asarray(inputs["state_ssm"], f32)[0]
    ck = np.asarray(inputs["cache_k_win"], f32)
    cv = np.asarray(inputs["cache_v_win"], f32)
    half = seq // 2
    in_maps = []
    for c in range(8):
        s, hf = c // 2, c % 2
        m = dict(w)
        if hf == 0:
            m["xpre"] = np.zeros((D, npre * 128), f32)
        else:
            m["xpre"] = np.ascontiguousarray(xp[s, 0:half].T)
        m["xmain"] = np.ascontiguousarray(xp[s, hf * half:(hf + 1) * half].T)
        m["cmask"] = np.full((128, 1), float(hf), f32)
        b0 = 16 * c
        m["xsmp"] = np.ascontiguousarray(xs[b0:b0 + 16].reshape(64, D).T)
        m["sconv"] = np.ascontiguousarray(sconv[b0:b0 + 16].reshape(16, 3, 24, 128).transpose(3, 2, 0, 1))
        m["sssm"] = np.ascontiguousarray(sssm[b0:b0 + 16].reshape(16, DIN, 128).transpose(0, 2, 1))
        kk = ck[b0:b0 + 16].reshape(16, 128, 2, 2, 64)
        m["ckT"] = np.ascontiguousarray(kk.transpose(0, 3, 4, 2, 1)).reshape(16, 128, 256)
        m["ck"] = np.ascontiguousarray(ck[b0:b0 + 16].reshape(16, 128, 256))
        m["cv"] = np.ascontiguousarray(cv[b0:b0 + 16].reshape(16, 128, 256))
        in_maps.append(m)
    if os.environ.get("KTRACE"):
        res = run_bass_kernel_spmd(nc, in_maps, core_ids=list(range(8)), trace=True)
        print("EXEC_TIME_NS", res.exec_time_ns)
    else:
        res = run_bass_kernel_spmd(nc, in_maps, core_ids=list(range(8)))
    R = res.results
    B = xp.shape[0]
    y_p = np.zeros((B, seq, D), f32)
    conv_p = np.zeros((1, B, 3, 3072), f32)
    ssm_p = np.zeros((1, B, 32, 64, 128), f32)
    k_p = np.zeros((B, 128, 4, 64), f32)
    v_p = np.zeros((B, 128, 4, 64), f32)
    y_s = np.zeros((128, 4, D), f32)
    conv_s = np.zeros((1, 128, 3, 3072), f32)
    ssm_s = np.zeros((1, 128, 32, 64, 128), f32)
    k_s = np.zeros((128, 128, 4, 64), f32)
    v_s = np.zeros((128, 128, 4, 64), f32)
    for c in range(8):
        s, hf = c // 2, c % 2
        r = R[c]
        y_p[s, hf * half:(hf + 1) * half] = r["yT"].T
        b0 = 16 * c
        y_s[b0:b0 + 16] = r["ysT"].T.reshape(16, 4, D)
        conv_s[0, b0:b0 + 16] = r["convs"].transpose(2, 3, 1, 0).reshape(16, 3, 3072)
        ssm_s[0, b0:b0 + 16] = r["ssms"].transpose(0, 2, 1).reshape(16, 32, 64, 128)
        k_s[b0:b0 + 16] = r["kws"].reshape(16, 128, 4, 64)
        v_s[b0:b0 + 16] = r["vws"].reshape(16, 128, 4, 64)
        if hf == 1:
            conv_p[0, s] = r["convp"].transpose(2, 1, 0).reshape(3, 3072)
            ssm_p[0, s] = r["ssmp"].T.reshape(32, 64, 128)
            k_p[s] = r["kwp"].reshape(128, 4, 64)
            v_p[s] = r["vwp"].reshape(128, 4, 64)
    return (y_p, y_s, conv_p, ssm_p, k_p, v_p, conv_s, ssm_s, k_s, v_s)


def kernel(**inputs):
    seq = int(np.asarray(inputs["x_prompt"]).shape[1])
    nchunks = seq // 128
    return run(inputs, seq, nchunks // 2, nchunks // 2)
```

```python
import os
import numpy as np
import ml_dtypes
from contextlib import ExitStack
import concourse.bass as bass
import concourse.mybir as mybir
from concourse.bass_utils import run_bass_kernel_spmd

F32, BF16 = mybir.dt.float32, mybir.dt.bfloat16
AF = mybir.ActivationFunctionType
ALU = mybir.AluOpType
AX = mybir.AxisListType

D = 1024
DFF = 2816
NF = 22
DIN = 2048
EPS = 1e-6
NEG = -30000.0
WSLOT = 4096
NWS = 3

QH = [[c, c + 4] if c < 4 else [c + 4, c + 8] for c in range(8)]


class Op:
    __slots__ = ("eng", "fn", "deps", "dma", "signal", "value")

    def __init__(self, eng, fn, deps, dma):
        self.eng, self.fn, self.deps, self.dma = eng, fn, deps, dma
        self.signal = False
        self.value = 0


class Sched:
    def __init__(self):
        self.ops = []
        self.lw = {}
        self.rd = {}

    def op(self, eng, fn, reads=(), writes=(), dma=None):
        idx = len(self.ops)
        deps = {}
        for k in reads:
            w = self.lw.get(k)
            if w is not None:
                deps[w] = 2
        for k in writes:
            w = self.lw.get(k)
            if w is not None:
                deps.setdefault(w, 1)
            r = self.rd.get(k)
            if r:
                for e_, i_ in r[0].items():
                    deps.setdefault(i_, 1)
                for i_ in r[1]:
                    deps.setdefault(i_, 1)
        for k in writes:
            self.lw[k] = idx
            self.rd[k] = [{}, []]
        for k in reads:
            r = self.rd.setdefault(k, [{}, []])
            if dma is None:
                r[0][eng] = idx
            else:
                r[1].append(idx)
        self.ops.append(Op(eng, fn, deps, dma))
        return idx

    def finalize(self):
        ops = self.ops
        for o in ops:
            need = []
            for d, kind in o.deps.items():
                p = ops[d]
                if p.dma is not None:
                    need.append(d)
                elif o.dma is None and p.eng == o.eng:
                    if o.eng != "pe" and kind == 2:
                        need.append(d)
                else:
                    need.append(d)
            o.deps = need
            for d in need:
                ops[d].signal = True
        cnt = {}
        for o in ops:
            if o.dma is not None:
                s, n = o.dma
                cnt[s] = cnt.get(s, 0) + 16 * n
                o.value = cnt[s]
            elif o.signal:
                cnt[o.eng] = cnt.get(o.eng, 0) + 1
                o.value = cnt[o.eng]
        self.semnames = sorted(cnt.keys())

    def emit_engine(self, engname, e, semh):
        ops = self.ops
        waited = {}
        for o in ops:
            if o.eng != engname:
                continue
            for d in o.deps:
                p = ops[d]
                s = p.dma[0] if p.dma is not None else p.eng
                if waited.get(s, 0) < p.value:
                    e.wait_ge(semh[s], p.value)
                    waited[s] = p.value
            ins = o.fn(e)
            if o.dma is not None:
                assert len(ins) == o.dma[1], (len(ins), o.dma)
                for i_ in ins:
                    i_.then_inc(semh[o.dma[0]], 16)
            elif o.signal:
                ins[-1].then_inc(semh[engname], 1)


class Rot:
    def __init__(self, tensors, name):
        self.t = tensors
        self.n = len(tensors)
        self.i = 0
        self.name = name

    def next(self):
        j = self.i % self.n
        self.i += 1
        return self.t[j], f"{self.name}{j}"


def subtiles(T):
    out = []
    c = 0
    while c < T:
        n = min(512, T - c)
        out.append((c, c + n))
        c += n
    return out


class Tile:
    def __init__(self, kind, src, col0, nch, sample=False, first=False, last=False, out0=0):
        self.kind = kind
        self.src = src
        self.col0 = col0
        self.nch = nch
        self.Tp = nch * 128
        self.sample = sample
        self.T = self.Tp + (64 if sample else 0)
        self.first = first
        self.last = last
        self.out0 = out0
        self.subs = subtiles(self.Tp) + ([(self.Tp, self.Tp + 64)] if sample else [])


def make_tiles(npre, nmain, tch=4):
    def split(n):
        k = -(-n // tch)
        base, rem = divmod(n, k)
        return [base + 1] * rem + [base] * (k - rem)
    tiles = []
    sp = split(npre)
    c = 0
    for i, n in enumerate(sp):
        tiles.append(Tile("prefull" if i == len(sp) - 1 else "pre", "xpre", c * 128, n))
        c += n
    sp = split(nmain)
    c = 0
    for i, n in enumerate(sp):
        tiles.append(Tile("main", "xmain", c * 128, n, sample=(i == len(sp) - 1), first=(i == 0),
                          last=(i == len(sp) - 1), out0=c * 128))
        c += n
    return tiles


PV = {}
_o = 0
for _n, _w in [("g", 48), ("kvn", 8), ("fin", 8), ("cw", 96), ("cb", 24), ("dsk", 16), ("ng", 16),
               ("bq", 8), ("bk", 2), ("bo", 8), ("one", 1), ("eps", 1)]:
    PV[_n] = _o
    _o += _w
NPV = _o
PR = {"dtb": 0, "alog": 32, "bv": 64, "bkr": 320, "snk": 576}
NPR = 592
CF = {"um": 0, "ums": 128, "mb": 192, "ms": 448, "mn": 576, "sm": 700, "ss": 716}
NCF = 720
CB = {"id": 0, "on": 128, "um": 256, "us": 384, "ums": 512, "uss": 576, "sel": 640}
NCB = 640 + 4096


def build_program(npre, nmain):
    tiles = make_tiles(npre, nmain)
    TMAX = max(t.T for t in tiles)
    nc = bass.Bass("TRN2", target_bir_lowering=False)
    S = Sched()

    def din(name, shape, dt=F32):
        return nc.dram_tensor(name, list(shape), dt, kind="ExternalInput").ap()

    def dout(name, shape):
        return nc.dram_tensor(name, list(shape), F32, kind="ExternalOutput").ap()

    NOUT = nmain * 128
    dr = {}
    dr["xpre"] = din("xpre", [D, npre * 128]).rearrange("(c p) t -> p c t", p=128)
    dr["xmain"] = din("xmain", [D, nmain * 128]).rearrange("(c p) t -> p c t", p=128)
    xsmp_d = din("xsmp", [D, 64]).rearrange("(c p) t -> p c t", p=128)
    wd = {
        "wgu": din("wgu", [4, 11, 128, WSLOT]),
        "wdn": din("wdn", [4, 8, 128, NF * 128]),
        "winz": din("winz", [4, 128, WSLOT]),
        "winx": din("winx", [6, 128, WSLOT]),
        "wout": din("wout", [4, 128, WSLOT]),
        "wkv": din("wkv", [1, 128, WSLOT]),
        "wq": din("wq", [2, 128, WSLOT]),
        "wo": din("wo", [2, 128, WSLOT]),
    }
    wdt_d = din("wdt", [128, 256])
    pv_d = din("pv", [128, NPV])
    prow_d = din("prow", [128, NPR])
    cf_d = din("cf", [128, NCF])
    cbf_d = din("cbf", [128, NCB], BF16)
    cmask_d = din("cmask", [128, 1])
    sconv_d = din("sconv", [128, 24, 16, 3])
    sssm_d = din("sssm", [16, 128, DIN])
    ckT_d = din("ckT", [16, 128, 256])
    ck_d = din("ck", [16, 128, 256])
    cv_d = din("cv", [16, 128, 256])
    yT_d = dout("yT", [D, NOUT]).rearrange("(c p) t -> p c t", p=128)
    ysT_d = dout("ysT", [D, 64]).rearrange("(c p) t -> p c t", p=128)
    convp_d = dout("convp", [128, 24, 3])
    ssmp_d = dout("ssmp", [128, DIN])
    kwp_d = dout("kwp", [128, 256])
    vwp_d = dout("vwp", [128, 256])
    convs_d = dout("convs", [128, 24, 16, 3])
    ssms_d = dout("ssms", [16, 128, DIN])
    kws_d = dout("kws", [16, 128, 256])
    vws_d = dout("vws", [16, 128, 256])

    es = ExitStack()

    def sb(name, shape, dt):
        return es.enter_context(nc.sbuf_tensor(name, list(shape), dt))

    xT = sb("xT", [128, 8, TMAX], F32)
    hn = sb("hn", [128, 8, TMAX], BF16)
    act = sb("act", [128, 40, TMAX], BF16)
    wsl = [sb(f"wsl{i}", [128, WSLOT], BF16) for i in range(NWS)]
    xsT = act[:, 16:40, :]
    XTM = Rot([sb(f"xtm{i}", [128, 2560], BF16) for i in range(2)], "xtm")
    hT = sb("hT", [128, DIN], F32)
    hTb = sb("hTb", [128, DIN], BF16)
    Ce = sb("Ce", [128, 32, 64], BF16)
    CEP = Rot([sb(f"cep{i}", [128, 2, 128], BF16) for i in range(2)], "cep")
    Gm = sb("Gm", [128, 4, 128], F32)
    SEG = Rot([sb(f"seg{i}", [128, 2, 128], F32) for i in range(2)], "seg")
    MTB = Rot([sb(f"mtb{i}", [128, 2, 128], BF16) for i in range(2)], "mtb")
    EBC = Rot([sb(f"ebc{i}", [128, 2, 128], F32) for i in range(2)], "ebc")
    edec = sb("edec", [128, 32, 16], F32)
    Y1 = Rot([sb(f"y1{i}", [128, 128], F32) for i in range(2)], "y1")
    YG = Rot([sb(f"yg{i}", [128, 4, 128], F32) for i in range(1)], "yg")
    SQ = Rot([sb(f"sq{i}", [128, 2, 512], BF16) for i in range(2)], "sq")
    RS = Rot([sb(f"rs{i}", [128, 512], F32) for i in range(1)], "rs")
    SG = Rot([sb(f"sg{i}", [128, 512], F32) for i in range(2)], "sg")
    XIN = Rot([sb(f"xin{i}", [128, 3 + TMAX], F32) for i in range(2)], "xin")
    XINS = Rot([sb(f"xins{i}", [128, 16, 7], F32) for i in range(2)], "xins")
    CACC = Rot([sb(f"cacc{i}", [128, TMAX], F32) for i in range(1)], "cacc")
    hist = sb("hist", [128, 24, 3], F32)
    sconv = sb("sconv_t", [128, 24, 16, 3], F32)
    convs = sconv
    DTT = Rot([sb(f"dtt{i}", [128, 320], F32) for i in range(2)], "dtt")
    AG3 = Rot([sb(f"ag3{i}", [128, 3, 32], BF16) for i in range(2)], "ag3")
    TMPF = Rot([sb(f"tmpf{i}", [128, 2, 32], F32) for i in range(2)], "tmpf")
    ACF = Rot([sb(f"acf{i}", [32, 3, 128], F32) for i in range(1)], "acf")
    AC3 = Rot([sb(f"ac3{i}", [32, 3, 128], BF16) for i in range(2)], "ac3")
    XW = Rot([sb(f"xw{i}", [128, 512], BF16) for i in range(2)], "xw")
    xwall = sb("xwall", [64, DIN], BF16)
    BM = Rot([sb(f"bm{i}", [64, 128], BF16) for i in range(2)], "bm")
    kT = sb("kT", [128, 2, 128 + TMAX], BF16)
    vtm = sb("vtm", [128, 6, 256], BF16)
    kvf = sb("kvf", [128, 512], F32)
    SM = Rot([sb(f"sm{i}", [128, 4, 256], F32) for i in range(1)], "sm")
    PN = Rot([sb(f"pn{i}", [128, 4, 256], BF16) for i in range(1)], "pn")
    PT = Rot([sb(f"pt{i}", [128, 4, 256], BF16) for i in range(1)], "pt")
    STAT = Rot([sb(f"stat{i}", [128, 4, 8], F32) for i in range(4)], "stat")
    qs = sb("qs", [128, 16, 8, 4], BF16)
    CKT = Rot([sb(f"ckt{i}", [128, 256], BF16) for i in range(2)], "ckt")
    CV = Rot([sb(f"cvt{i}", [128, 256], BF16) for i in range(2)], "cvt")
    wdt = sb("wdt_t", [128, 8, 32], BF16)
    pv = sb("pv_t", [128, NPV], F32)
    prow = sb("prow_t", [128, NPR], F32)
    abc = sb("abc", [128, 32], F32)
    cf = sb("cf_t", [128, NCF], F32)
    cbf = sb("cbf_t", [128, NCB], BF16)
    cmask = sb("cmask_t", [128, 1], F32)
    hbias = sb("hbias", [128, 1], F32)
    ps = [es.enter_context(nc.psum_tensor(f"ps{i}", [128, 512], F32)) for i in range(8)]
    pools = {"mm": [0, 1, 2, 3], "hd": [4, 5], "acc": [6, 7], "hd2": [2, 3], "pt": [0, 1]}
    pctr = {k_: 0 for k_ in pools}

    def PS(pool):
        l = pools[pool]
        i = l[pctr[pool] % len(l)]
        pctr[pool] += 1
        return ps[i], f"ps{i}"

    ident = cbf[:, 0:128]
    ones_bf = cbf[:, 128:256]
    pcol = lambda name, j=0: pv[:, PV[name] + j:PV[name] + j + 1]

    wseq = []
    wstate = {"dry": True, "i": 0, "issued": 0}

    def wdram(name, idx):
        a = wd[name]
        return a[idx[0], idx[1]] if len(idx) == 2 else a[idx[0]]

    def W_issue(upto):
        while wstate["issued"] <= min(upto, len(wseq) - 1):
            k = wstate["issued"]
            name, idx, nel = wseq[k]
            slot = wsl[k % NWS]
            src = wdram(name, idx)
            S.op("pool", (lambda e, slot=slot, src=src, nel=nel: [e.dma_start(out=slot[:, 0:nel], in_=src[:, 0:nel])]),
                 writes=[f"wsl{k % NWS}"], dma=(f"w{k % NWS}", 1))
            wstate["issued"] += 1

    def W_get(name, idx, nel):
        k = wstate["i"]
        wstate["i"] += 1
        if wstate["dry"]:
            wseq.append((name, idx, nel))
            return wsl[k % NWS], f"wsl{k % NWS}"
        assert wseq[k] == (name, idx, nel)
        W_issue(k + NWS - 1)
        return wsl[k % NWS], f"wsl{k % NWS}"

    def OP(eng, fn, reads=(), writes=(), dma=None):
        if wstate["dry"]:
            return
        S.op(eng, fn, reads, writes, dma)

    def barrier():
        if wstate["dry"] or os.environ.get("KNOBAR"):
            return
        last = {}
        for i_, o_ in enumerate(S.ops):
            if o_.dma is None and o_.eng in ("pe", "act", "dve"):
                last[o_.eng] = i_
        for eng in ("pe", "act", "dve"):
            S.op(eng, lambda e: [e.nop()])
            for en2, i_ in last.items():
                if en2 != eng:
                    S.ops[-1].deps[i_] = 2

    def mm_group(out_ap, pairs):
        def fn(e):
            r = []
            n = len(pairs)
            for i, (l, rr) in enumerate(pairs):
                r.append(e.matmul(out_ap, lhsT=l, rhs=rr, start=(i == 0), stop=(i == n - 1)))
            return r
        return fn

    def rmsnorm(tile, gbase, ndim=D):
        for (c0, c1) in tile.subs:
            n = c1 - c0
            pss, kps = PS("mm")
            for cp in range(4):
                sq, ksq = SQ.next()
                OP("act", lambda e, sq=sq, cp=cp, c0=c0, c1=c1, n=n: [
                    e.activation(out=sq[:, :, 0:n], in_=xT[:, 2 * cp:2 * cp + 2, c0:c1], func=AF.Square)],
                   reads=["xT"], writes=[ksq])
                OP("pe", lambda e, sq=sq, cp=cp, pss=pss, n=n: [
                    e.matmul(pss[:, 0:n], lhsT=ones_bf, rhs=sq[:, j, 0:n], start=(cp == 0 and j == 0),
                             stop=(cp == 3 and j == 1)) for j in range(2)],
                   reads=[ksq, "cbf"], writes=[kps])
            rs, krs = RS.next()
            OP("act", lambda e, rs=rs, pss=pss, n=n: [
                e.activation(out=rs[:, 0:n], in_=pss[:, 0:n], func=AF.Ln, bias=pcol("eps"), scale=1.0 / ndim)],
               reads=[kps, "pv"], writes=[krs])
            OP("act", lambda e, rs=rs, n=n: [
                e.activation(out=rs[:, 0:n], in_=rs[:, 0:n], func=AF.Exp, scale=-0.5)],
               reads=[krs], writes=[krs])
            OP("dve", lambda e, rs=rs, c0=c0, c1=c1, n=n: [
                e.scalar_tensor_tensor(out=hn[:, c, c0:c1], in0=xT[:, c, c0:c1], scalar=pv[:, gbase + c:gbase + c + 1],
                                       in1=rs[:, 0:n], op0=ALU.mult, op1=ALU.mult) for c in range(8)],
               reads=["xT", krs, "pv"], writes=["hn"])

    def ffn(tile, fi):
        rmsnorm(tile, PV["g"] + [0, 16, 24, 40][fi])
        for blk in range(11):
            wb, kw = W_get("wgu", (fi, blk), WSLOT)
            wv = wb[:, :].rearrange("p (a k n) -> p a k n", a=2, k=8)
            for jj in range(2):
                j = blk * 2 + jj
                for (c0, c1) in tile.subs:
                    n = c1 - c0
                    pg, kpg = PS("mm")
                    pu, kpu = PS("mm")
                    OP("pe", lambda e, wv=wv, jj=jj, c0=c0, c1=c1, n=n, pg=pg, pu=pu: (
                        mm_group(pg[:, 0:n], [(wv[:, 0, k, jj * 128:(jj + 1) * 128], hn[:, k, c0:c1]) for k in range(8)])(e)
                        + mm_group(pu[:, 0:n], [(wv[:, 1, k, jj * 128:(jj + 1) * 128], hn[:, k, c0:c1]) for k in range(8)])(e)),
                       reads=["hn", kw], writes=[kpg, kpu])
                    sg, ksg = SG.next()
                    OP("act", lambda e, sg=sg, pg=pg, n=n: [e.activation(out=sg[:, 0:n], in_=pg[:, 0:n], func=AF.Silu)],
                       reads=[kpg], writes=[ksg])
                    OP("dve", lambda e, sg=sg, pu=pu, j=j, c0=c0, c1=c1, n=n: [
                        e.tensor_tensor(out=act[:, j, c0:c1], in0=sg[:, 0:n], in1=pu[:, 0:n], op=ALU.mult)],
                       reads=[ksg, kpu], writes=["act"])
        for blk in range(8):
            wb, kw = W_get("wdn", (fi, blk), NF * 128)
            wv = wb[:, 0:NF * 128].rearrange("p (j n) -> p j n", j=NF)
            for (c0, c1) in tile.subs:
                n = c1 - c0
                pd, kpd = PS("mm")
                OP("pe", mm_group(pd[:, 0:n], [(wv[:, j, :], act[:, j, c0:c1]) for j in range(NF)]),
                   reads=["act", kw], writes=[kpd])
                OP("dve", lambda e, pd=pd, blk=blk, c0=c0, c1=c1, n=n: [
                    e.scalar_tensor_tensor(out=xT[:, blk, c0:c1], in0=pd[:, 0:n], scalar=0.5, in1=xT[:, blk, c0:c1],
                                           op0=ALU.mult, op1=ALU.add)],
                   reads=[kpd, "xT"], writes=["xT"])

    def load_x(tile):
        src = dr[tile.src]
        OP("sp", lambda e: [e.dma_start(out=xT[:, :, 0:tile.Tp], in_=src[:, :, tile.col0:tile.col0 + tile.Tp])],
           writes=["xT"], dma=("xld", 1))
        if tile.sample:
            OP("sp", lambda e: [e.dma_start(out=xT[:, :, tile.Tp:tile.T], in_=xsmp_d[:, :, :])],
               writes=["xT"], dma=("xld", 1))

    def conv_chunk(tile, m, pre_only):
        pass

    def mixer_in(tile, prefix):
        rmsnorm(tile, PV["g"] + 8)
        Tp = tile.Tp
        if not prefix:
            for blk in range(4):
                wb, kw = W_get("winz", (blk,), WSLOT)
                wv = wb[:, :].rearrange("p (k n) -> p k n", k=8)
                for jj in range(4):
                    m = blk * 4 + jj
                    for (c0, c1) in tile.subs:
                        n = c1 - c0
                        pz, kpz = PS("mm")
                        OP("pe", mm_group(pz[:, 0:n], [(wv[:, k, jj * 128:(jj + 1) * 128], hn[:, k, c0:c1]) for k in range(8)]),
                           reads=["hn", kw], writes=[kpz])
                        OP("act", lambda e, pz=pz, m=m, c0=c0, c1=c1, n=n: [
                            e.activation(out=act[:, m, c0:c1], in_=pz[:, 0:n], func=AF.Silu)],
                           reads=[kpz], writes=["sz"])
        nblk = 6
        for blk in range(nblk):
            wb, kw = W_get("winx", (blk,), WSLOT)
            wv = wb[:, :].rearrange("p (k n) -> p k n", k=8)
            for jj in range(4):
                m = blk * 4 + jj
                xin, kxin = XIN.next()
                xins, kxins = XINS.next()
                OP("dve", lambda e, xin=xin, m=m: [e.tensor_copy(out=xin[:, 0:3], in_=hist[:, m, :])],
                   reads=["hist"], writes=[kxin])
                for (c0, c1) in tile.subs:
                    n = c1 - c0
                    px, kpx = PS("mm")
                    OP("pe", mm_group(px[:, 0:n], [(wv[:, k, jj * 128:(jj + 1) * 128], hn[:, k, c0:c1]) for k in range(8)]),
                       reads=["hn", kw], writes=[kpx])
                    if c0 < Tp:
                        if c0 == 0:
                            cacc, kc = CACC.next()
                        OP("act", lambda e, px=px, xin=xin, c0=c0, c1=c1, n=n, cacc=cacc, m=m: [
                            e.activation(out=xin[:, 3 + c0:3 + c1], in_=px[:, 0:n], func=AF.Copy),
                            e.activation(out=cacc[:, c0:c1], in_=px[:, 0:n], func=AF.Identity,
                                         scale=pv[:, PV["cw"] + 3 * 24 + m:PV["cw"] + 3 * 24 + m + 1], bias=pcol("cb", m))],
                           reads=[kpx, "pv"], writes=[kxin, kc])
                    else:
                        OP("act", lambda e, px=px, xins=xins: [
                            e.activation(out=xins[:, :, 3:7], in_=px[:, 0:64].rearrange("p (b t) -> p b t", t=4), func=AF.Copy)],
                           reads=[kpx], writes=[kxins])
                wc = lambda k, m=m: pv[:, PV["cw"] + k * 24 + m:PV["cw"] + k * 24 + m + 1]
                for k in range(3):
                    OP("dve", lambda e, xin=xin, cacc=cacc, k=k, wc=wc: [
                        e.scalar_tensor_tensor(out=cacc[:, 0:Tp], in0=xin[:, k:k + Tp], scalar=wc(k), in1=cacc[:, 0:Tp],
                                               op0=ALU.mult, op1=ALU.add)],
                       reads=[kxin, kc, "pv"], writes=[kc])
                OP("act", lambda e, cacc=cacc, m=m: [e.activation(out=xsT[:, m, 0:Tp], in_=cacc[:, 0:Tp], func=AF.Silu)],
                   reads=[kc], writes=["xsT"])
                OP("dve", lambda e, xin=xin, m=m: [e.tensor_copy(out=hist[:, m, :], in_=xin[:, Tp:Tp + 3])],
                   reads=[kxin], writes=["hist"])
                if tile.sample:
                    OP("dve", lambda e, xins=xins, m=m: [e.tensor_copy(out=xins[:, :, 0:3], in_=sconv[:, m, :, :])],
                       reads=["sconv"], writes=[kxins])
                    cacc, kc = CACC.next()
                    cv3 = lambda cacc=cacc: cacc[:, 0:64].rearrange("p (b t) -> p b t", t=4)
                    OP("dve", lambda e, xins=xins, cv3=cv3, m=m, wc=wc: [
                        e.tensor_scalar(out=cv3(), in0=xins[:, :, 3:7], scalar1=wc(3), scalar2=pcol("cb", m),
                                        op0=ALU.mult, op1=ALU.add)],
                       reads=[kxins, "pv"], writes=[kc])
                    for k in range(3):
                        OP("dve", lambda e, xins=xins, cv3=cv3, k=k, wc=wc: [
                            e.scalar_tensor_tensor(out=cv3(), in0=xins[:, :, k:k + 4], scalar=wc(k), in1=cv3(),
                                                   op0=ALU.mult, op1=ALU.add)],
                           reads=[kxins, kc, "pv"], writes=[kc])
                    OP("act", lambda e, cacc=cacc, m=m: [e.activation(out=xsT[:, m, Tp:Tp + 64], in_=cacc[:, 0:64], func=AF.Silu)],
                       reads=[kc], writes=["xsT"])
                    OP("dve", lambda e, xins=xins, m=m: [e.tensor_copy(out=convs[:, m, :, :], in_=xins[:, :, 4:7])],
                       reads=[kxins], writes=["sconv"])

    def ssd_chunk(tile, ci, sample, prefix):
        L = 64 if sample else 128
        col0 = tile.Tp if sample else ci * 128
        c1 = col0 + L
        um = cf[0:L, CF["ums"]:CF["ums"] + L] if sample else cf[0:L, CF["um"]:CF["um"] + L]
        xtm, kx = XTM.next()
        for grp in range(3):
            ms = list(range(grp * 8, min(grp * 8 + 8, 20)))
            pb, kpb = PS("mm")
            pbv = pb[:, :].bitcast(BF16)
            OP("pe", lambda e, ms=ms, pbv=pbv: [
                e.transpose(pbv[0:L, j * 128:(j + 1) * 128], xsT[:, m, col0:c1], ident) for j, m in enumerate(ms)],
               reads=["xsT", "cbf"], writes=[kpb])
            eng = "act" if grp % 2 == 0 else "dve"
            w = len(ms) * 128
            if eng == "act":
                OP("act", lambda e, pbv=pbv, grp=grp, w=w: [
                    e.activation(out=xtm[0:L, grp * 1024:grp * 1024 + w], in_=pbv[0:L, 0:w], func=AF.Copy)],
                   reads=[kpb], writes=[kx])
            else:
                OP("dve", lambda e, pbv=pbv, grp=grp, w=w: [
                    e.tensor_copy(out=xtm[0:L, grp * 1024:grp * 1024 + w], in_=pbv[0:L, 0:w])],
                   reads=[kpb], writes=[kx])
        dtt, kd = DTT.next()
        pdt, kpd = PS("mm")
        OP("pe", mm_group(pdt[0:L, 0:32], [(hn[:, k, col0:c1], wdt[:, k, :]) for k in range(8)]),
           reads=["hn", "wdt"], writes=[kpd])
        XB, AXc, EX, LN1, DT, AG, WW, NAC = [slice(32 * i, 32 * i + 32) for i in range(8)]
        OP("dve", lambda e: [e.tensor_tensor(out=dtt[0:L, XB], in0=pdt[0:L, 0:32], in1=prow[0:L, PR["dtb"]:PR["dtb"] + 32], op=ALU.add)],
           reads=[kpd, "prow"], writes=[kd])
        OP("act", lambda e: [e.activation(out=dtt[0:L, AXc], in_=dtt[0:L, XB], func=AF.Abs)],
           reads=[kd], writes=[kd])
        OP("act", lambda e: [e.activation(out=dtt[0:L, EX], in_=dtt[0:L, AXc], func=AF.Exp, scale=-1.0)],
           reads=[kd], writes=[kd])
        OP("act", lambda e: [e.activation(out=dtt[0:L, LN1], in_=dtt[0:L, EX], func=AF.Ln, bias=pv[0:L, PV["one"]:PV["one"] + 1])],
           reads=[kd, "pv"], writes=[kd])
        OP("dve", lambda e: [e.scalar_tensor_tensor(out=dtt[0:L, DT], in0=dtt[0:L, XB], scalar=0.0, in1=dtt[0:L, LN1],
                                                    op0=ALU.max, op1=ALU.add)],
           reads=[kd], writes=[kd])
        OP("dve", lambda e: [e.tensor_tensor(out=dtt[0:L, AG], in0=dtt[0:L, DT], in1=abc[0:L, :], op=ALU.mult)],
           reads=[kd, "abc"], writes=[kd])
        ag3, kag = AG3.next()
        tmpf, ktf = TMPF.next()

        def split3(src, dst3, tmp, rk, wk, tk, P, n):
            OP("dve", lambda e: [e.tensor_copy(out=dst3[0:P, 0, 0:n], in_=src)], reads=[rk], writes=[wk])
            OP("dve", lambda e: [e.tensor_tensor(out=tmp[0:P, 0, 0:n], in0=src, in1=dst3[0:P, 0, 0:n], op=ALU.subtract)],
               reads=[rk, wk], writes=[tk])
            OP("dve", lambda e: [e.tensor_copy(out=dst3[0:P, 1, 0:n], in_=tmp[0:P, 0, 0:n])], reads=[tk], writes=[wk])
            OP("dve", lambda e: [e.tensor_tensor(out=tmp[0:P, 1, 0:n], in0=tmp[0:P, 0, 0:n], in1=dst3[0:P, 1, 0:n], op=ALU.subtract)],
               reads=[tk, wk], writes=[tk])
            OP("dve", lambda e: [e.tensor_copy(out=dst3[0:P, 2, 0:n], in_=tmp[0:P, 1, 0:n])], reads=[tk], writes=[wk])
        split3(dtt[0:L, AG], ag3, tmpf, kd, kag, ktf, L, 32)
        OP("dve", lambda e: [
            e.tensor_tensor(out=xtm[0:L, 0:DIN].rearrange("p (h d) -> p h d", d=64),
                            in0=xtm[0:L, 0:DIN].rearrange("p (h d) -> p h d", d=64),
                            in1=dtt[0:L, DT].unsqueeze(2).to_broadcast([L, 32, 64]), op=ALU.mult)],
           reads=[kx, kd], writes=[kx])
        umb = cbf[0:L, CB["ums"]:CB["ums"] + L] if sample else cbf[0:L, CB["um"]:CB["um"] + L]
        usb = cbf[0:L, CB["uss"]:CB["uss"] + L] if sample else cbf[0:L, CB["us"]:CB["us"] + L]
        yield
        pc, kpc = PS("mm")
        def cums(e):
            r = []
            for i in range(3):
                r.append(e.matmul(pc[0:L, 0:32], lhsT=usb, rhs=ag3[0:L, i, :], start=(i == 0), stop=(i == 2)))
            for i in range(3):
                r.append(e.matmul(pc[0:L, 32:64], lhsT=umb, rhs=ag3[0:L, i, :], start=(i == 0), stop=(i == 2)))
            for i in range(3):
                r.append(e.matmul(pc[0:32, 64:64 + L], lhsT=ag3[0:L, i, :], rhs=umb, start=(i == 0), stop=(i == 2)))
            if not sample:
                for i in range(3):
                    r.append(e.matmul(pc[:, 192:224], lhsT=ones_bf[0:L, :], rhs=ag3[0:L, i, :], start=(i == 0), stop=(i == 2)))
            return r
        OP("pe", cums, reads=[kag, "cbf"], writes=[kpc])
        OP("act", lambda e: [e.activation(out=dtt[0:L, WW], in_=pc[0:L, 0:32], func=AF.Exp)], reads=[kpc], writes=[kd])
        OP("dve", lambda e: [e.tensor_scalar(out=dtt[0:L, NAC], in0=pc[0:L, 32:64], scalar1=-1.0, scalar2=None, op0=ALU.mult)],
           reads=[kpc], writes=[kd])
        ETOT = slice(256, 288)
        if not sample:
            OP("act", lambda e: [e.activation(out=dtt[:, ETOT], in_=pc[:, 192:224], func=AF.Exp)], reads=[kpc], writes=[kd])
        if not prefix:
            acf, kacf = ACF.next()
            ac3, kac3 = AC3.next()
            OP("act", lambda e: [e.activation(out=acf[0:32, 0, 0:L], in_=pc[0:32, 64:64 + L], func=AF.Copy)], reads=[kpc], writes=[kacf])
            split3(acf[0:32, 0, 0:L], ac3, acf[:, 1:3, :], kacf, kac3, kacf + "t", 32, L)
            for g in range(4):
                pg, kpg = PS("mm")
                OP("pe", lambda e, pg=pg, g=g: [e.matmul(pg[0:L, 0:L], lhsT=xsT[:, 16 + g, col0:c1], rhs=xsT[:, 20 + g, col0:c1],
                                                         start=True, stop=True)],
                   reads=["xsT"], writes=[kpg])
                OP("dve", lambda e, pg=pg, g=g: [e.tensor_tensor(out=Gm[0:L, g, 0:L], in0=pg[0:L, 0:L], in1=um, op=ALU.mult)],
                   reads=[kpg, "cf"], writes=["Gm"])
        yield
        if not prefix:
            ypss = []
            if sample:
                ypss = [PS("acc"), PS("acc")]
            def bc_issue(hb, mode):
                heads = [2 * hb, 2 * hb + 1]
                pbc, kpbc = PS("mm" if mode == "B" else "hd")
                OP("pe", lambda e, pbc=pbc, heads=heads: [
                    e.matmul(pbc[:, j * L:(j + 1) * L], lhsT=cbf[0:32, CB["sel"] + h * 128:CB["sel"] + (h + 1) * 128],
                             rhs=ac3[0:32, i, 0:L], start=(i == 0), stop=(i == 2)) for j, h in enumerate(heads) for i in range(3)],
                   reads=[kac3, "cbf"], writes=[kpbc])
                return pbc, kpbc

            def head_pair(hb, mode, yint=None, pre=None):
                g = hb // 4
                heads = [2 * hb, 2 * hb + 1]
                pbc, kpbc = pre if pre is not None else bc_issue(hb, mode)
                if mode != "A":
                    seg, ksg = SEG.next()
                    OP("act", lambda e, pbc=pbc, heads=heads, seg=seg: [
                        e.activation(out=seg[0:L, j, 0:L], in_=pbc[0:L, j * L:(j + 1) * L], func=AF.Exp, bias=dtt[0:L, 224 + h:225 + h])
                        for j, h in enumerate(heads)], reads=[kpbc, kd], writes=[ksg])
                    mtb, kmt = MTB.next()
                    OP("dve", lambda e, seg=seg, mtb=mtb, heads=heads, g=g: [
                        e.scalar_tensor_tensor(out=mtb[0:L, j, 0:L], in0=seg[0:L, j, 0:L], scalar=1.0,
                                               in1=Gm[0:L, g, 0:L], op0=ALU.min, op1=ALU.mult) for j, h in enumerate(heads)],
                       reads=[ksg, "Gm"], writes=[kmt])
                if mode == "B":
                    yp, kyp = yint[hb // 8]
                    ypv = yp[:, :].rearrange("p (c t) -> p c t", t=64)
                    OP("pe", lambda e, mtb=mtb, heads=heads, ypv=ypv, hb=hb: [
                        e.matmul(ypv[(h % 2) * 64:(h % 2) * 64 + 64, hb % 8, :], lhsT=xtm[0:L, h * 64:(h + 1) * 64],
                                 rhs=mtb[0:L, j, 0:L], start=True, stop=True) for j, h in enumerate(heads)],
                       reads=[kx, kmt], writes=[kyp])
                    return
                ebc, keb = EBC.next()
                OP("act", lambda e, pbc=pbc, ebc=ebc: [
                    e.activation(out=ebc[:, :, 0:L], in_=pbc[:, 0:2 * L].rearrange("p (j t) -> p j t", j=2), func=AF.Exp)],
                   reads=[kpbc], writes=[keb])
                if mode == "A":
                    cet, kce, ceo = Ce, "Ce", 2 * hb
                else:
                    cet, kce = CEP.next()
                    ceo = 0
                OP("dve", lambda e, ebc=ebc, hb=hb, g=g, cet=cet, ceo=ceo: [
                    e.tensor_tensor(out=cet[:, ceo:ceo + 2, 0:L], in0=ebc[:, :, 0:L],
                                    in1=xsT[:, 20 + g, col0:c1].unsqueeze(1).to_broadcast([128, 2, L]), op=ALU.mult)],
                   reads=[keb, "xsT"], writes=[kce])
                if mode == "A":
                    OP("dve", lambda e, ebc=ebc, hb=hb: [
                        e.tensor_copy(out=edec[:, 2 * hb:2 * hb + 2, :],
                                      in_=ebc[:, :, 0:64].rearrange("p j (b t) -> p j b t", t=4)[:, :, :, 3])],
                       reads=[keb], writes=["edec"])
                    return
                yp, kyp = PS("acc")
                def yfn(e, mtb=mtb, heads=heads, yp=yp, cet=cet):
                    r = []
                    for j, h in enumerate(heads):
                        o = yp[j * 64:j * 64 + 64, 0:L]
                        r.append(e.matmul(o, lhsT=xtm[0:L, h * 64:(h + 1) * 64], rhs=mtb[0:L, j, 0:L], start=True, stop=False))
                        r.append(e.matmul(o, lhsT=hTb[:, h * 64:(h + 1) * 64], rhs=cet[:, j, 0:L], start=False, stop=True))
                    return r
                OP("pe", yfn, reads=[kx, kmt, "hTb", kce], writes=[kyp])
                post_y(tile, col0, L, [(hb, yp[:, 0:L], kyp, None, None)])

            hmode = "A" if sample else "all"
            pre = bc_issue(0, hmode)
            for hb in range(16):
                nxt = bc_issue(hb + 1, hmode) if hb + 1 < 16 else None
                head_pair(hb, hmode, None, pre)
                pre = nxt
        yield
        if sample:
            OP("dve", lambda e: [
                e.tensor_tensor(out=xwall[0:64, :].rearrange("p (h d) -> p h d", d=64),
                                in0=xtm[0:64, 0:DIN].rearrange("p (h d) -> p h d", d=64),
                                in1=dtt[0:64, WW].unsqueeze(2).to_broadcast([64, 32, 64]), op=ALU.mult)],
               reads=[kx, kd], writes=["xwall"])
            for b in range(16):
                OP("sp", lambda e, b=b: [e.dma_start(out=hT[:, :], in_=sssm_d[b])], writes=["hT"], dma=("hld", 1))
                OP("act", lambda e: [e.activation(out=hTb[:, :], in_=hT[:, :], func=AF.Copy)], reads=["hT"], writes=["hTb"])
                for half in range(2):
                    yp, kyp = ypss[half]
                    ypv = yp[:, :].rearrange("p (c t) -> p c t", t=64)
                    OP("pe", lambda e, b=b, half=half, ypv=ypv: [
                        e.matmul(ypv[(h % 2) * 64:(h % 2) * 64 + 64, (h // 2) % 8, 4 * b:4 * b + 4],
                                 lhsT=hTb[:, h * 64:(h + 1) * 64], rhs=Ce[:, h, 4 * b:4 * b + 4], start=True, stop=True)
                        for h in range(16 * half, 16 * half + 16)],
                       reads=["hTb", "Ce"], writes=[kyp])
                for g in range(4):
                    bm, kbm = BM.next()
                    OP("dve", lambda e, bm=bm, g=g, b=b: [
                        e.tensor_scalar(out=bm[:, :], in0=xtm[0:64, DIN + g * 128:DIN + (g + 1) * 128],
                                        scalar1=cf[0:64, CF["sm"] + b:CF["sm"] + b + 1], scalar2=None, op0=ALU.mult)],
                       reads=[kx, "cf"], writes=[kbm])
                    pst, kps_ = PS("mm")
                    OP("pe", lambda e, bm=bm, g=g, pst=pst: [
                        e.matmul(pst[:, :], lhsT=bm[:, :], rhs=xwall[0:64, g * 512:(g + 1) * 512], start=True, stop=True)],
                       reads=[kbm, "xwall"], writes=[kps_])
                    hv = hT[:, g * 512:(g + 1) * 512].rearrange("p (h d) -> p h d", d=64)
                    OP("dve", lambda e, hv=hv, g=g, b=b: [
                        e.tensor_tensor(out=hv, in0=hv, in1=edec[:, 8 * g:8 * g + 8, b:b + 1].to_broadcast([128, 8, 64]), op=ALU.mult)],
                       reads=["hT", "edec"], writes=["hT"])
                    OP("dve", lambda e, g=g, pst=pst: [
                        e.tensor_tensor(out=hT[:, g * 512:(g + 1) * 512], in0=hT[:, g * 512:(g + 1) * 512], in1=pst[:, :], op=ALU.add)],
                       reads=["hT", kps_], writes=["hT"])
                OP("sp", lambda e, b=b: [e.dma_start(out=ssms_d[b], in_=hT[:, :])], reads=["hT"], writes=["o_ssms"], dma=("ost", 1))
            yint = [PS("hd"), PS("hd")]
            for hb in range(16):
                head_pair(hb, "B", yint)
            post_y(tile, col0, L, [(c, ypss[c // 8][0][:, (c % 8) * 64:(c % 8) * 64 + 64], ypss[c // 8][1],
                                    yint[c // 8][0][:, (c % 8) * 64:(c % 8) * 64 + 64], yint[c // 8][1]) for c in range(16)])
        else:
            for g in range(4):
                xw, kxw = XW.next()
                OP("dve", lambda e, xw=xw, g=g: [
                    e.tensor_tensor(out=xw[0:L, :].rearrange("p (h d) -> p h d", d=64),
                                    in0=xtm[0:L, g * 512:(g + 1) * 512].rearrange("p (h d) -> p h d", d=64),
                                    in1=dtt[0:L, 192 + 8 * g:192 + 8 * g + 8].unsqueeze(2).to_broadcast([L, 8, 64]), op=ALU.mult)],
                   reads=[kx, kd], writes=[kxw])
                pst, kps_ = PS("mm")
                OP("pe", lambda e, xw=xw, g=g, pst=pst: [
                    e.matmul(pst[:, :], lhsT=xtm[0:L, DIN + g * 128:DIN + (g + 1) * 128], rhs=xw[0:L, :], start=True, stop=True)],
                   reads=[kx, kxw], writes=[kps_])
                hv = hT[:, g * 512:(g + 1) * 512].rearrange("p (h d) -> p h d", d=64)
                OP("dve", lambda e, hv=hv, g=g: [
                    e.tensor_tensor(out=hv, in0=hv, in1=dtt[:, 256 + 8 * g:256 + 8 * g + 8].unsqueeze(2).to_broadcast([128, 8, 64]),
                                    op=ALU.mult)],
                   reads=["hT", kd], writes=["hT"])
                OP("dve", lambda e, g=g, pst=pst: [
                    e.tensor_tensor(out=hT[:, g * 512:(g + 1) * 512], in0=hT[:, g * 512:(g + 1) * 512], in1=pst[:, :], op=ALU.add)],
                   reads=["hT", kps_], writes=["hT"])
                OP("act", lambda e, g=g: [e.activation(out=hTb[:, g * 512:(g + 1) * 512], in_=hT[:, g * 512:(g + 1) * 512], func=AF.Copy)],
                   reads=["hT"], writes=["hTb"])

    ygstate = {}

    def post_y(tile, col0, L, items):
        c1 = col0 + L
        for (c, yap, kyp, yap2, kyp2) in items:
            y1, ky1 = Y1.next()
            OP("dve", lambda e, y1=y1, c=c, yap=yap: [
                e.scalar_tensor_tensor(out=y1[:, 0:L], in0=xsT[:, c, col0:c1], scalar=pcol("dsk", c), in1=yap,
                                       op0=ALU.mult, op1=ALU.add)],
               reads=["xsT", kyp, "pv"], writes=[ky1])
            if yap2 is not None:
                OP("dve", lambda e, y1=y1, yap2=yap2: [e.tensor_tensor(out=y1[:, 0:L], in0=y1[:, 0:L], in1=yap2, op=ALU.add)],
                   reads=[ky1, kyp2], writes=[ky1])
            if c % 4 == 0:
                ygstate["yg"] = YG.next()
                ygstate["ss"] = PS("mm")
            yg, kyg = ygstate["yg"]
            pss, kss = ygstate["ss"]
            OP("dve", lambda e, y1=y1, yg=yg, c=c: [
                e.tensor_tensor(out=yg[:, c % 4, 0:L], in0=y1[:, 0:L], in1=act[:, c, col0:c1], op=ALU.mult)],
               reads=[ky1, "sz"], writes=[kyg])
            sq, ksq = SQ.next()
            OP("act", lambda e, sq=sq, yg=yg, c=c: [e.activation(out=sq[:, 0, 0:L], in_=yg[:, c % 4, 0:L], func=AF.Square)],
               reads=[kyg], writes=[ksq])

            def pss_op(sq=sq, ksq=ksq, pss=pss, kss=kss, c=c):
                OP("pe", lambda e: [
                    e.matmul(pss[:, 0:L], lhsT=ones_bf, rhs=sq[:, 0, 0:L], start=(c % 4 == 0), stop=(c % 4 == 3))],
                   reads=[ksq, "cbf"], writes=[kss])
            if ygstate.get("pend") is not None:
                ygstate.pop("pend")()
            if c % 4 == 3:
                pss_op()
            else:
                ygstate["pend"] = pss_op
            if c % 4 == 3:
                rs, krs = RS.next()
                OP("act", lambda e, rs=rs, pss=pss: [
                    e.activation(out=rs[:, 0:L], in_=pss[:, 0:L], func=AF.Ln, bias=pcol("eps"), scale=1.0 / 512)],
                   reads=[kss, "pv"], writes=[krs])
                OP("act", lambda e, rs=rs: [e.activation(out=rs[:, 0:L], in_=rs[:, 0:L], func=AF.Exp, scale=-0.5)],
                   reads=[krs], writes=[krs])
                OP("dve", lambda e, rs=rs, yg=yg, c=c: [
                    e.scalar_tensor_tensor(out=act[:, c - 3 + k, col0:c1], in0=yg[:, k, 0:L], scalar=pcol("ng", c - 3 + k),
                                           in1=rs[:, 0:L], op0=ALU.mult, op1=ALU.mult) for k in range(4)],
                   reads=[kyg, krs, "pv"], writes=["sz"])

    def mixer_out(tile):
        for blk in range(4):
            wb, kw = W_get("wout", (blk,), WSLOT)
            wv = wb[:, :].rearrange("p (k n) -> p k n", k=16)
            for jj in range(2):
                dm = blk * 2 + jj
                for (c0, c1) in tile.subs:
                    n = c1 - c0
                    po, kpo = PS("mm")
                    OP("pe", mm_group(po[:, 0:n], [(wv[:, k, jj * 128:(jj + 1) * 128], act[:, k, c0:c1]) for k in range(16)]),
                       reads=["sz", kw], writes=[kpo])
                    OP("dve", lambda e, po=po, dm=dm, c0=c0, c1=c1, n=n: [
                        e.tensor_tensor(out=xT[:, dm, c0:c1], in0=po[:, 0:n], in1=xT[:, dm, c0:c1], op=ALU.add)],
                       reads=[kpo, "xT"], writes=["xT"])

    def mixer0(tile, prefix):
        mixer_in(tile, prefix)
        barrier()
        gens = [ssd_chunk(tile, ci, False, prefix) for ci in range(tile.nch)]
        next(gens[0])
        next(gens[0])
        for ci in range(tile.nch):
            nxt = gens[ci + 1] if ci + 1 < tile.nch else None
            if nxt is not None:
                next(nxt)
            next(gens[ci])
            if nxt is not None:
                next(nxt)
            for _ in gens[ci]:
                pass
        if tile.last:
            OP("sp", lambda e: [e.dma_start(out=ssmp_d[:, :], in_=hT[:, :])], reads=["hT"], writes=["o_ssmp"], dma=("ost", 1))
            OP("sp", lambda e: [e.dma_start(out=convp_d[:, :, :], in_=hist[:, :, :])], reads=["hist"], writes=["o_convp"], dma=("ost", 1))
        if tile.sample:
            for _ in ssd_chunk(tile, None, True, False):
                pass
            OP("sp", lambda e: [e.dma_start(out=convs_d[:, :, :, :], in_=convs[:, :, :, :])], reads=["sconv"], writes=["o_convs"],
               dma=("ost", 1))
        barrier()
        if not prefix and not (os.environ.get("KSKIP") == "out" and tile.kind == "main"):
            mixer_out(tile)

    def kv_proj(tile):
        rmsnorm(tile, PV["kvn"])
        wb, kw = W_get("wkv", (0,), WSLOT)
        wv = wb[:, :].rearrange("p (k n) -> p k n", k=8)
        for m in range(2):
            for (c0, c1) in tile.subs:
                n = c1 - c0
                pk, kpk = PS("mm")
                OP("pe", mm_group(pk[:, 0:n], [(wv[:, k, m * 128:(m + 1) * 128], hn[:, k, c0:c1]) for k in range(8)]),
                   reads=["hn", kw], writes=[kpk])
                OP("act", lambda e, pk=pk, m=m, c0=c0, c1=c1, n=n: [
                    e.activation(out=kT[:, m, 128 + c0:128 + c1], in_=pk[:, 0:n], func=AF.Identity, bias=pcol("bk", m))],
                   reads=[kpk, "pv"], writes=["kT"])
        chunks = [(ci * 128, 128, 1 + ci) for ci in range(tile.nch)] + ([(tile.Tp, 64, 5)] if tile.sample else [])
        for (c0, L, slot) in chunks:
            pvv, kpv = PS("mm")
            OP("pe", mm_group(pvv[0:L, 0:256], [(hn[:, k, c0:c0 + L], wv[:, k, 256:512]) for k in range(8)]),
               reads=["hn", kw], writes=[kpv])
            OP("dve", lambda e, pvv=pvv, L=L, slot=slot: [
                e.tensor_tensor(out=vtm[0:L, slot, :], in0=pvv[0:L, 0:256], in1=prow[0:L, PR["bv"]:PR["bv"] + 256], op=ALU.add)],
               reads=[kpv, "prow"], writes=["vtm"])
            is_lastp = tile.last and slot == tile.nch
            if is_lastp or slot == 5:
                pkk, kpkk = PS("mm")
                OP("pe", mm_group(pkk[0:L, 0:256], [(hn[:, k, c0:c0 + L], wv[:, k, 0:256]) for k in range(8)]),
                   reads=["hn", kw], writes=[kpkk])
                OP("dve", lambda e, pkk=pkk, L=L: [
                    e.tensor_tensor(out=kvf[0:L, 0:256], in0=pkk[0:L, 0:256], in1=prow[0:L, PR["bkr"]:PR["bkr"] + 256], op=ALU.add)],
                   reads=[kpkk, "prow"], writes=["kvf"])
                OP("dve", lambda e, pvv=pvv, L=L: [
                    e.tensor_tensor(out=kvf[0:L, 256:512], in0=pvv[0:L, 0:256], in1=prow[0:L, PR["bv"]:PR["bv"] + 256], op=ALU.add)],
                   reads=[kpv, "prow"], writes=["kvf"])
                if is_lastp:
                    OP("sp", lambda e: [e.dma_start(out=kwp_d[:, :], in_=kvf[:, 0:256]),
                                        e.dma_start(out=vwp_d[:, :], in_=kvf[:, 256:512])],
                       reads=["kvf"], writes=["o_kvp"], dma=("ost", 2))
                else:
                    OP("sp", lambda e: (
                        [e.dma_start(out=kws_d[b, 124:128, :], in_=kvf[4 * b:4 * b + 4, 0:256]) for b in range(16)]
                        + [e.dma_start(out=vws_d[b, 124:128, :], in_=kvf[4 * b:4 * b + 4, 256:512]) for b in range(16)]
                        + [e.dma_start(out=kws_d[:, 0:124, :], in_=ck_d[:, 4:128, :]),
                           e.dma_start(out=vws_d[:, 0:124, :], in_=cv_d[:, 4:128, :])]),
                       reads=["kvf"], writes=["o_kvs"], dma=("ost", 34))

    def kv_carry(tile):
        n = tile.nch
        OP("dve", lambda e: [e.tensor_copy(out=kT[:, :, 0:128], in_=kT[:, :, n * 128:n * 128 + 128])], reads=["kT"], writes=["kT"])
        OP("dve", lambda e: [e.tensor_copy(out=vtm[:, 0, :], in_=vtm[:, n, :])], reads=["vtm"], writes=["vtm"])

    qT = act
    OT0 = 8

    def attn_batch(items, nq, segs_n, masks, extra_bias_first=None, spool="hd", ptpool="mm", stage=None, ctx=None):
        nk = sum(segs_n)
        nb = len(items)
        if ctx is not None:
            psS = ctx
        else:
            psS = [PS(spool), PS(spool)]

        def sfn(e):
            r = []
            for j, it in enumerate(items):
                o = 0
                for si, n_ in enumerate(segs_n):
                    r.append(e.matmul(psS[j % 2][0][0:nq, (j // 2) * 256 + o:(j // 2) * 256 + o + n_], lhsT=it["q"], rhs=it["k"][si],
                                      start=True, stop=True))
                    o += n_
            return r
        AST = 99
        if stage != "rest":
            OP("pe", sfn, reads=["qT", "kT", "qs", "ckt0", "ckt1"], writes=[psS[0][1], psS[1][1]])
        if stage == "scores":
            return psS
        sm, ksm = SM.next()
        pn, kpn = PN.next()
        pt, kpt = PT.next()
        st, kst = STAT.next()
        OP("dve", lambda e: [
            e.scalar_tensor_tensor(out=sm[0:nq, j, m0:m0 + ml], in0=psS[j % 2][0][0:nq, (j // 2) * 256 + m0:(j // 2) * 256 + m0 + ml],
                                   scalar=0.125, in1=map_, op0=ALU.mult, op1=ALU.add) for j in range(nb) for (m0, ml, map_) in masks],
           reads=[psS[0][1], psS[1][1], "cf"], writes=[ksm])
        OP("dve", lambda e: [e.memset(st[0:nq, 0:nb, 2:3], 0.0)], writes=[kst + "a"])
        if extra_bias_first is not None:
            OP("dve", lambda e: [
                e.tensor_scalar(out=sm[0:nq, 0:nb, 0:128], in0=sm[0:nq, 0:nb, 0:128], scalar1=extra_bias_first, scalar2=None, op0=ALU.add)],
               reads=[ksm, "hbias"], writes=[ksm])
        if AST < 2:
            return
        OP("dve", lambda e: [e.reduce_max(out=st[0:nq, 0:nb, 0:1], in_=sm[0:nq, 0:nb, 0:nk], axis=AX.X)], reads=[ksm], writes=[kst])
        OP("dve", lambda e: [
            e.tensor_scalar(out=st[0:nq, j, 1:2], in0=st[0:nq, j, 0:1], scalar1=it["sink"], scalar2=-1.0, op0=ALU.max, op1=ALU.mult)
            for j, it in enumerate(items)], reads=[kst, "prow", "cf"], writes=[kst])
        if AST < 3:
            return
        OP("act", lambda e: [
            e.activation(out=sm[0:nq, j, 0:nk], in_=sm[0:nq, j, 0:nk], func=AF.Exp, bias=st[0:nq, j, 1:2], accum_out=st[0:nq, j, 2:3])
            for j in range(nb)], reads=[ksm, kst], writes=[ksm, kst + "a"])
        OP("act", lambda e: [
            e.activation(out=st[0:nq, j, 3:4], in_=st[0:nq, j, 1:2], func=AF.Exp, bias=it["sink"]) for j, it in enumerate(items)],
           reads=[kst, "prow", "cf"], writes=[kst + "b"])
        OP("dve", lambda e: [e.tensor_tensor(out=st[0:nq, 0:nb, 4:5], in0=st[0:nq, 0:nb, 2:3], in1=st[0:nq, 0:nb, 3:4], op=ALU.add)],
           reads=[kst + "a", kst + "b"], writes=[kst + "c"])
        OP("dve", lambda e: [e.reciprocal(out=st[0:nq, 0:nb, 5:6], in_=st[0:nq, 0:nb, 4:5])], reads=[kst + "c"], writes=[kst + "d"])
        OP("dve", lambda e: [
            e.tensor_scalar(out=pn[0:nq, j, 0:nk], in0=sm[0:nq, j, 0:nk], scalar1=st[0:nq, j, 5:6], scalar2=None, op0=ALU.mult)
            for j in range(nb)], reads=[ksm, kst + "d"], writes=[kpn])
        if AST < 4:
            return
        ptp, kptp = PS(ptpool)
        ptv = ptp[:, :].bitcast(BF16)

        def tfn(e):
            r = []
            for j in range(nb):
                o = 0
                for si, n_ in enumerate(segs_n):
                    r.append(e.transpose(ptv[0:n_, j * 256 + si * 128:j * 256 + si * 128 + nq], pn[0:nq, j, o:o + n_], ident[0:nq, 0:nq]))
                    o += n_
            return r
        OP("pe", tfn, reads=[kpn, "cbf"], writes=[kptp])
        nmax = max(segs_n)

        def cfn(e):
            if len(set(segs_n)) == 1:
                return [e.activation(out=pt[0:nmax, 0:nb, :], in_=ptv[0:nmax, 0:nb * 256].rearrange("p (j t) -> p j t", t=256), func=AF.Copy)]
            r = []
            for si, n_ in enumerate(segs_n):
                r.append(e.activation(out=pt[0:n_, 0:nb, si * 128:si * 128 + nq],
                                      in_=ptv[0:n_, 0:nb * 256].rearrange("p (j t) -> p j t", t=256)[:, :, si * 128:si * 128 + nq],
                                      func=AF.Copy))
            return r
        OP("act", cfn, reads=[kptp], writes=[kpt])

        if AST < 5:
            return

        def pvfn(e):
            r = []
            for j, it in enumerate(items):
                for si, n_ in enumerate(segs_n):
                    r.append(e.matmul(it["out"], lhsT=it["v"][si], rhs=pt[0:n_, j, si * 128:si * 128 + nq],
                                      start=(si == 0), stop=(si == len(segs_n) - 1)))
            return r
        OP("pe", pvfn, reads=[kpt, "vtm", "cvt0", "cvt1"], writes=sorted(set(it["okey"] for it in items)))

    def attention(tile):
        rmsnorm(tile, PV["g"] + 32)
        for blk in range(2):
            if os.environ.get("KSKIP2") == "aq":
                continue
            wb, kw = W_get("wq", (blk,), WSLOT)
            wv = wb[:, :].rearrange("p (k n) -> p k n", k=8)
            for jj in range(4):
                c = blk * 4 + jj
                for (c0, c1) in tile.subs:
                    n = c1 - c0
                    pq, kpq = PS("mm")
                    OP("pe", mm_group(pq[:, 0:n], [(wv[:, k, jj * 128:(jj + 1) * 128], hn[:, k, c0:c1]) for k in range(8)]),
                       reads=["hn", kw], writes=[kpq])
                    OP("act", lambda e, pq=pq, c=c, c0=c0, c1=c1, n=n: [
                        e.activation(out=qT[:, c, c0:c1], in_=pq[:, 0:n], func=AF.Identity, bias=pcol("bq", c))],
                       reads=[kpq, "pv"], writes=["qT"])
        maskb = cf[:, CF["mb"]:CF["mb"] + 256]
        barrier()
        for ci in range(tile.nch):
            if os.environ.get("KSKIP") == "ablk":
                continue
            q0 = ci * 128
            oA = PS("acc")
            oB = PS("acc")
            obanks = [oA, oB]
            for c in range(8):
                for half in range(1):
                    pass
            batches = []
            for bi in range(4):
                items = []
                for cc in (2 * bi, 2 * bi + 1):
                    for e_ in range(2):
                        g = 2 * (cc // 4) + e_
                        pr = slice(e_ * 64, e_ * 64 + 64)
                        ob, kob = obanks[cc // 4]
                        items.append(dict(
                            q=qT[pr, cc, q0:q0 + 128],
                            k=[kT[pr, cc // 4, q0:q0 + 128], kT[pr, cc // 4, q0 + 128:q0 + 256]],
                            v=[vtm[:, ci, g * 64:(g + 1) * 64], vtm[:, ci + 1, g * 64:(g + 1) * 64]],
                            sink=prow[:, PR["snk"] + 2 * cc + e_:PR["snk"] + 2 * cc + e_ + 1],
                            out=ob[pr, (cc % 4) * 128:(cc % 4) * 128 + 128], okey=kob))
                batches.append(items)
            xb = (hbias[:, 0:1] if (tile.first and ci == 0) else None)
            spools = ["hd", "hd2"]
            ctxs = [None] * 4
            ctxs[0] = attn_batch(batches[0], 128, [128, 128], [(0, 256, maskb)], spool=spools[0], stage="scores")
            for bi in range(4):
                if bi + 1 < 4:
                    ctxs[bi + 1] = attn_batch(batches[bi + 1], 128, [128, 128], [(0, 256, maskb)], spool=spools[(bi + 1) % 2], stage="scores")
                attn_batch(batches[bi], 128, [128, 128], [(0, 256, maskb)], extra_bias_first=xb, ptpool="pt", stage="rest", ctx=ctxs[bi])
            for hb_, (ob, kob) in enumerate(obanks):
                if os.environ.get("KSKIP3") == "evac":
                    continue
                OP("act", lambda e, ob=ob, hb_=hb_, q0=q0: [
                    e.activation(out=act[:, OT0 + 4 * hb_:OT0 + 4 * hb_ + 4, q0:q0 + 128],
                                 in_=ob[:, :].rearrange("p (c t) -> p c t", t=128), func=AF.Copy)],
                   reads=[kob], writes=["oT"])
        if tile.sample:
            Tp = tile.Tp
            OP("act", lambda e: [
                e.activation(out=qs[:, :, c, :], in_=qT[:, c, Tp:Tp + 64].rearrange("p (b t) -> p b t", t=4), func=AF.Copy)
                for c in range(8)], reads=["qT"], writes=["qs"])
            pvp, kpvp = PS("acc")
            pvv = pvp[:, :].rearrange("p (b c q) -> p b c q", b=16, c=2)
            maskC = cf[0:16, CF["ms"]:CF["ms"] + 128]
            for b in range(16):
                ckt, kck = CKT.next()
                cvt, kcv = CV.next()
                OP("pool", lambda e, ckt=ckt, b=b: [e.dma_start(out=ckt[:, :], in_=ckT_d[b])], writes=[kck], dma=(kck, 1))
                OP("pool", lambda e, cvt=cvt, b=b: [e.dma_start(out=cvt[:, :], in_=cv_d[b])], writes=[kcv], dma=(kcv, 1))
                items = []
                for g in range(4):
                    e_ = g % 2
                    pr = slice(e_ * 64, e_ * 64 + 64)
                    cc0 = 4 * (g // 2)
                    items.append(dict(
                        q=qs[pr, b, cc0:cc0 + 4, :].rearrange("p c t -> p (c t)"),
                        k=[ckt[pr, (g // 2) * 128:(g // 2) * 128 + 128], kT[pr, g // 2, 128 + Tp:128 + Tp + 64]],
                        v=[cvt[:, g * 64:(g + 1) * 64], vtm[0:64, 5, g * 64:(g + 1) * 64]],
                        sink=cf[0:16, CF["ss"] + g:CF["ss"] + g + 1],
                        out=pvv[pr, b, g // 2, :], okey=kpvp))
                mN = cf[0:16, CF["mn"] + 60 - 4 * b:CF["mn"] + 60 - 4 * b + 64]
                attn_batch(items, 16, [128, 64], [(0, 128, maskC), (128, 64, mN)])
            for cc in range(2):
                OP("act", lambda e, cc=cc: [
                    e.activation(out=act[:, OT0 + 4 * cc:OT0 + 4 * cc + 4, Tp:Tp + 64].rearrange("p i (b t) -> p b i t", t=4),
                                 in_=pvv[:, :, cc, :].rearrange("p b (i t) -> p b i t", t=4), func=AF.Copy)],
                   reads=[kpvp], writes=["oT"])
        barrier()
        for blk in range(2):
            if os.environ.get("KSKIP2") == "ao":
                continue
            wb, kw = W_get("wo", (blk,), WSLOT)
            wv = wb[:, :].rearrange("p (k n) -> p k n", k=8)
            for jj in range(4):
                dm = blk * 4 + jj
                for (c0, c1) in tile.subs:
                    if tile.first and c1 <= 128:
                        pass
                    n = c1 - c0
                    po, kpo = PS("mm")
                    OP("pe", mm_group(po[:, 0:n], [(wv[:, k, jj * 128:(jj + 1) * 128], act[:, OT0 + k, c0:c1]) for k in range(8)]),
                       reads=["oT", kw], writes=[kpo])
                    OP("dve", lambda e, po=po, dm=dm, c0=c0, c1=c1, n=n: [
                        e.scalar_tensor_tensor(out=xT[:, dm, c0:c1], in0=po[:, 0:n], scalar=pcol("bo", dm), in1=xT[:, dm, c0:c1],
                                               op0=ALU.add, op1=ALU.add)],
                       reads=[kpo, "xT", "pv"], writes=["xT"])

    def final_out(tile):
        for (c0, c1) in tile.subs:
            n = c1 - c0
            pss, kps = PS("mm")
            for cp in range(4):
                sq, ksq = SQ.next()
                OP("act", lambda e, sq=sq, cp=cp, c0=c0, c1=c1, n=n: [
                    e.activation(out=sq[:, :, 0:n], in_=xT[:, 2 * cp:2 * cp + 2, c0:c1], func=AF.Square)],
                   reads=["xT"], writes=[ksq])
                OP("pe", lambda e, sq=sq, cp=cp, pss=pss, n=n: [
                    e.matmul(pss[:, 0:n], lhsT=ones_bf, rhs=sq[:, j, 0:n], start=(cp == 0 and j == 0),
                             stop=(cp == 3 and j == 1)) for j in range(2)],
                   reads=[ksq, "cbf"], writes=[kps])
            rs, krs = RS.next()
            OP("act", lambda e, rs=rs, pss=pss, n=n: [
                e.activation(out=rs[:, 0:n], in_=pss[:, 0:n], func=AF.Ln, bias=pcol("eps"), scale=1.0 / D)],
               reads=[kps, "pv"], writes=[krs])
            OP("act", lambda e, rs=rs, n=n: [e.activation(out=rs[:, 0:n], in_=rs[:, 0:n], func=AF.Exp, scale=-0.5)],
               reads=[krs], writes=[krs])
            OP("dve", lambda e, rs=rs, c0=c0, c1=c1, n=n: [
                e.scalar_tensor_tensor(out=xT[:, c, c0:c1], in0=xT[:, c, c0:c1], scalar=pcol("fin", c), in1=rs[:, 0:n],
                                       op0=ALU.mult, op1=ALU.mult) for c in range(8)],
               reads=["xT", krs, "pv"], writes=["xT"])
        s0 = 0
        nout = tile.Tp
        OP("sp", lambda e: [e.dma_start(out=yT_d[:, :, tile.out0:tile.out0 + nout], in_=xT[:, :, s0:tile.Tp])],
           reads=["xT"], writes=["o_y"], dma=("ost", 1))
        if tile.sample:
            OP("sp", lambda e: [e.dma_start(out=ysT_d[:, :, :], in_=xT[:, :, tile.Tp:tile.T])], reads=["xT"], writes=["o_ys"],
               dma=("ost", 1))

    def prologue():
        OP("sp", lambda e: [e.dma_start(out=pv[:, :], in_=pv_d[:, :]), e.dma_start(out=prow[:, :], in_=prow_d[:, :]),
                            e.dma_start(out=cf[:, :], in_=cf_d[:, :]), e.dma_start(out=cbf[:, :], in_=cbf_d[:, :]),
                            e.dma_start(out=cmask[:, :], in_=cmask_d[:, :]), e.dma_start(out=sconv[:, :, :, :], in_=sconv_d[:, :, :, :])],
           writes=["pv", "prow", "cf", "cbf", "cmask", "sconv"], dma=("cld", 6))
        OP("pool", lambda e: [e.dma_start(out=wdt[:, :, :], in_=wdt_d[:, :].rearrange("p (k n) -> p k n", k=8))], writes=["wdt"],
           dma=("wdtl", 1))
        OP("act", lambda e: [e.activation(out=abc[:, :], in_=prow[:, PR["alog"]:PR["alog"] + 32], func=AF.Exp)], reads=["prow"], writes=["abc"])
        OP("dve", lambda e: [e.tensor_scalar(out=abc[:, :], in0=abc[:, :], scalar1=-1.0, scalar2=None, op0=ALU.mult)], reads=["abc"], writes=["abc"])
        OP("dve", lambda e: [e.tensor_scalar(out=hbias[:, :], in0=cmask[:, :], scalar1=-1.0, scalar2=-NEG, op0=ALU.add, op1=ALU.mult)],
           reads=["cmask"], writes=["hbias"])
        OP("dve", lambda e: [e.memset(hT[:, :], 0.0)], writes=["hT"])
        OP("dve", lambda e: [e.memset(hTb[:, :], 0.0)], writes=["hTb"])
        OP("dve", lambda e: [e.memset(hist[:, :, :], 0.0)], writes=["hist"])
        OP("dve", lambda e: [e.memset(kT[:, :, :], 0.0)], writes=["kT"])
        OP("dve", lambda e: [e.memset(vtm[:, :, :], 0.0)], writes=["vtm"])

    import os
    KSTOP = int(os.environ.get("KSTOP", "100000"))

    def program():
        ph = [0]

        def step():
            ph[0] += 1
            return ph[0] > KSTOP

        def finish(dump=None):
            if dump is not None and KSTOP < 100000:
                t = dump
                OP("sp", lambda e: [e.dma_start(out=yT_d[:, :, 0:t.Tp], in_=xT[:, :, 0:t.Tp])], reads=["xT"], writes=["o_y"], dma=("ost", 1))
            OP("sp", lambda e: [], reads=["o_y", "o_ys", "o_ssmp", "o_convp", "o_convs", "o_ssms", "o_kvp", "o_kvs"])
            if not wstate["dry"]:
                last = {}
                for i_, o_ in enumerate(S.ops[:-1]):
                    last[o_.eng if o_.dma is None else o_.dma[0]] = i_
                for i_ in last.values():
                    S.ops[-1].deps.setdefault(i_, 2)

        prologue()
        if step(): return finish()
        for tile in tiles:
            load_x(tile)
            if step(): return finish(tile)
            ffn(tile, 0)
            barrier()
            if step(): return finish(tile)
            if tile.kind == "pre":
                mixer0(tile, True)
                barrier()
                if step(): return finish(tile)
                continue
            mixer0(tile, False)
            barrier()
            if step(): return finish(tile)
            ffn(tile, 1)
            barrier()
            if step(): return finish(tile)
            kv_proj(tile)
            barrier()
            if step(): return finish(tile)
            if tile.kind == "prefull":
                kv_carry(tile)
                OP("dve", lambda e: [e.tensor_scalar(out=hT[:, :], in0=hT[:, :], scalar1=cmask[:, 0:1], scalar2=None, op0=ALU.mult)],
                   reads=["hT", "cmask"], writes=["hT"])
                OP("act", lambda e: [e.activation(out=hTb[:, :], in_=hT[:, :], func=AF.Copy)], reads=["hT"], writes=["hTb"])
                continue
            ffn(tile, 2)
            barrier()
            if step(): return finish(tile)
            attention(tile)
            barrier()
            if step(): return finish(tile)
            kv_carry(tile)
            ffn(tile, 3)
            barrier()
            if step(): return finish(tile)
            final_out(tile)
            barrier()
            if step(): return finish()
        finish()

    program()
    wstate["dry"] = False
    wstate["i"] = 0
    for k in pctr:
        pctr[k] = 0
    for r_ in (XTM, SEG, MTB, EBC, Y1, YG, SQ, RS, SG, XIN, XINS, CACC, DTT, AG3, TMPF, ACF, AC3, CEP, XW, BM, SM, PN, PT, STAT, CKT, CV):
        r_.i = 0
    program()
    S.finalize()

    with ExitStack() as es2:
        semh = {n: es2.enter_context(nc.semaphore(f"s_{n}")) for n in S.semnames}
        for n in ("pe", "act", "dve", "pool"):
            if n not in semh:
                semh[n] = es2.enter_context(nc.semaphore(f"s_{n}"))
        with nc.Block() as block:
            @block.sync
            def _(e):
                S.emit_engine("sp", e, semh)

            @block.gpsimd
            def _(e):
                S.emit_engine("pool", e, semh)

            @block.tensor
            def _(e):
                S.emit_engine("pe", e, semh)

            @block.scalar
            def _(e):
                S.emit_engine("act", e, semh)

            @block.vector
            def _(e):
                S.emit_engine("dve", e, semh)
    es.close()
    return nc, len(S.ops)


def tile_w(Wm, nb):
    K, N = Wm.shape
    a = Wm.reshape(K // 128, 128, N // nb, nb)
    return np.ascontiguousarray(a.transpose(2, 1, 0, 3)).reshape(N // nb, 128, (K // 128) * nb)


def pad_last(a, n):
    if a.shape[-1] == n:
        return a
    out = np.zeros(a.shape[:-1] + (n,), a.dtype)
    out[..., :a.shape[-1]] = a
    return out


def host_consts():
    cfa = np.zeros((128, NCF), np.float32)
    i = np.arange(128)
    um = (i[:, None] <= i[None, :]).astype(np.float32)
    cfa[:, CF["um"]:CF["um"] + 128] = um
    usf = (i[:, None] > i[None, :]).astype(np.float32)
    j = np.arange(64)
    same = (j[:, None] // 4) == (j[None, :] // 4)
    cfa[:64, CF["ums"]:CF["ums"] + 64] = (same & (j[:, None] <= j[None, :])).astype(np.float32)
    ussf = (same & (j[:, None] > j[None, :])).astype(np.float32)
    mb = np.full((128, 256), NEG, np.float32)
    mb[:, :128][i[None, :] > i[:, None]] = 0.0
    mb[:, 128:][i[None, :] <= i[:, None]] = 0.0
    cfa[:, CF["mb"]:CF["mb"] + 256] = mb
    ms = np.full((16, 128), NEG, np.float32)
    mn = np.full((16, 124), NEG, np.float32)
    for r in range(16):
        t = r % 4
        ms[r, t + 1:128] = 0.0
        mn[r, 60:60 + t + 1] = 0.0
    cfa[:16, CF["ms"]:CF["ms"] + 128] = ms
    cfa[:16, CF["mn"]:CF["mn"] + 124] = mn
    cfa[:64, CF["sm"]:CF["sm"] + 16] = (j[:, None] // 4 == np.arange(16)[None, :]).astype(np.float32)
    cb = np.zeros((128, NCB), np.float32)
    cb[:, :128] = np.eye(128)
    cb[:, 128:256] = 1.0
    cb[:, CB["um"]:CB["um"] + 128] = cfa[:, CF["um"]:CF["um"] + 128]
    cb[:, CB["us"]:CB["us"] + 128] = usf
    cb[:64, CB["ums"]:CB["ums"] + 64] = cfa[:64, CF["ums"]:CF["ums"] + 64]
    cb[:64, CB["uss"]:CB["uss"] + 64] = ussf
    sel = np.zeros((32, 32, 128), np.float32)
    for h in range(32):
        sel[h, h, :] = 1.0
    cb[:32, CB["sel"]:CB["sel"] + 4096] = sel.reshape(32, 4096)
    return cfa, cb.astype(ml_dtypes.bfloat16)


def host_weights(p):
    f32 = np.float32
    out = {}
    wgu = np.zeros((4, 11, 128, WSLOT), f32)
    wdn = np.zeros((4, 8, 128, NF * 128), f32)
    for fi in range(4):
        l, i = fi // 2, fi % 2
        g = tile_w(np.asarray(p["ffn_w_gate"][l, i]), 256)
        u = tile_w(np.asarray(p["ffn_w_up"][l, i]), 256)
        wgu[fi] = np.concatenate([g, u], axis=2)
        wdn[fi] = tile_w(np.asarray(p["ffn_w_down"][l, i]), 128)
    out["wgu"], out["wdn"] = wgu, wdn
    win = np.asarray(p["ssm_w_in"][0])
    out["winz"] = tile_w(win[:, 0:2048], 512)
    out["winx"] = tile_w(win[:, 2048:5120], 512)
    out["wdt"] = np.ascontiguousarray(tile_w(win[:, 5120:5152], 32)[0])
    out["wout"] = tile_w(np.asarray(p["ssm_w_out"][0]), 256)
    out["wkv"] = tile_w(np.asarray(p["attn_w_kv"]), 512)
    perm = np.concatenate([np.arange(QH[c][e] * 64, QH[c][e] * 64 + 64) for c in range(8) for e in range(2)])
    out["wq"] = tile_w(np.asarray(p["attn_w_q"][0])[:, perm], 512)
    out["wo"] = tile_w(np.asarray(p["attn_w_o"][0])[perm, :], 512)
    pvh = np.zeros((128, NPV), f32)
    fm = lambda v: np.asarray(v, f32).reshape(-1, 128).T
    ng = np.asarray(p["norm_gain"])
    for l in range(2):
        for i in range(3):
            pvh[:, PV["g"] + (l * 3 + i) * 8:PV["g"] + (l * 3 + i) * 8 + 8] = fm(ng[l, i])
    pvh[:, PV["kvn"]:PV["kvn"] + 8] = fm(p["kv_norm"])
    pvh[:, PV["fin"]:PV["fin"] + 8] = fm(p["final_norm"])
    cw = np.asarray(p["ssm_conv_w"][0])
    for k in range(4):
        pvh[:, PV["cw"] + k * 24:PV["cw"] + k * 24 + 24] = fm(cw[k])
    pvh[:, PV["cb"]:PV["cb"] + 24] = fm(p["ssm_conv_b"][0])
    pvh[:, PV["dsk"]:PV["dsk"] + 16] = fm(np.repeat(np.asarray(p["ssm_d"][0]), 64))
    pvh[:, PV["ng"]:PV["ng"] + 16] = fm(p["ssm_norm"][0])
    pvh[:, PV["bq"]:PV["bq"] + 8] = fm(np.asarray(p["attn_b_q"][0])[perm])
    bkv = np.asarray(p["attn_b_kv"], f32)
    pvh[:, PV["bk"]:PV["bk"] + 2] = fm(bkv[:256])
    pvh[:, PV["bo"]:PV["bo"] + 8] = fm(p["attn_b_o"][0])
    pvh[:, PV["one"]] = 1.0
    pvh[:, PV["eps"]] = EPS
    out["pv"] = pvh
    pr = np.zeros((128, NPR), f32)
    pr[:, PR["dtb"]:PR["dtb"] + 32] = np.asarray(p["ssm_dt_bias"][0])[None, :]
    pr[:, PR["alog"]:PR["alog"] + 32] = np.asarray(p["ssm_a_log"][0])[None, :]
    pr[:, PR["bv"]:PR["bv"] + 256] = bkv[None, 256:]
    pr[:, PR["bkr"]:PR["bkr"] + 256] = bkv[None, :256]
    sinks = np.asarray(p["attn_sinks"][0], f32)
    pr[:, PR["snk"]:PR["snk"] + 16] = np.array([sinks[QH[c][e]] for c in range(8) for e in range(2)], f32)[None, :]
    out["prow"] = pr
    cfa, cb = host_consts()
    for g in range(4):
        for r in range(16):
            cfa[r, CF["ss"] + g] = sinks[QH[4 * (g // 2) + r // 4][g % 2]]
    out["cf"], out["cbf"] = cfa, cb
    return out


_PROG = {}


def run(inputs, seq, npre, nmain):
    f32 = np.float32
    key = (npre, nmain)
    if key not in _PROG:
        _PROG[key] = build_program(npre, nmain)
    nc, nops = _PROG[key]
    w = host_weights(inputs)
    xp = np.asarray(inputs["x_prompt"], f32)
    xs = np.asarray(inputs["x_sample"], f32)
    sconv = np.asarray(inputs["state_conv"], f32)[0]
    sssm = np.asarray(inputs["state_ssm"], f32)[0]
    ck = np.asarray(inputs["cache_k_win"], f32)
    cv = np.asarray(inputs["cache_v_win"], f32)
    half = seq // 2
    in_maps = []
    for c in range(8):
        s, hf = c // 2, c % 2
        m = dict(w)
        if hf == 0:
            m["xpre"] = np.zeros((D, npre * 128), f32)
        else:
            m["xpre"] = np.ascontiguousarray(xp[s, 0:half].T)
        m["xmain"] = np.ascontiguousarray(xp[s, hf * half:(hf + 1) * half].T)
        m["cmask"] = np.full((128, 1), float(hf), f32)
        b0 = 16 * c
        m["xsmp"] = np.ascontiguousarray(xs[b0:b0 + 16].reshape(64, D).T)
        m["sconv"] = np.ascontiguousarray(sconv[b0:b0 + 16].reshape(16, 3, 24, 128).transpose(3, 2, 0, 1))
        m["sssm"] = np.ascontiguousarray(sssm[b0:b0 + 16].reshape(16, DIN, 128).transpose(0, 2, 1))
        kk = ck[b0:b0 + 16].reshape(16, 128, 2, 2, 64)
        m["ckT"] = np.ascontiguousarray(kk.transpose(0, 3, 4, 2, 1)).reshape(16, 128, 256)
        m["ck"] = np.ascontiguousarray(ck[b0:b0 + 16].reshape(16, 128, 256))
        m["cv"] = np.ascontiguousarray(cv[b0:b0 + 16].reshape(16, 128, 256))
        in_maps.append(m)
    if os.environ.get("KTRACE"):
        res = run_bass_kernel_spmd(nc, in_maps, core_ids=list(range(8)), trace=True)
        print("EXEC_TIME_NS", res.exec_time_ns)
    else:
        res = run_bass_kernel_spmd(nc, in_maps, core_ids=list(range(8)))
    R = res.results
    B = xp.shape[0]
    y_p = np.zeros((B, seq, D), f32)
    conv_p = np.zeros((1, B, 3, 3072), f32)
    ssm_p = np.zeros((1, B, 32, 64, 128), f32)
    k_p = np.zeros((B, 128, 4, 64), f32)
    v_p = np.zeros((B, 128, 4, 64), f32)
    y_s = np.zeros((128, 4, D), f32)
    conv_s = np.zeros((1, 128, 3, 3072), f32)
    ssm_s = np.zeros((1, 128, 32, 64, 128), f32)
    k_s = np.zeros((128, 128, 4, 64), f32)
    v_s = np.zeros((128, 128, 4, 64), f32)
    for c in range(8):
        s, hf = c // 2, c % 2
        r = R[c]
        y_p[s, hf * half:(hf + 1) * half] = r["yT"].T
        b0 = 16 * c
        y_s[b0:b0 + 16] = r["ysT"].T.reshape(16, 4, D)
        conv_s[0, b0:b0 + 16] = r["convs"].transpose(2, 3, 1, 0).reshape(16, 3, 3072)
        ssm_s[0, b0:b0 + 16] = r["ssms"].transpose(0, 2, 1).reshape(16, 32, 64, 128)
        k_s[b0:b0 + 16] = r["kws"].reshape(16, 128, 4, 64)
        v_s[b0:b0 + 16] = r["vws"].reshape(16, 128, 4, 64)
        if hf == 1:
            conv_p[0, s] = r["convp"].transpose(2, 1, 0).reshape(3, 3072)
            ssm_p[0, s] = r["ssmp"].T.reshape(32, 64, 128)
            k_p[s] = r["kwp"].reshape(128, 4, 64)
            v_p[s] = r["vwp"].reshape(128, 4, 64)
    return (y_p, y_s, conv_p, ssm_p, k_p, v_p, conv_s, ssm_s, k_s, v_s)


def kernel(**inputs):
    seq = int(np.asarray(inputs["x_prompt"]).shape[1])
    nchunks = seq // 128
    return run(inputs, seq, nchunks // 2, nchunks // 2)
```
